# Optimizing a Trainium2 kernel written in Bass

```python
import math
import jax, jax.numpy as jnp
from jax import lax
import numpy as np

D_MODEL = 1024
BATCH = 32
SEQ = 2048
DEPTH = 4
DEC_BATCH = 16
DEC_SEQ = 32
PAST_LEN = 1024

CHUNK = 64
Q_BLOCK = 128
N_MIXERS = 2
N_SSD = (DEPTH + 1) // 2
N_SB = DEPTH // 2
SSD_EXPAND = 2
D_INNER = SSD_EXPAND * D_MODEL
SSD_HEADDIM = 64
SSD_HEADS = D_INNER // SSD_HEADDIM
SSD_GROUPS = 8
SSD_HPG = SSD_HEADS // SSD_GROUPS
SSD_STATE = 128
SSD_CONV = 4
CONV_DIM = D_INNER + 2 * SSD_GROUPS * SSD_STATE
D_IN_PROJ = 2 * D_INNER + 2 * SSD_GROUPS * SSD_STATE + SSD_HEADS
SB_HEADS = 16
SB_HEADDIM = D_MODEL // SB_HEADS
SB_SCALE = 1.0 / math.sqrt(SB_HEADDIM)
D_FF = 4 * D_MODEL
EPS = 1e-6

kernel_name = 'hybrid_ssd_stickbreaking_stream_step'


def rms_norm(x, g):
    x32 = x.astype(jnp.float32)
    y = x32 * lax.rsqrt(jnp.mean(x32 * x32, axis=-1, keepdims=True) + EPS)
    return (y * g.astype(jnp.float32)).astype(x.dtype)


def sq_relu_mlp(h, w_up, w_down):
    return jnp.square(jax.nn.relu(h @ w_up)) @ w_down


def causal_conv(xbc, conv_state, w, b):
    seqlen = xbc.shape[1]
    xpad = jnp.concatenate([conv_state.astype(xbc.dtype), xbc], axis=1)
    out = b
    for tap in range(SSD_CONV):
        out = out + w[tap] * xpad[:, tap:tap + seqlen]
    return jax.nn.silu(out), xpad[:, -(SSD_CONV - 1):]


def ssd_chunk_scan(x, dt, a, bm, cm, h0):
    bsz, seqlen = x.shape[0], x.shape[1]
    clen = min(CHUNK, seqlen)
    nc = seqlen // clen
    f32 = jnp.float32

    def to_chunks(t):
        return jnp.moveaxis(t.reshape((bsz, nc, clen) + t.shape[2:]), 1, 0)

    xdt = (x.astype(f32) * dt[..., None]).reshape(bsz, seqlen, SSD_GROUPS, SSD_HPG, SSD_HEADDIM)
    a_dt = (dt * a).reshape(bsz, seqlen, SSD_GROUPS, SSD_HPG)
    causal = jnp.tril(jnp.ones((clen, clen), dtype=bool))

    def step(h, inp):
        xdt_c, adt_c, b_c, c_c = inp
        a_cs = jnp.cumsum(adt_c, axis=1)
        seg = a_cs[:, :, None] - a_cs[:, None, :]
        decay = jnp.exp(jnp.where(causal[None, :, :, None, None], seg, -jnp.inf))
        cb = jnp.einsum('btgn,bsgn->btsg', c_c, b_c)
        y_diag = jnp.einsum('btsg,btsgr,bsgrp->btgrp', cb, decay, xdt_c)
        y_off = jnp.einsum('btgn,bgrpn,btgr->btgrp', c_c, h, jnp.exp(a_cs))
        decay_end = jnp.exp(a_cs[:, -1:] - a_cs)
        h_new = h * jnp.exp(a_cs[:, -1])[..., None, None] + jnp.einsum(
            'bsgn,bsgr,bsgrp->bgrpn', b_c, decay_end, xdt_c)
        return h_new, y_diag + y_off

    h_init = h0.astype(f32).reshape(bsz, SSD_GROUPS, SSD_HPG, SSD_HEADDIM, SSD_STATE)
    h_fin, y = lax.scan(step, h_init, (to_chunks(xdt), to_chunks(a_dt),
                                       to_chunks(bm.astype(f32)), to_chunks(cm.astype(f32))))
    y = jnp.moveaxis(y, 0, 1).reshape(bsz, seqlen, SSD_HEADS, SSD_HEADDIM)
    return y, h_fin.reshape(bsz, SSD_HEADS, SSD_HEADDIM, SSD_STATE)


def ssd_mixer(u, conv_state, ssm_state, in_w, conv_w, conv_b, dt_bias, a_log, d_skip, norm_w, out_w):
    bsz, seqlen, _ = u.shape
    f32 = jnp.float32
    zxbcdt = u @ in_w
    z, xbc, dt_raw = jnp.split(zxbcdt, [D_INNER, D_INNER + CONV_DIM], axis=-1)
    xbc, new_conv = causal_conv(xbc, conv_state, conv_w, conv_b)
    xs, bm, cm = jnp.split(xbc, [D_INNER, D_INNER + SSD_GROUPS * SSD_STATE], axis=-1)
    dt = jax.nn.softplus((dt_raw + dt_bias).astype(f32))
    a = -jnp.exp(a_log.astype(f32))
    x_h = xs.reshape(bsz, seqlen, SSD_HEADS, SSD_HEADDIM)
    y, h_new = ssd_chunk_scan(x_h, dt, a,
                              bm.reshape(bsz, seqlen, SSD_GROUPS, SSD_STATE),
                              cm.reshape(bsz, seqlen, SSD_GROUPS, SSD_STATE), ssm_state)
    y = y + d_skip.astype(f32)[:, None] * x_h.astype(f32)
    y = y.reshape(bsz, seqlen, D_INNER) * jax.nn.silu(z.astype(f32))
    yg = y.reshape(bsz, seqlen, SSD_GROUPS, D_INNER // SSD_GROUPS)
    yg = yg * lax.rsqrt(jnp.mean(yg * yg, axis=-1, keepdims=True) + EPS)
    y = (yg.reshape(bsz, seqlen, D_INNER) * norm_w.astype(f32)).astype(u.dtype)
    return y @ out_w, new_conv, h_new.astype(u.dtype)


def sb_project(u, qkv_w, q_gain, k_gain):
    q, k, v = jnp.split(u @ qkv_w, 3, axis=-1)
    shape = u.shape[:2] + (SB_HEADS, SB_HEADDIM)
    return rms_norm(q.reshape(shape), q_gain), rms_norm(k.reshape(shape), k_gain), v.reshape(shape)


def sb_attend(q, k, v, q_pos, k_pos):
    z = jnp.einsum('bqhd,bkhd->bhqk', q, k).astype(jnp.float32) * SB_SCALE
    mask = k_pos[None, :] < q_pos[:, None]
    log_1mb = jnp.where(mask, jax.nn.log_sigmoid(-z), 0.0)
    rev = lax.cumsum(log_1mb, axis=3, reverse=True)
    after = jnp.concatenate([rev[..., 1:], jnp.zeros_like(rev[..., :1])], axis=-1)
    w = jnp.where(mask, jnp.exp(jax.nn.log_sigmoid(z) + after), 0.0)
    return jnp.einsum('bhqk,bkhd->bqhd', w, v.astype(jnp.float32)).astype(v.dtype)


def sb_prompt(q, k, v):
    outs = []
    for blk in range(q.shape[1] // Q_BLOCK):
        start, end = blk * Q_BLOCK, (blk + 1) * Q_BLOCK
        outs.append(sb_attend(q[:, start:end], k[:, :end], v[:, :end],
                              jnp.arange(start, end), jnp.arange(end)))
    return jnp.concatenate(outs, axis=1)


def setup_inputs(seed: int = 0) -> dict:
    key = jax.random.key(seed)
    ks = jax.random.split(key, 24)
    f32 = jnp.float32

    def nrm(k, shape, scale):
        return scale * jax.random.normal(k, shape, f32)

    dt_init = jnp.exp(jax.random.uniform(ks[10], (N_SSD, SSD_HEADS), f32,
                                         math.log(1e-3), math.log(1e-1)))
    dt_bias = dt_init + jnp.log(-jnp.expm1(-dt_init))
    a_log = jnp.log(jax.random.uniform(ks[11], (N_SSD, SSD_HEADS), f32, 1.0, 16.0))
    return {
        'x_prompt': nrm(ks[0], (BATCH, SEQ, D_MODEL), 1.0),
        'x_sample': nrm(ks[1], (DEC_BATCH, DEC_SEQ, D_MODEL), 1.0),
        'state_ssm': nrm(ks[2], (N_SSD, DEC_BATCH, SSD_HEADS, SSD_HEADDIM, SSD_STATE), 0.1),
        'state_conv': nrm(ks[3], (N_SSD, DEC_BATCH, SSD_CONV - 1, CONV_DIM), 1.0),
        'cache_k': nrm(ks[4], (N_SB, DEC_BATCH, PAST_LEN, SB_HEADS, SB_HEADDIM), 1.0),
        'cache_v': nrm(ks[5], (N_SB, DEC_BATCH, PAST_LEN, SB_HEADS, SB_HEADDIM), 1.0),
        'norm_mix': 1.0 + nrm(ks[6], (DEPTH, D_MODEL), 0.01),
        'norm_mlp': 1.0 + nrm(ks[7], (DEPTH, D_MODEL), 0.01),
        'ssd_in_w': nrm(ks[8], (N_SSD, D_MODEL, D_IN_PROJ), D_MODEL ** -0.5),
        'ssd_conv_w': nrm(ks[9], (N_SSD, SSD_CONV, CONV_DIM), SSD_CONV ** -0.5),
        'ssd_conv_b': nrm(ks[12], (N_SSD, CONV_DIM), 0.01),
        'ssd_dt_bias': dt_bias,
        'ssd_a_log': a_log,
        'ssd_d': 1.0 + nrm(ks[13], (N_SSD, SSD_HEADS), 0.01),
        'ssd_norm_w': 1.0 + nrm(ks[14], (N_SSD, D_INNER), 0.01),
        'ssd_out_w': nrm(ks[15], (N_SSD, D_INNER, D_MODEL), D_INNER ** -0.5),
        'sb_qkv_w': nrm(ks[16], (N_SB, D_MODEL, 3 * D_MODEL), D_MODEL ** -0.5),
        'sb_q_gain': 1.0 + nrm(ks[17], (N_SB, SB_HEADDIM), 0.01),
        'sb_k_gain': 1.0 + nrm(ks[18], (N_SB, SB_HEADDIM), 0.01),
        'sb_out_w': nrm(ks[19], (N_SB, D_MODEL, D_MODEL), D_MODEL ** -0.5),
        'mlp_up': nrm(ks[20], (DEPTH, D_MODEL, D_FF), D_MODEL ** -0.5),
        'mlp_down': nrm(ks[21], (DEPTH, D_FF, D_MODEL), 0.5 * D_FF ** -0.5),
    }


def reference(x_prompt, x_sample, state_ssm, state_conv, cache_k, cache_v,
              norm_mix, norm_mlp, ssd_in_w, ssd_conv_w, ssd_conv_b, ssd_dt_bias, ssd_a_log,
              ssd_d, ssd_norm_w, ssd_out_w, sb_qkv_w, sb_q_gain, sb_k_gain, sb_out_w,
              mlp_up, mlp_down):
    bsz_p, seq_p = x_prompt.shape[0], x_prompt.shape[1]
    bsz_s, seq_s = x_sample.shape[0], x_sample.shape[1]
    past = cache_k.shape[2]
    y_p, y_s = x_prompt, x_sample
    p_ssm, p_conv, p_k, p_v = [], [], [], []
    s_ssm, s_conv, s_k, s_v = [], [], [], []
    for i in range(DEPTH):
        j = i // N_MIXERS
        h_p = rms_norm(y_p, norm_mix[i])
        h_s = rms_norm(y_s, norm_mix[i])
        if i % N_MIXERS == 0:
            params = (ssd_in_w[j], ssd_conv_w[j], ssd_conv_b[j], ssd_dt_bias[j], ssd_a_log[j],
                      ssd_d[j], ssd_norm_w[j], ssd_out_w[j])
            conv0 = jnp.zeros((bsz_p, SSD_CONV - 1, CONV_DIM), x_prompt.dtype)
            ssm0 = jnp.zeros((bsz_p, SSD_HEADS, SSD_HEADDIM, SSD_STATE), jnp.float32)
            m_p, c_p, st_p = ssd_mixer(h_p, conv0, ssm0, *params)
            m_s, c_s, st_s = ssd_mixer(h_s, state_conv[j], state_ssm[j], *params)
            p_ssm.append(st_p); p_conv.append(c_p)
            s_ssm.append(st_s); s_conv.append(c_s)
        else:
            q_p, k_p, v_p = sb_project(h_p, sb_qkv_w[j], sb_q_gain[j], sb_k_gain[j])
            o_p = sb_prompt(q_p, k_p, v_p)
            m_p = o_p.reshape(bsz_p, seq_p, D_MODEL) @ sb_out_w[j]
            q_s, k_s, v_s = sb_project(h_s, sb_qkv_w[j], sb_q_gain[j], sb_k_gain[j])
            k_all = jnp.concatenate([cache_k[j].astype(k_s.dtype), k_s], axis=1)
            v_all = jnp.concatenate([cache_v[j].astype(v_s.dtype), v_s], axis=1)
            o_s = sb_attend(q_s, k_all, v_all, past + jnp.arange(seq_s), jnp.arange(past + seq_s))
            m_s = o_s.reshape(bsz_s, seq_s, D_MODEL) @ sb_out_w[j]
            p_k.append(k_p); p_v.append(v_p)
            s_k.append(k_s); s_v.append(v_s)
        y_p = y_p + m_p
        y_s = y_s + m_s
        y_p = y_p + sq_relu_mlp(rms_norm(y_p, norm_mlp[i]), mlp_up[i], mlp_down[i])
        y_s = y_s + sq_relu_mlp(rms_norm(y_s, norm_mlp[i]), mlp_up[i], mlp_down[i])
    prompt_state_ssm = jnp.stack(p_ssm)
    prompt_state_conv = jnp.stack(p_conv)
    prompt_k = jnp.stack(p_k)
    prompt_v = jnp.stack(p_v)
    sample_state_ssm = jnp.stack(s_ssm)
    sample_state_conv = jnp.stack(s_conv)
    sample_k = jnp.stack(s_k)
    sample_v = jnp.stack(s_v)
    return (y_p, y_s, prompt_state_ssm, prompt_state_conv, prompt_k, prompt_v,
            sample_state_ssm, sample_state_conv, sample_k, sample_v)
```

```python
import numpy as np
from contextlib import ExitStack
import concourse.bass as bass
import concourse.mybir as mybir
from concourse.bass_utils import run_bass_kernel_spmd

F32 = mybir.dt.float32
BF16 = mybir.dt.bfloat16
AF = mybir.ActivationFunctionType
ALU = mybir.AluOpType
AX = mybir.AxisListType

D = 1024
DFF = 4096
DIN = 2048
NH = 32
HP = 64
NG = 8
NST = 128
CONVD = 4096
DPROJ = 6176
AH = 16
AD = 64
EPS = 1e-6
import os as _os
TRIM = _os.environ.get('K_TRIM', '1') == '1'


class Buf:
    __slots__ = ("name", "lw", "rd")

    def __init__(s, name):
        s.name = name
        s.lw = None
        s.rd = {}


class Prog:
    ENG = ("pe", "act", "dve", "pool", "sp")

    def __init__(s, nc, es):
        s.nc = nc
        s.es = es
        s.sems = {}
        s.cnt = {}
        s.ops = {e: [] for e in s.ENG}
        s.waited = {e: {} for e in s.ENG}
        s.nbuf = 0
        s.nops = 0
        s.free = []
        s.stage_keys = []

    def buf(s, name=None):
        s.nbuf += 1
        return Buf((name or "b") + "_%d" % s.nbuf)

    def sem(s, key):
        if key not in s.sems:
            if key in s.ENG or not s.free:
                s.sems[key] = s.es.enter_context(s.nc.semaphore("s%d" % len(s.sems)))
                s.cnt[key] = 0
            else:
                h, c0 = s.free.pop()
                s.sems[key] = h
                s.cnt[key] = c0
            if key not in s.ENG:
                s.stage_keys.append(key)
        return s.sems[key]

    def retire(s):
        for k in s.stage_keys:
            s.free.append((s.sems[k], s.cnt[k]))
        s.stage_keys = []

    def _waits(s, eng, deps):
        w = s.waited[eng]
        best = {}
        for d in deps:
            if d is None:
                continue
            k, v = d
            if w.get(k, 0) >= v:
                continue
            if best.get(k, 0) < v:
                best[k] = v
        out = []
        for k, v in best.items():
            w[k] = v
            out.append((k, v))
        return out

    def op(s, eng, name, kw, reads=(), writes=(), inc=True):
        deps = []
        for b in reads:
            deps.append(b.lw)
        for b in writes:
            deps.append(b.lw)
            for k, v in b.rd.items():
                if k != eng:
                    deps.append((k, v))
        if eng == "pe":
            deps = [d for d in deps if d is not None and d[0] != "pe"]
        waits = s._waits(eng, deps)
        s.sem(eng)
        val = s.cnt[eng] + 1
        if inc:
            s.cnt[eng] = val
        for b in reads:
            if b.rd.get(eng, 0) < val:
                b.rd[eng] = val
        for b in writes:
            b.lw = (eng, val)
            b.rd = {}
        s.ops[eng].append((waits, name, kw, (eng, 1) if inc else None))
        s.nops += 1

    def dma(s, q, out_ap, in_ap, dst, src, kind, **kw):
        key = ("ld:" + dst.name) if kind == "ld" else ("st:" + src.name)
        s.sem(key)
        deps = [src.lw, dst.lw] + list(dst.rd.items())
        if s.cnt[key] > 0:
            deps.append((key, s.cnt[key]))
        waits = s._waits(q, deps)
        val = s.cnt[key] + 16
        s.cnt[key] = val
        if src.rd.get(key, 0) < val:
            src.rd[key] = val
        dst.lw = (key, val)
        dst.rd = {}
        k2 = dict(out=out_ap, in_=in_ap)
        k2.update(kw)
        s.ops[q].append((waits, "dma_start", k2, (key, 16)))
        s.nops += 1

    def barrier(s):
        for e in s.ENG:
            deps = [(k, v) for k, v in s.cnt.items() if v > 0 and k != e]
            waits = s._waits(e, deps)
            if waits:
                s.ops[e].append((waits, None, None, None))

    def emit(s):
        nc = s.nc
        with nc.Block() as block:
            def mk(eng):
                def f(e):
                    for waits, name, kw, inc in s.ops[eng]:
                        for k, v in waits:
                            e.wait_ge(s.sems[k], v)
                        if name is not None:
                            ins = getattr(e, name)(**kw)
                            if inc:
                                ins.then_inc(s.sems[inc[0]], inc[1])
                return f
            block.tensor(mk("pe"))
            block.scalar(mk("act"))
            block.vector(mk("dve"))
            block.gpsimd(mk("pool"))
            block.sync(mk("sp"))


class T:
    __slots__ = ("t", "b")

    def __init__(s, t, b):
        s.t = t
        s.b = b

    def __getitem__(s, k):
        return s.t[k]


class Ring:
    def __init__(s, items):
        s.items = items
        s.i = 0

    def next(s):
        it = s.items[s.i % len(s.items)]
        s.i += 1
        return it


class Bank:
    __slots__ = ("f32", "bf", "b")

    def __init__(s, t, b):
        s.f32 = t
        s.bf = t[:].bitcast(BF16)
        s.b = b


def build(NB, SEQ, NS, DSEQ, PAST, DEPTH, only=None):
    nc = bass.Bass("TRN2", target_bir_lowering=False)
    NTP = NB * SEQ
    NTS = NS * DSEQ
    NTOK = NTP + NTS
    N_SSD = (DEPTH + 1) // 2
    N_SB = DEPTH // 2
    NSB1 = max(N_SB, 1)
    KTOT = PAST + DSEQ
    KPAD = ((KTOT + 127) // 128) * 128

    def di(n, sh, dt=F32):
        return nc.dram_tensor(n, list(sh), dt, kind="ExternalInput").ap()

    def do(n, sh):
        return nc.dram_tensor(n, list(sh), F32, kind="ExternalOutput").ap()

    def dx(n, sh, dt=F32):
        return nc.dram_tensor(n, list(sh), dt).ap()

    x_all = di("x_all", [NTOK, D])
    state_ssm = di("state_ssm", [N_SSD, NS, NH * HP, NST])
    state_conv = di("state_conv", [N_SSD, NS, 128, 32, 3])
    cache_k = di("cache_k", [NSB1, NS, PAST, D])
    cache_v = di("cache_v", [NSB1, NS, PAST, D])
    norm_mix = di("norm_mix", [DEPTH, D])
    norm_mlp = di("norm_mlp", [DEPTH, D])
    ssd_in_w = di("ssd_in_w", [N_SSD, D, DPROJ])
    ssd_conv_w = di("ssd_conv_w", [N_SSD, 128, 32, 4])
    ssd_conv_b = di("ssd_conv_b", [N_SSD, 128, 32])
    ssd_dt_bias = di("ssd_dt_bias", [N_SSD, NH])
    ssd_a_log = di("ssd_a_log", [N_SSD, NH])
    ssd_d = di("ssd_d", [N_SSD, NH])
    ssd_norm_w = di("ssd_norm_w", [N_SSD, DIN])
    ssd_out_w = di("ssd_out_w", [N_SSD, DIN, D])
    sb_qkv_w = di("sb_qkv_w", [NSB1, D, 3 * D])
    sb_q_gain = di("sb_q_gain", [NSB1, AD])
    sb_k_gain = di("sb_k_gain", [NSB1, AD])
    sb_out_w = di("sb_out_w", [NSB1, D, D])
    mlp_up = di("mlp_up", [DEPTH, D, DFF])
    mlp_down = di("mlp_down", [DEPTH, DFF, D])
    c_ident = di("c_ident", [128, 128])
    c_tri = di("c_tri", [128, 128])
    c_su = di("c_su", [128, 128])
    c_mask = di("c_mask", [128, 4, 512])
    c_masks = di("c_masks", [128, 32])

    y_all = do("y_all", [NTOK, D])
    o_pssm = do("o_pssm", [N_SSD, NB, NH * HP, NST])
    o_pconv = do("o_pconv", [N_SSD, NB, 128, 32, 3])
    o_pk = do("o_pk", [NSB1, NB, SEQ, D])
    o_pv = do("o_pv", [NSB1, NB, SEQ, D])
    o_sssm = do("o_sssm", [N_SSD, NS, NH * HP, NST])
    o_sconv = do("o_sconv", [N_SSD, NS, 128, 32, 3])
    o_sk = do("o_sk", [NSB1, NS, DSEQ, D])
    o_sv = do("o_sv", [NSB1, NS, DSEQ, D])

    xs = dx("xs", [NTOK, D])
    Ys = dx("Ys", [NTOK, DIN])
    QTp = dx("QTp", [NB, 128, 8, SEQ], BF16)
    KTp = dx("KTp", [NB, 128, 8, SEQ], BF16)
    Vp = dx("Vp", [NB, SEQ, D], BF16)
    QTs = dx("QTs", [NS, 128, 8, DSEQ], BF16)
    KTs = dx("KTs", [NS, 128, 8, KPAD], BF16)
    Vs = dx("Vs", [NS, DSEQ, D], BF16)

    top = ExitStack()
    with top:
        P = Prog(nc, top)
        dbufs = {}

        def db(name, idx=0):
            k = (name, idx)
            if k not in dbufs:
                dbufs[k] = P.buf("d_" + name)
            return dbufs[k]

        cst = db("const")

        banks = [Bank(top.enter_context(nc.psum_tensor("ps%d" % i, [128, 512], F32)), P.buf("ps%d" % i))
                 for i in range(8)]
        bstate = {"i": 0, "held": set()}

        def pb():
            while True:
                i = bstate["i"] % 8
                bstate["i"] += 1
                if i not in bstate["held"]:
                    return banks[i]

        def hold():
            b = pb()
            bstate["held"].add(banks.index(b))
            return b

        def release(b):
            bstate["held"].discard(banks.index(b))

        cur = {"es": None, "n": 0}

        def sb(shape, dt, name=None):
            cur["n"] += 1
            nm = (name or "t") + "_%d" % cur["n"]
            t = cur["es"].enter_context(nc.sbuf_tensor(nm, list(shape), dt))
            return T(t, P.buf(nm))

        def ring(n, shape, dt, name=None):
            return Ring([sb(shape, dt, name) for _ in range(n)])

        tiles = []
        for b in range(NB):
            for i in range(SEQ // 128):
                tiles.append(dict(row0=b * SEQ + i * 128, nt=128, grp="p", seq=b, pos=i * 128))
        for b in range(NS):
            tiles.append(dict(row0=NTP + b * DSEQ, nt=DSEQ, grp="s", seq=b, pos=0))

        def load_consts():
            c = {}
            c["identb"] = sb([128, 128], BF16, "identb")
            P.dma("pool", c["identb"][:], c_ident[:, :], c["identb"].b, cst, "ld")
            return c

        def front(C, src, tl, gB, dstT, col0, rings):
            nt = tl["nt"]
            r0 = tl["row0"]
            xin = rings["xin"].next()
            P.dma("sp", xin[:nt, :], src[r0:r0 + nt, :], xin.b, db(src.name, r0), "ld")
            ss = rings["ss"].next()
            junk = rings["junk"].next()
            P.op("act", "activation", dict(out=junk[:nt, :], in_=xin[:nt, :], func=AF.Square, accum_out=ss[:nt, 0:1]),
                 [xin.b], [junk.b, ss.b])
            P.op("act", "activation", dict(out=ss[:nt, 2:3], in_=ss[:nt, 0:1], func=AF.Ln, scale=1.0 / D, bias=EPS),
                 [ss.b], [ss.b])
            P.op("act", "activation", dict(out=ss[:nt, 3:4], in_=ss[:nt, 2:3], func=AF.Exp, scale=-0.5), [ss.b], [ss.b])
            xn = rings["xn"].next()
            P.op("dve", "scalar_tensor_tensor", dict(out=xn[:nt, :], in0=xin[:nt, :], scalar=ss[:nt, 3:4], in1=gB[:nt, :],
                                                     op0=ALU.mult, op1=ALU.mult), [xin.b, ss.b, gB.b], [xn.b])
            bk = pb()
            for kc in range(8):
                P.op("pe", "transpose", dict(out=bk.bf[:, kc * 128:kc * 128 + nt], in_=xn[:nt, kc * 128:(kc + 1) * 128],
                                             identity=C["identb"][:nt, :nt]), [xn.b, C["identb"].b], [bk.b], inc=(kc == 7))
            P.op("act", "activation", dict(out=dstT[:, :, col0:col0 + nt],
                                           in_=bk.bf.rearrange("p (k t) -> p k t", k=8)[:, :, :nt], func=AF.Copy),
                 [bk.b], [dstT.b])
            return xin

        def load_w(dst, dram_ap, rows_per_dma=128):
            K, N = dram_ap.shape
            for kc in range(K // 128):
                for n0 in range(0, N, 2048):
                    n1 = min(N, n0 + 2048)
                    P.dma("pool", dst[:, kc, n0:n1], dram_ap[kc * 128:(kc + 1) * 128, n0:n1], dst.b, cst, "ld")

        def load_bcast(dst, dram_1d):
            P.dma("sp", dst[:], dram_1d.partition_broadcast(128), dst.b, cst, "ld")

        def begin_stage():
            P.barrier()
            cur["es"] = ExitStack()
            cur["es"].__enter__()

        def end_stage():
            P.barrier()
            P.retire()
            cur["es"].__exit__(None, None, None)
            cur["es"] = None

        def run_streams(items, make_gen, width=2, stagger=0):
            active = []
            free_lanes = list(range(width))
            it = iter(items)
            pending = [True]
            first = [True]

            def start_more():
                while free_lanes and pending[0]:
                    try:
                        x = next(it)
                    except StopIteration:
                        pending[0] = False
                        break
                    lane = free_lanes.pop(0)
                    g = make_gen(x, lane)
                    active.append((lane, g))
                    if first[0]:
                        first[0] = False
                        for _ in range(stagger):
                            try:
                                next(g)
                            except StopIteration:
                                active.remove((lane, g))
                                free_lanes.append(lane)
                                break
            while True:
                start_more()
                if not active:
                    break
                for (lane, g) in list(active):
                    try:
                        next(g)
                    except StopIteration:
                        active.remove((lane, g))
                        free_lanes.append(lane)

        def stage_copy(src, dst):
            begin_stage()
            rg = ring(2, [128, D], F32, "cp")
            for tl in tiles:
                t = rg.next()
                nt, r0 = tl["nt"], tl["row0"]
                P.dma("sp", t[:nt, :], src[r0:r0 + nt, :], t.b, db(src.name, r0), "ld")
                P.dma("pool", dst[r0:r0 + nt, :], t[:nt, :], db(dst.name, r0), t.b, "st")
            end_stage()

        def stage_mlp(layer, src, dst):
            begin_stage()
            C = load_consts()
            Wup = sb([128, 8, DFF], BF16, "Wup")
            Wdn = sb([128, 32, D], BF16, "Wdn")
            load_w(Wup, mlp_up[layer])
            load_w(Wdn, mlp_down[layer])
            gB = sb([128, D], F32, "gB")
            load_bcast(gB, norm_mlp[layer])
            rings = dict(xin=ring(4, [128, D], F32, "xin"), ss=ring(4, [128, 4], F32, "ss"),
                         junk=ring(1, [128, D], BF16, "junk"), xn=ring(2, [128, D], BF16, "xn"))
            xnTs = ring(2, [128, 8, 256], BF16, "xnT")
            hT = sb([128, 32, 256], BF16, "hT")
            rr = ring(2, [128, 256], F32, "relu")
            yst = ring(2, [128, D], F32, "yst")
            macros = []
            curm = []
            tot = 0
            for tl in tiles:
                if tot + tl["nt"] > 256 or (curm and curm[-1]["grp"] != tl["grp"]):
                    macros.append(curm)
                    curm = []
                    tot = 0
                curm.append(tl)
                tot += tl["nt"]
            if curm:
                macros.append(curm)

            def do_front(m):
                xnT = xnTs.next()
                col = 0
                ent = []
                for tl in m:
                    xin = front(C, src, tl, gB, xnT, col, rings)
                    ent.append((xin, col, tl))
                    col += tl["nt"]
                return xnT, ent, col

            nxt = do_front(macros[0])
            for mi, m in enumerate(macros):
                xnT, ent, NT = nxt
                for f in range(32):
                    bk = pb()
                    for kc in range(8):
                        P.op("pe", "matmul", dict(out=bk.f32[:, :NT], lhsT=Wup[:, kc, f * 128:(f + 1) * 128],
                                                  rhs=xnT[:, kc, :NT], start=(kc == 0), stop=(kc == 7)),
                             [Wup.b, xnT.b], [bk.b], inc=(kc == 7))
                    r = rr.next()
                    P.op("act", "activation", dict(out=r[:, :NT], in_=bk.f32[:, :NT], func=AF.Relu), [bk.b], [r.b])
                    P.op("dve", "tensor_tensor", dict(out=hT[:, f, :NT], in0=r[:, :NT], in1=r[:, :NT], op=ALU.mult),
                         [r.b], [hT.b])
                if mi + 1 < len(macros):
                    nxt = do_front(macros[mi + 1])
                for (xin, col, tl) in ent:
                    nt, r0 = tl["nt"], tl["row0"]
                    y = yst.next()
                    for c in range(2):
                        bk = pb()
                        for f in range(32):
                            P.op("pe", "matmul", dict(out=bk.f32[:nt, :], lhsT=hT[:, f, col:col + nt],
                                                      rhs=Wdn[:, f, c * 512:(c + 1) * 512], start=(f == 0), stop=(f == 31)),
                                 [hT.b, Wdn.b], [bk.b], inc=(f == 31))
                        P.op("dve", "tensor_tensor", dict(out=y[:nt, c * 512:(c + 1) * 512], in0=bk.f32[:nt, :],
                                                          in1=xin[:nt, c * 512:(c + 1) * 512], op=ALU.add),
                             [bk.b, xin.b], [y.b])
                    P.dma("pool", dst[r0:r0 + nt, :], y[:nt, :], db(dst.name, r0), y.b, "st")
            end_stage()

        def run_sched(lane_queues):
            events = set()
            n = len(lane_queues)
            curg = [None] * n
            idx = [0] * n
            waiting = [None] * n
            while True:
                progressed = False
                alive = False
                for li, q in enumerate(lane_queues):
                    if curg[li] is None:
                        if idx[li] < len(q):
                            curg[li] = q[idx[li]]()
                            idx[li] += 1
                            waiting[li] = None
                        else:
                            continue
                    alive = True
                    if waiting[li] is not None:
                        if waiting[li] in events:
                            waiting[li] = None
                        else:
                            continue
                    try:
                        r = next(curg[li])
                        progressed = True
                        while isinstance(r, tuple) and r[0] == "set":
                            events.add(r[1])
                            r = next(curg[li])
                        if isinstance(r, tuple) and r[0] == "wait" and r[1] not in events:
                            waiting[li] = r[1]
                    except StopIteration:
                        curg[li] = None
                        progressed = True
                if not alive:
                    break
                assert progressed, "emission scheduler deadlock"

        def stage_ssd1(layer, j, src):
            begin_stage()
            C = load_consts()
            Wx = sb([128, 8, CONVD + NH], BF16, "Wx")
            for kc in range(8):
                for n0 in range(0, CONVD + NH, 2048):
                    n1 = min(CONVD + NH, n0 + 2048)
                    P.dma("pool", Wx[:, kc, n0:n1], ssd_in_w[j, kc * 128:(kc + 1) * 128, DIN + n0:DIN + n1], Wx.b, cst, "ld")
            gB = sb([128, D], F32, "gB")
            load_bcast(gB, norm_mix[layer])
            trib = sb([128, 128], BF16, "trib")
            sub = sb([128, 128], BF16, "sub")
            onesb = sb([128, 128], BF16, "onesb")
            identf = sb([128, 128], F32, "identf")
            P.dma("pool", trib[:], c_tri[:, :], trib.b, cst, "ld")
            P.dma("pool", sub[:], c_su[:, :], sub.b, cst, "ld")
            P.dma("sp", identf[:], c_ident[:, :], identf.b, cst, "ld")
            P.op("dve", "memset", dict(ap=onesb[:], constant=1.0), [], [onesb.b])
            cw = sb([128, 32, 4], F32, "cw")
            cbias = sb([128, 32], F32, "cbias")
            P.dma("sp", cw[:], ssd_conv_w[j], cw.b, cst, "ld")
            P.dma("sp", cbias[:], ssd_conv_b[j], cbias.b, cst, "ld")
            Abc = sb([128, NH], F32, "Abc")
            dtb = sb([128, NH], F32, "dtb")
            Dbc = sb([128, NH], F32, "Dbc")
            load_bcast(Abc, ssd_a_log[j])
            load_bcast(dtb, ssd_dt_bias[j])
            load_bcast(Dbc, ssd_d[j])
            P.op("act", "activation", dict(out=Abc[:], in_=Abc[:], func=AF.Exp), [Abc.b], [Abc.b])
            P.op("dve", "tensor_scalar", dict(out=Abc[:], in0=Abc[:], scalar1=-1.0, scalar2=None, op0=ALU.mult),
                 [Abc.b], [Abc.b])
            DI = sb([128, NH, 128], BF16, "DI")
            for h in range(NH):
                P.op("dve", "tensor_scalar", dict(out=DI[:, h, :], in0=C["identb"][:], scalar1=Dbc[:, h:h + 1], scalar2=None,
                                                  op0=ALU.mult), [C["identb"].b, Dbc.b], [DI.b])
            xnr = ring(2, [128, D], BF16, "xn")
            rings = dict(xin=ring(2, [128, D], F32, "xin"), ss=ring(4, [128, 4], F32, "ss"), junk=xnr, xn=xnr)
            MTM = 256
            hnT = sb([128, 8, MTM], BF16, "hnT")
            xTs = [sb([128, 16, MTM], BF16, "xT") for _ in range(2)]
            BTs = [sb([128, 8, MTM], BF16, "BT") for _ in range(2)]
            CTs = [sb([128, 8, MTM], BF16, "CT") for _ in range(2)]
            dtraws = [sb([128, 2, NH], F32, "dtraw") for _ in range(2)]
            Rr = ring(3, [128, MTM + 3], F32, "R")
            accr = ring(4, [128, MTM], F32, "acc")
            sgr = ring(3, [128, MTM], F32, "sg")
            hTq = [sb([128, 512], F32, "hT") for _ in range(4)]
            hTbq = [sb([128, 512], BF16, "hTb") for _ in range(4)]
            halo = sb([128, 32, 3], F32, "halo")
            tmpr = ring(2, [128, 512], F32, "ytmp")
            ystr = ring(2, [128, 1024], F32, "yst")
            lanes = []
            for ln in range(2):
                lanes.append(dict(sm=sb([128, 12, NH], F32, "sm"), adtb=sb([128, NH], BF16, "adtb"),
                                  xtok=sb([128, DIN], BF16, "xtok"), Btok=sb([128, D], BF16, "Btok"),
                                  Wbh=sb([128, 16 * 128], BF16, "Wbh"), Ebh=sb([128, 16 * 128], BF16, "Ebh"),
                                  cbm=sb([128, NG * 128], BF16, "cbm"), xddh=sb([128, 1024], BF16, "xddh")))

            seqs = [("p", b, SEQ) for b in range(NB)] + [("s", b, DSEQ) for b in range(NS)]
            for si, (grp, b, L) in enumerate(seqs):
                base = (b * SEQ) if grp == "p" else (NTP + b * DSEQ)
                CL = min(128, L)
                MT = min(MTM, L)
                NCH = MT // CL
                NM = L // MT
                if grp == "p":
                    for q in range(4):
                        P.op("dve", "memset", dict(ap=hTq[q][:], constant=0.0), [], [hTq[q].b])
                    P.op("dve", "memset", dict(ap=halo[:], constant=0.0), [], [halo.b])
                else:
                    P.dma("sp", halo[:], state_conv[j, b], halo.b, cst, "ld")
                    for hf in range(2):
                        stg = ystr.next()
                        P.dma("sp", stg[:].rearrange("p (t n) -> p t n", t=8),
                              state_ssm[j, b, hf * 1024:(hf + 1) * 1024, :].rearrange("(t p) n -> p t n", p=128), stg.b, cst, "ld")
                        for qq in range(2):
                            q = hf * 2 + qq
                            bk = pb()
                            for i in range(4):
                                t = qq * 4 + i
                                P.op("pe", "transpose", dict(out=bk.f32[:, i * 128:(i + 1) * 128], in_=stg[:, t * 128:(t + 1) * 128],
                                                             identity=identf[:]), [stg.b, identf.b], [bk.b], inc=(i == 3))
                            P.op("act", "activation", dict(out=hTq[q][:], in_=bk.f32[:, :], func=AF.Copy), [bk.b], [hTq[q].b])
                for q in range(4):
                    P.op("act", "activation", dict(out=hTbq[q][:], in_=hTq[q][:], func=AF.Copy), [hTq[q].b], [hTbq[q].b])

                def conv_stream(m, si=si, base=base, CL=CL, MT=MT, NCH=NCH):
                    st = m % 2
                    xT, BT, CT, dtraw = xTs[st], BTs[st], CTs[st], dtraws[st]
                    m0 = m * MT
                    if m >= 2:
                        for ci in range(NCH):
                            yield ("wait", ("cdone", si, (m - 2) * NCH + ci))
                    if m >= 1:
                        yield ("wait", ("convdone", si, m - 1))
                    for i in range(NCH):
                        tl = dict(row0=base + m0 + i * CL, nt=CL)
                        front(C, src, tl, gB, hnT, i * CL, rings)
                        yield
                    for ci in range(NCH):
                        bk = pb()
                        for kc in range(8):
                            P.op("pe", "matmul", dict(out=bk.f32[:CL, :NH], lhsT=hnT[:, kc, ci * CL:(ci + 1) * CL],
                                                      rhs=Wx[:, kc, CONVD:CONVD + NH], start=(kc == 0), stop=(kc == 7)),
                                 [hnT.b, Wx.b], [bk.b], inc=(kc == 7))
                        P.op("act", "activation", dict(out=dtraw[:CL, ci, :], in_=bk.f32[:CL, :NH], func=AF.Copy), [bk.b], [dtraw.b])
                    yield
                    pipe = {}

                    def s1(ct):
                        bk = pb()
                        for kc in range(8):
                            P.op("pe", "matmul", dict(out=bk.f32[:, :MT], lhsT=Wx[:, kc, ct * 128:(ct + 1) * 128],
                                                      rhs=hnT[:, kc, :MT], start=(kc == 0), stop=(kc == 7)),
                                 [Wx.b, hnT.b], [bk.b], inc=(kc == 7))
                        R = Rr.next()
                        P.op("dve", "tensor_copy", dict(out=R[:, 0:3], in_=halo[:, ct, :]), [halo.b], [R.b])
                        P.op("act", "activation", dict(out=R[:, 3:3 + MT], in_=bk.f32[:, :MT], func=AF.Copy), [bk.b], [R.b])
                        P.op("dve", "tensor_copy", dict(out=halo[:, ct, :], in_=R[:, MT:MT + 3]), [R.b], [halo.b])
                        pipe[ct] = dict(R=R)

                    def s2(ct):
                        R = pipe[ct]["R"]
                        acc = accr.next()
                        P.op("dve", "tensor_scalar", dict(out=acc[:, :MT], in0=R[:, 3:3 + MT], scalar1=cw[:, ct, 3:4],
                                                          scalar2=cbias[:, ct:ct + 1], op0=ALU.mult, op1=ALU.add),
                             [R.b, cw.b, cbias.b], [acc.b])
                        for tap in (2, 1, 0):
                            P.op("dve", "scalar_tensor_tensor", dict(out=acc[:, :MT], in0=R[:, tap:tap + MT],
                                                                     scalar=cw[:, ct, tap:tap + 1], in1=acc[:, :MT],
                                                                     op0=ALU.mult, op1=ALU.add), [R.b, cw.b, acc.b], [acc.b])
                        pipe[ct]["acc"] = acc

                    def s3(ct):
                        acc = pipe[ct]["acc"]
                        sg = sgr.next()
                        P.op("act", "activation", dict(out=sg[:, :MT], in_=acc[:, :MT], func=AF.Exp, scale=-1.0), [acc.b], [sg.b])
                        P.op("act", "activation", dict(out=sg[:, :MT], in_=sg[:, :MT], func=AF.Ln, bias=1.0), [sg.b], [sg.b])
                        P.op("act", "activation", dict(out=sg[:, :MT], in_=sg[:, :MT], func=AF.Exp, scale=-1.0), [sg.b], [sg.b])
                        pipe[ct]["sg"] = sg

                    def s4(ct):
                        acc, sg = pipe[ct]["acc"], pipe[ct]["sg"]
                        if ct < 16:
                            dT, dap = xT, xT[:, ct, :MT]
                        elif ct < 24:
                            dT, dap = BT, BT[:, ct - 16, :MT]
                        else:
                            dT, dap = CT, CT[:, ct - 24, :MT]
                        P.op("dve", "tensor_tensor", dict(out=dap, in0=acc[:, :MT], in1=sg[:, :MT], op=ALU.mult), [acc.b, sg.b], [dT.b])
                        del pipe[ct]

                    for step in range(32 + 3):
                        if step < 32:
                            s1(step)
                        if 0 <= step - 1 < 32:
                            s2(step - 1)
                        if 0 <= step - 2 < 32:
                            s3(step - 2)
                        if 0 <= step - 3 < 32:
                            s4(step - 3)
                        yield
                    yield ("set", ("convdone", si, m))

                def chunk_stream(c, lane, si=si, base=base, CL=CL, MT=MT, NCH=NCH):
                    Ln = lanes[lane]
                    m, ci = divmod(c, NCH)
                    st = m % 2
                    xT, BT, CT, dtraw = xTs[st], BTs[st], CTs[st], dtraws[st]
                    c0 = ci * CL
                    row0 = base + m * MT + c0
                    sm, adtb, xtok, Btok, Wbh, Ebh, cbm, xddh = (Ln["sm"], Ln["adtb"], Ln["xtok"], Ln["Btok"], Ln["Wbh"],
                                                                 Ln["Ebh"], Ln["cbm"], Ln["xddh"])
                    yield ("wait", ("convdone", si, m))
                    P.op("dve", "tensor_tensor", dict(out=sm[:CL, 0, :], in0=dtraw[:CL, ci, :], in1=dtb[:CL, :], op=ALU.add),
                         [dtraw.b, dtb.b], [sm.b])
                    P.op("dve", "scalar_tensor_tensor", dict(out=sm[:CL, 1, :], in0=sm[:CL, 0, :], scalar=-1.0,
                                                             in1=sm[:CL, 0, :], op0=ALU.mult, op1=ALU.max), [sm.b], [sm.b])
                    P.op("act", "activation", dict(out=sm[:CL, 2, :], in_=sm[:CL, 1, :], func=AF.Exp, scale=-1.0), [sm.b], [sm.b])
                    P.op("act", "activation", dict(out=sm[:CL, 3, :], in_=sm[:CL, 2, :], func=AF.Ln, bias=1.0), [sm.b], [sm.b])
                    P.op("dve", "scalar_tensor_tensor", dict(out=sm[:CL, 4, :], in0=sm[:CL, 0, :], scalar=0.0,
                                                             in1=sm[:CL, 3, :], op0=ALU.max, op1=ALU.add), [sm.b], [sm.b])
                    dt = sm[:CL, 4, :]
                    P.op("dve", "tensor_tensor", dict(out=adtb[:CL, :], in0=dt, in1=Abc[:CL, :], op=ALU.mult),
                         [sm.b, Abc.b], [adtb.b])
                    P.op("act", "activation", dict(out=sm[:CL, 5, :], in_=dt, func=AF.Ln), [sm.b], [sm.b])
                    P.op("dve", "tensor_copy", dict(out=sm[:CL, 1, :], in_=adtb[:CL, :]), [adtb.b], [sm.b])
                    yield
                    bk2 = pb()
                    P.op("pe", "matmul", dict(out=bk2.f32[:CL, 0:NH], lhsT=trib[:CL, :CL], rhs=adtb[:CL, :],
                                              start=True, stop=True), [trib.b, adtb.b], [bk2.b], inc=False)
                    P.op("pe", "matmul", dict(out=bk2.f32[:, NH:2 * NH], lhsT=onesb[:CL, :], rhs=adtb[:CL, :],
                                              start=True, stop=True), [onesb.b, adtb.b], [bk2.b])
                    P.op("act", "activation", dict(out=sm[:CL, 6, :], in_=bk2.f32[:CL, 0:NH], func=AF.Exp), [bk2.b], [sm.b])
                    P.op("act", "activation", dict(out=sm[:, 7, :], in_=bk2.f32[:, NH:2 * NH], func=AF.Exp), [bk2.b], [sm.b])
                    P.op("act", "activation", dict(out=sm[:CL, 8, :], in_=bk2.f32[:CL, 0:NH], func=AF.Copy), [bk2.b], [sm.b])
                    P.op("dve", "tensor_tensor", dict(out=sm[:CL, 9, :], in0=bk2.f32[:CL, NH:2 * NH], in1=sm[:CL, 8, :],
                                                      op=ALU.subtract), [bk2.b, sm.b], [sm.b])
                    P.op("act", "activation", dict(out=sm[:CL, 10, :], in_=sm[:CL, 9, :], func=AF.Exp), [sm.b], [sm.b])
                    P.op("dve", "tensor_tensor", dict(out=sm[:CL, 11, :], in0=sm[:CL, 10, :], in1=dt, op=ALU.mult), [sm.b], [sm.b])
                    yield
                    for half in range(2):
                        bk = pb()
                        for i in range(8):
                            P.op("pe", "transpose", dict(out=bk.bf[:CL, i * 128:(i + 1) * 128],
                                                         in_=xT[:, half * 8 + i, c0:c0 + CL], identity=C["identb"][:]),
                                 [xT.b, C["identb"].b], [bk.b], inc=(i == 7))
                        P.op("act", "activation", dict(out=xtok[:CL, half * 1024:(half + 1) * 1024], in_=bk.bf[:CL, :],
                                                       func=AF.Copy), [bk.b], [xtok.b])
                        yield
                    bk = pb()
                    for g in range(8):
                        P.op("pe", "transpose", dict(out=bk.bf[:CL, g * 128:(g + 1) * 128], in_=BT[:, g, c0:c0 + CL],
                                                     identity=C["identb"][:]), [BT.b, C["identb"].b], [bk.b], inc=(g == 7))
                    P.op("act", "activation", dict(out=Btok[:CL, :], in_=bk.bf[:CL, :], func=AF.Copy), [bk.b], [Btok.b])
                    yield
                    cv = cbm[:CL, 0:NG * CL].rearrange("p (g t) -> p g t", g=NG)
                    for half in range(2):
                        bk = pb()
                        for gi in range(4):
                            g = half * 4 + gi
                            P.op("pe", "matmul", dict(out=bk.f32[:CL, gi * CL:(gi + 1) * CL], lhsT=BT[:, g, c0:c0 + CL],
                                                      rhs=CT[:, g, c0:c0 + CL], start=True, stop=True),
                                 [BT.b, CT.b], [bk.b], inc=(gi == 3))
                        P.op("dve", "tensor_tensor", dict(out=cv[:, half * 4:(half + 1) * 4, :],
                                                          in0=bk.f32[:CL, 0:4 * CL].rearrange("p (g t) -> p g t", g=4),
                                                          in1=trib[:CL, :CL].unsqueeze(1).to_broadcast([CL, 4, CL]),
                                                          op=ALU.mult), [bk.b, trib.b], [cbm.b])
                        yield
                    Wv = Wbh[:CL, 0:16 * CL].rearrange("p (h t) -> p h t", h=16)
                    Ev = Ebh[:CL, 0:16 * CL].rearrange("p (h t) -> p h t", h=16)
                    for hf in range(2):
                        for hl in range(16):
                            h = hf * 16 + hl
                            P.op("act", "activation", dict(out=Wv[:, hl, :], in_=trib[:CL, :CL], func=AF.Copy,
                                                           scale=sm[:CL, 1, h:h + 1]), [trib.b, sm.b], [Wbh.b])
                        for gi in range(4):
                            bk = pb()
                            P.op("pe", "matmul", dict(out=bk.f32[:CL, 0:4 * CL], lhsT=sub[:CL, :CL],
                                                      rhs=Wbh[:CL, gi * 4 * CL:(gi + 1) * 4 * CL], start=True, stop=True),
                                 [sub.b, Wbh.b], [bk.b])
                            for hi in range(4):
                                hl = gi * 4 + hi
                                h = hf * 16 + hl
                                P.op("act", "activation", dict(out=Ev[:, hl, :], in_=bk.f32[:CL, hi * CL:(hi + 1) * CL],
                                                               func=AF.Exp, bias=sm[:CL, 5, h:h + 1]), [bk.b, sm.b], [Ebh.b])
                            yield
                        E4 = Ebh[:CL, 0:16 * CL].rearrange("p (g r t) -> p g r t", g=4, r=4)
                        P.op("dve", "tensor_tensor", dict(out=E4, in0=E4,
                                                          in1=cv[:, hf * 4:(hf + 1) * 4, :].unsqueeze(2).to_broadcast([CL, 4, 4, CL]),
                                                          op=ALU.mult), [Ebh.b, cbm.b], [Ebh.b])
                        yield
                        if c > 0:
                            yield ("wait", ("state", si, c - 1, hf))
                        yst = ystr.next()
                        for qq in range(2):
                            q = hf * 2 + qq
                            bkY = pb()
                            for hh in range(8):
                                hl = qq * 8 + hh
                                h = hf * 16 + hl
                                P.op("pe", "matmul", dict(out=bkY.f32[:CL, hh * 64:(hh + 1) * 64], lhsT=Ev[:, hl, :],
                                                          rhs=xtok[:CL, h * 64:(h + 1) * 64], start=True, stop=False),
                                     [Ebh.b, xtok.b], [bkY.b], inc=False)
                                P.op("pe", "matmul", dict(out=bkY.f32[:CL, hh * 64:(hh + 1) * 64], lhsT=DI[:CL, h, :CL],
                                                          rhs=xtok[:CL, h * 64:(h + 1) * 64], start=False, stop=True),
                                     [DI.b, xtok.b], [bkY.b], inc=(hh == 7))
                            bkO = pb()
                            for gg in range(2):
                                g = q * 2 + gg
                                P.op("pe", "matmul", dict(out=bkO.f32[:CL, gg * 256:(gg + 1) * 256], lhsT=CT[:, g, c0:c0 + CL],
                                                          rhs=hTbq[q][:, gg * 256:(gg + 1) * 256], start=True, stop=True),
                                     [CT.b, hTbq[q].b], [bkO.b], inc=(gg == 1))
                            tmp = tmpr.next()
                            P.op("dve", "tensor_tensor", dict(out=tmp[:CL, :].rearrange("p (h d) -> p h d", h=8),
                                                              in0=bkO.f32[:CL, :].rearrange("p (h d) -> p h d", h=8),
                                                              in1=sm[:CL, 6, q * 8:(q + 1) * 8].unsqueeze(2).to_broadcast([CL, 8, 64]),
                                                              op=ALU.mult), [bkO.b, sm.b], [tmp.b])
                            P.op("dve", "tensor_tensor", dict(out=yst[:CL, qq * 512:(qq + 1) * 512], in0=bkY.f32[:CL, :],
                                                              in1=tmp[:CL, :], op=ALU.add), [bkY.b, tmp.b], [yst.b])
                            yield
                        P.dma("pool", Ys[row0:row0 + CL, hf * 1024:(hf + 1) * 1024], yst[:CL, :], db("Ys", row0), yst.b, "st")
                        P.op("dve", "tensor_tensor", dict(out=xddh[:CL, :].rearrange("p (h d) -> p h d", h=16),
                                                          in0=xtok[:CL, hf * 1024:(hf + 1) * 1024].rearrange("p (h d) -> p h d", h=16),
                                                          in1=sm[:CL, 11, hf * 16:(hf + 1) * 16].unsqueeze(2).to_broadcast([CL, 16, 64]),
                                                          op=ALU.mult), [xtok.b, sm.b], [xddh.b])
                        for qq in range(2):
                            q = hf * 2 + qq
                            bkH = pb()
                            for gg in range(2):
                                g = q * 2 + gg
                                P.op("pe", "matmul", dict(out=bkH.f32[:, gg * 256:(gg + 1) * 256], lhsT=Btok[:CL, g * 128:(g + 1) * 128],
                                                          rhs=xddh[:CL, (qq * 2 + gg) * 256:(qq * 2 + gg + 1) * 256], start=True, stop=True),
                                     [Btok.b, xddh.b], [bkH.b], inc=(gg == 1))
                            hv = hTq[q]
                            P.op("dve", "tensor_tensor", dict(out=hv[:].rearrange("p (h d) -> p h d", h=8),
                                                              in0=hv[:].rearrange("p (h d) -> p h d", h=8),
                                                              in1=sm[:, 7, q * 8:(q + 1) * 8].unsqueeze(2).to_broadcast([128, 8, 64]),
                                                              op=ALU.mult), [hv.b, sm.b], [hv.b])
                            P.op("dve", "tensor_tensor", dict(out=hv[:], in0=bkH.f32[:, :], in1=hv[:], op=ALU.add), [bkH.b, hv.b], [hv.b])
                            P.op("act", "activation", dict(out=hTbq[q][:], in_=hv[:], func=AF.Copy), [hv.b], [hTbq[q].b])
                            yield
                        yield ("set", ("state", si, c, hf))
                    yield ("set", ("cdone", si, c))

                nchunks = NM * NCH
                convq = [(lambda m=m: conv_stream(m)) for m in range(NM)]
                laneA = [(lambda c=c: chunk_stream(c, 0)) for c in range(0, nchunks, 2)]
                laneB = [(lambda c=c: chunk_stream(c, 1)) for c in range(1, nchunks, 2)]
                run_sched([convq, laneA, laneB])

                o_ssm = (o_pssm if grp == "p" else o_sssm)[j, b]
                o_conv = (o_pconv if grp == "p" else o_sconv)[j, b]
                for hf in range(2):
                    stg = ystr.next()
                    for qq in range(2):
                        q = hf * 2 + qq
                        bk = pb()
                        for i in range(4):
                            P.op("pe", "transpose", dict(out=bk.f32[:, i * 128:(i + 1) * 128], in_=hTq[q][:, i * 128:(i + 1) * 128],
                                                         identity=identf[:]), [hTq[q].b, identf.b], [bk.b], inc=(i == 3))
                        P.op("act", "activation", dict(out=stg[:, qq * 512:(qq + 1) * 512], in_=bk.f32[:, :], func=AF.Copy),
                             [bk.b], [stg.b])
                    P.dma("pool", o_ssm[hf * 1024:(hf + 1) * 1024, :].rearrange("(t p) n -> p t n", p=128),
                          stg[:].rearrange("p (t n) -> p t n", t=8), db("o_ssm_%s" % grp, b), stg.b, "st")
                P.dma("pool", o_conv, halo[:], db("o_conv_%s" % grp, b), halo.b, "st")
            end_stage()

        def stage_ssd2(layer, j, src, dst):
            begin_stage()
            C = load_consts()
            Wz = sb([128, 8, DIN], BF16, "Wz")
            for kc in range(8):
                P.dma("pool", Wz[:, kc, :], ssd_in_w[j, kc * 128:(kc + 1) * 128, 0:DIN], Wz.b, cst, "ld")
            Wo = sb([128, 16, D], BF16, "Wo")
            load_w(Wo, ssd_out_w[j])
            gB = sb([128, D], F32, "gB")
            load_bcast(gB, norm_mix[layer])
            nwB = sb([128, DIN], F32, "nwB")
            load_bcast(nwB, ssd_norm_w[j])
            shared = dict(ss=ring(4, [128, 4], F32, "ss"), junk=ring(1, [128, D], BF16, "junk"),
                          xn=ring(2, [128, D], BF16, "xn"))
            junk2 = sb([128, 256], F32, "junk2")
            lanes = []
            for ln in range(2):
                rg = dict(shared)
                rg["xin"] = ring(1, [128, D], F32, "xin")
                lanes.append(dict(rings=rg, hnT=sb([128, 8, 128], BF16, "hnT"), yin=sb([128, DIN], F32, "yin"),
                                  sz=sb([128, DIN], F32, "sz"), gs=sb([128, 4, NG], F32, "gs"),
                                  ynb=sb([128, DIN], BF16, "ynb"), ynT=sb([128, 16, 128], BF16, "ynT"),
                                  y=sb([128, D], F32, "yst")))

            def tile_stream(tl, lane):
                L = lanes[lane]
                nt, r0 = tl["nt"], tl["row0"]
                hnT, yin, sz, gs, ynb, ynT, y = L["hnT"], L["yin"], L["sz"], L["gs"], L["ynb"], L["ynT"], L["y"]
                P.dma("sp", yin[:nt, :], Ys[r0:r0 + nt, :], yin.b, db("Ys", r0), "ld")
                xin = front(C, src, tl, gB, hnT, 0, L["rings"])
                yield
                for c in range(4):
                    bk = pb()
                    for kc in range(8):
                        P.op("pe", "matmul", dict(out=bk.f32[:nt, :], lhsT=hnT[:, kc, :nt], rhs=Wz[:, kc, c * 512:(c + 1) * 512],
                                                  start=(kc == 0), stop=(kc == 7)), [hnT.b, Wz.b], [bk.b], inc=(kc == 7))
                    sl = sz[:nt, c * 512:(c + 1) * 512]
                    P.op("act", "activation", dict(out=sl, in_=bk.f32[:nt, :], func=AF.Exp, scale=-1.0), [bk.b], [sz.b])
                    P.op("act", "activation", dict(out=sl, in_=sl, func=AF.Ln, bias=1.0), [sz.b], [sz.b])
                    P.op("act", "activation", dict(out=sl, in_=sl, func=AF.Exp, scale=-1.0), [sz.b], [sz.b])
                    P.op("dve", "tensor_tensor", dict(out=sl, in0=sl, in1=yin[:nt, c * 512:(c + 1) * 512], op=ALU.mult),
                         [sz.b, yin.b], [sz.b])
                    P.op("dve", "tensor_tensor", dict(out=sl, in0=bk.f32[:nt, :], in1=sl, op=ALU.mult), [bk.b, sz.b], [sz.b])
                    yield
                for g in range(NG):
                    P.op("act", "activation", dict(out=junk2[:nt, :], in_=sz[:nt, g * 256:(g + 1) * 256], func=AF.Square,
                                                   accum_out=gs[:nt, 0, g:g + 1]), [sz.b], [junk2.b, gs.b])
                yield
                P.op("act", "activation", dict(out=gs[:nt, 2, :], in_=gs[:nt, 0, :], func=AF.Ln, scale=1.0 / 256, bias=EPS),
                     [gs.b], [gs.b])
                P.op("act", "activation", dict(out=gs[:nt, 3, :], in_=gs[:nt, 2, :], func=AF.Exp, scale=-0.5), [gs.b], [gs.b])
                yield
                P.op("dve", "tensor_tensor", dict(out=sz[:nt, :].rearrange("p (g d) -> p g d", g=NG),
                                                  in0=sz[:nt, :].rearrange("p (g d) -> p g d", g=NG),
                                                  in1=gs[:nt, 3, :].unsqueeze(2).to_broadcast([nt, NG, 256]), op=ALU.mult),
                     [sz.b, gs.b], [sz.b])
                P.op("dve", "tensor_tensor", dict(out=ynb[:nt, :], in0=sz[:nt, :], in1=nwB[:nt, :], op=ALU.mult),
                     [sz.b, nwB.b], [ynb.b])
                yield
                for half in range(2):
                    bk = pb()
                    for i in range(8):
                        ct = half * 8 + i
                        P.op("pe", "transpose", dict(out=bk.bf[:, i * 128:i * 128 + nt], in_=ynb[:nt, ct * 128:(ct + 1) * 128],
                                                     identity=C["identb"][:nt, :nt]), [ynb.b, C["identb"].b], [bk.b], inc=(i == 7))
                    P.op("act", "activation", dict(out=ynT[:, half * 8:(half + 1) * 8, :nt],
                                                   in_=bk.bf.rearrange("p (k t) -> p k t", k=8)[:, :, :nt], func=AF.Copy),
                         [bk.b], [ynT.b])
                    yield
                for c in range(2):
                    bk = pb()
                    for ct in range(16):
                        P.op("pe", "matmul", dict(out=bk.f32[:nt, :], lhsT=ynT[:, ct, :nt], rhs=Wo[:, ct, c * 512:(c + 1) * 512],
                                                  start=(ct == 0), stop=(ct == 15)), [ynT.b, Wo.b], [bk.b], inc=(ct == 15))
                    P.op("dve", "tensor_tensor", dict(out=y[:nt, c * 512:(c + 1) * 512], in0=bk.f32[:nt, :],
                                                      in1=xin[:nt, c * 512:(c + 1) * 512], op=ALU.add), [bk.b, xin.b], [y.b])
                    yield
                P.dma("pool", dst[r0:r0 + nt, :], y[:nt, :], db(dst.name, r0), y.b, "st")

            run_streams(tiles, tile_stream, width=2, stagger=6)
            end_stage()

        def stage_sb1(layer, j, src):
            begin_stage()
            C = load_consts()
            Wqkv = sb([128, 8, 3 * D], BF16, "Wqkv")
            load_w(Wqkv, sb_qkv_w[j])
            gB = sb([128, D], F32, "gB")
            load_bcast(gB, norm_mix[layer])
            qg = sb([128, AD], F32, "qg")
            kg = sb([128, AD], F32, "kg")
            load_bcast(qg, sb_q_gain[j])
            load_bcast(kg, sb_k_gain[j])
            P.op("dve", "tensor_scalar", dict(out=qg[:], in0=qg[:], scalar1=0.125, scalar2=None, op0=ALU.mult), [qg.b], [qg.b])
            rings = dict(xin=ring(2, [128, D], F32, "xin"), ss=ring(4, [128, 4], F32, "ss"),
                         junk=ring(1, [128, D], BF16, "junk"), xn=ring(2, [128, D], BF16, "xn"))
            xnTs = ring(2, [128, 8, 128], BF16, "xnT")
            sqr = ring(2, [128, 512], F32, "sq")
            t1r = ring(2, [128, 512], F32, "t1")
            ssqr = ring(4, [128, 4, 8], F32, "ssq")
            koutr = ring(2, [128, D], F32, "kout")
            voutr = ring(2, [128, D], F32, "vout")
            knbr = ring(2, [128, D], BF16, "knb")
            qnbr = ring(2, [128, D], BF16, "qnb")
            vbr = ring(2, [128, D], BF16, "vb")
            qTr = ring(2, [128, 8, 128], BF16, "qT")
            kTr = ring(2, [128, 8, 128], BF16, "kT")
            ckr = ring(2, [128, D], BF16, "ck")

            def transpose_store(srcT, nt, stg, dram_ap, dbuf):
                bk = pb()
                for hp in range(8):
                    P.op("pe", "transpose", dict(out=bk.bf[:, hp * 128:hp * 128 + nt], in_=srcT[:nt, hp * 128:(hp + 1) * 128],
                                                 identity=C["identb"][:nt, :nt]), [srcT.b, C["identb"].b], [bk.b], inc=(hp == 7))
                P.op("act", "activation", dict(out=stg[:, :, :nt], in_=bk.bf.rearrange("p (k t) -> p k t", k=8)[:, :, :nt],
                                               func=AF.Copy), [bk.b], [stg.b])
                P.dma("pool", dram_ap, stg[:, :, :nt], dbuf, stg.b, "st")

            for tl in tiles:
                nt, r0, grp, b, pos = tl["nt"], tl["row0"], tl["grp"], tl["seq"], tl["pos"]
                xnT = xnTs.next()
                front(C, src, tl, gB, xnT, 0, rings)
                kout = koutr.next()
                vout = voutr.next()
                knb = knbr.next()
                qnb = qnbr.next()
                vb = vbr.next()
                for c in range(6):
                    bk = pb()
                    for kc in range(8):
                        P.op("pe", "matmul", dict(out=bk.f32[:nt, :], lhsT=xnT[:, kc, :nt], rhs=Wqkv[:, kc, c * 512:(c + 1) * 512],
                                                  start=(kc == 0), stop=(kc == 7)), [xnT.b, Wqkv.b], [bk.b], inc=(kc == 7))
                    if c < 4:
                        sq = sqr.next()
                        ssq = ssqr.next()
                        t1 = t1r.next()
                        P.op("act", "activation", dict(out=sq[:nt, :], in_=bk.f32[:nt, :], func=AF.Square), [bk.b], [sq.b])
                        P.op("dve", "tensor_reduce", dict(out=ssq[:nt, 0, :], in_=sq[:nt, :].rearrange("p (h d) -> p h d", h=8),
                                                          axis=AX.X, op=ALU.add), [sq.b], [ssq.b])
                        P.op("act", "activation", dict(out=ssq[:nt, 2, :], in_=ssq[:nt, 0, :], func=AF.Ln, scale=1.0 / AD, bias=EPS),
                             [ssq.b], [ssq.b])
                        P.op("act", "activation", dict(out=ssq[:nt, 3, :], in_=ssq[:nt, 2, :], func=AF.Exp, scale=-0.5),
                             [ssq.b], [ssq.b])
                        P.op("dve", "tensor_tensor", dict(out=t1[:nt, :].rearrange("p (h d) -> p h d", h=8),
                                                          in0=bk.f32[:nt, :].rearrange("p (h d) -> p h d", h=8),
                                                          in1=ssq[:nt, 3, :].unsqueeze(2).to_broadcast([nt, 8, AD]), op=ALU.mult),
                             [bk.b, ssq.b], [t1.b])
                        if c < 2:
                            P.op("dve", "tensor_tensor", dict(out=qnb[:nt, c * 512:(c + 1) * 512].rearrange("p (h d) -> p h d", h=8),
                                                              in0=t1[:nt, :].rearrange("p (h d) -> p h d", h=8),
                                                              in1=qg[:nt, :].unsqueeze(1).to_broadcast([nt, 8, AD]), op=ALU.mult),
                                 [t1.b, qg.b], [qnb.b])
                        else:
                            cc = c - 2
                            P.op("dve", "tensor_tensor", dict(out=kout[:nt, cc * 512:(cc + 1) * 512].rearrange("p (h d) -> p h d", h=8),
                                                              in0=t1[:nt, :].rearrange("p (h d) -> p h d", h=8),
                                                              in1=kg[:nt, :].unsqueeze(1).to_broadcast([nt, 8, AD]), op=ALU.mult),
                                 [t1.b, kg.b], [kout.b])
                            P.op("act", "activation", dict(out=knb[:nt, cc * 512:(cc + 1) * 512], in_=kout[:nt, cc * 512:(cc + 1) * 512],
                                                           func=AF.Copy), [kout.b], [knb.b])
                    else:
                        cc = c - 4
                        P.op("act", "activation", dict(out=vout[:nt, cc * 512:(cc + 1) * 512], in_=bk.f32[:nt, :], func=AF.Copy),
                             [bk.b], [vout.b])
                        P.op("dve", "tensor_copy", dict(out=vb[:nt, cc * 512:(cc + 1) * 512], in_=bk.f32[:nt, :]), [bk.b], [vb.b])
                if grp == "p":
                    P.dma("pool", o_pk[j, b, pos:pos + nt, :], kout[:nt, :], db("o_pk", r0), kout.b, "st")
                    P.dma("pool", o_pv[j, b, pos:pos + nt, :], vout[:nt, :], db("o_pv", r0), vout.b, "st")
                    P.dma("pool", Vp[b, pos:pos + nt, :], vb[:nt, :], db("Vp", b), vb.b, "st")
                    transpose_store(qnb, nt, qTr.next(), QTp[b, :, :, pos:pos + nt], db("QTp", b))
                    transpose_store(knb, nt, kTr.next(), KTp[b, :, :, pos:pos + nt], db("KTp", b))
                else:
                    P.dma("pool", o_sk[j, b, 0:nt, :], kout[:nt, :], db("o_sk", r0), kout.b, "st")
                    P.dma("pool", o_sv[j, b, 0:nt, :], vout[:nt, :], db("o_sv", r0), vout.b, "st")
                    P.dma("pool", Vs[b, 0:nt, :], vb[:nt, :], db("Vs", b), vb.b, "st")
                    transpose_store(qnb, nt, qTr.next(), QTs[b, :, :, 0:nt], db("QTs", b))
                    transpose_store(knb, nt, kTr.next(), KTs[b, :, :, PAST:PAST + nt], db("KTs", b))
            for b in range(NS):
                for kb in range(PAST // 128):
                    ck = ckr.next()
                    for hf in range(2):
                        P.dma("pool", ck[:, hf * 512:(hf + 1) * 512], cache_k[j, b, kb * 128:(kb + 1) * 128, hf * 512:(hf + 1) * 512],
                              ck.b, cst, "ld")
                    transpose_store(ck, 128, kTr.next(), KTs[b, :, :, kb * 128:(kb + 1) * 128], db("KTs", b))
            end_stage()

        def stage_sb2(layer, j, src, dst):
            begin_stage()
            Wo = sb([128, 8, D], BF16, "Wo")
            load_w(Wo, sb_out_w[j])
            suin = sb([128, 128], BF16, "suin")
            sutmp = sb([128, 128], BF16, "sutmp")
            onesb = sb([128, 128], BF16, "onesb")
            nonesb = sb([128, 128], BF16, "nonesb")
            maskb = sb([128, 4, 512], BF16, "maskb")
            masksb = sb([128, 32], BF16, "masksb")
            P.dma("pool", sutmp[:], c_tri[:, :], sutmp.b, cst, "ld")
            P.dma("pool", suin[:], c_su[:, :], suin.b, cst, "ld")
            P.dma("pool", sutmp[:], c_ident[:, :], sutmp.b, cst, "ld")
            P.op("dve", "tensor_tensor", dict(out=suin[:], in0=suin[:], in1=sutmp[:], op=ALU.add), [suin.b, sutmp.b], [suin.b])
            P.op("dve", "tensor_scalar", dict(out=suin[:], in0=suin[:], scalar1=-1.0, scalar2=None, op0=ALU.mult), [suin.b], [suin.b])
            P.op("dve", "memset", dict(ap=onesb[:], constant=1.0), [], [onesb.b])
            P.op("dve", "tensor_scalar", dict(out=nonesb[:], in0=sutmp[:], scalar1=-1.0, scalar2=None, op0=ALU.mult),
                 [sutmp.b], [nonesb.b])
            for i in range(4):
                P.dma("pool", maskb[:, i, :], c_mask[:, i, :], maskb.b, cst, "ld")
            P.dma("pool", masksb[:], c_masks[:, :], masksb.b, cst, "ld")
            KLEN = max(SEQ, KPAD)
            KT = sb([128, 8, KLEN], BF16, "KT")
            Vb = sb([128, KLEN // 128, D], BF16, "Vb")
            QWM = 512
            QTr = ring(2, [128, 8, QWM], BF16, "QT")
            OT = sb([128, 8, QWM], BF16, "OT")
            e1r = ring(3, [128, QWM], F32, "e1")
            lgr = ring(4, [128, QWM], BF16, "lg")
            wbr = ring(3, [128, QWM], BF16, "wb")
            Sbfr = [ring(2, [128, QWM], BF16, "Sbf0"), ring(2, [128, QWM], BF16, "Sbf1")]
            xinr = ring(2, [128, D], F32, "xin")
            ystr = ring(2, [128, D], F32, "yst")
            bkR = [hold(), hold()]
            bkOs = [hold(), hold()]

            seqs = [("p", b) for b in range(NB)] + [("s", b) for b in range(NS)]
            for (grp, b) in seqs:
                if grp == "p":
                    base, L, QW = b * SEQ, SEQ, 512
                    for hp in range(8):
                        P.dma("sp", KT[:, hp, :SEQ], KTp[b, :, hp, :], KT.b, db("KTp", b), "ld")
                    for t4 in range(0, SEQ // 128, 4):
                        P.dma("sp", Vb[:, t4:t4 + 4, :], Vp[b, t4 * 128:(t4 + 4) * 128, :].rearrange("(t p) c -> p t c", p=128),
                              Vb.b, db("Vp", b), "ld")
                else:
                    base, L, QW = NTP + b * DSEQ, DSEQ, DSEQ
                    nkc = PAST // 128
                    if KPAD > KTOT:
                        P.op("dve", "memset", dict(ap=KT[:, :, KTOT:KPAD], constant=0.0), [], [KT.b])
                    P.op("dve", "memset", dict(ap=Vb[:, nkc, :], constant=0.0), [], [Vb.b])
                    for hp in range(8):
                        P.dma("sp", KT[:, hp, :KTOT], KTs[b, :, hp, :KTOT], KT.b, db("KTs", b), "ld")
                    for t in range(nkc):
                        for hf in range(2):
                            P.dma("pool", Vb[:, t, hf * 512:(hf + 1) * 512],
                                  cache_v[j, b, t * 128:(t + 1) * 128, hf * 512:(hf + 1) * 512], Vb.b, cst, "ld")
                    P.dma("sp", Vb[:DSEQ, nkc, :], Vs[b, :, :], Vb.b, db("Vs", b), "ld")
                for q0 in range(0, L, QW):
                    QT = QTr.next()
                    if grp == "p":
                        P.dma("sp", QT[:, :, :QW], QTp[b, :, :, q0:q0 + QW], QT.b, db("QTp", b), "ld")
                        nkb = (q0 + QW) // 128
                        def mask_of(kb, q0=q0):
                            i = kb - q0 // 128
                            return maskb[:, i, :] if i >= 0 else None
                    else:
                        P.dma("sp", QT[:, :, :QW], QTs[b, :, :, :], QT.b, db("QTs", b), "ld")
                        nkb = KPAD // 128
                        def mask_of(kb, nkb=nkb):
                            return masksb[:, :] if kb == nkb - 1 else None
                    its = []
                    for hp in range(8):
                        for kb in range(nkb - 1, -1, -1):
                            for hh in range(2):
                                its.append((hp, kb, hh))
                    st = {}

                    def col0_of(kb, q0=q0, grp=grp):
                        if grp != "p" or not TRIM:
                            return 0
                        return max(0, (kb - q0 // 128) * 128)

                    def stageA(it):
                        hp, kb, hh = it
                        po = hh * 64
                        a = col0_of(kb)
                        bkT = pb()
                        P.op("pe", "matmul", dict(out=bkT.f32[:, a:QW], lhsT=KT[po:po + 64, hp, kb * 128:(kb + 1) * 128],
                                                  rhs=QT[po:po + 64, hp, a:QW], start=True, stop=False), [KT.b, QT.b], [bkT.b])
                        e1 = e1r.next()
                        P.op("act", "activation", dict(out=e1[:, a:QW], in_=bkT.f32[:, a:QW], func=AF.Exp), [bkT.b], [e1.b])
                        lg = lgr.next()
                        P.op("act", "activation", dict(out=lg[:, a:QW], in_=e1[:, a:QW], func=AF.Ln, bias=1.0), [e1.b], [lg.b])
                        m = mask_of(kb)
                        if m is not None:
                            a2 = min(QW, a + 128)
                            P.op("dve", "tensor_tensor", dict(out=lg[:, a:a2], in0=lg[:, a:a2], in1=m[:, a:a2], op=ALU.mult),
                                 [lg.b, maskb.b, masksb.b], [lg.b])
                        st[it] = dict(bkT=bkT, lg=lg)

                    def stageB(it):
                        hp, kb, hh = it
                        d = st[it]
                        bkT, lg = d["bkT"], d["lg"]
                        first = (kb == nkb - 1)
                        lastb = (kb == 0)
                        a = col0_of(kb)
                        P.op("pe", "matmul", dict(out=bkT.f32[:, a:QW], lhsT=suin[:], rhs=lg[:, a:QW], start=False, stop=first),
                             [suin.b, lg.b], [bkT.b], inc=first)
                        if not first:
                            Sbf = d["Sbf"] = st[(hp, kb + 1, hh)]["Snext"]
                            P.op("pe", "matmul", dict(out=bkT.f32[:, a:QW], lhsT=nonesb[:], rhs=Sbf[:, a:QW], start=False, stop=True),
                                 [nonesb.b, Sbf.b], [bkT.b])
                        if not lastb:
                            P.op("pe", "matmul", dict(out=bkR[hh].f32[:, a:QW], lhsT=onesb[:], rhs=lg[:, a:QW], start=first,
                                                      stop=(kb == 1)), [onesb.b, lg.b], [bkR[hh].b])
                            Sn = Sbfr[hh].next()
                            P.op("dve", "tensor_copy", dict(out=Sn[:, a:QW], in_=bkR[hh].f32[:, a:QW]), [bkR[hh].b], [Sn.b])
                            an = col0_of(kb - 1)
                            if an < a:
                                P.op("dve", "memset", dict(ap=Sn[:, an:a], constant=0.0), [], [Sn.b])
                            d["Snext"] = Sn
                        wb = wbr.next()
                        P.op("act", "activation", dict(out=wb[:, a:QW], in_=bkT.f32[:, a:QW], func=AF.Exp), [bkT.b], [wb.b])
                        m = mask_of(kb)
                        if m is not None:
                            a2 = min(QW, a + 128)
                            P.op("dve", "tensor_tensor", dict(out=wb[:, a:a2], in0=wb[:, a:a2], in1=m[:, a:a2], op=ALU.mult),
                                 [wb.b, maskb.b, masksb.b], [wb.b])
                        d["wb"] = wb

                    def stageC(it):
                        hp, kb, hh = it
                        d = st[it]
                        po = hh * 64
                        h = hp * 2 + hh
                        a = col0_of(kb)
                        bkO = bkOs[hp % 2]
                        P.op("pe", "matmul", dict(out=bkO.f32[po:po + 64, a:QW], lhsT=Vb[:, kb, h * 64:(h + 1) * 64],
                                                  rhs=d["wb"][:, a:QW], start=(kb == nkb - 1), stop=(kb == 0)),
                             [Vb.b, d["wb"].b], [bkO.b])
                        if kb == 0 and hh == 1:
                            P.op("act", "activation", dict(out=OT[:, hp, :QW], in_=bkO.f32[:, :QW], func=AF.Copy), [bkO.b], [OT.b])

                    n = len(its)
                    for step in range(n + 2):
                        if step < n:
                            stageA(its[step])
                        if 0 <= step - 1 < n:
                            stageB(its[step - 1])
                        if 0 <= step - 2 < n:
                            stageC(its[step - 2])
                    for i0 in range(0, QW, 128):
                        nt = min(128, QW - i0)
                        r0 = base + q0 + i0
                        xin = xinr.next()
                        P.dma("sp", xin[:nt, :], src[r0:r0 + nt, :], xin.b, db(src.name, r0), "ld")
                        y = ystr.next()
                        for c in range(2):
                            bk = pb()
                            for hp in range(8):
                                P.op("pe", "matmul", dict(out=bk.f32[:nt, :], lhsT=OT[:, hp, i0:i0 + nt], rhs=Wo[:, hp, c * 512:(c + 1) * 512],
                                                          start=(hp == 0), stop=(hp == 7)), [OT.b, Wo.b], [bk.b], inc=(hp == 7))
                            P.op("dve", "tensor_tensor", dict(out=y[:nt, c * 512:(c + 1) * 512], in0=bk.f32[:nt, :],
                                                              in1=xin[:nt, c * 512:(c + 1) * 512], op=ALU.add), [bk.b, xin.b], [y.b])
                        P.dma("pool", dst[r0:r0 + nt, :], y[:nt, :], db(dst.name, r0), y.b, "st")
            for bk in bkR + bkOs:
                release(bk)
            end_stage()

        src = x_all
        for layer in range(DEPTH):
            j = layer // 2
            last = (layer == DEPTH - 1)
            if only is not None and "mix" not in only:
                stage_copy(src, xs)
            elif layer % 2 == 0:
                stage_ssd1(layer, j, src)
                stage_ssd2(layer, j, src, xs)
            else:
                stage_sb1(layer, j, src)
                stage_sb2(layer, j, src, xs)
            src = xs
            stage_mlp(layer, xs, y_all if last else xs)
        P.barrier()
        P.emit()
    return nc


def _consts():
    i = np.arange(128)
    ident = np.eye(128, dtype=np.float32)
    tri = (i[:, None] <= i[None, :]).astype(np.float32)
    su = (i[:, None] > i[None, :]).astype(np.float32)
    q = np.arange(512)
    mask = np.stack([(i[:, None] + 128 * k < q[None, :]) for k in range(4)], axis=1).astype(np.float32)
    qs = np.arange(32)
    masks = ((i[:, None] < qs[None, :]) & (i[:, None] < 32)).astype(np.float32)
    return dict(c_ident=ident, c_tri=tri, c_su=su, c_mask=np.ascontiguousarray(mask), c_masks=masks)


def make_in_maps(inp, n_cores, NB, SEQ, NS, DSEQ, PAST, DEPTH):
    N_SSD = (DEPTH + 1) // 2
    N_SB = DEPTH // 2
    f = lambda a: np.ascontiguousarray(np.asarray(a, dtype=np.float32))
    shared = dict(
        norm_mix=f(inp["norm_mix"]), norm_mlp=f(inp["norm_mlp"]), ssd_in_w=f(inp["ssd_in_w"]),
        ssd_conv_w=f(np.asarray(inp["ssd_conv_w"]).reshape(N_SSD, 4, 32, 128).transpose(0, 3, 2, 1)),
        ssd_conv_b=f(np.asarray(inp["ssd_conv_b"]).reshape(N_SSD, 32, 128).transpose(0, 2, 1)),
        ssd_dt_bias=f(inp["ssd_dt_bias"]), ssd_a_log=f(inp["ssd_a_log"]), ssd_d=f(inp["ssd_d"]),
        ssd_norm_w=f(inp["ssd_norm_w"]), ssd_out_w=f(inp["ssd_out_w"]),
        mlp_up=f(inp["mlp_up"]), mlp_down=f(inp["mlp_down"]))
    if N_SB > 0:
        shared.update(sb_qkv_w=f(inp["sb_qkv_w"]), sb_q_gain=f(inp["sb_q_gain"]), sb_k_gain=f(inp["sb_k_gain"]),
                      sb_out_w=f(inp["sb_out_w"]))
    else:
        shared.update(sb_qkv_w=np.zeros((1, D, 3 * D), np.float32), sb_q_gain=np.zeros((1, AD), np.float32),
                      sb_k_gain=np.zeros((1, AD), np.float32), sb_out_w=np.zeros((1, D, D), np.float32))
    shared.update(_consts())
    xp = np.asarray(inp["x_prompt"], dtype=np.float32)
    xsm = np.asarray(inp["x_sample"], dtype=np.float32)
    sssm = np.asarray(inp["state_ssm"], dtype=np.float32)
    sconv = np.asarray(inp["state_conv"], dtype=np.float32)
    ck = np.asarray(inp["cache_k"], dtype=np.float32)
    cv = np.asarray(inp["cache_v"], dtype=np.float32)
    maps = []
    for c in range(n_cores):
        m = dict(shared)
        m["x_all"] = f(np.concatenate([xp[c * NB:(c + 1) * NB].reshape(NB * SEQ, D),
                                       xsm[c * NS:(c + 1) * NS].reshape(NS * DSEQ, D)], axis=0))
        m["state_ssm"] = f(sssm[:, c * NS:(c + 1) * NS].reshape(N_SSD, NS, NH * HP, NST))
        m["state_conv"] = f(sconv[:, c * NS:(c + 1) * NS].reshape(N_SSD, NS, 3, 32, 128).transpose(0, 1, 4, 3, 2))
        if N_SB > 0:
            m["cache_k"] = f(ck[:, c * NS:(c + 1) * NS].reshape(N_SB, NS, PAST, D))
            m["cache_v"] = f(cv[:, c * NS:(c + 1) * NS].reshape(N_SB, NS, PAST, D))
        else:
            m["cache_k"] = np.zeros((1, NS, PAST, D), np.float32)
            m["cache_v"] = np.zeros((1, NS, PAST, D), np.float32)
        maps.append(m)
    return maps


def assemble(results, n_cores, NB, SEQ, NS, DSEQ, PAST, DEPTH):
    N_SSD = (DEPTH + 1) // 2
    N_SB = DEPTH // 2
    cat = lambda k, ax: np.concatenate([np.asarray(r[k]) for r in results], axis=ax)
    y_all = [np.asarray(r["y_all"]) for r in results]
    y_p = np.concatenate([y[:NB * SEQ].reshape(NB, SEQ, D) for y in y_all], axis=0)
    y_s = np.concatenate([y[NB * SEQ:].reshape(NS, DSEQ, D) for y in y_all], axis=0)
    pssm = cat("o_pssm", 1).reshape(N_SSD, n_cores * NB, NH, HP, NST)
    sssm = cat("o_sssm", 1).reshape(N_SSD, n_cores * NS, NH, HP, NST)
    pconv = np.ascontiguousarray(cat("o_pconv", 1).transpose(0, 1, 4, 3, 2)).reshape(N_SSD, n_cores * NB, 3, CONVD)
    sconv = np.ascontiguousarray(cat("o_sconv", 1).transpose(0, 1, 4, 3, 2)).reshape(N_SSD, n_cores * NS, 3, CONVD)
    pk = cat("o_pk", 1)[:N_SB].reshape(N_SB, n_cores * NB, SEQ, AH, AD)
    pv = cat("o_pv", 1)[:N_SB].reshape(N_SB, n_cores * NB, SEQ, AH, AD)
    sk = cat("o_sk", 1)[:N_SB].reshape(N_SB, n_cores * NS, DSEQ, AH, AD)
    sv = cat("o_sv", 1)[:N_SB].reshape(N_SB, n_cores * NS, DSEQ, AH, AD)
    outs = (y_p, y_s, pssm, pconv, pk, pv, sssm, sconv, sk, sv)
    return tuple(np.ascontiguousarray(o, dtype=np.float32) for o in outs)


def kernel(**inputs):
    n = 8
    NB, SEQ, NS, DSEQ, PAST, DEPTH = 4, 2048, 2, 32, 1024, 4
    nc = build(NB, SEQ, NS, DSEQ, PAST, DEPTH)
    maps = make_in_maps(inputs, n, NB, SEQ, NS, DSEQ, PAST, DEPTH)
    res = run_bass_kernel_spmd(nc, maps, core_ids=list(range(n)))
    return assemble(res.results, n, NB, SEQ, NS, DSEQ, PAST, DEPTH)
```

```python
import numpy as np
from contextlib import ExitStack
import concourse.bass as bass
import concourse.mybir as mybir
from concourse.bass_utils import run_bass_kernel_spmd

F32 = mybir.dt.float32
BF16 = mybir.dt.bfloat16
AF = mybir.ActivationFunctionType
ALU = mybir.AluOpType
AX = mybir.AxisListType

D = 1024
DFF = 4096
DIN = 2048
NH = 32
HP = 64
NG = 8
NST = 128
CONVD = 4096
DPROJ = 6176
AH = 16
AD = 64
EPS = 1e-6
import os as _os
TRIM = _os.environ.get('K_TRIM', '1') == '1'


class Buf:
    __slots__ = ("name", "lw", "rd")

    def __init__(s, name):
        s.name = name
        s.lw = None
        s.rd = {}


class Prog:
    ENG = ("pe", "act", "dve", "pool", "sp")

    def __init__(s, nc, es):
        s.nc = nc
        s.es = es
        s.sems = {}
        s.cnt = {}
        s.ops = {e: [] for e in s.ENG}
        s.waited = {e: {} for e in s.ENG}
        s.nbuf = 0
        s.nops = 0
        s.free = []
        s.stage_keys = []

    def buf(s, name=None):
        s.nbuf += 1
        return Buf((name or "b") + "_%d" % s.nbuf)

    def sem(s, key):
        if key not in s.sems:
            if key in s.ENG or not s.free:
                s.sems[key] = s.es.enter_context(s.nc.semaphore("s%d" % len(s.sems)))
                s.cnt[key] = 0
            else:
                h, c0 = s.free.pop()
                s.sems[key] = h
                s.cnt[key] = c0
            if key not in s.ENG:
                s.stage_keys.append(key)
        return s.sems[key]

    def retire(s):
        for k in s.stage_keys:
            s.free.append((s.sems[k], s.cnt[k]))
        s.stage_keys = []

    def _waits(s, eng, deps):
        w = s.waited[eng]
        best = {}
        for d in deps:
            if d is None:
                continue
            k, v = d
            if w.get(k, 0) >= v:
                continue
            if best.get(k, 0) < v:
                best[k] = v
        out = []
        for k, v in best.items():
            w[k] = v
            out.append((k, v))
        return out

    def op(s, eng, name, kw, reads=(), writes=(), inc=True):
        deps = []
        for b in reads:
            deps.append(b.lw)
        for b in writes:
            deps.append(b.lw)
            for k, v in b.rd.items():
                if k != eng:
                    deps.append((k, v))
        if eng == "pe":
            deps = [d for d in deps if d is not None and d[0] != "pe"]
        waits = s._waits(eng, deps)
        s.sem(eng)
        val = s.cnt[eng] + 1
        if inc:
            s.cnt[eng] = val
        for b in reads:
            if b.rd.get(eng, 0) < val:
                b.rd[eng] = val
        for b in writes:
            b.lw = (eng, val)
            b.rd = {}
        s.ops[eng].append((waits, name, kw, (eng, 1) if inc else None))
        s.nops += 1

    def dma(s, q, out_ap, in_ap, dst, src, kind, **kw):
        key = ("ld:" + dst.name) if kind == "ld" else ("st:" + src.name)
        s.sem(key)
        deps = [src.lw, dst.lw] + list(dst.rd.items())
        if s.cnt[key] > 0:
            deps.append((key, s.cnt[key]))
        waits = s._waits(q, deps)
        val = s.cnt[key] + 16
        s.cnt[key] = val
        if src.rd.get(key, 0) < val:
            src.rd[key] = val
        dst.lw = (key, val)
        dst.rd = {}
        k2 = dict(out=out_ap, in_=in_ap)
        k2.update(kw)
        s.ops[q].append((waits, "dma_start", k2, (key, 16)))
        s.nops += 1

    def barrier(s):
        for e in s.ENG:
            deps = [(k, v) for k, v in s.cnt.items() if v > 0 and k != e]
            waits = s._waits(e, deps)
            if waits:
                s.ops[e].append((waits, None, None, None))

    def emit(s):
        nc = s.nc
        with nc.Block() as block:
            def mk(eng):
                def f(e):
                    for waits, name, kw, inc in s.ops[eng]:
                        for k, v in waits:
                            e.wait_ge(s.sems[k], v)
                        if name is not None:
                            ins = getattr(e, name)(**kw)
                            if inc:
                                ins.then_inc(s.sems[inc[0]], inc[1])
                return f
            block.tensor(mk("pe"))
            block.scalar(mk("act"))
            block.vector(mk("dve"))
            block.gpsimd(mk("pool"))
            block.sync(mk("sp"))


class T:
    __slots__ = ("t", "b")

    def __init__(s, t, b):
        s.t = t
        s.b = b

    def __getitem__(s, k):
        return s.t[k]


class Ring:
    def __init__(s, items):
        s.items = items
        s.i = 0

    def next(s):
        it = s.items[s.i % len(s.items)]
        s.i += 1
        return it


class Bank:
    __slots__ = ("f32", "bf", "b")

    def __init__(s, t, b):
        s.f32 = t
        s.bf = t[:].bitcast(BF16)
        s.b = b


def build(NB, SEQ, NS, DSEQ, PAST, DEPTH, only=None):
    nc = bass.Bass("TRN2", target_bir_lowering=False)
    NTP = NB * SEQ
    NTS = NS * DSEQ
    NTOK = NTP + NTS
    N_SSD = (DEPTH + 1) // 2
    N_SB = DEPTH // 2
    NSB1 = max(N_SB, 1)
    KTOT = PAST + DSEQ
    KPAD = ((KTOT + 127) // 128) * 128

    def di(n, sh, dt=F32):
        return nc.dram_tensor(n, list(sh), dt, kind="ExternalInput").ap()

    def do(n, sh):
        return nc.dram_tensor(n, list(sh), F32, kind="ExternalOutput").ap()

    def dx(n, sh, dt=F32):
        return nc.dram_tensor(n, list(sh), dt).ap()

    x_all = di("x_all", [NTOK, D])
    state_ssm = di("state_ssm", [N_SSD, NS, NH * HP, NST])
    state_conv = di("state_conv", [N_SSD, NS, 128, 32, 3])
    cache_k = di("cache_k", [NSB1, NS, PAST, D])
    cache_v = di("cache_v", [NSB1, NS, PAST, D])
    norm_mix = di("norm_mix", [DEPTH, D])
    norm_mlp = di("norm_mlp", [DEPTH, D])
    ssd_in_w = di("ssd_in_w", [N_SSD, D, DPROJ])
    ssd_conv_w = di("ssd_conv_w", [N_SSD, 128, 32, 4])
    ssd_conv_b = di("ssd_conv_b", [N_SSD, 128, 32])
    ssd_dt_bias = di("ssd_dt_bias", [N_SSD, NH])
    ssd_a_log = di("ssd_a_log", [N_SSD, NH])
    ssd_d = di("ssd_d", [N_SSD, NH])
    ssd_norm_w = di("ssd_norm_w", [N_SSD, DIN])
    ssd_out_w = di("ssd_out_w", [N_SSD, DIN, D])
    sb_qkv_w = di("sb_qkv_w", [NSB1, D, 3 * D])
    sb_q_gain = di("sb_q_gain", [NSB1, AD])
    sb_k_gain = di("sb_k_gain", [NSB1, AD])
    sb_out_w = di("sb_out_w", [NSB1, D, D])
    mlp_up = di("mlp_up", [DEPTH, D, DFF])
    mlp_down = di("mlp_down", [DEPTH, DFF, D])
    c_ident = di("c_ident", [128, 128])
    c_tri = di("c_tri", [128, 128])
    c_su = di("c_su", [128, 128])
    c_mask = di("c_mask", [128, 4, 512])
    c_masks = di("c_masks", [128, 32])

    y_all = do("y_all", [NTOK, D])
    o_pssm = do("o_pssm", [N_SSD, NB, NH * HP, NST])
    o_pconv = do("o_pconv", [N_SSD, NB, 128, 32, 3])
    o_pk = do("o_pk", [NSB1, NB, SEQ, D])
    o_pv = do("o_pv", [NSB1, NB, SEQ, D])
    o_sssm = do("o_sssm", [N_SSD, NS, NH * HP, NST])
    o_sconv = do("o_sconv", [N_SSD, NS, 128, 32, 3])
    o_sk = do("o_sk", [NSB1, NS, DSEQ, D])
    o_sv = do("o_sv", [NSB1, NS, DSEQ, D])

    xs = dx("xs", [NTOK, D])
    Ys = dx("Ys", [NTOK, DIN])
    QTp = dx("QTp", [NB, 128, 8, SEQ], BF16)
    KTp = dx("KTp", [NB, 128, 8, SEQ], BF16)
    Vp = dx("Vp", [NB, SEQ, D], BF16)
    QTs = dx("QTs", [NS, 128, 8, DSEQ], BF16)
    KTs = dx("KTs", [NS, 128, 8, KPAD], BF16)
    Vs = dx("Vs", [NS, DSEQ, D], BF16)

    top = ExitStack()
    with top:
        P = Prog(nc, top)
        dbufs = {}

        def db(name, idx=0):
            k = (name, idx)
            if k not in dbufs:
                dbufs[k] = P.buf("d_" + name)
            return dbufs[k]

        cst = db("const")

        banks = [Bank(top.enter_context(nc.psum_tensor("ps%d" % i, [128, 512], F32)), P.buf("ps%d" % i))
                 for i in range(8)]
        bstate = {"i": 0, "held": set()}

        def pb():
            while True:
                i = bstate["i"] % 8
                bstate["i"] += 1
                if i not in bstate["held"]:
                    return banks[i]

        def hold():
            b = pb()
            bstate["held"].add(banks.index(b))
            return b

        def release(b):
            bstate["held"].discard(banks.index(b))

        cur = {"es": None, "n": 0}

        def sb(shape, dt, name=None):
            cur["n"] += 1
            nm = (name or "t") + "_%d" % cur["n"]
            t = cur["es"].enter_context(nc.sbuf_tensor(nm, list(shape), dt))
            return T(t, P.buf(nm))

        def ring(n, shape, dt, name=None):
            return Ring([sb(shape, dt, name) for _ in range(n)])

        tiles = []
        for b in range(NB):
            for i in range(SEQ // 128):
                tiles.append(dict(row0=b * SEQ + i * 128, nt=128, grp="p", seq=b, pos=i * 128))
        for b in range(NS):
            tiles.append(dict(row0=NTP + b * DSEQ, nt=DSEQ, grp="s", seq=b, pos=0))

        def load_consts():
            c = {}
            c["identb"] = sb([128, 128], BF16, "identb")
            P.dma("pool", c["identb"][:], c_ident[:, :], c["identb"].b, cst, "ld")
            return c

        def front(C, src, tl, gB, dstT, col0, rings):
            nt = tl["nt"]
            r0 = tl["row0"]
            xin = rings["xin"].next()
            P.dma("sp", xin[:nt, :], src[r0:r0 + nt, :], xin.b, db(src.name, r0), "ld")
            ss = rings["ss"].next()
            junk = rings["junk"].next()
            P.op("act", "activation", dict(out=junk[:nt, :], in_=xin[:nt, :], func=AF.Square, accum_out=ss[:nt, 0:1]),
                 [xin.b], [junk.b, ss.b])
            P.op("act", "activation", dict(out=ss[:nt, 2:3], in_=ss[:nt, 0:1], func=AF.Ln, scale=1.0 / D, bias=EPS),
                 [ss.b], [ss.b])
            P.op("act", "activation", dict(out=ss[:nt, 3:4], in_=ss[:nt, 2:3], func=AF.Exp, scale=-0.5), [ss.b], [ss.b])
            xn = rings["xn"].next()
            P.op("dve", "scalar_tensor_tensor", dict(out=xn[:nt, :], in0=xin[:nt, :], scalar=ss[:nt, 3:4], in1=gB[:nt, :],
                                                     op0=ALU.mult, op1=ALU.mult), [xin.b, ss.b, gB.b], [xn.b])
            bk = pb()
            for kc in range(8):
                P.op("pe", "transpose", dict(out=bk.bf[:, kc * 128:kc * 128 + nt], in_=xn[:nt, kc * 128:(kc + 1) * 128],
                                             identity=C["identb"][:nt, :nt]), [xn.b, C["identb"].b], [bk.b], inc=(kc == 7))
            P.op("act", "activation", dict(out=dstT[:, :, col0:col0 + nt],
                                           in_=bk.bf.rearrange("p (k t) -> p k t", k=8)[:, :, :nt], func=AF.Copy),
                 [bk.b], [dstT.b])
            return xin

        def load_w(dst, dram_ap, rows_per_dma=128):
            K, N = dram_ap.shape
            for kc in range(K // 128):
                for n0 in range(0, N, 2048):
                    n1 = min(N, n0 + 2048)
                    P.dma("pool", dst[:, kc, n0:n1], dram_ap[kc * 128:(kc + 1) * 128, n0:n1], dst.b, cst, "ld")

        def load_bcast(dst, dram_1d):
            P.dma("sp", dst[:], dram_1d.partition_broadcast(128), dst.b, cst, "ld")

        def begin_stage():
            P.barrier()
            cur["es"] = ExitStack()
            cur["es"].__enter__()

        def end_stage():
            P.barrier()
            P.retire()
            cur["es"].__exit__(None, None, None)
            cur["es"] = None

        def run_streams(items, make_gen, width=2, stagger=0):
            active = []
            free_lanes = list(range(width))
            it = iter(items)
            pending = [True]
            first = [True]

            def start_more():
                while free_lanes and pending[0]:
                    try:
                        x = next(it)
                    except StopIteration:
                        pending[0] = False
                        break
                    lane = free_lanes.pop(0)
                    g = make_gen(x, lane)
                    active.append((lane, g))
                    if first[0]:
                        first[0] = False
                        for _ in range(stagger):
                            try:
                                next(g)
                            except StopIteration:
                                active.remove((lane, g))
                                free_lanes.append(lane)
                                break
            while True:
                start_more()
                if not active:
                    break
                for (lane, g) in list(active):
                    try:
                        next(g)
                    except StopIteration:
                        active.remove((lane, g))
                        free_lanes.append(lane)

        def stage_copy(src, dst):
            begin_stage()
            rg = ring(2, [128, D], F32, "cp")
            for tl in tiles:
                t = rg.next()
                nt, r0 = tl["nt"], tl["row0"]
                P.dma("sp", t[:nt, :], src[r0:r0 + nt, :], t.b, db(src.name, r0), "ld")
                P.dma("pool", dst[r0:r0 + nt, :], t[:nt, :], db(dst.name, r0), t.b, "st")
            end_stage()

        def stage_mlp(layer, src, dst):
            begin_stage()
            C = load_consts()
            Wup = sb([128, 8, DFF], BF16, "Wup")
            Wdn = sb([128, 32, D], BF16, "Wdn")
            load_w(Wup, mlp_up[layer])
            load_w(Wdn, mlp_down[layer])
            gB = sb([128, D], F32, "gB")
            load_bcast(gB, norm_mlp[layer])
            rings = dict(xin=ring(4, [128, D], F32, "xin"), ss=ring(4, [128, 4], F32, "ss"),
                         junk=ring(1, [128, D], BF16, "junk"), xn=ring(2, [128, D], BF16, "xn"))
            xnTs = ring(2, [128, 8, 256], BF16, "xnT")
            hT = sb([128, 32, 256], BF16, "hT")
            rr = ring(2, [128, 256], F32, "relu")
            yst = ring(2, [128, D], F32, "yst")
            macros = []
            curm = []
            tot = 0
            for tl in tiles:
                if tot + tl["nt"] > 256 or (curm and curm[-1]["grp"] != tl["grp"]):
                    macros.append(curm)
                    curm = []
                    tot = 0
                curm.append(tl)
                tot += tl["nt"]
            if curm:
                macros.append(curm)

            def do_front(m):
                xnT = xnTs.next()
                col = 0
                ent = []
                for tl in m:
                    xin = front(C, src, tl, gB, xnT, col, rings)
                    ent.append((xin, col, tl))
                    col += tl["nt"]
                return xnT, ent, col

            nxt = do_front(macros[0])
            for mi, m in enumerate(macros):
                xnT, ent, NT = nxt
                for f in range(32):
                    bk = pb()
                    for kc in range(8):
                        P.op("pe", "matmul", dict(out=bk.f32[:, :NT], lhsT=Wup[:, kc, f * 128:(f + 1) * 128],
                                                  rhs=xnT[:, kc, :NT], start=(kc == 0), stop=(kc == 7)),
                             [Wup.b, xnT.b], [bk.b], inc=(kc == 7))
                    r = rr.next()
                    P.op("act", "activation", dict(out=r[:, :NT], in_=bk.f32[:, :NT], func=AF.Relu), [bk.b], [r.b])
                    P.op("dve", "tensor_tensor", dict(out=hT[:, f, :NT], in0=r[:, :NT], in1=r[:, :NT], op=ALU.mult),
                         [r.b], [hT.b])
                if mi + 1 < len(macros):
                    nxt = do_front(macros[mi + 1])
                for (xin, col, tl) in ent:
                    nt, r0 = tl["nt"], tl["row0"]
                    y = yst.next()
                    for c in range(2):
                        bk = pb()
                        for f in range(32):
                            P.op("pe", "matmul", dict(out=bk.f32[:nt, :], lhsT=hT[:, f, col:col + nt],
                                                      rhs=Wdn[:, f, c * 512:(c + 1) * 512], start=(f == 0), stop=(f == 31)),
                                 [hT.b, Wdn.b], [bk.b], inc=(f == 31))
                        P.op("dve", "tensor_tensor", dict(out=y[:nt, c * 512:(c + 1) * 512], in0=bk.f32[:nt, :],
                                                          in1=xin[:nt, c * 512:(c + 1) * 512], op=ALU.add),
                             [bk.b, xin.b], [y.b])
                    P.dma("pool", dst[r0:r0 + nt, :], y[:nt, :], db(dst.name, r0), y.b, "st")
            end_stage()

        def run_sched(lane_queues):
            events = set()
            n = len(lane_queues)
            curg = [None] * n
            idx = [0] * n
            waiting = [None] * n
            while True:
                progressed = False
                alive = False
                for li, q in enumerate(lane_queues):
                    if curg[li] is None:
                        if idx[li] < len(q):
                            curg[li] = q[idx[li]]()
                            idx[li] += 1
                            waiting[li] = None
                        else:
                            continue
                    alive = True
                    if waiting[li] is not None:
                        if waiting[li] in events:
                            waiting[li] = None
                        else:
                            continue
                    try:
                        r = next(curg[li])
                        progressed = True
                        while isinstance(r, tuple) and r[0] == "set":
                            events.add(r[1])
                            r = next(curg[li])
                        if isinstance(r, tuple) and r[0] == "wait" and r[1] not in events:
                            waiting[li] = r[1]
                    except StopIteration:
                        curg[li] = None
                        progressed = True
                if not alive:
                    break
                assert progressed, "emission scheduler deadlock"

        def stage_ssd1(layer, j, src):
            begin_stage()
            C = load_consts()
            Wx = sb([128, 8, CONVD + NH], BF16, "Wx")
            for kc in range(8):
                for n0 in range(0, CONVD + NH, 2048):
                    n1 = min(CONVD + NH, n0 + 2048)
                    P.dma("pool", Wx[:, kc, n0:n1], ssd_in_w[j, kc * 128:(kc + 1) * 128, DIN + n0:DIN + n1], Wx.b, cst, "ld")
            gB = sb([128, D], F32, "gB")
            load_bcast(gB, norm_mix[layer])
            trib = sb([128, 128], BF16, "trib")
            sub = sb([128, 128], BF16, "sub")
            onesb = sb([128, 128], BF16, "onesb")
            identf = sb([128, 128], F32, "identf")
            P.dma("pool", trib[:], c_tri[:, :], trib.b, cst, "ld")
            P.dma("pool", sub[:], c_su[:, :], sub.b, cst, "ld")
            P.dma("sp", identf[:], c_ident[:, :], identf.b, cst, "ld")
            P.op("dve", "memset", dict(ap=onesb[:], constant=1.0), [], [onesb.b])
            cw = sb([128, 32, 4], F32, "cw")
            cbias = sb([128, 32], F32, "cbias")
            P.dma("sp", cw[:], ssd_conv_w[j], cw.b, cst, "ld")
            P.dma("sp", cbias[:], ssd_conv_b[j], cbias.b, cst, "ld")
            Abc = sb([128, NH], F32, "Abc")
            dtb = sb([128, NH], F32, "dtb")
            Dbc = sb([128, NH], F32, "Dbc")
            load_bcast(Abc, ssd_a_log[j])
            load_bcast(dtb, ssd_dt_bias[j])
            load_bcast(Dbc, ssd_d[j])
            P.op("act", "activation", dict(out=Abc[:], in_=Abc[:], func=AF.Exp), [Abc.b], [Abc.b])
            P.op("dve", "tensor_scalar", dict(out=Abc[:], in0=Abc[:], scalar1=-1.0, scalar2=None, op0=ALU.mult),
                 [Abc.b], [Abc.b])
            DI = sb([128, NH, 128], BF16, "DI")
            for h in range(NH):
                P.op("dve", "tensor_scalar", dict(out=DI[:, h, :], in0=C["identb"][:], scalar1=Dbc[:, h:h + 1], scalar2=None,
                                                  op0=ALU.mult), [C["identb"].b, Dbc.b], [DI.b])
            xnr = ring(2, [128, D], BF16, "xn")
            rings = dict(xin=ring(2, [128, D], F32, "xin"), ss=ring(4, [128, 4], F32, "ss"), junk=xnr, xn=xnr)
            MTM = 256
            hnT = sb([128, 8, MTM], BF16, "hnT")
            xTs = [sb([128, 16, MTM], BF16, "xT") for _ in range(2)]
            BTs = [sb([128, 8, MTM], BF16, "BT") for _ in range(2)]
            CTs = [sb([128, 8, MTM], BF16, "CT") for _ in range(2)]
            dtraws = [sb([128, 2, NH], F32, "dtraw") for _ in range(2)]
            Rr = ring(3, [128, MTM + 4], BF16, "R")
            dgr = ring(12, [128, 128], BF16, "dg")
            sgr = ring(3, [128, MTM], F32, "sg")
            halo16 = sb([128, 32, 3], BF16, "halo16")
            ncb = sb([128, 32], F32, "ncb")
            P.op("dve", "tensor_scalar", dict(out=ncb[:], in0=cbias[:], scalar1=-1.0, scalar2=None, op0=ALU.mult), [cbias.b], [ncb.b])
            hTq = [sb([128, 512], F32, "hT") for _ in range(4)]
            hTbq = [sb([128, 512], BF16, "hTb") for _ in range(4)]
            halo = sb([128, 32, 3], F32, "halo")
            tmpr = ring(2, [128, 512], F32, "ytmp")
            ystr = ring(2, [128, 1024], F32, "yst")
            lanes = []
            for ln in range(2):
                lanes.append(dict(sm=sb([128, 12, NH], F32, "sm"), adtb=sb([128, NH], BF16, "adtb"),
                                  xtok=sb([128, DIN], BF16, "xtok"), Btok=sb([128, D], BF16, "Btok"),
                                  Wbh=sb([128, 16 * 128], BF16, "Wbh"), Ebh=sb([128, 16 * 128], BF16, "Ebh"),
                                  cbm=sb([128, NG * 128], BF16, "cbm"), xddh=sb([128, 1024], BF16, "xddh")))

            seqs = [("p", b, SEQ) for b in range(NB)] + [("s", b, DSEQ) for b in range(NS)]
            for si, (grp, b, L) in enumerate(seqs):
                base = (b * SEQ) if grp == "p" else (NTP + b * DSEQ)
                CL = min(128, L)
                MT = min(MTM, L)
                NCH = MT // CL
                NM = L // MT
                if grp == "p":
                    for q in range(4):
                        P.op("dve", "memset", dict(ap=hTq[q][:], constant=0.0), [], [hTq[q].b])
                    P.op("dve", "memset", dict(ap=halo[:], constant=0.0), [], [halo.b])
                    P.op("dve", "memset", dict(ap=halo16[:], constant=0.0), [], [halo16.b])
                else:
                    P.dma("sp", halo[:], state_conv[j, b], halo.b, cst, "ld")
                    P.op("dve", "tensor_copy", dict(out=halo16[:], in_=halo[:]), [halo.b], [halo16.b])
                    for hf in range(2):
                        stg = ystr.next()
                        P.dma("sp", stg[:].rearrange("p (t n) -> p t n", t=8),
                              state_ssm[j, b, hf * 1024:(hf + 1) * 1024, :].rearrange("(t p) n -> p t n", p=128), stg.b, cst, "ld")
                        for qq in range(2):
                            q = hf * 2 + qq
                            bk = pb()
                            for i in range(4):
                                t = qq * 4 + i
                                P.op("pe", "transpose", dict(out=bk.f32[:, i * 128:(i + 1) * 128], in_=stg[:, t * 128:(t + 1) * 128],
                                                             identity=identf[:]), [stg.b, identf.b], [bk.b], inc=(i == 3))
                            P.op("act", "activation", dict(out=hTq[q][:], in_=bk.f32[:, :], func=AF.Copy), [bk.b], [hTq[q].b])
                for q in range(4):
                    P.op("act", "activation", dict(out=hTbq[q][:], in_=hTq[q][:], func=AF.Copy), [hTq[q].b], [hTbq[q].b])

                def conv_stream(m, si=si, base=base, CL=CL, MT=MT, NCH=NCH, NM=NM):
                    st = m % 2
                    xT, BT, CT, dtraw = xTs[st], BTs[st], CTs[st], dtraws[st]
                    m0 = m * MT
                    if m >= 2:
                        for ci in range(NCH):
                            yield ("wait", ("cdone", si, (m - 2) * NCH + ci))
                    if m >= 1:
                        yield ("wait", ("convdone", si, m - 1))
                    for i in range(NCH):
                        tl = dict(row0=base + m0 + i * CL, nt=CL)
                        front(C, src, tl, gB, hnT, i * CL, rings)
                        yield
                    for ci in range(NCH):
                        bk = pb()
                        for kc in range(8):
                            P.op("pe", "matmul", dict(out=bk.f32[:CL, :NH], lhsT=hnT[:, kc, ci * CL:(ci + 1) * CL],
                                                      rhs=Wx[:, kc, CONVD:CONVD + NH], start=(kc == 0), stop=(kc == 7)),
                                 [hnT.b, Wx.b], [bk.b], inc=(kc == 7))
                        P.op("act", "activation", dict(out=dtraw[:CL, ci, :], in_=bk.f32[:CL, :NH], func=AF.Copy), [bk.b], [dtraw.b])
                    yield
                    pipe = {}

                    def s1(ct):
                        bk = pb()
                        for kc in range(8):
                            P.op("pe", "matmul", dict(out=bk.f32[:, :MT], lhsT=Wx[:, kc, ct * 128:(ct + 1) * 128],
                                                      rhs=hnT[:, kc, :MT], start=(kc == 0), stop=(kc == 7)),
                                 [Wx.b, hnT.b], [bk.b], inc=(kc == 7))
                        R = Rr.next()
                        P.op("dve", "tensor_copy", dict(out=R[:, 0:3], in_=halo16[:, ct, :]), [halo16.b], [R.b])
                        P.op("act", "activation", dict(out=R[:, 3:3 + MT], in_=bk.f32[:, :MT], func=AF.Copy), [bk.b], [R.b])
                        P.op("dve", "tensor_copy", dict(out=halo16[:, ct, :], in_=R[:, MT:MT + 3]), [R.b], [halo16.b])
                        if m == NM - 1:
                            P.op("dve", "tensor_copy", dict(out=halo[:, ct, :], in_=bk.f32[:, MT - 3:MT]), [bk.b], [halo.b])
                        dgs = []
                        for tap in range(4):
                            dg = dgr.next()
                            P.op("dve", "tensor_scalar", dict(out=dg[:], in0=C["identb"][:], scalar1=cw[:, ct, tap:tap + 1], scalar2=None,
                                                              op0=ALU.mult), [C["identb"].b, cw.b], [dg.b])
                            dgs.append(dg)
                        pipe[ct] = dict(R=R, dgs=dgs)

                    def s2(ct):
                        R, dgs = pipe[ct]["R"], pipe[ct]["dgs"]
                        bk2 = pb()
                        for tap in range(4):
                            P.op("pe", "matmul", dict(out=bk2.f32[:, :MT], lhsT=dgs[tap][:], rhs=R[:, tap:tap + MT],
                                                      start=(tap == 0), stop=(tap == 3)), [dgs[tap].b, R.b], [bk2.b], inc=(tap == 3))
                        sg = sgr.next()
                        P.op("act", "activation", dict(out=sg[:, :MT], in_=bk2.f32[:, :MT], func=AF.Exp, scale=-1.0,
                                                       bias=ncb[:, ct:ct + 1]), [bk2.b, ncb.b], [sg.b])
                        P.op("act", "activation", dict(out=sg[:, :MT], in_=sg[:, :MT], func=AF.Ln, bias=1.0), [sg.b], [sg.b])
                        P.op("act", "activation", dict(out=sg[:, :MT], in_=sg[:, :MT], func=AF.Exp, scale=-1.0), [sg.b], [sg.b])
                        if ct < 16:
                            dT, dap = xT, xT[:, ct, :MT]
                        elif ct < 24:
                            dT, dap = BT, BT[:, ct - 16, :MT]
                        else:
                            dT, dap = CT, CT[:, ct - 24, :MT]
                        P.op("dve", "scalar_tensor_tensor", dict(out=dap, in0=bk2.f32[:, :MT], scalar=cbias[:, ct:ct + 1], in1=sg[:, :MT],
                                                                 op0=ALU.add, op1=ALU.mult), [bk2.b, cbias.b, sg.b], [dT.b])
                        del pipe[ct]

                    def s3(ct):
                        pass

                    def s4(ct):
                        pass

                    for step in range(32 + 1):
                        if step < 32:
                            s1(step)
                        if 0 <= step - 1 < 32:
                            s2(step - 1)
                        yield
                    yield ("set", ("convdone", si, m))

                def chunk_stream(c, lane, si=si, base=base, CL=CL, MT=MT, NCH=NCH):
                    Ln = lanes[lane]
                    m, ci = divmod(c, NCH)
                    st = m % 2
                    xT, BT, CT, dtraw = xTs[st], BTs[st], CTs[st], dtraws[st]
                    c0 = ci * CL
                    row0 = base + m * MT + c0
                    sm, adtb, xtok, Btok, Wbh, Ebh, cbm, xddh = (Ln["sm"], Ln["adtb"], Ln["xtok"], Ln["Btok"], Ln["Wbh"],
                                                                 Ln["Ebh"], Ln["cbm"], Ln["xddh"])
                    yield ("wait", ("convdone", si, m))
                    P.op("dve", "tensor_tensor", dict(out=sm[:CL, 0, :], in0=dtraw[:CL, ci, :], in1=dtb[:CL, :], op=ALU.add),
                         [dtraw.b, dtb.b], [sm.b])
                    P.op("dve", "scalar_tensor_tensor", dict(out=sm[:CL, 1, :], in0=sm[:CL, 0, :], scalar=-1.0,
                                                             in1=sm[:CL, 0, :], op0=ALU.mult, op1=ALU.max), [sm.b], [sm.b])
                    P.op("act", "activation", dict(out=sm[:CL, 2, :], in_=sm[:CL, 1, :], func=AF.Exp, scale=-1.0), [sm.b], [sm.b])
                    P.op("act", "activation", dict(out=sm[:CL, 3, :], in_=sm[:CL, 2, :], func=AF.Ln, bias=1.0), [sm.b], [sm.b])
                    P.op("dve", "scalar_tensor_tensor", dict(out=sm[:CL, 4, :], in0=sm[:CL, 0, :], scalar=0.0,
                                                             in1=sm[:CL, 3, :], op0=ALU.max, op1=ALU.add), [sm.b], [sm.b])
                    dt = sm[:CL, 4, :]
                    P.op("dve", "tensor_tensor", dict(out=adtb[:CL, :], in0=dt, in1=Abc[:CL, :], op=ALU.mult),
                         [sm.b, Abc.b], [adtb.b])
                    P.op("act", "activation", dict(out=sm[:CL, 5, :], in_=dt, func=AF.Ln), [sm.b], [sm.b])
                    yield
                    bk2 = pb()
                    P.op("pe", "matmul", dict(out=bk2.f32[:CL, 0:NH], lhsT=trib[:CL, :CL], rhs=adtb[:CL, :],
                                              start=True, stop=True), [trib.b, adtb.b], [bk2.b], inc=False)
                    P.op("pe", "matmul", dict(out=bk2.f32[:, NH:2 * NH], lhsT=onesb[:CL, :], rhs=adtb[:CL, :],
                                              start=True, stop=True), [onesb.b, adtb.b], [bk2.b])
                    P.op("act", "activation", dict(out=sm[:CL, 6, :], in_=bk2.f32[:CL, 0:NH], func=AF.Exp), [bk2.b], [sm.b])
                    P.op("act", "activation", dict(out=sm[:, 7, :], in_=bk2.f32[:, NH:2 * NH], func=AF.Exp), [bk2.b], [sm.b])
                    P.op("act", "activation", dict(out=sm[:CL, 8, :], in_=bk2.f32[:CL, 0:NH], func=AF.Copy), [bk2.b], [sm.b])
                    P.op("dve", "tensor_tensor", dict(out=sm[:CL, 9, :], in0=bk2.f32[:CL, NH:2 * NH], in1=sm[:CL, 8, :],
                                                      op=ALU.subtract), [bk2.b, sm.b], [sm.b])
                    P.op("act", "activation", dict(out=sm[:CL, 10, :], in_=sm[:CL, 9, :], func=AF.Exp), [sm.b], [sm.b])
                    P.op("dve", "tensor_tensor", dict(out=sm[:CL, 11, :], in0=sm[:CL, 10, :], in1=dt, op=ALU.mult), [sm.b], [sm.b])
                    yield
                    for half in range(2):
                        bk = pb()
                        for i in range(8):
                            P.op("pe", "transpose", dict(out=bk.bf[:CL, i * 128:(i + 1) * 128],
                                                         in_=xT[:, half * 8 + i, c0:c0 + CL], identity=C["identb"][:]),
                                 [xT.b, C["identb"].b], [bk.b], inc=(i == 7))
                        P.op("act", "activation", dict(out=xtok[:CL, half * 1024:(half + 1) * 1024], in_=bk.bf[:CL, :],
                                                       func=AF.Copy), [bk.b], [xtok.b])
                        yield
                    bk = pb()
                    for g in range(8):
                        P.op("pe", "transpose", dict(out=bk.bf[:CL, g * 128:(g + 1) * 128], in_=BT[:, g, c0:c0 + CL],
                                                     identity=C["identb"][:]), [BT.b, C["identb"].b], [bk.b], inc=(g == 7))
                    P.op("act", "activation", dict(out=Btok[:CL, :], in_=bk.bf[:CL, :], func=AF.Copy), [bk.b], [Btok.b])
                    yield
                    cv = cbm[:CL, 0:NG * CL].rearrange("p (g t) -> p g t", g=NG)
                    for half in range(2):
                        bk = pb()
                        for gi in range(4):
                            g = half * 4 + gi
                            P.op("pe", "matmul", dict(out=bk.f32[:CL, gi * CL:(gi + 1) * CL], lhsT=BT[:, g, c0:c0 + CL],
                                                      rhs=CT[:, g, c0:c0 + CL], start=True, stop=True),
                                 [BT.b, CT.b], [bk.b], inc=(gi == 3))
                        P.op("dve", "tensor_tensor", dict(out=cv[:, half * 4:(half + 1) * 4, :],
                                                          in0=bk.f32[:CL, 0:4 * CL].rearrange("p (g t) -> p g t", g=4),
                                                          in1=trib[:CL, :CL].unsqueeze(1).to_broadcast([CL, 4, CL]),
                                                          op=ALU.mult), [bk.b, trib.b], [cbm.b])
                        yield
                    Wv = Wbh[:CL, 0:16 * CL].rearrange("p (h t) -> p h t", h=16)
                    Ev = Ebh[:CL, 0:16 * CL].rearrange("p (h t) -> p h t", h=16)
                    for hf in range(2):
                        P.op("dve", "tensor_tensor", dict(out=Wv, in0=trib[:CL, :CL].unsqueeze(1).to_broadcast([CL, 16, CL]),
                                                           in1=adtb[:CL, hf * 16:(hf + 1) * 16].unsqueeze(2).to_broadcast([CL, 16, CL]),
                                                           op=ALU.mult), [trib.b, adtb.b], [Wbh.b])
                        for gi in range(4):
                            bk = pb()
                            P.op("pe", "matmul", dict(out=bk.f32[:CL, 0:4 * CL], lhsT=sub[:CL, :CL],
                                                      rhs=Wbh[:CL, gi * 4 * CL:(gi + 1) * 4 * CL], start=True, stop=True),
                                 [sub.b, Wbh.b], [bk.b])
                            for hi in range(4):
                                hl = gi * 4 + hi
                                h = hf * 16 + hl
                                P.op("act", "activation", dict(out=Ev[:, hl, :], in_=bk.f32[:CL, hi * CL:(hi + 1) * CL],
                                                               func=AF.Exp, bias=sm[:CL, 5, h:h + 1]), [bk.b, sm.b], [Ebh.b])
                            yield
                        E4 = Ebh[:CL, 0:16 * CL].rearrange("p (g r t) -> p g r t", g=4, r=4)
                        P.op("dve", "tensor_tensor", dict(out=E4, in0=E4,
                                                          in1=cv[:, hf * 4:(hf + 1) * 4, :].unsqueeze(2).to_broadcast([CL, 4, 4, CL]),
                                                          op=ALU.mult), [Ebh.b, cbm.b], [Ebh.b])
                        yield
                        if c > 0:
                            yield ("wait", ("state", si, c - 1, hf))
                        yst = ystr.next()
                        for qq in range(2):
                            q = hf * 2 + qq
                            bkY = pb()
                            for hh in range(8):
                                hl = qq * 8 + hh
                                h = hf * 16 + hl
                                P.op("pe", "matmul", dict(out=bkY.f32[:CL, hh * 64:(hh + 1) * 64], lhsT=Ev[:, hl, :],
                                                          rhs=xtok[:CL, h * 64:(h + 1) * 64], start=True, stop=False),
                                     [Ebh.b, xtok.b], [bkY.b], inc=False)
                                P.op("pe", "matmul", dict(out=bkY.f32[:CL, hh * 64:(hh + 1) * 64], lhsT=DI[:CL, h, :CL],
                                                          rhs=xtok[:CL, h * 64:(h + 1) * 64], start=False, stop=True),
                                     [DI.b, xtok.b], [bkY.b], inc=(hh == 7))
                            bkO = pb()
                            for gg in range(2):
                                g = q * 2 + gg
                                P.op("pe", "matmul", dict(out=bkO.f32[:CL, gg * 256:(gg + 1) * 256], lhsT=CT[:, g, c0:c0 + CL],
                                                          rhs=hTbq[q][:, gg * 256:(gg + 1) * 256], start=True, stop=True),
                                     [CT.b, hTbq[q].b], [bkO.b], inc=(gg == 1))
                            tmp = tmpr.next()
                            P.op("dve", "tensor_tensor", dict(out=tmp[:CL, :].rearrange("p (h d) -> p h d", h=8),
                                                              in0=bkO.f32[:CL, :].rearrange("p (h d) -> p h d", h=8),
                                                              in1=sm[:CL, 6, q * 8:(q + 1) * 8].unsqueeze(2).to_broadcast([CL, 8, 64]),
                                                              op=ALU.mult), [bkO.b, sm.b], [tmp.b])
                            P.op("dve", "tensor_tensor", dict(out=yst[:CL, qq * 512:(qq + 1) * 512], in0=bkY.f32[:CL, :],
                                                              in1=tmp[:CL, :], op=ALU.add), [bkY.b, tmp.b], [yst.b])
                            yield
                        P.dma("pool", Ys[row0:row0 + CL, hf * 1024:(hf + 1) * 1024], yst[:CL, :], db("Ys", row0), yst.b, "st")
                        P.op("dve", "tensor_tensor", dict(out=xddh[:CL, :].rearrange("p (h d) -> p h d", h=16),
                                                          in0=xtok[:CL, hf * 1024:(hf + 1) * 1024].rearrange("p (h d) -> p h d", h=16),
                                                          in1=sm[:CL, 11, hf * 16:(hf + 1) * 16].unsqueeze(2).to_broadcast([CL, 16, 64]),
                                                          op=ALU.mult), [xtok.b, sm.b], [xddh.b])
                        for qq in range(2):
                            q = hf * 2 + qq
                            bkH = pb()
                            for gg in range(2):
                                g = q * 2 + gg
                                P.op("pe", "matmul", dict(out=bkH.f32[:, gg * 256:(gg + 1) * 256], lhsT=Btok[:CL, g * 128:(g + 1) * 128],
                                                          rhs=xddh[:CL, (qq * 2 + gg) * 256:(qq * 2 + gg + 1) * 256], start=True, stop=True),
                                     [Btok.b, xddh.b], [bkH.b], inc=(gg == 1))
                            hv = hTq[q]
                            P.op("dve", "tensor_tensor", dict(out=hv[:].rearrange("p (h d) -> p h d", h=8),
                                                              in0=hv[:].rearrange("p (h d) -> p h d", h=8),
                                                              in1=sm[:, 7, q * 8:(q + 1) * 8].unsqueeze(2).to_broadcast([128, 8, 64]),
                                                              op=ALU.mult), [hv.b, sm.b], [hv.b])
                            P.op("dve", "tensor_tensor", dict(out=hv[:], in0=bkH.f32[:, :], in1=hv[:], op=ALU.add), [bkH.b, hv.b], [hv.b])
                            P.op("act", "activation", dict(out=hTbq[q][:], in_=hv[:], func=AF.Copy), [hv.b], [hTbq[q].b])
                            yield
                        yield ("set", ("state", si, c, hf))
                    yield ("set", ("cdone", si, c))

                nchunks = NM * NCH
                convq = [(lambda m=m: conv_stream(m)) for m in range(NM)]
                laneA = [(lambda c=c: chunk_stream(c, 0)) for c in range(0, nchunks, 2)]
                laneB = [(lambda c=c: chunk_stream(c, 1)) for c in range(1, nchunks, 2)]
                run_sched([convq, laneA, laneB])

                o_ssm = (o_pssm if grp == "p" else o_sssm)[j, b]
                o_conv = (o_pconv if grp == "p" else o_sconv)[j, b]
                for hf in range(2):
                    stg = ystr.next()
                    for qq in range(2):
                        q = hf * 2 + qq
                        bk = pb()
                        for i in range(4):
                            P.op("pe", "transpose", dict(out=bk.f32[:, i * 128:(i + 1) * 128], in_=hTq[q][:, i * 128:(i + 1) * 128],
                                                         identity=identf[:]), [hTq[q].b, identf.b], [bk.b], inc=(i == 3))
                        P.op("act", "activation", dict(out=stg[:, qq * 512:(qq + 1) * 512], in_=bk.f32[:, :], func=AF.Copy),
                             [bk.b], [stg.b])
                    P.dma("pool", o_ssm[hf * 1024:(hf + 1) * 1024, :].rearrange("(t p) n -> p t n", p=128),
                          stg[:].rearrange("p (t n) -> p t n", t=8), db("o_ssm_%s" % grp, b), stg.b, "st")
                P.dma("pool", o_conv, halo[:], db("o_conv_%s" % grp, b), halo.b, "st")
            end_stage()

        def stage_ssd2(layer, j, src, dst):
            begin_stage()
            C = load_consts()
            Wz = sb([128, 8, DIN], BF16, "Wz")
            for kc in range(8):
                P.dma("pool", Wz[:, kc, :], ssd_in_w[j, kc * 128:(kc + 1) * 128, 0:DIN], Wz.b, cst, "ld")
            Wo = sb([128, 16, D], BF16, "Wo")
            load_w(Wo, ssd_out_w[j])
            gB = sb([128, D], F32, "gB")
            load_bcast(gB, norm_mix[layer])
            nwB = sb([128, DIN], F32, "nwB")
            load_bcast(nwB, ssd_norm_w[j])
            shared = dict(ss=ring(4, [128, 4], F32, "ss"), junk=ring(1, [128, D], BF16, "junk"),
                          xn=ring(2, [128, D], BF16, "xn"))
            junk2 = sb([128, 256], F32, "junk2")
            lanes = []
            for ln in range(2):
                rg = dict(shared)
                rg["xin"] = ring(1, [128, D], F32, "xin")
                lanes.append(dict(rings=rg, hnT=sb([128, 8, 128], BF16, "hnT"), yin=sb([128, DIN], F32, "yin"),
                                  sz=sb([128, DIN], F32, "sz"), gs=sb([128, 4, NG], F32, "gs"),
                                  ynb=sb([128, DIN], BF16, "ynb"), ynT=sb([128, 16, 128], BF16, "ynT"),
                                  y=sb([128, D], F32, "yst")))

            def tile_stream(tl, lane):
                L = lanes[lane]
                nt, r0 = tl["nt"], tl["row0"]
                hnT, yin, sz, gs, ynb, ynT, y = L["hnT"], L["yin"], L["sz"], L["gs"], L["ynb"], L["ynT"], L["y"]
                P.dma("sp", yin[:nt, :], Ys[r0:r0 + nt, :], yin.b, db("Ys", r0), "ld")
                xin = front(C, src, tl, gB, hnT, 0, L["rings"])
                yield
                for c in range(4):
                    bk = pb()
                    for kc in range(8):
                        P.op("pe", "matmul", dict(out=bk.f32[:nt, :], lhsT=hnT[:, kc, :nt], rhs=Wz[:, kc, c * 512:(c + 1) * 512],
                                                  start=(kc == 0), stop=(kc == 7)), [hnT.b, Wz.b], [bk.b], inc=(kc == 7))
                    sl = sz[:nt, c * 512:(c + 1) * 512]
                    P.op("act", "activation", dict(out=sl, in_=bk.f32[:nt, :], func=AF.Exp, scale=-1.0), [bk.b], [sz.b])
                    P.op("act", "activation", dict(out=sl, in_=sl, func=AF.Ln, bias=1.0), [sz.b], [sz.b])
                    P.op("act", "activation", dict(out=sl, in_=sl, func=AF.Exp, scale=-1.0), [sz.b], [sz.b])
                    P.op("dve", "tensor_tensor", dict(out=sl, in0=sl, in1=yin[:nt, c * 512:(c + 1) * 512], op=ALU.mult),
                         [sz.b, yin.b], [sz.b])
                    P.op("dve", "tensor_tensor", dict(out=sl, in0=bk.f32[:nt, :], in1=sl, op=ALU.mult), [bk.b, sz.b], [sz.b])
                    yield
                for g in range(NG):
                    P.op("act", "activation", dict(out=junk2[:nt, :], in_=sz[:nt, g * 256:(g + 1) * 256], func=AF.Square,
                                                   accum_out=gs[:nt, 0, g:g + 1]), [sz.b], [junk2.b, gs.b])
                yield
                P.op("act", "activation", dict(out=gs[:nt, 2, :], in_=gs[:nt, 0, :], func=AF.Ln, scale=1.0 / 256, bias=EPS),
                     [gs.b], [gs.b])
                P.op("act", "activation", dict(out=gs[:nt, 3, :], in_=gs[:nt, 2, :], func=AF.Exp, scale=-0.5), [gs.b], [gs.b])
                yield
                P.op("dve", "tensor_tensor", dict(out=sz[:nt, :].rearrange("p (g d) -> p g d", g=NG),
                                                  in0=sz[:nt, :].rearrange("p (g d) -> p g d", g=NG),
                                                  in1=gs[:nt, 3, :].unsqueeze(2).to_broadcast([nt, NG, 256]), op=ALU.mult),
                     [sz.b, gs.b], [sz.b])
                P.op("dve", "tensor_tensor", dict(out=ynb[:nt, :], in0=sz[:nt, :], in1=nwB[:nt, :], op=ALU.mult),
                     [sz.b, nwB.b], [ynb.b])
                yield
                for half in range(2):
                    bk = pb()
                    for i in range(8):
                        ct = half * 8 + i
                        P.op("pe", "transpose", dict(out=bk.bf[:, i * 128:i * 128 + nt], in_=ynb[:nt, ct * 128:(ct + 1) * 128],
                                                     identity=C["identb"][:nt, :nt]), [ynb.b, C["identb"].b], [bk.b], inc=(i == 7))
                    P.op("act", "activation", dict(out=ynT[:, half * 8:(half + 1) * 8, :nt],
                                                   in_=bk.bf.rearrange("p (k t) -> p k t", k=8)[:, :, :nt], func=AF.Copy),
                         [bk.b], [ynT.b])
                    yield
                for c in range(2):
                    bk = pb()
                    for ct in range(16):
                        P.op("pe", "matmul", dict(out=bk.f32[:nt, :], lhsT=ynT[:, ct, :nt], rhs=Wo[:, ct, c * 512:(c + 1) * 512],
                                                  start=(ct == 0), stop=(ct == 15)), [ynT.b, Wo.b], [bk.b], inc=(ct == 15))
                    P.op("dve", "tensor_tensor", dict(out=y[:nt, c * 512:(c + 1) * 512], in0=bk.f32[:nt, :],
                                                      in1=xin[:nt, c * 512:(c + 1) * 512], op=ALU.add), [bk.b, xin.b], [y.b])
                    yield
                P.dma("pool", dst[r0:r0 + nt, :], y[:nt, :], db(dst.name, r0), y.b, "st")

            run_streams(tiles, tile_stream, width=2, stagger=6)
            end_stage()

        def stage_sb1(layer, j, src):
            begin_stage()
            C = load_consts()
            Wqkv = sb([128, 8, 3 * D], BF16, "Wqkv")
            load_w(Wqkv, sb_qkv_w[j])
            gB = sb([128, D], F32, "gB")
            load_bcast(gB, norm_mix[layer])
            qg = sb([128, AD], F32, "qg")
            kg = sb([128, AD], F32, "kg")
            load_bcast(qg, sb_q_gain[j])
            load_bcast(kg, sb_k_gain[j])
            P.op("dve", "tensor_scalar", dict(out=qg[:], in0=qg[:], scalar1=0.125, scalar2=None, op0=ALU.mult), [qg.b], [qg.b])
            rings = dict(xin=ring(2, [128, D], F32, "xin"), ss=ring(4, [128, 4], F32, "ss"),
                         junk=ring(1, [128, D], BF16, "junk"), xn=ring(2, [128, D], BF16, "xn"))
            xnTs = ring(2, [128, 8, 128], BF16, "xnT")
            sqr = ring(2, [128, 512], F32, "sq")
            t1r = ring(2, [128, 512], F32, "t1")
            ssqr = ring(4, [128, 4, 8], F32, "ssq")
            koutr = ring(2, [128, D], F32, "kout")
            voutr = ring(2, [128, D], F32, "vout")
            knbr = ring(2, [128, D], BF16, "knb")
            qnbr = ring(2, [128, D], BF16, "qnb")
            vbr = ring(2, [128, D], BF16, "vb")
            qTr = ring(2, [128, 8, 128], BF16, "qT")
            kTr = ring(2, [128, 8, 128], BF16, "kT")
            ckr = ring(2, [128, D], BF16, "ck")

            def transpose_store(srcT, nt, stg, dram_ap, dbuf):
                bk = pb()
                for hp in range(8):
                    P.op("pe", "transpose", dict(out=bk.bf[:, hp * 128:hp * 128 + nt], in_=srcT[:nt, hp * 128:(hp + 1) * 128],
                                                 identity=C["identb"][:nt, :nt]), [srcT.b, C["identb"].b], [bk.b], inc=(hp == 7))
                P.op("act", "activation", dict(out=stg[:, :, :nt], in_=bk.bf.rearrange("p (k t) -> p k t", k=8)[:, :, :nt],
                                               func=AF.Copy), [bk.b], [stg.b])
                P.dma("pool", dram_ap, stg[:, :, :nt], dbuf, stg.b, "st")

            for tl in tiles:
                nt, r0, grp, b, pos = tl["nt"], tl["row0"], tl["grp"], tl["seq"], tl["pos"]
                xnT = xnTs.next()
                front(C, src, tl, gB, xnT, 0, rings)
                kout = koutr.next()
                vout = voutr.next()
                knb = knbr.next()
                qnb = qnbr.next()
                vb = vbr.next()
                for c in range(6):
                    bk = pb()
                    for kc in range(8):
                        P.op("pe", "matmul", dict(out=bk.f32[:nt, :], lhsT=xnT[:, kc, :nt], rhs=Wqkv[:, kc, c * 512:(c + 1) * 512],
                                                  start=(kc == 0), stop=(kc == 7)), [xnT.b, Wqkv.b], [bk.b], inc=(kc == 7))
                    if c < 4:
                        sq = sqr.next()
                        ssq = ssqr.next()
                        t1 = t1r.next()
                        P.op("act", "activation", dict(out=sq[:nt, :], in_=bk.f32[:nt, :], func=AF.Square), [bk.b], [sq.b])
                        P.op("dve", "tensor_reduce", dict(out=ssq[:nt, 0, :], in_=sq[:nt, :].rearrange("p (h d) -> p h d", h=8),
                                                          axis=AX.X, op=ALU.add), [sq.b], [ssq.b])
                        P.op("act", "activation", dict(out=ssq[:nt, 2, :], in_=ssq[:nt, 0, :], func=AF.Ln, scale=1.0 / AD, bias=EPS),
                             [ssq.b], [ssq.b])
                        P.op("act", "activation", dict(out=ssq[:nt, 3, :], in_=ssq[:nt, 2, :], func=AF.Exp, scale=-0.5),
                             [ssq.b], [ssq.b])
                        P.op("dve", "tensor_tensor", dict(out=t1[:nt, :].rearrange("p (h d) -> p h d", h=8),
                                                          in0=bk.f32[:nt, :].rearrange("p (h d) -> p h d", h=8),
                                                          in1=ssq[:nt, 3, :].unsqueeze(2).to_broadcast([nt, 8, AD]), op=ALU.mult),
                             [bk.b, ssq.b], [t1.b])
                        if c < 2:
                            P.op("dve", "tensor_tensor", dict(out=qnb[:nt, c * 512:(c + 1) * 512].rearrange("p (h d) -> p h d", h=8),
                                                              in0=t1[:nt, :].rearrange("p (h d) -> p h d", h=8),
                                                              in1=qg[:nt, :].unsqueeze(1).to_broadcast([nt, 8, AD]), op=ALU.mult),
                                 [t1.b, qg.b], [qnb.b])
                        else:
                            cc = c - 2
                            P.op("dve", "tensor_tensor", dict(out=kout[:nt, cc * 512:(cc + 1) * 512].rearrange("p (h d) -> p h d", h=8),
                                                              in0=t1[:nt, :].rearrange("p (h d) -> p h d", h=8),
                                                              in1=kg[:nt, :].unsqueeze(1).to_broadcast([nt, 8, AD]), op=ALU.mult),
                                 [t1.b, kg.b], [kout.b])
                            P.op("act", "activation", dict(out=knb[:nt, cc * 512:(cc + 1) * 512], in_=kout[:nt, cc * 512:(cc + 1) * 512],
                                                           func=AF.Copy), [kout.b], [knb.b])
                    else:
                        cc = c - 4
                        P.op("act", "activation", dict(out=vout[:nt, cc * 512:(cc + 1) * 512], in_=bk.f32[:nt, :], func=AF.Copy),
                             [bk.b], [vout.b])
                        P.op("dve", "tensor_copy", dict(out=vb[:nt, cc * 512:(cc + 1) * 512], in_=bk.f32[:nt, :]), [bk.b], [vb.b])
                if grp == "p":
                    P.dma("pool", o_pk[j, b, pos:pos + nt, :], kout[:nt, :], db("o_pk", r0), kout.b, "st")
                    P.dma("pool", o_pv[j, b, pos:pos + nt, :], vout[:nt, :], db("o_pv", r0), vout.b, "st")
                    P.dma("pool", Vp[b, pos:pos + nt, :], vb[:nt, :], db("Vp", b), vb.b, "st")
                    transpose_store(qnb, nt, qTr.next(), QTp[b, :, :, pos:pos + nt], db("QTp", b))
                    transpose_store(knb, nt, kTr.next(), KTp[b, :, :, pos:pos + nt], db("KTp", b))
                else:
                    P.dma("pool", o_sk[j, b, 0:nt, :], kout[:nt, :], db("o_sk", r0), kout.b, "st")
                    P.dma("pool", o_sv[j, b, 0:nt, :], vout[:nt, :], db("o_sv", r0), vout.b, "st")
                    P.dma("pool", Vs[b, 0:nt, :], vb[:nt, :], db("Vs", b), vb.b, "st")
                    transpose_store(qnb, nt, qTr.next(), QTs[b, :, :, 0:nt], db("QTs", b))
                    transpose_store(knb, nt, kTr.next(), KTs[b, :, :, PAST:PAST + nt], db("KTs", b))
            for b in range(NS):
                for kb in range(PAST // 128):
                    ck = ckr.next()
                    for hf in range(2):
                        P.dma("pool", ck[:, hf * 512:(hf + 1) * 512], cache_k[j, b, kb * 128:(kb + 1) * 128, hf * 512:(hf + 1) * 512],
                              ck.b, cst, "ld")
                    transpose_store(ck, 128, kTr.next(), KTs[b, :, :, kb * 128:(kb + 1) * 128], db("KTs", b))
            end_stage()

        def stage_sb2(layer, j, src, dst):
            begin_stage()
            Wo = sb([128, 8, D], BF16, "Wo")
            load_w(Wo, sb_out_w[j])
            suin = sb([128, 128], BF16, "suin")
            sutmp = sb([128, 128], BF16, "sutmp")
            onesb = sb([128, 128], BF16, "onesb")
            nonesb = sb([128, 128], BF16, "nonesb")
            maskb = sb([128, 4, 512], BF16, "maskb")
            masksb = sb([128, 32], BF16, "masksb")
            P.dma("pool", sutmp[:], c_tri[:, :], sutmp.b, cst, "ld")
            P.dma("pool", suin[:], c_su[:, :], suin.b, cst, "ld")
            P.dma("pool", sutmp[:], c_ident[:, :], sutmp.b, cst, "ld")
            P.op("dve", "tensor_tensor", dict(out=suin[:], in0=suin[:], in1=sutmp[:], op=ALU.add), [suin.b, sutmp.b], [suin.b])
            P.op("dve", "tensor_scalar", dict(out=suin[:], in0=suin[:], scalar1=-1.0, scalar2=None, op0=ALU.mult), [suin.b], [suin.b])
            P.op("dve", "memset", dict(ap=onesb[:], constant=1.0), [], [onesb.b])
            P.op("dve", "tensor_scalar", dict(out=nonesb[:], in0=sutmp[:], scalar1=-1.0, scalar2=None, op0=ALU.mult),
                 [sutmp.b], [nonesb.b])
            for i in range(4):
                P.dma("pool", maskb[:, i, :], c_mask[:, i, :], maskb.b, cst, "ld")
            P.dma("pool", masksb[:], c_masks[:, :], masksb.b, cst, "ld")
            KLEN = max(SEQ, KPAD)
            KT = sb([128, 8, KLEN], BF16, "KT")
            Vb = sb([128, KLEN // 128, D], BF16, "Vb")
            QWM = 512
            QTr = ring(2, [128, 8, QWM], BF16, "QT")
            OT = sb([128, 8, QWM], BF16, "OT")
            e1r = ring(4, [128, QWM], F32, "e1")
            lgr = ring(6, [128, QWM], BF16, "lg")
            wbr = ring(6, [128, QWM], BF16, "wb")
            Sbfr = [ring(2, [128, QWM], BF16, "Sbf0"), ring(2, [128, QWM], BF16, "Sbf1")]
            xinr = ring(2, [128, D], F32, "xin")
            ystr = ring(2, [128, D], F32, "yst")
            bkR = [hold(), hold()]
            bkOs = [hold(), hold()]

            seqs = [("p", b) for b in range(NB)] + [("s", b) for b in range(NS)]
            for (grp, b) in seqs:
                if grp == "p":
                    base, L, QW = b * SEQ, SEQ, 512
                    for hp in range(8):
                        P.dma("sp", KT[:, hp, :SEQ], KTp[b, :, hp, :], KT.b, db("KTp", b), "ld")
                    for t4 in range(0, SEQ // 128, 4):
                        P.dma("sp", Vb[:, t4:t4 + 4, :], Vp[b, t4 * 128:(t4 + 4) * 128, :].rearrange("(t p) c -> p t c", p=128),
                              Vb.b, db("Vp", b), "ld")
                else:
                    base, L, QW = NTP + b * DSEQ, DSEQ, DSEQ
                    nkc = PAST // 128
                    if KPAD > KTOT:
                        P.op("dve", "memset", dict(ap=KT[:, :, KTOT:KPAD], constant=0.0), [], [KT.b])
                    P.op("dve", "memset", dict(ap=Vb[:, nkc, :], constant=0.0), [], [Vb.b])
                    for hp in range(8):
                        P.dma("sp", KT[:, hp, :KTOT], KTs[b, :, hp, :KTOT], KT.b, db("KTs", b), "ld")
                    for t in range(nkc):
                        for hf in range(2):
                            P.dma("pool", Vb[:, t, hf * 512:(hf + 1) * 512],
                                  cache_v[j, b, t * 128:(t + 1) * 128, hf * 512:(hf + 1) * 512], Vb.b, cst, "ld")
                    P.dma("sp", Vb[:DSEQ, nkc, :], Vs[b, :, :], Vb.b, db("Vs", b), "ld")
                for q0 in range(0, L, QW):
                    QT = QTr.next()
                    if grp == "p":
                        P.dma("sp", QT[:, :, :QW], QTp[b, :, :, q0:q0 + QW], QT.b, db("QTp", b), "ld")
                        nkb = (q0 + QW) // 128
                        def mask_of(kb, q0=q0):
                            i = kb - q0 // 128
                            return maskb[:, i, :] if i >= 0 else None
                    else:
                        P.dma("sp", QT[:, :, :QW], QTs[b, :, :, :], QT.b, db("QTs", b), "ld")
                        nkb = KPAD // 128
                        def mask_of(kb, nkb=nkb):
                            return masksb[:, :] if kb == nkb - 1 else None
                    its = []
                    for hp in range(8):
                        for kb in range(nkb - 1, -1, -1):
                            for hh in range(2):
                                its.append((hp, kb, hh))
                    st = {}

                    def col0_of(kb, q0=q0, grp=grp):
                        if grp != "p" or not TRIM:
                            return 0
                        return max(0, (kb - q0 // 128) * 128)

                    def stageA(it):
                        hp, kb, hh = it
                        po = hh * 64
                        a = col0_of(kb)
                        bkT = pb()
                        P.op("pe", "matmul", dict(out=bkT.f32[:, a:QW], lhsT=KT[po:po + 64, hp, kb * 128:(kb + 1) * 128],
                                                  rhs=QT[po:po + 64, hp, a:QW], start=True, stop=False), [KT.b, QT.b], [bkT.b])
                        e1 = e1r.next()
                        P.op("act", "activation", dict(out=e1[:, a:QW], in_=bkT.f32[:, a:QW], func=AF.Exp), [bkT.b], [e1.b])
                        lg = lgr.next()
                        P.op("act", "activation", dict(out=lg[:, a:QW], in_=e1[:, a:QW], func=AF.Ln, bias=1.0), [e1.b], [lg.b])
                        m = mask_of(kb)
                        if m is not None:
                            a2 = min(QW, a + 128)
                            P.op("dve", "tensor_tensor", dict(out=lg[:, a:a2], in0=lg[:, a:a2], in1=m[:, a:a2], op=ALU.mult),
                                 [lg.b, maskb.b, masksb.b], [lg.b])
                        st[it] = dict(bkT=bkT, lg=lg)

                    def stageA2(it0, it1):
                        pre = []
                        for it in (it0, it1):
                            hp, kb, hh = it
                            po = hh * 64
                            a = col0_of(kb)
                            bkT = pb()
                            P.op("pe", "matmul", dict(out=bkT.f32[:, a:QW], lhsT=KT[po:po + 64, hp, kb * 128:(kb + 1) * 128],
                                                      rhs=QT[po:po + 64, hp, a:QW], start=True, stop=False), [KT.b, QT.b], [bkT.b])
                            pre.append((it, bkT, a))
                        for (it, bkT, a) in pre:
                            hp, kb, hh = it
                            e1 = e1r.next()
                            P.op("act", "activation", dict(out=e1[:, a:QW], in_=bkT.f32[:, a:QW], func=AF.Exp), [bkT.b], [e1.b])
                            lg = lgr.next()
                            P.op("act", "activation", dict(out=lg[:, a:QW], in_=e1[:, a:QW], func=AF.Ln, bias=1.0), [e1.b], [lg.b])
                            m = mask_of(kb)
                            if m is not None:
                                a2 = min(QW, a + 128)
                                P.op("dve", "tensor_tensor", dict(out=lg[:, a:a2], in0=lg[:, a:a2], in1=m[:, a:a2], op=ALU.mult),
                                     [lg.b, maskb.b, masksb.b], [lg.b])
                            st[it] = dict(bkT=bkT, lg=lg)

                    def stageB(it):
                        hp, kb, hh = it
                        d = st[it]
                        bkT, lg = d["bkT"], d["lg"]
                        first = (kb == nkb - 1)
                        lastb = (kb == 0)
                        a = col0_of(kb)
                        P.op("pe", "matmul", dict(out=bkT.f32[:, a:QW], lhsT=suin[:], rhs=lg[:, a:QW], start=False, stop=first),
                             [suin.b, lg.b], [bkT.b], inc=first)
                        if not first:
                            Sbf = d["Sbf"] = st[(hp, kb + 1, hh)]["Snext"]
                            P.op("pe", "matmul", dict(out=bkT.f32[:, a:QW], lhsT=nonesb[:], rhs=Sbf[:, a:QW], start=False, stop=True),
                                 [nonesb.b, Sbf.b], [bkT.b])
                        if not lastb:
                            P.op("pe", "matmul", dict(out=bkR[hh].f32[:, a:QW], lhsT=onesb[:], rhs=lg[:, a:QW], start=first,
                                                      stop=(kb == 1)), [onesb.b, lg.b], [bkR[hh].b])
                            Sn = Sbfr[hh].next()
                            P.op("dve", "tensor_copy", dict(out=Sn[:, a:QW], in_=bkR[hh].f32[:, a:QW]), [bkR[hh].b], [Sn.b])
                            an = col0_of(kb - 1)
                            if an < a:
                                P.op("dve", "memset", dict(ap=Sn[:, an:a], constant=0.0), [], [Sn.b])
                            d["Snext"] = Sn
                        wb = wbr.next()
                        P.op("act", "activation", dict(out=wb[:, a:QW], in_=bkT.f32[:, a:QW], func=AF.Exp), [bkT.b], [wb.b])
                        m = mask_of(kb)
                        if m is not None:
                            a2 = min(QW, a + 128)
                            P.op("dve", "tensor_tensor", dict(out=wb[:, a:a2], in0=wb[:, a:a2], in1=m[:, a:a2], op=ALU.mult),
                                 [wb.b, maskb.b, masksb.b], [wb.b])
                        d["wb"] = wb

                    def stageC(it):
                        hp, kb, hh = it
                        d = st[it]
                        po = hh * 64
                        h = hp * 2 + hh
                        a = col0_of(kb)
                        bkO = bkOs[hp % 2]
                        P.op("pe", "matmul", dict(out=bkO.f32[po:po + 64, a:QW], lhsT=Vb[:, kb, h * 64:(h + 1) * 64],
                                                  rhs=d["wb"][:, a:QW], start=(kb == nkb - 1), stop=(kb == 0)),
                             [Vb.b, d["wb"].b], [bkO.b])
                        if kb == 0 and hh == 1:
                            P.op("act", "activation", dict(out=OT[:, hp, :QW], in_=bkO.f32[:, :QW], func=AF.Copy), [bkO.b], [OT.b])

                    n = len(its) // 2
                    for step in range(n + 2):
                        if step < n:
                            stageA2(its[2 * step], its[2 * step + 1])
                        if 0 <= step - 1 < n:
                            stageB(its[2 * (step - 1)])
                            stageB(its[2 * (step - 1) + 1])
                        if 0 <= step - 2 < n:
                            stageC(its[2 * (step - 2)])
                            stageC(its[2 * (step - 2) + 1])
                    for i0 in range(0, QW, 128):
                        nt = min(128, QW - i0)
                        r0 = base + q0 + i0
                        xin = xinr.next()
                        P.dma("sp", xin[:nt, :], src[r0:r0 + nt, :], xin.b, db(src.name, r0), "ld")
                        y = ystr.next()
                        for c in range(2):
                            bk = pb()
                            for hp in range(8):
                                P.op("pe", "matmul", dict(out=bk.f32[:nt, :], lhsT=OT[:, hp, i0:i0 + nt], rhs=Wo[:, hp, c * 512:(c + 1) * 512],
                                                          start=(hp == 0), stop=(hp == 7)), [OT.b, Wo.b], [bk.b], inc=(hp == 7))
                            P.op("dve", "tensor_tensor", dict(out=y[:nt, c * 512:(c + 1) * 512], in0=bk.f32[:nt, :],
                                                              in1=xin[:nt, c * 512:(c + 1) * 512], op=ALU.add), [bk.b, xin.b], [y.b])
                        P.dma("pool", dst[r0:r0 + nt, :], y[:nt, :], db(dst.name, r0), y.b, "st")
            for bk in bkR + bkOs:
                release(bk)
            end_stage()

        src = x_all
        for layer in range(DEPTH):
            j = layer // 2
            last = (layer == DEPTH - 1)
            if only is not None and "mix" not in only:
                stage_copy(src, xs)
            elif layer % 2 == 0:
                stage_ssd1(layer, j, src)
                stage_ssd2(layer, j, src, xs)
            else:
                stage_sb1(layer, j, src)
                stage_sb2(layer, j, src, xs)
            src = xs
            stage_mlp(layer, xs, y_all if last else xs)
        P.barrier()
        P.emit()
    return nc


def _consts():
    i = np.arange(128)
    ident = np.eye(128, dtype=np.float32)
    tri = (i[:, None] <= i[None, :]).astype(np.float32)
    su = (i[:, None] > i[None, :]).astype(np.float32)
    q = np.arange(512)
    mask = np.stack([(i[:, None] + 128 * k < q[None, :]) for k in range(4)], axis=1).astype(np.float32)
    qs = np.arange(32)
    masks = ((i[:, None] < qs[None, :]) & (i[:, None] < 32)).astype(np.float32)
    return dict(c_ident=ident, c_tri=tri, c_su=su, c_mask=np.ascontiguousarray(mask), c_masks=masks)


def make_in_maps(inp, n_cores, NB, SEQ, NS, DSEQ, PAST, DEPTH):
    N_SSD = (DEPTH + 1) // 2
    N_SB = DEPTH // 2
    f = lambda a: np.ascontiguousarray(np.asarray(a, dtype=np.float32))
    shared = dict(
        norm_mix=f(inp["norm_mix"]), norm_mlp=f(inp["norm_mlp"]), ssd_in_w=f(inp["ssd_in_w"]),
        ssd_conv_w=f(np.asarray(inp["ssd_conv_w"]).reshape(N_SSD, 4, 32, 128).transpose(0, 3, 2, 1)),
        ssd_conv_b=f(np.asarray(inp["ssd_conv_b"]).reshape(N_SSD, 32, 128).transpose(0, 2, 1)),
        ssd_dt_bias=f(inp["ssd_dt_bias"]), ssd_a_log=f(inp["ssd_a_log"]), ssd_d=f(inp["ssd_d"]),
        ssd_norm_w=f(inp["ssd_norm_w"]), ssd_out_w=f(inp["ssd_out_w"]),
        mlp_up=f(inp["mlp_up"]), mlp_down=f(inp["mlp_down"]))
    if N_SB > 0:
        shared.update(sb_qkv_w=f(inp["sb_qkv_w"]), sb_q_gain=f(inp["sb_q_gain"]), sb_k_gain=f(inp["sb_k_gain"]),
                      sb_out_w=f(inp["sb_out_w"]))
    else:
        shared.update(sb_qkv_w=np.zeros((1, D, 3 * D), np.float32), sb_q_gain=np.zeros((1, AD), np.float32),
                      sb_k_gain=np.zeros((1, AD), np.float32), sb_out_w=np.zeros((1, D, D), np.float32))
    shared.update(_consts())
    xp = np.asarray(inp["x_prompt"], dtype=np.float32)
    xsm = np.asarray(inp["x_sample"], dtype=np.float32)
    sssm = np.asarray(inp["state_ssm"], dtype=np.float32)
    sconv = np.asarray(inp["state_conv"], dtype=np.float32)
    ck = np.asarray(inp["cache_k"], dtype=np.float32)
    cv = np.asarray(inp["cache_v"], dtype=np.float32)
    maps = []
    for c in range(n_cores):
        m = dict(shared)
        m["x_all"] = f(np.concatenate([xp[c * NB:(c + 1) * NB].reshape(NB * SEQ, D),
                                       xsm[c * NS:(c + 1) * NS].reshape(NS * DSEQ, D)], axis=0))
        m["state_ssm"] = f(sssm[:, c * NS:(c + 1) * NS].reshape(N_SSD, NS, NH * HP, NST))
        m["state_conv"] = f(sconv[:, c * NS:(c + 1) * NS].reshape(N_SSD, NS, 3, 32, 128).transpose(0, 1, 4, 3, 2))
        if N_SB > 0:
            m["cache_k"] = f(ck[:, c * NS:(c + 1) * NS].reshape(N_SB, NS, PAST, D))
            m["cache_v"] = f(cv[:, c * NS:(c + 1) * NS].reshape(N_SB, NS, PAST, D))
        else:
            m["cache_k"] = np.zeros((1, NS, PAST, D), np.float32)
            m["cache_v"] = np.zeros((1, NS, PAST, D), np.float32)
        maps.append(m)
    return maps


def assemble(results, n_cores, NB, SEQ, NS, DSEQ, PAST, DEPTH):
    N_SSD = (DEPTH + 1) // 2
    N_SB = DEPTH // 2
    cat = lambda k, ax: np.concatenate([np.asarray(r[k]) for r in results], axis=ax)
    y_all = [np.asarray(r["y_all"]) for r in results]
    y_p = np.concatenate([y[:NB * SEQ].reshape(NB, SEQ, D) for y in y_all], axis=0)
    y_s = np.concatenate([y[NB * SEQ:].reshape(NS, DSEQ, D) for y in y_all], axis=0)
    pssm = cat("o_pssm", 1).reshape(N_SSD, n_cores * NB, NH, HP, NST)
    sssm = cat("o_sssm", 1).reshape(N_SSD, n_cores * NS, NH, HP, NST)
    pconv = np.ascontiguousarray(cat("o_pconv", 1).transpose(0, 1, 4, 3, 2)).reshape(N_SSD, n_cores * NB, 3, CONVD)
    sconv = np.ascontiguousarray(cat("o_sconv", 1).transpose(0, 1, 4, 3, 2)).reshape(N_SSD, n_cores * NS, 3, CONVD)
    pk = cat("o_pk", 1)[:N_SB].reshape(N_SB, n_cores * NB, SEQ, AH, AD)
    pv = cat("o_pv", 1)[:N_SB].reshape(N_SB, n_cores * NB, SEQ, AH, AD)
    sk = cat("o_sk", 1)[:N_SB].reshape(N_SB, n_cores * NS, DSEQ, AH, AD)
    sv = cat("o_sv", 1)[:N_SB].reshape(N_SB, n_cores * NS, DSEQ, AH, AD)
    outs = (y_p, y_s, pssm, pconv, pk, pv, sssm, sconv, sk, sv)
    return tuple(np.ascontiguousarray(o, dtype=np.float32) for o in outs)


def kernel(**inputs):
    n = 8
    NB, SEQ, NS, DSEQ, PAST, DEPTH = 4, 2048, 2, 32, 1024, 4
    nc = build(NB, SEQ, NS, DSEQ, PAST, DEPTH)
    maps = make_in_maps(inputs, n, NB, SEQ, NS, DSEQ, PAST, DEPTH)
    res = run_bass_kernel_spmd(nc, maps, core_ids=list(range(n)))
    return assemble(res.results, n, NB, SEQ, NS, DSEQ, PAST, DEPTH)
```

```python
import numpy as np
from contextlib import ExitStack
import concourse.bass as bass
import concourse.mybir as mybir
from concourse.bass_utils import run_bass_kernel_spmd

F32 = mybir.dt.float32
BF16 = mybir.dt.bfloat16
AF = mybir.ActivationFunctionType
ALU = mybir.AluOpType
AX = mybir.AxisListType

D = 1024
DFF = 4096
DIN = 2048
NH = 32
HP = 64
NG = 8
NST = 128
CONVD = 4096
DPROJ = 6176
AH = 16
AD = 64
EPS = 1e-6
import os as _os
TRIM = _os.environ.get('K_TRIM', '1') == '1'


class Buf:
    __slots__ = ("name", "lw", "rd")

    def __init__(s, name):
        s.name = name
        s.lw = None
        s.rd = {}


class Prog:
    ENG = ("pe", "act", "dve", "pool", "sp")

    def __init__(s, nc, es):
        s.nc = nc
        s.es = es
        s.sems = {}
        s.cnt = {}
        s.ops = {e: [] for e in s.ENG}
        s.waited = {e: {} for e in s.ENG}
        s.nbuf = 0
        s.nops = 0
        s.free = []
        s.stage_keys = []

    def buf(s, name=None):
        s.nbuf += 1
        return Buf((name or "b") + "_%d" % s.nbuf)

    def sem(s, key):
        if key not in s.sems:
            if key in s.ENG or not s.free:
                s.sems[key] = s.es.enter_context(s.nc.semaphore("s%d" % len(s.sems)))
                s.cnt[key] = 0
            else:
                h, c0 = s.free.pop()
                s.sems[key] = h
                s.cnt[key] = c0
            if key not in s.ENG:
                s.stage_keys.append(key)
        return s.sems[key]

    def retire(s):
        for k in s.stage_keys:
            s.free.append((s.sems[k], s.cnt[k]))
        s.stage_keys = []

    def _waits(s, eng, deps):
        w = s.waited[eng]
        best = {}
        for d in deps:
            if d is None:
                continue
            k, v = d
            if w.get(k, 0) >= v:
                continue
            if best.get(k, 0) < v:
                best[k] = v
        out = []
        for k, v in best.items():
            w[k] = v
            out.append((k, v))
        return out

    def op(s, eng, name, kw, reads=(), writes=(), inc=True):
        deps = []
        for b in reads:
            deps.append(b.lw)
        for b in writes:
            deps.append(b.lw)
            for k, v in b.rd.items():
                if k != eng:
                    deps.append((k, v))
        if eng == "pe":
            deps = [d for d in deps if d is not None and d[0] != "pe"]
        waits = s._waits(eng, deps)
        s.sem(eng)
        val = s.cnt[eng] + 1
        if inc:
            s.cnt[eng] = val
        for b in reads:
            if b.rd.get(eng, 0) < val:
                b.rd[eng] = val
        for b in writes:
            b.lw = (eng, val)
            b.rd = {}
        s.ops[eng].append((waits, name, kw, (eng, 1) if inc else None))
        s.nops += 1

    def dma(s, q, out_ap, in_ap, dst, src, kind, par=False, **kw):
        key = ("ld:" + dst.name) if kind == "ld" else ("st:" + src.name)
        s.sem(key)
        deps = [src.lw] + list(dst.rd.items())
        if par and kind == "ld" and dst.lw is not None and dst.lw[0] == key:
            pass
        else:
            deps.append(dst.lw)
            if s.cnt[key] > 0:
                deps.append((key, s.cnt[key]))
        waits = s._waits(q, deps)
        val = s.cnt[key] + 16
        s.cnt[key] = val
        if src.rd.get(key, 0) < val:
            src.rd[key] = val
        dst.lw = (key, val)
        dst.rd = {}
        k2 = dict(out=out_ap, in_=in_ap)
        k2.update(kw)
        s.ops[q].append((waits, "dma_start", k2, (key, 16)))
        s.nops += 1

    def barrier(s):
        for e in s.ENG:
            deps = [(k, v) for k, v in s.cnt.items() if v > 0 and k != e]
            waits = s._waits(e, deps)
            if waits:
                s.ops[e].append((waits, None, None, None))

    def emit(s):
        nc = s.nc
        with nc.Block() as block:
            def mk(eng):
                def f(e):
                    for waits, name, kw, inc in s.ops[eng]:
                        for k, v in waits:
                            e.wait_ge(s.sems[k], v)
                        if name is not None:
                            ins = getattr(e, name)(**kw)
                            if inc:
                                ins.then_inc(s.sems[inc[0]], inc[1])
                return f
            block.tensor(mk("pe"))
            block.scalar(mk("act"))
            block.vector(mk("dve"))
            block.gpsimd(mk("pool"))
            block.sync(mk("sp"))


class T:
    __slots__ = ("t", "b")

    def __init__(s, t, b):
        s.t = t
        s.b = b

    def __getitem__(s, k):
        return s.t[k]


class Ring:
    def __init__(s, items):
        s.items = items
        s.i = 0

    def next(s):
        it = s.items[s.i % len(s.items)]
        s.i += 1
        return it


class Bank:
    __slots__ = ("f32", "bf", "b")

    def __init__(s, t, b):
        s.f32 = t
        s.bf = t[:].bitcast(BF16)
        s.b = b


def build(NB, SEQ, NS, DSEQ, PAST, DEPTH, only=None):
    nc = bass.Bass("TRN2", target_bir_lowering=False)
    NTP = NB * SEQ
    NTS = NS * DSEQ
    NTOK = NTP + NTS
    N_SSD = (DEPTH + 1) // 2
    N_SB = DEPTH // 2
    NSB1 = max(N_SB, 1)
    KTOT = PAST + DSEQ
    KPAD = ((KTOT + 127) // 128) * 128

    def di(n, sh, dt=F32):
        return nc.dram_tensor(n, list(sh), dt, kind="ExternalInput").ap()

    def do(n, sh):
        return nc.dram_tensor(n, list(sh), F32, kind="ExternalOutput").ap()

    def dx(n, sh, dt=F32):
        return nc.dram_tensor(n, list(sh), dt).ap()

    x_all = di("x_all", [NTOK, D])
    state_ssm = di("state_ssm", [N_SSD, NS, NH * HP, NST])
    state_conv = di("state_conv", [N_SSD, NS, 128, 32, 3])
    cache_k = di("cache_k", [NSB1, NS, PAST, D])
    cache_v = di("cache_v", [NSB1, NS, PAST, D])
    norm_mix = di("norm_mix", [DEPTH, D])
    norm_mlp = di("norm_mlp", [DEPTH, D])
    ssd_in_w = di("ssd_in_w", [N_SSD, D, DPROJ])
    ssd_conv_w = di("ssd_conv_w", [N_SSD, 128, 32, 4])
    ssd_conv_b = di("ssd_conv_b", [N_SSD, 128, 32])
    ssd_dt_bias = di("ssd_dt_bias", [N_SSD, NH])
    ssd_a_log = di("ssd_a_log", [N_SSD, NH])
    ssd_d = di("ssd_d", [N_SSD, NH])
    ssd_norm_w = di("ssd_norm_w", [N_SSD, DIN])
    ssd_out_w = di("ssd_out_w", [N_SSD, DIN, D])
    sb_qkv_w = di("sb_qkv_w", [NSB1, D, 3 * D])
    sb_q_gain = di("sb_q_gain", [NSB1, AD])
    sb_k_gain = di("sb_k_gain", [NSB1, AD])
    sb_out_w = di("sb_out_w", [NSB1, D, D])
    mlp_up = di("mlp_up", [DEPTH, D, DFF])
    mlp_down = di("mlp_down", [DEPTH, DFF, D])
    c_ident = di("c_ident", [128, 128])
    c_tri = di("c_tri", [128, 128])
    c_su = di("c_su", [128, 128])
    c_mask = di("c_mask", [128, 4, 512])
    c_masks = di("c_masks", [128, 32])

    y_all = do("y_all", [NTOK, D])
    o_pssm = do("o_pssm", [N_SSD, NB, NH * HP, NST])
    o_pconv = do("o_pconv", [N_SSD, NB, 128, 32, 3])
    o_pk = do("o_pk", [NSB1, NB, SEQ, D])
    o_pv = do("o_pv", [NSB1, NB, SEQ, D])
    o_sssm = do("o_sssm", [N_SSD, NS, NH * HP, NST])
    o_sconv = do("o_sconv", [N_SSD, NS, 128, 32, 3])
    o_sk = do("o_sk", [NSB1, NS, DSEQ, D])
    o_sv = do("o_sv", [NSB1, NS, DSEQ, D])

    xs = dx("xs", [NTOK, D])
    Ys = dx("Ys", [NTOK, DIN])
    QTp = dx("QTp", [NB, 128, 8, SEQ], BF16)
    KTp = dx("KTp", [NB, 128, 8, SEQ], BF16)
    Vp = dx("Vp", [NB, SEQ, D], BF16)
    QTs = dx("QTs", [NS, 128, 8, DSEQ], BF16)
    KTs = dx("KTs", [NS, 128, 8, KPAD], BF16)
    Vs = dx("Vs", [NS, DSEQ, D], BF16)

    top = ExitStack()
    with top:
        P = Prog(nc, top)
        dbufs = {}

        def db(name, idx=0):
            k = (name, idx)
            if k not in dbufs:
                dbufs[k] = P.buf("d_" + name)
            return dbufs[k]

        cst = db("const")

        banks = [Bank(top.enter_context(nc.psum_tensor("ps%d" % i, [128, 512], F32)), P.buf("ps%d" % i))
                 for i in range(8)]
        bstate = {"i": 0, "held": set()}

        def pb():
            while True:
                i = bstate["i"] % 8
                bstate["i"] += 1
                if i not in bstate["held"]:
                    return banks[i]

        def hold():
            b = pb()
            bstate["held"].add(banks.index(b))
            return b

        def release(b):
            bstate["held"].discard(banks.index(b))

        cur = {"es": None, "n": 0}

        def sb(shape, dt, name=None):
            cur["n"] += 1
            nm = (name or "t") + "_%d" % cur["n"]
            t = cur["es"].enter_context(nc.sbuf_tensor(nm, list(shape), dt))
            return T(t, P.buf(nm))

        def ring(n, shape, dt, name=None):
            return Ring([sb(shape, dt, name) for _ in range(n)])

        tiles = []
        for b in range(NB):
            for i in range(SEQ // 128):
                tiles.append(dict(row0=b * SEQ + i * 128, nt=128, grp="p", seq=b, pos=i * 128))
        for b in range(NS):
            tiles.append(dict(row0=NTP + b * DSEQ, nt=DSEQ, grp="s", seq=b, pos=0))

        def load_consts():
            c = {}
            c["identb"] = sb([128, 128], BF16, "identb")
            P.dma("pool", c["identb"][:], c_ident[:, :], c["identb"].b, cst, "ld")
            return c

        def front(C, src, tl, gB, dstT, col0, rings):
            nt = tl["nt"]
            r0 = tl["row0"]
            xin = rings["xin"].next()
            P.dma("sp", xin[:nt, :], src[r0:r0 + nt, :], xin.b, db(src.name, r0), "ld")
            ss = rings["ss"].next()
            junk = rings["junk"].next()
            P.op("act", "activation", dict(out=junk[:nt, :], in_=xin[:nt, :], func=AF.Square, accum_out=ss[:nt, 0:1]),
                 [xin.b], [junk.b, ss.b])
            P.op("act", "activation", dict(out=ss[:nt, 2:3], in_=ss[:nt, 0:1], func=AF.Ln, scale=1.0 / D, bias=EPS),
                 [ss.b], [ss.b])
            P.op("act", "activation", dict(out=ss[:nt, 3:4], in_=ss[:nt, 2:3], func=AF.Exp, scale=-0.5), [ss.b], [ss.b])
            xn = rings["xn"].next()
            P.op("dve", "scalar_tensor_tensor", dict(out=xn[:nt, :], in0=xin[:nt, :], scalar=ss[:nt, 3:4], in1=gB[:nt, :],
                                                     op0=ALU.mult, op1=ALU.mult), [xin.b, ss.b, gB.b], [xn.b])
            bk = pb()
            for kc in range(8):
                P.op("pe", "transpose", dict(out=bk.bf[:, kc * 128:kc * 128 + nt], in_=xn[:nt, kc * 128:(kc + 1) * 128],
                                             identity=C["identb"][:nt, :nt]), [xn.b, C["identb"].b], [bk.b], inc=(kc == 7))
            P.op("act", "activation", dict(out=dstT[:, :, col0:col0 + nt],
                                           in_=bk.bf.rearrange("p (k t) -> p k t", k=8)[:, :, :nt], func=AF.Copy),
                 [bk.b], [dstT.b])
            return xin

        def load_w(dst, dram_ap, rows_per_dma=128):
            K, N = dram_ap.shape
            for kc in range(K // 128):
                for n0 in range(0, N, 2048):
                    n1 = min(N, n0 + 2048)
                    P.dma("pool", dst[:, kc, n0:n1], dram_ap[kc * 128:(kc + 1) * 128, n0:n1], dst.b, cst, "ld", par=True)

        def load_bcast(dst, dram_1d):
            P.dma("sp", dst[:], dram_1d.partition_broadcast(128), dst.b, cst, "ld")

        def begin_stage():
            P.barrier()
            cur["es"] = ExitStack()
            cur["es"].__enter__()

        def end_stage():
            P.barrier()
            P.retire()
            cur["es"].__exit__(None, None, None)
            cur["es"] = None

        def run_streams(items, make_gen, width=2, stagger=0):
            active = []
            free_lanes = list(range(width))
            it = iter(items)
            pending = [True]
            first = [True]

            def start_more():
                while free_lanes and pending[0]:
                    try:
                        x = next(it)
                    except StopIteration:
                        pending[0] = False
                        break
                    lane = free_lanes.pop(0)
                    g = make_gen(x, lane)
                    active.append((lane, g))
                    if first[0]:
                        first[0] = False
                        for _ in range(stagger):
                            try:
                                next(g)
                            except StopIteration:
                                active.remove((lane, g))
                                free_lanes.append(lane)
                                break
            while True:
                start_more()
                if not active:
                    break
                for (lane, g) in list(active):
                    try:
                        next(g)
                    except StopIteration:
                        active.remove((lane, g))
                        free_lanes.append(lane)

        def stage_copy(src, dst):
            begin_stage()
            rg = ring(2, [128, D], F32, "cp")
            for tl in tiles:
                t = rg.next()
                nt, r0 = tl["nt"], tl["row0"]
                P.dma("sp", t[:nt, :], src[r0:r0 + nt, :], t.b, db(src.name, r0), "ld")
                P.dma("pool", dst[r0:r0 + nt, :], t[:nt, :], db(dst.name, r0), t.b, "st")
            end_stage()

        def stage_mlp(layer, src, dst):
            begin_stage()
            C = load_consts()
            Wup = sb([128, 8, DFF], BF16, "Wup")
            Wdn = sb([128, 32, D], BF16, "Wdn")
            load_w(Wup, mlp_up[layer])
            load_w(Wdn, mlp_down[layer])
            gB = sb([128, D], F32, "gB")
            load_bcast(gB, norm_mlp[layer])
            rings = dict(xin=ring(4, [128, D], F32, "xin"), ss=ring(4, [128, 4], F32, "ss"),
                         junk=ring(1, [128, D], BF16, "junk"), xn=ring(2, [128, D], BF16, "xn"))
            xnTs = ring(2, [128, 8, 256], BF16, "xnT")
            hT = sb([128, 32, 256], BF16, "hT")
            rr = ring(2, [128, 256], F32, "relu")
            yst = ring(2, [128, D], F32, "yst")
            macros = []
            curm = []
            tot = 0
            for tl in tiles:
                if tot + tl["nt"] > 256 or (curm and curm[-1]["grp"] != tl["grp"]):
                    macros.append(curm)
                    curm = []
                    tot = 0
                curm.append(tl)
                tot += tl["nt"]
            if curm:
                macros.append(curm)

            def do_front(m):
                xnT = xnTs.next()
                col = 0
                ent = []
                for tl in m:
                    xin = front(C, src, tl, gB, xnT, col, rings)
                    ent.append((xin, col, tl))
                    col += tl["nt"]
                return xnT, ent, col

            nxt = do_front(macros[0])
            for mi, m in enumerate(macros):
                xnT, ent, NT = nxt
                for f in range(32):
                    bk = pb()
                    for kc in range(8):
                        P.op("pe", "matmul", dict(out=bk.f32[:, :NT], lhsT=Wup[:, kc, f * 128:(f + 1) * 128],
                                                  rhs=xnT[:, kc, :NT], start=(kc == 0), stop=(kc == 7)),
                             [Wup.b, xnT.b], [bk.b], inc=(kc == 7))
                    r = rr.next()
                    P.op("act", "activation", dict(out=r[:, :NT], in_=bk.f32[:, :NT], func=AF.Relu), [bk.b], [r.b])
                    P.op("dve", "tensor_tensor", dict(out=hT[:, f, :NT], in0=r[:, :NT], in1=r[:, :NT], op=ALU.mult),
                         [r.b], [hT.b])
                if mi + 1 < len(macros):
                    nxt = do_front(macros[mi + 1])
                for (xin, col, tl) in ent:
                    nt, r0 = tl["nt"], tl["row0"]
                    y = yst.next()
                    for c in range(2):
                        bk = pb()
                        for f in range(32):
                            P.op("pe", "matmul", dict(out=bk.f32[:nt, :], lhsT=hT[:, f, col:col + nt],
                                                      rhs=Wdn[:, f, c * 512:(c + 1) * 512], start=(f == 0), stop=(f == 31)),
                                 [hT.b, Wdn.b], [bk.b], inc=(f == 31))
                        P.op("dve", "tensor_tensor", dict(out=y[:nt, c * 512:(c + 1) * 512], in0=bk.f32[:nt, :],
                                                          in1=xin[:nt, c * 512:(c + 1) * 512], op=ALU.add),
                             [bk.b, xin.b], [y.b])
                    P.dma("pool", dst[r0:r0 + nt, :], y[:nt, :], db(dst.name, r0), y.b, "st")
            end_stage()

        def run_sched(lane_queues):
            events = set()
            n = len(lane_queues)
            curg = [None] * n
            idx = [0] * n
            waiting = [None] * n
            while True:
                progressed = False
                alive = False
                for li, q in enumerate(lane_queues):
                    if curg[li] is None:
                        if idx[li] < len(q):
                            curg[li] = q[idx[li]]()
                            idx[li] += 1
                            waiting[li] = None
                        else:
                            continue
                    alive = True
                    if waiting[li] is not None:
                        if waiting[li] in events:
                            waiting[li] = None
                        else:
                            continue
                    try:
                        r = next(curg[li])
                        progressed = True
                        while isinstance(r, tuple) and r[0] == "set":
                            events.add(r[1])
                            r = next(curg[li])
                        if isinstance(r, tuple) and r[0] == "wait" and r[1] not in events:
                            waiting[li] = r[1]
                    except StopIteration:
                        curg[li] = None
                        progressed = True
                if not alive:
                    break
                assert progressed, "emission scheduler deadlock"

        def stage_ssd1(layer, j, src):
            begin_stage()
            C = load_consts()
            Wx = sb([128, 8, CONVD + NH], BF16, "Wx")
            for kc in range(8):
                for n0 in range(0, CONVD + NH, 2048):
                    n1 = min(CONVD + NH, n0 + 2048)
                    P.dma("pool", Wx[:, kc, n0:n1], ssd_in_w[j, kc * 128:(kc + 1) * 128, DIN + n0:DIN + n1], Wx.b, cst, "ld", par=True)
            gB = sb([128, D], F32, "gB")
            load_bcast(gB, norm_mix[layer])
            trib = sb([128, 128], BF16, "trib")
            sub = sb([128, 128], BF16, "sub")
            onesb = sb([128, 128], BF16, "onesb")
            identf = sb([128, 128], F32, "identf")
            P.dma("pool", trib[:], c_tri[:, :], trib.b, cst, "ld")
            P.dma("pool", sub[:], c_su[:, :], sub.b, cst, "ld")
            P.dma("sp", identf[:], c_ident[:, :], identf.b, cst, "ld")
            P.op("dve", "memset", dict(ap=onesb[:], constant=1.0), [], [onesb.b])
            cw = sb([128, 32, 4], F32, "cw")
            cbias = sb([128, 32], F32, "cbias")
            P.dma("sp", cw[:], ssd_conv_w[j], cw.b, cst, "ld")
            P.dma("sp", cbias[:], ssd_conv_b[j], cbias.b, cst, "ld")
            Abc = sb([128, NH], F32, "Abc")
            dtb = sb([128, NH], F32, "dtb")
            Dbc = sb([128, NH], F32, "Dbc")
            load_bcast(Abc, ssd_a_log[j])
            load_bcast(dtb, ssd_dt_bias[j])
            load_bcast(Dbc, ssd_d[j])
            P.op("act", "activation", dict(out=Abc[:], in_=Abc[:], func=AF.Exp), [Abc.b], [Abc.b])
            P.op("dve", "tensor_scalar", dict(out=Abc[:], in0=Abc[:], scalar1=-1.0, scalar2=None, op0=ALU.mult),
                 [Abc.b], [Abc.b])
            DI = sb([128, NH, 128], BF16, "DI")
            for h in range(NH):
                P.op("dve", "tensor_scalar", dict(out=DI[:, h, :], in0=C["identb"][:], scalar1=Dbc[:, h:h + 1], scalar2=None,
                                                  op0=ALU.mult), [C["identb"].b, Dbc.b], [DI.b])
            xnr = ring(2, [128, D], BF16, "xn")
            rings = dict(xin=ring(2, [128, D], F32, "xin"), ss=ring(4, [128, 4], F32, "ss"), junk=xnr, xn=xnr)
            MTM = 256
            hnT = sb([128, 8, MTM], BF16, "hnT")
            xTs = [sb([128, 16, MTM], BF16, "xT") for _ in range(2)]
            BTs = [sb([128, 8, MTM], BF16, "BT") for _ in range(2)]
            CTs = [sb([128, 8, MTM], BF16, "CT") for _ in range(2)]
            dtraws = [sb([128, 2, NH], F32, "dtraw") for _ in range(2)]
            Rr = ring(3, [128, MTM + 4], BF16, "R")
            dgr = ring(12, [128, 128], BF16, "dg")
            sgr = ring(3, [128, MTM], F32, "sg")
            halo16 = sb([128, 32, 3], BF16, "halo16")
            ncb = sb([128, 32], F32, "ncb")
            P.op("dve", "tensor_scalar", dict(out=ncb[:], in0=cbias[:], scalar1=-1.0, scalar2=None, op0=ALU.mult), [cbias.b], [ncb.b])
            hTq = [sb([128, 512], F32, "hT") for _ in range(4)]
            hTbq = [sb([128, 512], BF16, "hTb") for _ in range(4)]
            halo = sb([128, 32, 3], F32, "halo")
            tmpr = ring(2, [128, 512], F32, "ytmp")
            ystr = ring(2, [128, 1024], F32, "yst")
            lanes = []
            for ln in range(2):
                lanes.append(dict(sm=sb([128, 12, NH], F32, "sm"), adtb=sb([128, NH], BF16, "adtb"),
                                  xtok=sb([128, DIN], BF16, "xtok"), Btok=sb([128, D], BF16, "Btok"),
                                  Wbh=sb([128, 16 * 128], BF16, "Wbh"), Ebh=sb([128, 16 * 128], BF16, "Ebh"),
                                  cbm=sb([128, NG * 128], BF16, "cbm"), xddh=sb([128, 1024], BF16, "xddh")))

            seqs = [("p", b, SEQ) for b in range(NB)] + [("s", b, DSEQ) for b in range(NS)]
            for si, (grp, b, L) in enumerate(seqs):
                base = (b * SEQ) if grp == "p" else (NTP + b * DSEQ)
                CL = min(128, L)
                MT = min(MTM, L)
                NCH = MT // CL
                NM = L // MT
                if grp == "p":
                    for q in range(4):
                        P.op("dve", "memset", dict(ap=hTq[q][:], constant=0.0), [], [hTq[q].b])
                    P.op("dve", "memset", dict(ap=halo[:], constant=0.0), [], [halo.b])
                    P.op("dve", "memset", dict(ap=halo16[:], constant=0.0), [], [halo16.b])
                else:
                    P.dma("sp", halo[:], state_conv[j, b], halo.b, cst, "ld")
                    P.op("dve", "tensor_copy", dict(out=halo16[:], in_=halo[:]), [halo.b], [halo16.b])
                    for hf in range(2):
                        stg = ystr.next()
                        P.dma("sp", stg[:].rearrange("p (t n) -> p t n", t=8),
                              state_ssm[j, b, hf * 1024:(hf + 1) * 1024, :].rearrange("(t p) n -> p t n", p=128), stg.b, cst, "ld")
                        for qq in range(2):
                            q = hf * 2 + qq
                            bk = pb()
                            for i in range(4):
                                t = qq * 4 + i
                                P.op("pe", "transpose", dict(out=bk.f32[:, i * 128:(i + 1) * 128], in_=stg[:, t * 128:(t + 1) * 128],
                                                             identity=identf[:]), [stg.b, identf.b], [bk.b], inc=(i == 3))
                            P.op("act", "activation", dict(out=hTq[q][:], in_=bk.f32[:, :], func=AF.Copy), [bk.b], [hTq[q].b])
                for q in range(4):
                    P.op("act", "activation", dict(out=hTbq[q][:], in_=hTq[q][:], func=AF.Copy), [hTq[q].b], [hTbq[q].b])

                def conv_stream(m, si=si, base=base, CL=CL, MT=MT, NCH=NCH, NM=NM):
                    st = m % 2
                    xT, BT, CT, dtraw = xTs[st], BTs[st], CTs[st], dtraws[st]
                    m0 = m * MT
                    if m >= 2:
                        for ci in range(NCH):
                            yield ("wait", ("cdone", si, (m - 2) * NCH + ci))
                    if m >= 1:
                        yield ("wait", ("convdone", si, m - 1))
                    for i in range(NCH):
                        tl = dict(row0=base + m0 + i * CL, nt=CL)
                        front(C, src, tl, gB, hnT, i * CL, rings)
                        yield
                    for ci in range(NCH):
                        bk = pb()
                        for kc in range(8):
                            P.op("pe", "matmul", dict(out=bk.f32[:CL, :NH], lhsT=hnT[:, kc, ci * CL:(ci + 1) * CL],
                                                      rhs=Wx[:, kc, CONVD:CONVD + NH], start=(kc == 0), stop=(kc == 7)),
                                 [hnT.b, Wx.b], [bk.b], inc=(kc == 7))
                        P.op("act", "activation", dict(out=dtraw[:CL, ci, :], in_=bk.f32[:CL, :NH], func=AF.Copy), [bk.b], [dtraw.b])
                    yield
                    pipe = {}

                    def s1(ct):
                        bk = pb()
                        for kc in range(8):
                            P.op("pe", "matmul", dict(out=bk.f32[:, :MT], lhsT=Wx[:, kc, ct * 128:(ct + 1) * 128],
                                                      rhs=hnT[:, kc, :MT], start=(kc == 0), stop=(kc == 7)),
                                 [Wx.b, hnT.b], [bk.b], inc=(kc == 7))
                        R = Rr.next()
                        P.op("dve", "tensor_copy", dict(out=R[:, 0:3], in_=halo16[:, ct, :]), [halo16.b], [R.b])
                        P.op("act", "activation", dict(out=R[:, 3:3 + MT], in_=bk.f32[:, :MT], func=AF.Copy), [bk.b], [R.b])
                        P.op("dve", "tensor_copy", dict(out=halo16[:, ct, :], in_=R[:, MT:MT + 3]), [R.b], [halo16.b])
                        if m == NM - 1:
                            P.op("dve", "tensor_copy", dict(out=halo[:, ct, :], in_=bk.f32[:, MT - 3:MT]), [bk.b], [halo.b])
                        dgs = []
                        for tap in range(4):
                            dg = dgr.next()
                            P.op("dve", "tensor_scalar", dict(out=dg[:], in0=C["identb"][:], scalar1=cw[:, ct, tap:tap + 1], scalar2=None,
                                                              op0=ALU.mult), [C["identb"].b, cw.b], [dg.b])
                            dgs.append(dg)
                        pipe[ct] = dict(R=R, dgs=dgs)

                    def s2(ct):
                        R, dgs = pipe[ct]["R"], pipe[ct]["dgs"]
                        bk2 = pb()
                        for tap in range(4):
                            P.op("pe", "matmul", dict(out=bk2.f32[:, :MT], lhsT=dgs[tap][:], rhs=R[:, tap:tap + MT],
                                                      start=(tap == 0), stop=(tap == 3)), [dgs[tap].b, R.b], [bk2.b], inc=(tap == 3))
                        sg = sgr.next()
                        P.op("act", "activation", dict(out=sg[:, :MT], in_=bk2.f32[:, :MT], func=AF.Exp, scale=-1.0,
                                                       bias=ncb[:, ct:ct + 1]), [bk2.b, ncb.b], [sg.b])
                        P.op("act", "activation", dict(out=sg[:, :MT], in_=sg[:, :MT], func=AF.Ln, bias=1.0), [sg.b], [sg.b])
                        P.op("act", "activation", dict(out=sg[:, :MT], in_=sg[:, :MT], func=AF.Exp, scale=-1.0), [sg.b], [sg.b])
                        if ct < 16:
                            dT, dap = xT, xT[:, ct, :MT]
                        elif ct < 24:
                            dT, dap = BT, BT[:, ct - 16, :MT]
                        else:
                            dT, dap = CT, CT[:, ct - 24, :MT]
                        P.op("dve", "scalar_tensor_tensor", dict(out=dap, in0=bk2.f32[:, :MT], scalar=cbias[:, ct:ct + 1], in1=sg[:, :MT],
                                                                 op0=ALU.add, op1=ALU.mult), [bk2.b, cbias.b, sg.b], [dT.b])
                        del pipe[ct]

                    def s3(ct):
                        pass

                    def s4(ct):
                        pass

                    for step in range(32 + 1):
                        if step < 32:
                            s1(step)
                        if 0 <= step - 1 < 32:
                            s2(step - 1)
                        yield
                    yield ("set", ("convdone", si, m))

                def chunk_stream(c, lane, si=si, base=base, CL=CL, MT=MT, NCH=NCH):
                    Ln = lanes[lane]
                    m, ci = divmod(c, NCH)
                    st = m % 2
                    xT, BT, CT, dtraw = xTs[st], BTs[st], CTs[st], dtraws[st]
                    c0 = ci * CL
                    row0 = base + m * MT + c0
                    sm, adtb, xtok, Btok, Wbh, Ebh, cbm, xddh = (Ln["sm"], Ln["adtb"], Ln["xtok"], Ln["Btok"], Ln["Wbh"],
                                                                 Ln["Ebh"], Ln["cbm"], Ln["xddh"])
                    yield ("wait", ("convdone", si, m))
                    P.op("dve", "tensor_tensor", dict(out=sm[:CL, 0, :], in0=dtraw[:CL, ci, :], in1=dtb[:CL, :], op=ALU.add),
                         [dtraw.b, dtb.b], [sm.b])
                    P.op("dve", "scalar_tensor_tensor", dict(out=sm[:CL, 1, :], in0=sm[:CL, 0, :], scalar=-1.0,
                                                             in1=sm[:CL, 0, :], op0=ALU.mult, op1=ALU.max), [sm.b], [sm.b])
                    P.op("act", "activation", dict(out=sm[:CL, 2, :], in_=sm[:CL, 1, :], func=AF.Exp, scale=-1.0), [sm.b], [sm.b])
                    P.op("act", "activation", dict(out=sm[:CL, 3, :], in_=sm[:CL, 2, :], func=AF.Ln, bias=1.0), [sm.b], [sm.b])
                    P.op("dve", "scalar_tensor_tensor", dict(out=sm[:CL, 4, :], in0=sm[:CL, 0, :], scalar=0.0,
                                                             in1=sm[:CL, 3, :], op0=ALU.max, op1=ALU.add), [sm.b], [sm.b])
                    dt = sm[:CL, 4, :]
                    P.op("dve", "tensor_tensor", dict(out=adtb[:CL, :], in0=dt, in1=Abc[:CL, :], op=ALU.mult),
                         [sm.b, Abc.b], [adtb.b])
                    P.op("act", "activation", dict(out=sm[:CL, 5, :], in_=dt, func=AF.Ln), [sm.b], [sm.b])
                    yield
                    bk2 = pb()
                    P.op("pe", "matmul", dict(out=bk2.f32[:CL, 0:NH], lhsT=trib[:CL, :CL], rhs=adtb[:CL, :],
                                              start=True, stop=True), [trib.b, adtb.b], [bk2.b], inc=False)
                    P.op("pe", "matmul", dict(out=bk2.f32[:, NH:2 * NH], lhsT=onesb[:CL, :], rhs=adtb[:CL, :],
                                              start=True, stop=True), [onesb.b, adtb.b], [bk2.b])
                    P.op("act", "activation", dict(out=sm[:CL, 6, :], in_=bk2.f32[:CL, 0:NH], func=AF.Exp), [bk2.b], [sm.b])
                    P.op("act", "activation", dict(out=sm[:, 7, :], in_=bk2.f32[:, NH:2 * NH], func=AF.Exp), [bk2.b], [sm.b])
                    P.op("act", "activation", dict(out=sm[:CL, 8, :], in_=bk2.f32[:CL, 0:NH], func=AF.Copy), [bk2.b], [sm.b])
                    P.op("dve", "tensor_tensor", dict(out=sm[:CL, 9, :], in0=bk2.f32[:CL, NH:2 * NH], in1=sm[:CL, 8, :],
                                                      op=ALU.subtract), [bk2.b, sm.b], [sm.b])
                    P.op("act", "activation", dict(out=sm[:CL, 10, :], in_=sm[:CL, 9, :], func=AF.Exp), [sm.b], [sm.b])
                    P.op("dve", "tensor_tensor", dict(out=sm[:CL, 11, :], in0=sm[:CL, 10, :], in1=dt, op=ALU.mult), [sm.b], [sm.b])
                    yield
                    for half in range(2):
                        bk = pb()
                        for i in range(8):
                            P.op("pe", "transpose", dict(out=bk.bf[:CL, i * 128:(i + 1) * 128],
                                                         in_=xT[:, half * 8 + i, c0:c0 + CL], identity=C["identb"][:]),
                                 [xT.b, C["identb"].b], [bk.b], inc=(i == 7))
                        P.op("act", "activation", dict(out=xtok[:CL, half * 1024:(half + 1) * 1024], in_=bk.bf[:CL, :],
                                                       func=AF.Copy), [bk.b], [xtok.b])
                        yield
                    bk = pb()
                    for g in range(8):
                        P.op("pe", "transpose", dict(out=bk.bf[:CL, g * 128:(g + 1) * 128], in_=BT[:, g, c0:c0 + CL],
                                                     identity=C["identb"][:]), [BT.b, C["identb"].b], [bk.b], inc=(g == 7))
                    P.op("act", "activation", dict(out=Btok[:CL, :], in_=bk.bf[:CL, :], func=AF.Copy), [bk.b], [Btok.b])
                    yield
                    cv = cbm[:CL, 0:NG * CL].rearrange("p (g t) -> p g t", g=NG)
                    for half in range(2):
                        bk = pb()
                        for gi in range(4):
                            g = half * 4 + gi
                            P.op("pe", "matmul", dict(out=bk.f32[:CL, gi * CL:(gi + 1) * CL], lhsT=BT[:, g, c0:c0 + CL],
                                                      rhs=CT[:, g, c0:c0 + CL], start=True, stop=True),
                                 [BT.b, CT.b], [bk.b], inc=(gi == 3))
                        P.op("dve", "tensor_tensor", dict(out=cv[:, half * 4:(half + 1) * 4, :],
                                                          in0=bk.f32[:CL, 0:4 * CL].rearrange("p (g t) -> p g t", g=4),
                                                          in1=trib[:CL, :CL].unsqueeze(1).to_broadcast([CL, 4, CL]),
                                                          op=ALU.mult), [bk.b, trib.b], [cbm.b])
                        yield
                    Wv = Wbh[:CL, 0:16 * CL].rearrange("p (h t) -> p h t", h=16)
                    Ev = Ebh[:CL, 0:16 * CL].rearrange("p (h t) -> p h t", h=16)
                    for hf in range(2):
                        P.op("dve", "tensor_tensor", dict(out=Wv, in0=trib[:CL, :CL].unsqueeze(1).to_broadcast([CL, 16, CL]),
                                                           in1=adtb[:CL, hf * 16:(hf + 1) * 16].unsqueeze(2).to_broadcast([CL, 16, CL]),
                                                           op=ALU.mult), [trib.b, adtb.b], [Wbh.b])
                        for gi in range(4):
                            bk = pb()
                            P.op("pe", "matmul", dict(out=bk.f32[:CL, 0:4 * CL], lhsT=sub[:CL, :CL],
                                                      rhs=Wbh[:CL, gi * 4 * CL:(gi + 1) * 4 * CL], start=True, stop=True),
                                 [sub.b, Wbh.b], [bk.b])
                            for hi in range(4):
                                hl = gi * 4 + hi
                                h = hf * 16 + hl
                                P.op("act", "activation", dict(out=Ev[:, hl, :], in_=bk.f32[:CL, hi * CL:(hi + 1) * CL],
                                                               func=AF.Exp, bias=sm[:CL, 5, h:h + 1]), [bk.b, sm.b], [Ebh.b])
                            yield
                        E4 = Ebh[:CL, 0:16 * CL].rearrange("p (g r t) -> p g r t", g=4, r=4)
                        P.op("dve", "tensor_tensor", dict(out=E4, in0=E4,
                                                          in1=cv[:, hf * 4:(hf + 1) * 4, :].unsqueeze(2).to_broadcast([CL, 4, 4, CL]),
                                                          op=ALU.mult), [Ebh.b, cbm.b], [Ebh.b])
                        yield
                        if c > 0:
                            yield ("wait", ("state", si, c - 1, hf))
                        yst = ystr.next()
                        for qq in range(2):
                            q = hf * 2 + qq
                            bkY = pb()
                            for hh in range(8):
                                hl = qq * 8 + hh
                                h = hf * 16 + hl
                                P.op("pe", "matmul", dict(out=bkY.f32[:CL, hh * 64:(hh + 1) * 64], lhsT=Ev[:, hl, :],
                                                          rhs=xtok[:CL, h * 64:(h + 1) * 64], start=True, stop=False),
                                     [Ebh.b, xtok.b], [bkY.b], inc=False)
                                P.op("pe", "matmul", dict(out=bkY.f32[:CL, hh * 64:(hh + 1) * 64], lhsT=DI[:CL, h, :CL],
                                                          rhs=xtok[:CL, h * 64:(h + 1) * 64], start=False, stop=True),
                                     [DI.b, xtok.b], [bkY.b], inc=(hh == 7))
                            bkO = pb()
                            for gg in range(2):
                                g = q * 2 + gg
                                P.op("pe", "matmul", dict(out=bkO.f32[:CL, gg * 256:(gg + 1) * 256], lhsT=CT[:, g, c0:c0 + CL],
                                                          rhs=hTbq[q][:, gg * 256:(gg + 1) * 256], start=True, stop=True),
                                     [CT.b, hTbq[q].b], [bkO.b], inc=(gg == 1))
                            tmp = tmpr.next()
                            P.op("dve", "tensor_tensor", dict(out=tmp[:CL, :].rearrange("p (h d) -> p h d", h=8),
                                                              in0=bkO.f32[:CL, :].rearrange("p (h d) -> p h d", h=8),
                                                              in1=sm[:CL, 6, q * 8:(q + 1) * 8].unsqueeze(2).to_broadcast([CL, 8, 64]),
                                                              op=ALU.mult), [bkO.b, sm.b], [tmp.b])
                            P.op("dve", "tensor_tensor", dict(out=yst[:CL, qq * 512:(qq + 1) * 512], in0=bkY.f32[:CL, :],
                                                              in1=tmp[:CL, :], op=ALU.add), [bkY.b, tmp.b], [yst.b])
                            yield
                        P.dma("pool", Ys[row0:row0 + CL, hf * 1024:(hf + 1) * 1024], yst[:CL, :], db("Ys", row0), yst.b, "st")
                        P.op("dve", "tensor_tensor", dict(out=xddh[:CL, :].rearrange("p (h d) -> p h d", h=16),
                                                          in0=xtok[:CL, hf * 1024:(hf + 1) * 1024].rearrange("p (h d) -> p h d", h=16),
                                                          in1=sm[:CL, 11, hf * 16:(hf + 1) * 16].unsqueeze(2).to_broadcast([CL, 16, 64]),
                                                          op=ALU.mult), [xtok.b, sm.b], [xddh.b])
                        for qq in range(2):
                            q = hf * 2 + qq
                            bkH = pb()
                            for gg in range(2):
                                g = q * 2 + gg
                                P.op("pe", "matmul", dict(out=bkH.f32[:, gg * 256:(gg + 1) * 256], lhsT=Btok[:CL, g * 128:(g + 1) * 128],
                                                          rhs=xddh[:CL, (qq * 2 + gg) * 256:(qq * 2 + gg + 1) * 256], start=True, stop=True),
                                     [Btok.b, xddh.b], [bkH.b], inc=(gg == 1))
                            hv = hTq[q]
                            P.op("dve", "tensor_tensor", dict(out=hv[:].rearrange("p (h d) -> p h d", h=8),
                                                              in0=hv[:].rearrange("p (h d) -> p h d", h=8),
                                                              in1=sm[:, 7, q * 8:(q + 1) * 8].unsqueeze(2).to_broadcast([128, 8, 64]),
                                                              op=ALU.mult), [hv.b, sm.b], [hv.b])
                            P.op("dve", "tensor_tensor", dict(out=hv[:], in0=bkH.f32[:, :], in1=hv[:], op=ALU.add), [bkH.b, hv.b], [hv.b])
                            P.op("act", "activation", dict(out=hTbq[q][:], in_=hv[:], func=AF.Copy), [hv.b], [hTbq[q].b])
                            yield
                        yield ("set", ("state", si, c, hf))
                    yield ("set", ("cdone", si, c))

                nchunks = NM * NCH
                convq = [(lambda m=m: conv_stream(m)) for m in range(NM)]
                laneA = [(lambda c=c: chunk_stream(c, 0)) for c in range(0, nchunks, 2)]
                laneB = [(lambda c=c: chunk_stream(c, 1)) for c in range(1, nchunks, 2)]
                run_sched([convq, laneA, laneB])

                o_ssm = (o_pssm if grp == "p" else o_sssm)[j, b]
                o_conv = (o_pconv if grp == "p" else o_sconv)[j, b]
                for hf in range(2):
                    stg = ystr.next()
                    for qq in range(2):
                        q = hf * 2 + qq
                        bk = pb()
                        for i in range(4):
                            P.op("pe", "transpose", dict(out=bk.f32[:, i * 128:(i + 1) * 128], in_=hTq[q][:, i * 128:(i + 1) * 128],
                                                         identity=identf[:]), [hTq[q].b, identf.b], [bk.b], inc=(i == 3))
                        P.op("act", "activation", dict(out=stg[:, qq * 512:(qq + 1) * 512], in_=bk.f32[:, :], func=AF.Copy),
                             [bk.b], [stg.b])
                    P.dma("pool", o_ssm[hf * 1024:(hf + 1) * 1024, :].rearrange("(t p) n -> p t n", p=128),
                          stg[:].rearrange("p (t n) -> p t n", t=8), db("o_ssm_%s" % grp, b), stg.b, "st")
                P.dma("pool", o_conv, halo[:], db("o_conv_%s" % grp, b), halo.b, "st")
            end_stage()

        def stage_ssd2(layer, j, src, dst):
            begin_stage()
            C = load_consts()
            Wz = sb([128, 8, DIN], BF16, "Wz")
            for kc in range(8):
                P.dma("pool", Wz[:, kc, :], ssd_in_w[j, kc * 128:(kc + 1) * 128, 0:DIN], Wz.b, cst, "ld", par=True)
            Wo = sb([128, 16, D], BF16, "Wo")
            load_w(Wo, ssd_out_w[j])
            gB = sb([128, D], F32, "gB")
            load_bcast(gB, norm_mix[layer])
            nwB = sb([128, DIN], F32, "nwB")
            load_bcast(nwB, ssd_norm_w[j])
            shared = dict(ss=ring(6, [128, 4], F32, "ss"), junk=ring(1, [128, D], BF16, "junk"),
                          xn=ring(3, [128, D], BF16, "xn"))
            junk2 = sb([128, 256], F32, "junk2")
            lanes = []
            for ln in range(3):
                rg = dict(shared)
                rg["xin"] = ring(1, [128, D], F32, "xin")
                lanes.append(dict(rings=rg, hnT=sb([128, 8, 128], BF16, "hnT"), yin=sb([128, DIN], F32, "yin"),
                                  sz=sb([128, DIN], F32, "sz"), gs=sb([128, 4, NG], F32, "gs"),
                                  ynb=sb([128, DIN], BF16, "ynb"), ynT=sb([128, 16, 128], BF16, "ynT"),
                                  y=sb([128, D], F32, "yst")))

            def tile_stream(tl, lane):
                L = lanes[lane]
                nt, r0 = tl["nt"], tl["row0"]
                hnT, yin, sz, gs, ynb, ynT, y = L["hnT"], L["yin"], L["sz"], L["gs"], L["ynb"], L["ynT"], L["y"]
                P.dma("sp", yin[:nt, :], Ys[r0:r0 + nt, :], yin.b, db("Ys", r0), "ld")
                xin = front(C, src, tl, gB, hnT, 0, L["rings"])
                yield
                for c in range(4):
                    bk = pb()
                    for kc in range(8):
                        P.op("pe", "matmul", dict(out=bk.f32[:nt, :], lhsT=hnT[:, kc, :nt], rhs=Wz[:, kc, c * 512:(c + 1) * 512],
                                                  start=(kc == 0), stop=(kc == 7)), [hnT.b, Wz.b], [bk.b], inc=(kc == 7))
                    sl = sz[:nt, c * 512:(c + 1) * 512]
                    P.op("act", "activation", dict(out=sl, in_=bk.f32[:nt, :], func=AF.Exp, scale=-1.0), [bk.b], [sz.b])
                    P.op("act", "activation", dict(out=sl, in_=sl, func=AF.Ln, bias=1.0), [sz.b], [sz.b])
                    P.op("act", "activation", dict(out=sl, in_=sl, func=AF.Exp, scale=-1.0), [sz.b], [sz.b])
                    P.op("dve", "tensor_tensor", dict(out=sl, in0=sl, in1=yin[:nt, c * 512:(c + 1) * 512], op=ALU.mult),
                         [sz.b, yin.b], [sz.b])
                    P.op("dve", "tensor_tensor", dict(out=sl, in0=bk.f32[:nt, :], in1=sl, op=ALU.mult), [bk.b, sz.b], [sz.b])
                    yield
                for g in range(NG):
                    P.op("act", "activation", dict(out=junk2[:nt, :], in_=sz[:nt, g * 256:(g + 1) * 256], func=AF.Square,
                                                   accum_out=gs[:nt, 0, g:g + 1]), [sz.b], [junk2.b, gs.b])
                yield
                P.op("act", "activation", dict(out=gs[:nt, 2, :], in_=gs[:nt, 0, :], func=AF.Ln, scale=1.0 / 256, bias=EPS),
                     [gs.b], [gs.b])
                P.op("act", "activation", dict(out=gs[:nt, 3, :], in_=gs[:nt, 2, :], func=AF.Exp, scale=-0.5), [gs.b], [gs.b])
                yield
                P.op("dve", "tensor_tensor", dict(out=sz[:nt, :].rearrange("p (g d) -> p g d", g=NG),
                                                  in0=sz[:nt, :].rearrange("p (g d) -> p g d", g=NG),
                                                  in1=gs[:nt, 3, :].unsqueeze(2).to_broadcast([nt, NG, 256]), op=ALU.mult),
                     [sz.b, gs.b], [sz.b])
                P.op("dve", "tensor_tensor", dict(out=ynb[:nt, :], in0=sz[:nt, :], in1=nwB[:nt, :], op=ALU.mult),
                     [sz.b, nwB.b], [ynb.b])
                yield
                for half in range(2):
                    bk = pb()
                    for i in range(8):
                        ct = half * 8 + i
                        P.op("pe", "transpose", dict(out=bk.bf[:, i * 128:i * 128 + nt], in_=ynb[:nt, ct * 128:(ct + 1) * 128],
                                                     identity=C["identb"][:nt, :nt]), [ynb.b, C["identb"].b], [bk.b], inc=(i == 7))
                    P.op("act", "activation", dict(out=ynT[:, half * 8:(half + 1) * 8, :nt],
                                                   in_=bk.bf.rearrange("p (k t) -> p k t", k=8)[:, :, :nt], func=AF.Copy),
                         [bk.b], [ynT.b])
                    yield
                for c in range(2):
                    bk = pb()
                    for ct in range(16):
                        P.op("pe", "matmul", dict(out=bk.f32[:nt, :], lhsT=ynT[:, ct, :nt], rhs=Wo[:, ct, c * 512:(c + 1) * 512],
                                                  start=(ct == 0), stop=(ct == 15)), [ynT.b, Wo.b], [bk.b], inc=(ct == 15))
                    P.op("dve", "tensor_tensor", dict(out=y[:nt, c * 512:(c + 1) * 512], in0=bk.f32[:nt, :],
                                                      in1=xin[:nt, c * 512:(c + 1) * 512], op=ALU.add), [bk.b, xin.b], [y.b])
                    yield
                P.dma("pool", dst[r0:r0 + nt, :], y[:nt, :], db(dst.name, r0), y.b, "st")

            run_streams(tiles, tile_stream, width=3, stagger=5)
            end_stage()

        def stage_sb1(layer, j, src):
            begin_stage()
            C = load_consts()
            Wqkv = sb([128, 8, 3 * D], BF16, "Wqkv")
            load_w(Wqkv, sb_qkv_w[j])
            gB = sb([128, D], F32, "gB")
            load_bcast(gB, norm_mix[layer])
            qg = sb([128, AD], F32, "qg")
            kg = sb([128, AD], F32, "kg")
            load_bcast(qg, sb_q_gain[j])
            load_bcast(kg, sb_k_gain[j])
            P.op("dve", "tensor_scalar", dict(out=qg[:], in0=qg[:], scalar1=0.125, scalar2=None, op0=ALU.mult), [qg.b], [qg.b])
            rings = dict(xin=ring(2, [128, D], F32, "xin"), ss=ring(4, [128, 4], F32, "ss"),
                         junk=ring(1, [128, D], BF16, "junk"), xn=ring(2, [128, D], BF16, "xn"))
            xnTs = ring(2, [128, 8, 128], BF16, "xnT")
            sqr = ring(2, [128, 512], F32, "sq")
            t1r = ring(2, [128, 512], F32, "t1")
            ssqr = ring(4, [128, 4, 8], F32, "ssq")
            koutr = ring(2, [128, D], F32, "kout")
            voutr = ring(2, [128, D], F32, "vout")
            knbr = ring(2, [128, D], BF16, "knb")
            qnbr = ring(2, [128, D], BF16, "qnb")
            vbr = ring(2, [128, D], BF16, "vb")
            qTr = ring(2, [128, 8, 128], BF16, "qT")
            kTr = ring(2, [128, 8, 128], BF16, "kT")
            ckr = ring(2, [128, D], BF16, "ck")

            def transpose_store(srcT, nt, stg, dram_ap, dbuf):
                bk = pb()
                for hp in range(8):
                    P.op("pe", "transpose", dict(out=bk.bf[:, hp * 128:hp * 128 + nt], in_=srcT[:nt, hp * 128:(hp + 1) * 128],
                                                 identity=C["identb"][:nt, :nt]), [srcT.b, C["identb"].b], [bk.b], inc=(hp == 7))
                P.op("act", "activation", dict(out=stg[:, :, :nt], in_=bk.bf.rearrange("p (k t) -> p k t", k=8)[:, :, :nt],
                                               func=AF.Copy), [bk.b], [stg.b])
                P.dma("pool", dram_ap, stg[:, :, :nt], dbuf, stg.b, "st")

            for tl in tiles:
                nt, r0, grp, b, pos = tl["nt"], tl["row0"], tl["grp"], tl["seq"], tl["pos"]
                xnT = xnTs.next()
                front(C, src, tl, gB, xnT, 0, rings)
                kout = koutr.next()
                vout = voutr.next()
                knb = knbr.next()
                qnb = qnbr.next()
                vb = vbr.next()
                for c in range(6):
                    bk = pb()
                    for kc in range(8):
                        P.op("pe", "matmul", dict(out=bk.f32[:nt, :], lhsT=xnT[:, kc, :nt], rhs=Wqkv[:, kc, c * 512:(c + 1) * 512],
                                                  start=(kc == 0), stop=(kc == 7)), [xnT.b, Wqkv.b], [bk.b], inc=(kc == 7))
                    if c < 4:
                        sq = sqr.next()
                        ssq = ssqr.next()
                        t1 = t1r.next()
                        P.op("act", "activation", dict(out=sq[:nt, :], in_=bk.f32[:nt, :], func=AF.Square), [bk.b], [sq.b])
                        P.op("dve", "tensor_reduce", dict(out=ssq[:nt, 0, :], in_=sq[:nt, :].rearrange("p (h d) -> p h d", h=8),
                                                          axis=AX.X, op=ALU.add), [sq.b], [ssq.b])
                        P.op("act", "activation", dict(out=ssq[:nt, 2, :], in_=ssq[:nt, 0, :], func=AF.Ln, scale=1.0 / AD, bias=EPS),
                             [ssq.b], [ssq.b])
                        P.op("act", "activation", dict(out=ssq[:nt, 3, :], in_=ssq[:nt, 2, :], func=AF.Exp, scale=-0.5),
                             [ssq.b], [ssq.b])
                        P.op("dve", "tensor_tensor", dict(out=t1[:nt, :].rearrange("p (h d) -> p h d", h=8),
                                                          in0=bk.f32[:nt, :].rearrange("p (h d) -> p h d", h=8),
                                                          in1=ssq[:nt, 3, :].unsqueeze(2).to_broadcast([nt, 8, AD]), op=ALU.mult),
                             [bk.b, ssq.b], [t1.b])
                        if c < 2:
                            P.op("dve", "tensor_tensor", dict(out=qnb[:nt, c * 512:(c + 1) * 512].rearrange("p (h d) -> p h d", h=8),
                                                              in0=t1[:nt, :].rearrange("p (h d) -> p h d", h=8),
                                                              in1=qg[:nt, :].unsqueeze(1).to_broadcast([nt, 8, AD]), op=ALU.mult),
                                 [t1.b, qg.b], [qnb.b])
                        else:
                            cc = c - 2
                            P.op("dve", "tensor_tensor", dict(out=kout[:nt, cc * 512:(cc + 1) * 512].rearrange("p (h d) -> p h d", h=8),
                                                              in0=t1[:nt, :].rearrange("p (h d) -> p h d", h=8),
                                                              in1=kg[:nt, :].unsqueeze(1).to_broadcast([nt, 8, AD]), op=ALU.mult),
                                 [t1.b, kg.b], [kout.b])
                            P.op("act", "activation", dict(out=knb[:nt, cc * 512:(cc + 1) * 512], in_=kout[:nt, cc * 512:(cc + 1) * 512],
                                                           func=AF.Copy), [kout.b], [knb.b])
                    else:
                        cc = c - 4
                        P.op("act", "activation", dict(out=vout[:nt, cc * 512:(cc + 1) * 512], in_=bk.f32[:nt, :], func=AF.Copy),
                             [bk.b], [vout.b])
                        P.op("dve", "tensor_copy", dict(out=vb[:nt, cc * 512:(cc + 1) * 512], in_=bk.f32[:nt, :]), [bk.b], [vb.b])
                if grp == "p":
                    P.dma("pool", o_pk[j, b, pos:pos + nt, :], kout[:nt, :], db("o_pk", r0), kout.b, "st")
                    P.dma("pool", o_pv[j, b, pos:pos + nt, :], vout[:nt, :], db("o_pv", r0), vout.b, "st")
                    P.dma("pool", Vp[b, pos:pos + nt, :], vb[:nt, :], db("Vp", b), vb.b, "st")
                    transpose_store(qnb, nt, qTr.next(), QTp[b, :, :, pos:pos + nt], db("QTp", b))
                    transpose_store(knb, nt, kTr.next(), KTp[b, :, :, pos:pos + nt], db("KTp", b))
                else:
                    P.dma("pool", o_sk[j, b, 0:nt, :], kout[:nt, :], db("o_sk", r0), kout.b, "st")
                    P.dma("pool", o_sv[j, b, 0:nt, :], vout[:nt, :], db("o_sv", r0), vout.b, "st")
                    P.dma("pool", Vs[b, 0:nt, :], vb[:nt, :], db("Vs", b), vb.b, "st")
                    transpose_store(qnb, nt, qTr.next(), QTs[b, :, :, 0:nt], db("QTs", b))
                    transpose_store(knb, nt, kTr.next(), KTs[b, :, :, PAST:PAST + nt], db("KTs", b))
            for b in range(NS):
                for kb in range(PAST // 128):
                    ck = ckr.next()
                    for hf in range(2):
                        P.dma("pool", ck[:, hf * 512:(hf + 1) * 512], cache_k[j, b, kb * 128:(kb + 1) * 128, hf * 512:(hf + 1) * 512],
                              ck.b, cst, "ld")
                    transpose_store(ck, 128, kTr.next(), KTs[b, :, :, kb * 128:(kb + 1) * 128], db("KTs", b))
            end_stage()

        def stage_sb2(layer, j, src, dst):
            begin_stage()
            Wo = sb([128, 8, D], BF16, "Wo")
            load_w(Wo, sb_out_w[j])
            suin = sb([128, 128], BF16, "suin")
            sutmp = sb([128, 128], BF16, "sutmp")
            onesb = sb([128, 128], BF16, "onesb")
            nonesb = sb([128, 128], BF16, "nonesb")
            maskb = sb([128, 4, 512], BF16, "maskb")
            masksb = sb([128, 32], BF16, "masksb")
            P.dma("pool", sutmp[:], c_tri[:, :], sutmp.b, cst, "ld")
            P.dma("pool", suin[:], c_su[:, :], suin.b, cst, "ld")
            P.dma("pool", sutmp[:], c_ident[:, :], sutmp.b, cst, "ld")
            P.op("dve", "tensor_tensor", dict(out=suin[:], in0=suin[:], in1=sutmp[:], op=ALU.add), [suin.b, sutmp.b], [suin.b])
            P.op("dve", "tensor_scalar", dict(out=suin[:], in0=suin[:], scalar1=-1.0, scalar2=None, op0=ALU.mult), [suin.b], [suin.b])
            P.op("dve", "memset", dict(ap=onesb[:], constant=1.0), [], [onesb.b])
            P.op("dve", "tensor_scalar", dict(out=nonesb[:], in0=sutmp[:], scalar1=-1.0, scalar2=None, op0=ALU.mult),
                 [sutmp.b], [nonesb.b])
            for i in range(4):
                P.dma("pool", maskb[:, i, :], c_mask[:, i, :], maskb.b, cst, "ld")
            P.dma("pool", masksb[:], c_masks[:, :], masksb.b, cst, "ld")
            KLEN = max(SEQ, KPAD)
            KT = sb([128, 8, KLEN], BF16, "KT")
            Vb = sb([128, KLEN // 128, D], BF16, "Vb")
            QWM = 512
            QTr = ring(2, [128, 8, QWM], BF16, "QT")
            OT = sb([128, 8, QWM], BF16, "OT")
            e1r = ring(4, [128, QWM], F32, "e1")
            lgr = ring(6, [128, QWM], BF16, "lg")
            wbr = ring(6, [128, QWM], BF16, "wb")
            Sbfr = [ring(2, [128, QWM], BF16, "Sbf0"), ring(2, [128, QWM], BF16, "Sbf1")]
            xinr = ring(2, [128, D], F32, "xin")
            ystr = ring(2, [128, D], F32, "yst")
            bkR = [hold(), hold()]
            bkOs = [hold(), hold()]

            seqs = [("p", b) for b in range(NB)] + [("s", b) for b in range(NS)]
            for (grp, b) in seqs:
                if grp == "p":
                    base, L, QW = b * SEQ, SEQ, 512
                    for hp in range(8):
                        P.dma("sp", KT[:, hp, :SEQ], KTp[b, :, hp, :], KT.b, db("KTp", b), "ld", par=True)
                    for t4 in range(0, SEQ // 128, 4):
                        P.dma("sp", Vb[:, t4:t4 + 4, :], Vp[b, t4 * 128:(t4 + 4) * 128, :].rearrange("(t p) c -> p t c", p=128),
                              Vb.b, db("Vp", b), "ld", par=True)
                else:
                    base, L, QW = NTP + b * DSEQ, DSEQ, DSEQ
                    nkc = PAST // 128
                    if KPAD > KTOT:
                        P.op("dve", "memset", dict(ap=KT[:, :, KTOT:KPAD], constant=0.0), [], [KT.b])
                    P.op("dve", "memset", dict(ap=Vb[:, nkc, :], constant=0.0), [], [Vb.b])
                    for hp in range(8):
                        P.dma("sp", KT[:, hp, :KTOT], KTs[b, :, hp, :KTOT], KT.b, db("KTs", b), "ld")
                    for t in range(nkc):
                        for hf in range(2):
                            P.dma("pool", Vb[:, t, hf * 512:(hf + 1) * 512],
                                  cache_v[j, b, t * 128:(t + 1) * 128, hf * 512:(hf + 1) * 512], Vb.b, cst, "ld")
                    P.dma("sp", Vb[:DSEQ, nkc, :], Vs[b, :, :], Vb.b, db("Vs", b), "ld")
                for q0 in range(0, L, QW):
                    QT = QTr.next()
                    if grp == "p":
                        P.dma("sp", QT[:, :, :QW], QTp[b, :, :, q0:q0 + QW], QT.b, db("QTp", b), "ld")
                        nkb = (q0 + QW) // 128
                        def mask_of(kb, q0=q0):
                            i = kb - q0 // 128
                            return maskb[:, i, :] if i >= 0 else None
                    else:
                        P.dma("sp", QT[:, :, :QW], QTs[b, :, :, :], QT.b, db("QTs", b), "ld")
                        nkb = KPAD // 128
                        def mask_of(kb, nkb=nkb):
                            return masksb[:, :] if kb == nkb - 1 else None
                    its = []
                    for hp in range(8):
                        for kb in range(nkb - 1, -1, -1):
                            for hh in range(2):
                                its.append((hp, kb, hh))
                    st = {}

                    def col0_of(kb, q0=q0, grp=grp):
                        if grp != "p" or not TRIM:
                            return 0
                        return max(0, (kb - q0 // 128) * 128)

                    def stageA(it):
                        hp, kb, hh = it
                        po = hh * 64
                        a = col0_of(kb)
                        bkT = pb()
                        P.op("pe", "matmul", dict(out=bkT.f32[:, a:QW], lhsT=KT[po:po + 64, hp, kb * 128:(kb + 1) * 128],
                                                  rhs=QT[po:po + 64, hp, a:QW], start=True, stop=False), [KT.b, QT.b], [bkT.b])
                        e1 = e1r.next()
                        P.op("act", "activation", dict(out=e1[:, a:QW], in_=bkT.f32[:, a:QW], func=AF.Exp), [bkT.b], [e1.b])
                        lg = lgr.next()
                        P.op("act", "activation", dict(out=lg[:, a:QW], in_=e1[:, a:QW], func=AF.Ln, bias=1.0), [e1.b], [lg.b])
                        m = mask_of(kb)
                        if m is not None:
                            a2 = min(QW, a + 128)
                            P.op("dve", "tensor_tensor", dict(out=lg[:, a:a2], in0=lg[:, a:a2], in1=m[:, a:a2], op=ALU.mult),
                                 [lg.b, maskb.b, masksb.b], [lg.b])
                        st[it] = dict(bkT=bkT, lg=lg)

                    def stageA2(it0, it1):
                        pre = []
                        for it in (it0, it1):
                            hp, kb, hh = it
                            po = hh * 64
                            a = col0_of(kb)
                            bkT = pb()
                            P.op("pe", "matmul", dict(out=bkT.f32[:, a:QW], lhsT=KT[po:po + 64, hp, kb * 128:(kb + 1) * 128],
                                                      rhs=QT[po:po + 64, hp, a:QW], start=True, stop=False), [KT.b, QT.b], [bkT.b])
                            pre.append((it, bkT, a))
                        e1s = []
                        for (it, bkT, a) in pre:
                            e1 = e1r.next()
                            P.op("act", "activation", dict(out=e1[:, a:QW], in_=bkT.f32[:, a:QW], func=AF.Exp), [bkT.b], [e1.b])
                            e1s.append(e1)
                        for (it, bkT, a), e1 in zip(pre, e1s):
                            hp, kb, hh = it
                            lg = lgr.next()
                            P.op("act", "activation", dict(out=lg[:, a:QW], in_=e1[:, a:QW], func=AF.Ln, bias=1.0), [e1.b], [lg.b])
                            m = mask_of(kb)
                            if m is not None:
                                a2 = min(QW, a + 128)
                                P.op("dve", "tensor_tensor", dict(out=lg[:, a:a2], in0=lg[:, a:a2], in1=m[:, a:a2], op=ALU.mult),
                                     [lg.b, maskb.b, masksb.b], [lg.b])
                            st[it] = dict(bkT=bkT, lg=lg)

                    def stageB(it):
                        hp, kb, hh = it
                        d = st[it]
                        bkT, lg = d["bkT"], d["lg"]
                        first = (kb == nkb - 1)
                        lastb = (kb == 0)
                        a = col0_of(kb)
                        P.op("pe", "matmul", dict(out=bkT.f32[:, a:QW], lhsT=suin[:], rhs=lg[:, a:QW], start=False, stop=first),
                             [suin.b, lg.b], [bkT.b], inc=first)
                        if not first:
                            Sbf = d["Sbf"] = st[(hp, kb + 1, hh)]["Snext"]
                            P.op("pe", "matmul", dict(out=bkT.f32[:, a:QW], lhsT=nonesb[:], rhs=Sbf[:, a:QW], start=False, stop=True),
                                 [nonesb.b, Sbf.b], [bkT.b])
                        if not lastb:
                            P.op("pe", "matmul", dict(out=bkR[hh].f32[:, a:QW], lhsT=onesb[:], rhs=lg[:, a:QW], start=first,
                                                      stop=(kb == 1)), [onesb.b, lg.b], [bkR[hh].b])
                            Sn = Sbfr[hh].next()
                            P.op("dve", "tensor_copy", dict(out=Sn[:, a:QW], in_=bkR[hh].f32[:, a:QW]), [bkR[hh].b], [Sn.b])
                            an = col0_of(kb - 1)
                            if an < a:
                                P.op("dve", "memset", dict(ap=Sn[:, an:a], constant=0.0), [], [Sn.b])
                            d["Snext"] = Sn
                        wb = wbr.next()
                        P.op("act", "activation", dict(out=wb[:, a:QW], in_=bkT.f32[:, a:QW], func=AF.Exp), [bkT.b], [wb.b])
                        m = mask_of(kb)
                        if m is not None:
                            a2 = min(QW, a + 128)
                            P.op("dve", "tensor_tensor", dict(out=wb[:, a:a2], in0=wb[:, a:a2], in1=m[:, a:a2], op=ALU.mult),
                                 [wb.b, maskb.b, masksb.b], [wb.b])
                        d["wb"] = wb

                    def stageC(it):
                        hp, kb, hh = it
                        d = st[it]
                        po = hh * 64
                        h = hp * 2 + hh
                        a = col0_of(kb)
                        bkO = bkOs[hp % 2]
                        P.op("pe", "matmul", dict(out=bkO.f32[po:po + 64, a:QW], lhsT=Vb[:, kb, h * 64:(h + 1) * 64],
                                                  rhs=d["wb"][:, a:QW], start=(kb == nkb - 1), stop=(kb == 0)),
                             [Vb.b, d["wb"].b], [bkO.b])
                        if kb == 0 and hh == 1:
                            P.op("act", "activation", dict(out=OT[:, hp, :QW], in_=bkO.f32[:, :QW], func=AF.Copy), [bkO.b], [OT.b])

                    n = len(its) // 2
                    for step in range(n + 2):
                        if step < n:
                            stageA2(its[2 * step], its[2 * step + 1])
                        if 0 <= step - 1 < n:
                            stageB(its[2 * (step - 1)])
                            stageB(its[2 * (step - 1) + 1])
                        if 0 <= step - 2 < n:
                            stageC(its[2 * (step - 2)])
                            stageC(its[2 * (step - 2) + 1])
                    for i0 in range(0, QW, 128):
                        nt = min(128, QW - i0)
                        r0 = base + q0 + i0
                        xin = xinr.next()
                        P.dma("sp", xin[:nt, :], src[r0:r0 + nt, :], xin.b, db(src.name, r0), "ld")
                        y = ystr.next()
                        for c in range(2):
                            bk = pb()
                            for hp in range(8):
                                P.op("pe", "matmul", dict(out=bk.f32[:nt, :], lhsT=OT[:, hp, i0:i0 + nt], rhs=Wo[:, hp, c * 512:(c + 1) * 512],
                                                          start=(hp == 0), stop=(hp == 7)), [OT.b, Wo.b], [bk.b], inc=(hp == 7))
                            P.op("dve", "tensor_tensor", dict(out=y[:nt, c * 512:(c + 1) * 512], in0=bk.f32[:nt, :],
                                                              in1=xin[:nt, c * 512:(c + 1) * 512], op=ALU.add), [bk.b, xin.b], [y.b])
                        P.dma("pool", dst[r0:r0 + nt, :], y[:nt, :], db(dst.name, r0), y.b, "st")
            for bk in bkR + bkOs:
                release(bk)
            end_stage()

        src = x_all
        for layer in range(DEPTH):
            j = layer // 2
            last = (layer == DEPTH - 1)
            if only is not None and "mix" not in only:
                stage_copy(src, xs)
            elif layer % 2 == 0:
                stage_ssd1(layer, j, src)
                stage_ssd2(layer, j, src, xs)
            else:
                stage_sb1(layer, j, src)
                stage_sb2(layer, j, src, xs)
            src = xs
            stage_mlp(layer, xs, y_all if last else xs)
        P.barrier()
        P.emit()
    return nc


def _consts():
    i = np.arange(128)
    ident = np.eye(128, dtype=np.float32)
    tri = (i[:, None] <= i[None, :]).astype(np.float32)
    su = (i[:, None] > i[None, :]).astype(np.float32)
    q = np.arange(512)
    mask = np.stack([(i[:, None] + 128 * k < q[None, :]) for k in range(4)], axis=1).astype(np.float32)
    qs = np.arange(32)
    masks = ((i[:, None] < qs[None, :]) & (i[:, None] < 32)).astype(np.float32)
    return dict(c_ident=ident, c_tri=tri, c_su=su, c_mask=np.ascontiguousarray(mask), c_masks=masks)


def make_in_maps(inp, n_cores, NB, SEQ, NS, DSEQ, PAST, DEPTH):
    N_SSD = (DEPTH + 1) // 2
    N_SB = DEPTH // 2
    f = lambda a: np.ascontiguousarray(np.asarray(a, dtype=np.float32))
    shared = dict(
        norm_mix=f(inp["norm_mix"]), norm_mlp=f(inp["norm_mlp"]), ssd_in_w=f(inp["ssd_in_w"]),
        ssd_conv_w=f(np.asarray(inp["ssd_conv_w"]).reshape(N_SSD, 4, 32, 128).transpose(0, 3, 2, 1)),
        ssd_conv_b=f(np.asarray(inp["ssd_conv_b"]).reshape(N_SSD, 32, 128).transpose(0, 2, 1)),
        ssd_dt_bias=f(inp["ssd_dt_bias"]), ssd_a_log=f(inp["ssd_a_log"]), ssd_d=f(inp["ssd_d"]),
        ssd_norm_w=f(inp["ssd_norm_w"]), ssd_out_w=f(inp["ssd_out_w"]),
        mlp_up=f(inp["mlp_up"]), mlp_down=f(inp["mlp_down"]))
    if N_SB > 0:
        shared.update(sb_qkv_w=f(inp["sb_qkv_w"]), sb_q_gain=f(inp["sb_q_gain"]), sb_k_gain=f(inp["sb_k_gain"]),
                      sb_out_w=f(inp["sb_out_w"]))
    else:
        shared.update(sb_qkv_w=np.zeros((1, D, 3 * D), np.float32), sb_q_gain=np.zeros((1, AD), np.float32),
                      sb_k_gain=np.zeros((1, AD), np.float32), sb_out_w=np.zeros((1, D, D), np.float32))
    shared.update(_consts())
    xp = np.asarray(inp["x_prompt"], dtype=np.float32)
    xsm = np.asarray(inp["x_sample"], dtype=np.float32)
    sssm = np.asarray(inp["state_ssm"], dtype=np.float32)
    sconv = np.asarray(inp["state_conv"], dtype=np.float32)
    ck = np.asarray(inp["cache_k"], dtype=np.float32)
    cv = np.asarray(inp["cache_v"], dtype=np.float32)
    maps = []
    for c in range(n_cores):
        m = dict(shared)
        m["x_all"] = f(np.concatenate([xp[c * NB:(c + 1) * NB].reshape(NB * SEQ, D),
                                       xsm[c * NS:(c + 1) * NS].reshape(NS * DSEQ, D)], axis=0))
        m["state_ssm"] = f(sssm[:, c * NS:(c + 1) * NS].reshape(N_SSD, NS, NH * HP, NST))
        m["state_conv"] = f(sconv[:, c * NS:(c + 1) * NS].reshape(N_SSD, NS, 3, 32, 128).transpose(0, 1, 4, 3, 2))
        if N_SB > 0:
            m["cache_k"] = f(ck[:, c * NS:(c + 1) * NS].reshape(N_SB, NS, PAST, D))
            m["cache_v"] = f(cv[:, c * NS:(c + 1) * NS].reshape(N_SB, NS, PAST, D))
        else:
            m["cache_k"] = np.zeros((1, NS, PAST, D), np.float32)
            m["cache_v"] = np.zeros((1, NS, PAST, D), np.float32)
        maps.append(m)
    return maps


def assemble(results, n_cores, NB, SEQ, NS, DSEQ, PAST, DEPTH):
    N_SSD = (DEPTH + 1) // 2
    N_SB = DEPTH // 2
    cat = lambda k, ax: np.concatenate([np.asarray(r[k]) for r in results], axis=ax)
    y_all = [np.asarray(r["y_all"]) for r in results]
    y_p = np.concatenate([y[:NB * SEQ].reshape(NB, SEQ, D) for y in y_all], axis=0)
    y_s = np.concatenate([y[NB * SEQ:].reshape(NS, DSEQ, D) for y in y_all], axis=0)
    pssm = cat("o_pssm", 1).reshape(N_SSD, n_cores * NB, NH, HP, NST)
    sssm = cat("o_sssm", 1).reshape(N_SSD, n_cores * NS, NH, HP, NST)
    pconv = np.ascontiguousarray(cat("o_pconv", 1).transpose(0, 1, 4, 3, 2)).reshape(N_SSD, n_cores * NB, 3, CONVD)
    sconv = np.ascontiguousarray(cat("o_sconv", 1).transpose(0, 1, 4, 3, 2)).reshape(N_SSD, n_cores * NS, 3, CONVD)
    pk = cat("o_pk", 1)[:N_SB].reshape(N_SB, n_cores * NB, SEQ, AH, AD)
    pv = cat("o_pv", 1)[:N_SB].reshape(N_SB, n_cores * NB, SEQ, AH, AD)
    sk = cat("o_sk", 1)[:N_SB].reshape(N_SB, n_cores * NS, DSEQ, AH, AD)
    sv = cat("o_sv", 1)[:N_SB].reshape(N_SB, n_cores * NS, DSEQ, AH, AD)
    outs = (y_p, y_s, pssm, pconv, pk, pv, sssm, sconv, sk, sv)
    return tuple(np.ascontiguousarray(o, dtype=np.float32) for o in outs)


def kernel(**inputs):
    n = 8
    NB, SEQ, NS, DSEQ, PAST, DEPTH = 4, 2048, 2, 32, 1024, 4
    nc = build(NB, SEQ, NS, DSEQ, PAST, DEPTH)
    maps = make_in_maps(inputs, n, NB, SEQ, NS, DSEQ, PAST, DEPTH)
    res = run_bass_kernel_spmd(nc, maps, core_ids=list(range(n)))
    return assemble(res.results, n, NB, SEQ, NS, DSEQ, PAST, DEPTH)
```

```python
import numpy as np
from contextlib import ExitStack
import concourse.bass as bass
import concourse.mybir as mybir
from concourse.bass_utils import run_bass_kernel_spmd

F32 = mybir.dt.float32
BF16 = mybir.dt.bfloat16
AF = mybir.ActivationFunctionType
ALU = mybir.AluOpType
AX = mybir.AxisListType

D = 1024
DFF = 4096
DIN = 2048
NH = 32
HP = 64
NG = 8
NST = 128
CONVD = 4096
DPROJ = 6176
AH = 16
AD = 64
EPS = 1e-6
import os as _os
TRIM = _os.environ.get('K_TRIM', '1') == '1'


class Buf:
    __slots__ = ("name", "lw", "rd")

    def __init__(s, name):
        s.name = name
        s.lw = None
        s.rd = {}


class Prog:
    ENG = ("pe", "act", "dve", "pool", "sp")

    def __init__(s, nc, es):
        s.nc = nc
        s.es = es
        s.sems = {}
        s.cnt = {}
        s.ops = {e: [] for e in s.ENG}
        s.waited = {e: {} for e in s.ENG}
        s.nbuf = 0
        s.nops = 0
        s.free = []
        s.stage_keys = []

    def buf(s, name=None):
        s.nbuf += 1
        return Buf((name or "b") + "_%d" % s.nbuf)

    def sem(s, key):
        if key not in s.sems:
            if key in s.ENG or not s.free:
                s.sems[key] = s.es.enter_context(s.nc.semaphore("s%d" % len(s.sems)))
                s.cnt[key] = 0
            else:
                h, c0 = s.free.pop()
                s.sems[key] = h
                s.cnt[key] = c0
            if key not in s.ENG:
                s.stage_keys.append(key)
        return s.sems[key]

    def retire(s):
        for k in s.stage_keys:
            s.free.append((s.sems[k], s.cnt[k]))
        s.stage_keys = []

    def _waits(s, eng, deps):
        w = s.waited[eng]
        best = {}
        for d in deps:
            if d is None:
                continue
            k, v = d
            if w.get(k, 0) >= v:
                continue
            if best.get(k, 0) < v:
                best[k] = v
        out = []
        for k, v in best.items():
            w[k] = v
            out.append((k, v))
        return out

    def op(s, eng, name, kw, reads=(), writes=(), inc=True):
        deps = []
        for b in reads:
            deps.append(b.lw)
        for b in writes:
            deps.append(b.lw)
            for k, v in b.rd.items():
                if k != eng:
                    deps.append((k, v))
        if eng == "pe":
            deps = [d for d in deps if d is not None and d[0] != "pe"]
        waits = s._waits(eng, deps)
        s.sem(eng)
        val = s.cnt[eng] + 1
        if inc:
            s.cnt[eng] = val
        for b in reads:
            if b.rd.get(eng, 0) < val:
                b.rd[eng] = val
        for b in writes:
            b.lw = (eng, val)
            b.rd = {}
        s.ops[eng].append((waits, name, kw, (eng, 1) if inc else None))
        s.nops += 1

    def dma(s, q, out_ap, in_ap, dst, src, kind, par=False, **kw):
        key = ("ld:" + dst.name) if kind == "ld" else ("st:" + src.name)
        s.sem(key)
        deps = [src.lw] + list(dst.rd.items())
        if par and kind == "ld" and dst.lw is not None and dst.lw[0] == key:
            pass
        else:
            deps.append(dst.lw)
            if s.cnt[key] > 0:
                deps.append((key, s.cnt[key]))
        waits = s._waits(q, deps)
        val = s.cnt[key] + 16
        s.cnt[key] = val
        if src.rd.get(key, 0) < val:
            src.rd[key] = val
        dst.lw = (key, val)
        dst.rd = {}
        k2 = dict(out=out_ap, in_=in_ap)
        k2.update(kw)
        s.ops[q].append((waits, "dma_start", k2, (key, 16)))
        s.nops += 1

    def barrier(s):
        for e in s.ENG:
            deps = [(k, v) for k, v in s.cnt.items() if v > 0 and k != e]
            waits = s._waits(e, deps)
            if waits:
                s.ops[e].append((waits, None, None, None))

    def emit(s):
        nc = s.nc
        with nc.Block() as block:
            def mk(eng):
                def f(e):
                    for waits, name, kw, inc in s.ops[eng]:
                        for k, v in waits:
                            e.wait_ge(s.sems[k], v)
                        if name is not None:
                            ins = getattr(e, name)(**kw)
                            if inc:
                                ins.then_inc(s.sems[inc[0]], inc[1])
                return f
            block.tensor(mk("pe"))
            block.scalar(mk("act"))
            block.vector(mk("dve"))
            block.gpsimd(mk("pool"))
            block.sync(mk("sp"))


class T:
    __slots__ = ("t", "b")

    def __init__(s, t, b):
        s.t = t
        s.b = b

    def __getitem__(s, k):
        return s.t[k]


class Ring:
    def __init__(s, items):
        s.items = items
        s.i = 0

    def next(s):
        it = s.items[s.i % len(s.items)]
        s.i += 1
        return it


class Bank:
    __slots__ = ("f32", "bf", "b")

    def __init__(s, t, b):
        s.f32 = t
        s.bf = t[:].bitcast(BF16)
        s.b = b


def build(NB, SEQ, NS, DSEQ, PAST, DEPTH, only=None):
    nc = bass.Bass("TRN2", target_bir_lowering=False)
    NTP = NB * SEQ
    NTS = NS * DSEQ
    NTOK = NTP + NTS
    N_SSD = (DEPTH + 1) // 2
    N_SB = DEPTH // 2
    NSB1 = max(N_SB, 1)
    KTOT = PAST + DSEQ
    KPAD = ((KTOT + 127) // 128) * 128

    def di(n, sh, dt=F32):
        return nc.dram_tensor(n, list(sh), dt, kind="ExternalInput").ap()

    def do(n, sh):
        return nc.dram_tensor(n, list(sh), F32, kind="ExternalOutput").ap()

    def dx(n, sh, dt=F32):
        return nc.dram_tensor(n, list(sh), dt).ap()

    x_all = di("x_all", [NTOK, D])
    state_ssm = di("state_ssm", [N_SSD, NS, NH * HP, NST])
    state_conv = di("state_conv", [N_SSD, NS, 128, 32, 3])
    cache_k = di("cache_k", [NSB1, NS, PAST, D])
    cache_v = di("cache_v", [NSB1, NS, PAST, D])
    norm_mix = di("norm_mix", [DEPTH, D])
    norm_mlp = di("norm_mlp", [DEPTH, D])
    ssd_in_w = di("ssd_in_w", [N_SSD, D, DPROJ])
    ssd_conv_w = di("ssd_conv_w", [N_SSD, 128, 32, 4])
    ssd_conv_b = di("ssd_conv_b", [N_SSD, 128, 32])
    ssd_dt_bias = di("ssd_dt_bias", [N_SSD, NH])
    ssd_a_log = di("ssd_a_log", [N_SSD, NH])
    ssd_d = di("ssd_d", [N_SSD, NH])
    ssd_norm_w = di("ssd_norm_w", [N_SSD, DIN])
    ssd_out_w = di("ssd_out_w", [N_SSD, DIN, D])
    sb_qkv_w = di("sb_qkv_w", [NSB1, D, 3 * D])
    sb_q_gain = di("sb_q_gain", [NSB1, AD])
    sb_k_gain = di("sb_k_gain", [NSB1, AD])
    sb_out_w = di("sb_out_w", [NSB1, D, D])
    mlp_up = di("mlp_up", [DEPTH, D, DFF])
    mlp_down = di("mlp_down", [DEPTH, DFF, D])
    c_ident = di("c_ident", [128, 128])
    c_tri = di("c_tri", [128, 128])
    c_su = di("c_su", [128, 128])
    c_mask = di("c_mask", [128, 4, 512])
    c_masks = di("c_masks", [128, 32])

    y_all = do("y_all", [NTOK, D])
    o_pssm = do("o_pssm", [N_SSD, NB, NH * HP, NST])
    o_pconv = do("o_pconv", [N_SSD, NB, 128, 32, 3])
    o_pk = do("o_pk", [NSB1, NB, SEQ, D])
    o_pv = do("o_pv", [NSB1, NB, SEQ, D])
    o_sssm = do("o_sssm", [N_SSD, NS, NH * HP, NST])
    o_sconv = do("o_sconv", [N_SSD, NS, 128, 32, 3])
    o_sk = do("o_sk", [NSB1, NS, DSEQ, D])
    o_sv = do("o_sv", [NSB1, NS, DSEQ, D])

    xs = dx("xs", [NTOK, D])
    Ys = dx("Ys", [NTOK, DIN])
    QTp = dx("QTp", [NB, 128, 8, SEQ], BF16)
    KTp = dx("KTp", [NB, 128, 8, SEQ], BF16)
    Vp = dx("Vp", [NB, SEQ, D], BF16)
    QTs = dx("QTs", [NS, 128, 8, DSEQ], BF16)
    KTs = dx("KTs", [NS, 128, 8, KPAD], BF16)
    Vs = dx("Vs", [NS, DSEQ, D], BF16)

    top = ExitStack()
    with top:
        P = Prog(nc, top)
        dbufs = {}

        def db(name, idx=0):
            k = (name, idx)
            if k not in dbufs:
                dbufs[k] = P.buf("d_" + name)
            return dbufs[k]

        cst = db("const")

        banks = [Bank(top.enter_context(nc.psum_tensor("ps%d" % i, [128, 512], F32)), P.buf("ps%d" % i))
                 for i in range(8)]
        bstate = {"i": 0, "held": set()}

        def pb():
            while True:
                i = bstate["i"] % 8
                bstate["i"] += 1
                if i not in bstate["held"]:
                    return banks[i]

        def hold():
            b = pb()
            bstate["held"].add(banks.index(b))
            return b

        def release(b):
            bstate["held"].discard(banks.index(b))

        cur = {"es": None, "n": 0}

        def sb(shape, dt, name=None):
            cur["n"] += 1
            nm = (name or "t") + "_%d" % cur["n"]
            t = cur["es"].enter_context(nc.sbuf_tensor(nm, list(shape), dt))
            return T(t, P.buf(nm))

        def ring(n, shape, dt, name=None):
            return Ring([sb(shape, dt, name) for _ in range(n)])

        tiles = []
        for b in range(NB):
            for i in range(SEQ // 128):
                tiles.append(dict(row0=b * SEQ + i * 128, nt=128, grp="p", seq=b, pos=i * 128))
        for b in range(NS):
            tiles.append(dict(row0=NTP + b * DSEQ, nt=DSEQ, grp="s", seq=b, pos=0))

        def load_consts():
            c = {}
            c["identb"] = sb([128, 128], BF16, "identb")
            P.dma("pool", c["identb"][:], c_ident[:, :], c["identb"].b, cst, "ld")
            return c

        def front(C, src, tl, gB, dstT, col0, rings):
            nt = tl["nt"]
            r0 = tl["row0"]
            xin = rings["xin"].next()
            P.dma("sp", xin[:nt, :], src[r0:r0 + nt, :], xin.b, db(src.name, r0), "ld")
            ss = rings["ss"].next()
            junk = rings["junk"].next()
            P.op("act", "activation", dict(out=junk[:nt, :], in_=xin[:nt, :], func=AF.Square, accum_out=ss[:nt, 0:1]),
                 [xin.b], [junk.b, ss.b])
            P.op("act", "activation", dict(out=ss[:nt, 2:3], in_=ss[:nt, 0:1], func=AF.Ln, scale=1.0 / D, bias=EPS),
                 [ss.b], [ss.b])
            P.op("act", "activation", dict(out=ss[:nt, 3:4], in_=ss[:nt, 2:3], func=AF.Exp, scale=-0.5), [ss.b], [ss.b])
            xn = rings["xn"].next()
            P.op("dve", "scalar_tensor_tensor", dict(out=xn[:nt, :], in0=xin[:nt, :], scalar=ss[:nt, 3:4], in1=gB[:nt, :],
                                                     op0=ALU.mult, op1=ALU.mult), [xin.b, ss.b, gB.b], [xn.b])
            bk = pb()
            for kc in range(8):
                P.op("pe", "transpose", dict(out=bk.bf[:, kc * 128:kc * 128 + nt], in_=xn[:nt, kc * 128:(kc + 1) * 128],
                                             identity=C["identb"][:nt, :nt]), [xn.b, C["identb"].b], [bk.b], inc=(kc == 7))
            P.op("act", "activation", dict(out=dstT[:, :, col0:col0 + nt],
                                           in_=bk.bf.rearrange("p (k t) -> p k t", k=8)[:, :, :nt], func=AF.Copy),
                 [bk.b], [dstT.b])
            return xin

        def load_w(dst, dram_ap, rows_per_dma=128):
            K, N = dram_ap.shape
            for kc in range(K // 128):
                for n0 in range(0, N, 2048):
                    n1 = min(N, n0 + 2048)
                    P.dma("pool", dst[:, kc, n0:n1], dram_ap[kc * 128:(kc + 1) * 128, n0:n1], dst.b, cst, "ld", par=True)

        def load_bcast(dst, dram_1d):
            P.dma("sp", dst[:], dram_1d.partition_broadcast(128), dst.b, cst, "ld")

        def begin_stage():
            P.barrier()
            cur["es"] = ExitStack()
            cur["es"].__enter__()

        def end_stage():
            P.barrier()
            P.retire()
            cur["es"].__exit__(None, None, None)
            cur["es"] = None

        def run_streams(items, make_gen, width=2, stagger=0):
            active = []
            free_lanes = list(range(width))
            it = iter(items)
            pending = [True]
            first = [True]

            def start_more():
                while free_lanes and pending[0]:
                    try:
                        x = next(it)
                    except StopIteration:
                        pending[0] = False
                        break
                    lane = free_lanes.pop(0)
                    g = make_gen(x, lane)
                    active.append((lane, g))
                    if first[0]:
                        first[0] = False
                        for _ in range(stagger):
                            try:
                                next(g)
                            except StopIteration:
                                active.remove((lane, g))
                                free_lanes.append(lane)
                                break
            while True:
                start_more()
                if not active:
                    break
                for (lane, g) in list(active):
                    try:
                        next(g)
                    except StopIteration:
                        active.remove((lane, g))
                        free_lanes.append(lane)

        def stage_copy(src, dst):
            begin_stage()
            rg = ring(2, [128, D], F32, "cp")
            for tl in tiles:
                t = rg.next()
                nt, r0 = tl["nt"], tl["row0"]
                P.dma("sp", t[:nt, :], src[r0:r0 + nt, :], t.b, db(src.name, r0), "ld")
                P.dma("pool", dst[r0:r0 + nt, :], t[:nt, :], db(dst.name, r0), t.b, "st")
            end_stage()

        def stage_mlp(layer, src, dst):
            begin_stage()
            C = load_consts()
            Wup = sb([128, 8, DFF], BF16, "Wup")
            Wdn = sb([128, 32, D], BF16, "Wdn")
            load_w(Wup, mlp_up[layer])
            load_w(Wdn, mlp_down[layer])
            gB = sb([128, D], F32, "gB")
            load_bcast(gB, norm_mlp[layer])
            rings = dict(xin=ring(4, [128, D], F32, "xin"), ss=ring(4, [128, 4], F32, "ss"),
                         junk=ring(1, [128, D], BF16, "junk"), xn=ring(2, [128, D], BF16, "xn"))
            xnTs = ring(2, [128, 8, 256], BF16, "xnT")
            hT = sb([128, 32, 256], BF16, "hT")
            rr = ring(2, [128, 256], F32, "relu")
            yst = ring(2, [128, D], F32, "yst")
            macros = []
            curm = []
            tot = 0
            for tl in tiles:
                if tot + tl["nt"] > 256 or (curm and curm[-1]["grp"] != tl["grp"]):
                    macros.append(curm)
                    curm = []
                    tot = 0
                curm.append(tl)
                tot += tl["nt"]
            if curm:
                macros.append(curm)

            def do_front(m):
                xnT = xnTs.next()
                col = 0
                ent = []
                for tl in m:
                    xin = front(C, src, tl, gB, xnT, col, rings)
                    ent.append((xin, col, tl))
                    col += tl["nt"]
                return xnT, ent, col

            nxt = do_front(macros[0])
            for mi, m in enumerate(macros):
                xnT, ent, NT = nxt
                for f in range(32):
                    bk = pb()
                    for kc in range(8):
                        P.op("pe", "matmul", dict(out=bk.f32[:, :NT], lhsT=Wup[:, kc, f * 128:(f + 1) * 128],
                                                  rhs=xnT[:, kc, :NT], start=(kc == 0), stop=(kc == 7)),
                             [Wup.b, xnT.b], [bk.b], inc=(kc == 7))
                    r = rr.next()
                    P.op("act", "activation", dict(out=r[:, :NT], in_=bk.f32[:, :NT], func=AF.Relu), [bk.b], [r.b])
                    P.op("dve", "tensor_tensor", dict(out=hT[:, f, :NT], in0=r[:, :NT], in1=r[:, :NT], op=ALU.mult),
                         [r.b], [hT.b])
                if mi + 1 < len(macros):
                    nxt = do_front(macros[mi + 1])
                for (xin, col, tl) in ent:
                    nt, r0 = tl["nt"], tl["row0"]
                    y = yst.next()
                    for c in range(2):
                        bk = pb()
                        for f in range(32):
                            P.op("pe", "matmul", dict(out=bk.f32[:nt, :], lhsT=hT[:, f, col:col + nt],
                                                      rhs=Wdn[:, f, c * 512:(c + 1) * 512], start=(f == 0), stop=(f == 31)),
                                 [hT.b, Wdn.b], [bk.b], inc=(f == 31))
                        P.op("dve", "tensor_tensor", dict(out=y[:nt, c * 512:(c + 1) * 512], in0=bk.f32[:nt, :],
                                                          in1=xin[:nt, c * 512:(c + 1) * 512], op=ALU.add),
                             [bk.b, xin.b], [y.b])
                    P.dma("pool", dst[r0:r0 + nt, :], y[:nt, :], db(dst.name, r0), y.b, "st")
            end_stage()

        def run_sched(lane_queues):
            events = set()
            n = len(lane_queues)
            curg = [None] * n
            idx = [0] * n
            waiting = [None] * n
            while True:
                progressed = False
                alive = False
                for li, q in enumerate(lane_queues):
                    if curg[li] is None:
                        if idx[li] < len(q):
                            curg[li] = q[idx[li]]()
                            idx[li] += 1
                            waiting[li] = None
                        else:
                            continue
                    alive = True
                    if waiting[li] is not None:
                        if waiting[li] in events:
                            waiting[li] = None
                        else:
                            continue
                    try:
                        r = next(curg[li])
                        progressed = True
                        while isinstance(r, tuple) and r[0] == "set":
                            events.add(r[1])
                            r = next(curg[li])
                        if isinstance(r, tuple) and r[0] == "wait" and r[1] not in events:
                            waiting[li] = r[1]
                    except StopIteration:
                        curg[li] = None
                        progressed = True
                if not alive:
                    break
                assert progressed, "emission scheduler deadlock"

        def stage_ssd1(layer, j, src):
            begin_stage()
            C = load_consts()
            Wx = sb([128, 8, CONVD + NH], BF16, "Wx")
            for kc in range(8):
                for n0 in range(0, CONVD + NH, 2048):
                    n1 = min(CONVD + NH, n0 + 2048)
                    P.dma("pool", Wx[:, kc, n0:n1], ssd_in_w[j, kc * 128:(kc + 1) * 128, DIN + n0:DIN + n1], Wx.b, cst, "ld", par=True)
            gB = sb([128, D], F32, "gB")
            load_bcast(gB, norm_mix[layer])
            trib = sb([128, 128], BF16, "trib")
            sub = sb([128, 128], BF16, "sub")
            onesb = sb([128, 128], BF16, "onesb")
            identf = sb([128, 128], F32, "identf")
            P.dma("pool", trib[:], c_tri[:, :], trib.b, cst, "ld")
            P.dma("pool", sub[:], c_su[:, :], sub.b, cst, "ld")
            P.dma("sp", identf[:], c_ident[:, :], identf.b, cst, "ld")
            P.op("dve", "memset", dict(ap=onesb[:], constant=1.0), [], [onesb.b])
            cw = sb([128, 32, 4], F32, "cw")
            cbias = sb([128, 32], F32, "cbias")
            P.dma("sp", cw[:], ssd_conv_w[j], cw.b, cst, "ld")
            P.dma("sp", cbias[:], ssd_conv_b[j], cbias.b, cst, "ld")
            Abc = sb([128, NH], F32, "Abc")
            dtb = sb([128, NH], F32, "dtb")
            Dbc = sb([128, NH], F32, "Dbc")
            load_bcast(Abc, ssd_a_log[j])
            load_bcast(dtb, ssd_dt_bias[j])
            load_bcast(Dbc, ssd_d[j])
            P.op("act", "activation", dict(out=Abc[:], in_=Abc[:], func=AF.Exp), [Abc.b], [Abc.b])
            P.op("dve", "tensor_scalar", dict(out=Abc[:], in0=Abc[:], scalar1=-1.0, scalar2=None, op0=ALU.mult),
                 [Abc.b], [Abc.b])
            DI = sb([128, NH, 128], BF16, "DI")
            for h in range(NH):
                P.op("dve", "tensor_scalar", dict(out=DI[:, h, :], in0=C["identb"][:], scalar1=Dbc[:, h:h + 1], scalar2=None,
                                                  op0=ALU.mult), [C["identb"].b, Dbc.b], [DI.b])
            xnr = ring(2, [128, D], BF16, "xn")
            rings = dict(xin=ring(2, [128, D], F32, "xin"), ss=ring(4, [128, 4], F32, "ss"), junk=xnr, xn=xnr)
            MTM = 256
            hnT = sb([128, 8, MTM], BF16, "hnT")
            xTs = [sb([128, 16, MTM], BF16, "xT") for _ in range(2)]
            BTs = [sb([128, 8, MTM], BF16, "BT") for _ in range(2)]
            CTs = [sb([128, 8, MTM], BF16, "CT") for _ in range(2)]
            dtraws = [sb([128, 2, NH], F32, "dtraw") for _ in range(2)]
            Rr = ring(3, [128, MTM + 4], BF16, "R")
            dgr = ring(16, [128, 128], BF16, "dg")
            sgr = ring(3, [128, MTM], F32, "sg")
            halo16 = sb([128, 32, 3], BF16, "halo16")
            ncb = sb([128, 32], F32, "ncb")
            P.op("dve", "tensor_scalar", dict(out=ncb[:], in0=cbias[:], scalar1=-1.0, scalar2=None, op0=ALU.mult), [cbias.b], [ncb.b])
            hTq = [sb([128, 512], F32, "hT") for _ in range(4)]
            hTbq = [sb([128, 512], BF16, "hTb") for _ in range(4)]
            halo = sb([128, 32, 3], F32, "halo")
            tmpr = ring(2, [128, 512], F32, "ytmp")
            ystr = ring(2, [128, 1024], F32, "yst")
            lanes = []
            for ln in range(2):
                lanes.append(dict(sm=sb([128, 12, NH], F32, "sm"), adtb=sb([128, NH], BF16, "adtb"),
                                  xtok=sb([128, DIN], BF16, "xtok"), Btok=sb([128, D], BF16, "Btok"),
                                  Wbh=sb([128, 16 * 128], BF16, "Wbh"), Ebh=sb([128, 16 * 128], BF16, "Ebh"),
                                  cbm=sb([128, NG * 128], BF16, "cbm"), xddh=sb([128, 1024], BF16, "xddh")))

            seqs = [("p", b, SEQ) for b in range(NB)] + [("s", b, DSEQ) for b in range(NS)]
            for si, (grp, b, L) in enumerate(seqs):
                base = (b * SEQ) if grp == "p" else (NTP + b * DSEQ)
                CL = min(128, L)
                MT = min(MTM, L)
                NCH = MT // CL
                NM = L // MT
                if grp == "p":
                    for q in range(4):
                        P.op("dve", "memset", dict(ap=hTq[q][:], constant=0.0), [], [hTq[q].b])
                    P.op("dve", "memset", dict(ap=halo[:], constant=0.0), [], [halo.b])
                    P.op("dve", "memset", dict(ap=halo16[:], constant=0.0), [], [halo16.b])
                else:
                    P.dma("sp", halo[:], state_conv[j, b], halo.b, cst, "ld")
                    P.op("dve", "tensor_copy", dict(out=halo16[:], in_=halo[:]), [halo.b], [halo16.b])
                    for hf in range(2):
                        stg = ystr.next()
                        P.dma("sp", stg[:].rearrange("p (t n) -> p t n", t=8),
                              state_ssm[j, b, hf * 1024:(hf + 1) * 1024, :].rearrange("(t p) n -> p t n", p=128), stg.b, cst, "ld")
                        for qq in range(2):
                            q = hf * 2 + qq
                            bk = pb()
                            for i in range(4):
                                t = qq * 4 + i
                                P.op("pe", "transpose", dict(out=bk.f32[:, i * 128:(i + 1) * 128], in_=stg[:, t * 128:(t + 1) * 128],
                                                             identity=identf[:]), [stg.b, identf.b], [bk.b], inc=(i == 3))
                            P.op("act", "activation", dict(out=hTq[q][:], in_=bk.f32[:, :], func=AF.Copy), [bk.b], [hTq[q].b])
                for q in range(4):
                    P.op("act", "activation", dict(out=hTbq[q][:], in_=hTq[q][:], func=AF.Copy), [hTq[q].b], [hTbq[q].b])

                def conv_stream(m, si=si, base=base, CL=CL, MT=MT, NCH=NCH, NM=NM):
                    st = m % 2
                    xT, BT, CT, dtraw = xTs[st], BTs[st], CTs[st], dtraws[st]
                    m0 = m * MT
                    if m >= 2:
                        for ci in range(NCH):
                            yield ("wait", ("cdone", si, (m - 2) * NCH + ci))
                    if m >= 1:
                        yield ("wait", ("convdone", si, m - 1))
                    for i in range(NCH):
                        tl = dict(row0=base + m0 + i * CL, nt=CL)
                        front(C, src, tl, gB, hnT, i * CL, rings)
                        yield
                    for ci in range(NCH):
                        bk = pb()
                        for kc in range(8):
                            P.op("pe", "matmul", dict(out=bk.f32[:CL, :NH], lhsT=hnT[:, kc, ci * CL:(ci + 1) * CL],
                                                      rhs=Wx[:, kc, CONVD:CONVD + NH], start=(kc == 0), stop=(kc == 7)),
                                 [hnT.b, Wx.b], [bk.b], inc=(kc == 7))
                        P.op("act", "activation", dict(out=dtraw[:CL, ci, :], in_=bk.f32[:CL, :NH], func=AF.Copy), [bk.b], [dtraw.b])
                    yield
                    pipe = {}

                    def s0(ct):
                        dgs = []
                        for tap in range(4):
                            dg = dgr.next()
                            P.op("dve", "tensor_scalar", dict(out=dg[:], in0=C["identb"][:], scalar1=cw[:, ct, tap:tap + 1], scalar2=None,
                                                              op0=ALU.mult), [C["identb"].b, cw.b], [dg.b])
                            dgs.append(dg)
                        pipe[ct] = dict(dgs=dgs)

                    def s1(ct):
                        bk = pb()
                        for kc in range(8):
                            P.op("pe", "matmul", dict(out=bk.f32[:, :MT], lhsT=Wx[:, kc, ct * 128:(ct + 1) * 128],
                                                      rhs=hnT[:, kc, :MT], start=(kc == 0), stop=(kc == 7)),
                                 [Wx.b, hnT.b], [bk.b], inc=(kc == 7))
                        R = Rr.next()
                        P.op("dve", "tensor_copy", dict(out=R[:, 0:3], in_=halo16[:, ct, :]), [halo16.b], [R.b])
                        P.op("act", "activation", dict(out=R[:, 3:3 + MT], in_=bk.f32[:, :MT], func=AF.Copy), [bk.b], [R.b])
                        P.op("dve", "tensor_copy", dict(out=halo16[:, ct, :], in_=R[:, MT:MT + 3]), [R.b], [halo16.b])
                        if m == NM - 1:
                            P.op("dve", "tensor_copy", dict(out=halo[:, ct, :], in_=bk.f32[:, MT - 3:MT]), [bk.b], [halo.b])
                        pipe[ct]["R"] = R

                    def s2(ct):
                        R, dgs = pipe[ct]["R"], pipe[ct]["dgs"]
                        bk2 = hold()
                        for tap in range(4):
                            P.op("pe", "matmul", dict(out=bk2.f32[:, :MT], lhsT=dgs[tap][:], rhs=R[:, tap:tap + MT],
                                                      start=(tap == 0), stop=(tap == 3)), [dgs[tap].b, R.b], [bk2.b], inc=(tap == 3))
                        sg = sgr.next()
                        P.op("act", "activation", dict(out=sg[:, :MT], in_=bk2.f32[:, :MT], func=AF.Exp, scale=-1.0,
                                                       bias=ncb[:, ct:ct + 1]), [bk2.b, ncb.b], [sg.b])
                        P.op("act", "activation", dict(out=sg[:, :MT], in_=sg[:, :MT], func=AF.Ln, bias=1.0), [sg.b], [sg.b])
                        P.op("act", "activation", dict(out=sg[:, :MT], in_=sg[:, :MT], func=AF.Exp, scale=-1.0), [sg.b], [sg.b])
                        pipe[ct]["bk2"] = bk2
                        pipe[ct]["sg"] = sg

                    def s3(ct):
                        bk2, sg = pipe[ct]["bk2"], pipe[ct]["sg"]
                        if ct < 16:
                            dT, dap = xT, xT[:, ct, :MT]
                        elif ct < 24:
                            dT, dap = BT, BT[:, ct - 16, :MT]
                        else:
                            dT, dap = CT, CT[:, ct - 24, :MT]
                        P.op("dve", "scalar_tensor_tensor", dict(out=dap, in0=bk2.f32[:, :MT], scalar=cbias[:, ct:ct + 1], in1=sg[:, :MT],
                                                                 op0=ALU.add, op1=ALU.mult), [bk2.b, cbias.b, sg.b], [dT.b])
                        release(bk2)
                        del pipe[ct]

                    for step in range(32 + 3):
                        if step < 32:
                            s0(step)
                        if 0 <= step - 1 < 32:
                            s1(step - 1)
                        if 0 <= step - 2 < 32:
                            s2(step - 2)
                        if 0 <= step - 3 < 32:
                            s3(step - 3)
                        yield
                    yield ("set", ("convdone", si, m))

                def chunk_stream(c, lane, si=si, base=base, CL=CL, MT=MT, NCH=NCH):
                    Ln = lanes[lane]
                    m, ci = divmod(c, NCH)
                    st = m % 2
                    xT, BT, CT, dtraw = xTs[st], BTs[st], CTs[st], dtraws[st]
                    c0 = ci * CL
                    row0 = base + m * MT + c0
                    sm, adtb, xtok, Btok, Wbh, Ebh, cbm, xddh = (Ln["sm"], Ln["adtb"], Ln["xtok"], Ln["Btok"], Ln["Wbh"],
                                                                 Ln["Ebh"], Ln["cbm"], Ln["xddh"])
                    yield ("wait", ("convdone", si, m))
                    P.op("dve", "tensor_tensor", dict(out=sm[:CL, 0, :], in0=dtraw[:CL, ci, :], in1=dtb[:CL, :], op=ALU.add),
                         [dtraw.b, dtb.b], [sm.b])
                    P.op("dve", "scalar_tensor_tensor", dict(out=sm[:CL, 1, :], in0=sm[:CL, 0, :], scalar=-1.0,
                                                             in1=sm[:CL, 0, :], op0=ALU.mult, op1=ALU.max), [sm.b], [sm.b])
                    P.op("act", "activation", dict(out=sm[:CL, 2, :], in_=sm[:CL, 1, :], func=AF.Exp, scale=-1.0), [sm.b], [sm.b])
                    P.op("act", "activation", dict(out=sm[:CL, 3, :], in_=sm[:CL, 2, :], func=AF.Ln, bias=1.0), [sm.b], [sm.b])
                    P.op("dve", "scalar_tensor_tensor", dict(out=sm[:CL, 4, :], in0=sm[:CL, 0, :], scalar=0.0,
                                                             in1=sm[:CL, 3, :], op0=ALU.max, op1=ALU.add), [sm.b], [sm.b])
                    dt = sm[:CL, 4, :]
                    P.op("dve", "tensor_tensor", dict(out=adtb[:CL, :], in0=dt, in1=Abc[:CL, :], op=ALU.mult),
                         [sm.b, Abc.b], [adtb.b])
                    P.op("act", "activation", dict(out=sm[:CL, 5, :], in_=dt, func=AF.Ln), [sm.b], [sm.b])
                    yield
                    bk2 = pb()
                    P.op("pe", "matmul", dict(out=bk2.f32[:CL, 0:NH], lhsT=trib[:CL, :CL], rhs=adtb[:CL, :],
                                              start=True, stop=True), [trib.b, adtb.b], [bk2.b], inc=False)
                    P.op("pe", "matmul", dict(out=bk2.f32[:, NH:2 * NH], lhsT=onesb[:CL, :], rhs=adtb[:CL, :],
                                              start=True, stop=True), [onesb.b, adtb.b], [bk2.b])
                    P.op("act", "activation", dict(out=sm[:CL, 6, :], in_=bk2.f32[:CL, 0:NH], func=AF.Exp), [bk2.b], [sm.b])
                    P.op("act", "activation", dict(out=sm[:, 7, :], in_=bk2.f32[:, NH:2 * NH], func=AF.Exp), [bk2.b], [sm.b])
                    P.op("act", "activation", dict(out=sm[:CL, 8, :], in_=bk2.f32[:CL, 0:NH], func=AF.Copy), [bk2.b], [sm.b])
                    P.op("dve", "tensor_tensor", dict(out=sm[:CL, 9, :], in0=bk2.f32[:CL, NH:2 * NH], in1=sm[:CL, 8, :],
                                                      op=ALU.subtract), [bk2.b, sm.b], [sm.b])
                    P.op("act", "activation", dict(out=sm[:CL, 10, :], in_=sm[:CL, 9, :], func=AF.Exp), [sm.b], [sm.b])
                    P.op("dve", "tensor_tensor", dict(out=sm[:CL, 11, :], in0=sm[:CL, 10, :], in1=dt, op=ALU.mult), [sm.b], [sm.b])
                    yield
                    for half in range(2):
                        bk = pb()
                        for i in range(8):
                            P.op("pe", "transpose", dict(out=bk.bf[:CL, i * 128:(i + 1) * 128],
                                                         in_=xT[:, half * 8 + i, c0:c0 + CL], identity=C["identb"][:]),
                                 [xT.b, C["identb"].b], [bk.b], inc=(i == 7))
                        P.op("act", "activation", dict(out=xtok[:CL, half * 1024:(half + 1) * 1024], in_=bk.bf[:CL, :],
                                                       func=AF.Copy), [bk.b], [xtok.b])
                        yield
                    bk = pb()
                    for g in range(8):
                        P.op("pe", "transpose", dict(out=bk.bf[:CL, g * 128:(g + 1) * 128], in_=BT[:, g, c0:c0 + CL],
                                                     identity=C["identb"][:]), [BT.b, C["identb"].b], [bk.b], inc=(g == 7))
                    P.op("act", "activation", dict(out=Btok[:CL, :], in_=bk.bf[:CL, :], func=AF.Copy), [bk.b], [Btok.b])
                    yield
                    cv = cbm[:CL, 0:NG * CL].rearrange("p (g t) -> p g t", g=NG)
                    for half in range(2):
                        bk = pb()
                        for gi in range(4):
                            g = half * 4 + gi
                            P.op("pe", "matmul", dict(out=bk.f32[:CL, gi * CL:(gi + 1) * CL], lhsT=BT[:, g, c0:c0 + CL],
                                                      rhs=CT[:, g, c0:c0 + CL], start=True, stop=True),
                                 [BT.b, CT.b], [bk.b], inc=(gi == 3))
                        P.op("dve", "tensor_tensor", dict(out=cv[:, half * 4:(half + 1) * 4, :],
                                                          in0=bk.f32[:CL, 0:4 * CL].rearrange("p (g t) -> p g t", g=4),
                                                          in1=trib[:CL, :CL].unsqueeze(1).to_broadcast([CL, 4, CL]),
                                                          op=ALU.mult), [bk.b, trib.b], [cbm.b])
                        yield
                    Wv = Wbh[:CL, 0:16 * CL].rearrange("p (h t) -> p h t", h=16)
                    Ev = Ebh[:CL, 0:16 * CL].rearrange("p (h t) -> p h t", h=16)
                    for hf in range(2):
                        P.op("dve", "tensor_tensor", dict(out=Wv, in0=trib[:CL, :CL].unsqueeze(1).to_broadcast([CL, 16, CL]),
                                                           in1=adtb[:CL, hf * 16:(hf + 1) * 16].unsqueeze(2).to_broadcast([CL, 16, CL]),
                                                           op=ALU.mult), [trib.b, adtb.b], [Wbh.b])
                        for gi in range(4):
                            bk = pb()
                            P.op("pe", "matmul", dict(out=bk.f32[:CL, 0:4 * CL], lhsT=sub[:CL, :CL],
                                                      rhs=Wbh[:CL, gi * 4 * CL:(gi + 1) * 4 * CL], start=True, stop=True),
                                 [sub.b, Wbh.b], [bk.b])
                            for hi in range(4):
                                hl = gi * 4 + hi
                                h = hf * 16 + hl
                                P.op("act", "activation", dict(out=Ev[:, hl, :], in_=bk.f32[:CL, hi * CL:(hi + 1) * CL],
                                                               func=AF.Exp, bias=sm[:CL, 5, h:h + 1]), [bk.b, sm.b], [Ebh.b])
                            yield
                        E4 = Ebh[:CL, 0:16 * CL].rearrange("p (g r t) -> p g r t", g=4, r=4)
                        P.op("dve", "tensor_tensor", dict(out=E4, in0=E4,
                                                          in1=cv[:, hf * 4:(hf + 1) * 4, :].unsqueeze(2).to_broadcast([CL, 4, 4, CL]),
                                                          op=ALU.mult), [Ebh.b, cbm.b], [Ebh.b])
                        yield
                        if c > 0:
                            yield ("wait", ("state", si, c - 1, hf))
                        yst = ystr.next()
                        for qq in range(2):
                            q = hf * 2 + qq
                            bkY = pb()
                            for hh in range(8):
                                hl = qq * 8 + hh
                                h = hf * 16 + hl
                                P.op("pe", "matmul", dict(out=bkY.f32[:CL, hh * 64:(hh + 1) * 64], lhsT=Ev[:, hl, :],
                                                          rhs=xtok[:CL, h * 64:(h + 1) * 64], start=True, stop=False),
                                     [Ebh.b, xtok.b], [bkY.b], inc=False)
                                P.op("pe", "matmul", dict(out=bkY.f32[:CL, hh * 64:(hh + 1) * 64], lhsT=DI[:CL, h, :CL],
                                                          rhs=xtok[:CL, h * 64:(h + 1) * 64], start=False, stop=True),
                                     [DI.b, xtok.b], [bkY.b], inc=(hh == 7))
                            bkO = pb()
                            for gg in range(2):
                                g = q * 2 + gg
                                P.op("pe", "matmul", dict(out=bkO.f32[:CL, gg * 256:(gg + 1) * 256], lhsT=CT[:, g, c0:c0 + CL],
                                                          rhs=hTbq[q][:, gg * 256:(gg + 1) * 256], start=True, stop=True),
                                     [CT.b, hTbq[q].b], [bkO.b], inc=(gg == 1))
                            tmp = tmpr.next()
                            P.op("dve", "tensor_tensor", dict(out=tmp[:CL, :].rearrange("p (h d) -> p h d", h=8),
                                                              in0=bkO.f32[:CL, :].rearrange("p (h d) -> p h d", h=8),
                                                              in1=sm[:CL, 6, q * 8:(q + 1) * 8].unsqueeze(2).to_broadcast([CL, 8, 64]),
                                                              op=ALU.mult), [bkO.b, sm.b], [tmp.b])
                            P.op("dve", "tensor_tensor", dict(out=yst[:CL, qq * 512:(qq + 1) * 512], in0=bkY.f32[:CL, :],
                                                              in1=tmp[:CL, :], op=ALU.add), [bkY.b, tmp.b], [yst.b])
                            yield
                        P.dma("pool", Ys[row0:row0 + CL, hf * 1024:(hf + 1) * 1024], yst[:CL, :], db("Ys", row0), yst.b, "st")
                        P.op("dve", "tensor_tensor", dict(out=xddh[:CL, :].rearrange("p (h d) -> p h d", h=16),
                                                          in0=xtok[:CL, hf * 1024:(hf + 1) * 1024].rearrange("p (h d) -> p h d", h=16),
                                                          in1=sm[:CL, 11, hf * 16:(hf + 1) * 16].unsqueeze(2).to_broadcast([CL, 16, 64]),
                                                          op=ALU.mult), [xtok.b, sm.b], [xddh.b])
                        for qq in range(2):
                            q = hf * 2 + qq
                            bkH = pb()
                            for gg in range(2):
                                g = q * 2 + gg
                                P.op("pe", "matmul", dict(out=bkH.f32[:, gg * 256:(gg + 1) * 256], lhsT=Btok[:CL, g * 128:(g + 1) * 128],
                                                          rhs=xddh[:CL, (qq * 2 + gg) * 256:(qq * 2 + gg + 1) * 256], start=True, stop=True),
                                     [Btok.b, xddh.b], [bkH.b], inc=(gg == 1))
                            hv = hTq[q]
                            P.op("dve", "tensor_tensor", dict(out=hv[:].rearrange("p (h d) -> p h d", h=8),
                                                              in0=hv[:].rearrange("p (h d) -> p h d", h=8),
                                                              in1=sm[:, 7, q * 8:(q + 1) * 8].unsqueeze(2).to_broadcast([128, 8, 64]),
                                                              op=ALU.mult), [hv.b, sm.b], [hv.b])
                            P.op("dve", "tensor_tensor", dict(out=hv[:], in0=bkH.f32[:, :], in1=hv[:], op=ALU.add), [bkH.b, hv.b], [hv.b])
                            P.op("act", "activation", dict(out=hTbq[q][:], in_=hv[:], func=AF.Copy), [hv.b], [hTbq[q].b])
                            yield
                        yield ("set", ("state", si, c, hf))
                    yield ("set", ("cdone", si, c))

                nchunks = NM * NCH
                convq = [(lambda m=m: conv_stream(m)) for m in range(NM)]
                laneA = [(lambda c=c: chunk_stream(c, 0)) for c in range(0, nchunks, 2)]
                laneB = [(lambda c=c: chunk_stream(c, 1)) for c in range(1, nchunks, 2)]
                run_sched([convq, laneA, laneB])

                o_ssm = (o_pssm if grp == "p" else o_sssm)[j, b]
                o_conv = (o_pconv if grp == "p" else o_sconv)[j, b]
                for hf in range(2):
                    stg = ystr.next()
                    for qq in range(2):
                        q = hf * 2 + qq
                        bk = pb()
                        for i in range(4):
                            P.op("pe", "transpose", dict(out=bk.f32[:, i * 128:(i + 1) * 128], in_=hTq[q][:, i * 128:(i + 1) * 128],
                                                         identity=identf[:]), [hTq[q].b, identf.b], [bk.b], inc=(i == 3))
                        P.op("act", "activation", dict(out=stg[:, qq * 512:(qq + 1) * 512], in_=bk.f32[:, :], func=AF.Copy),
                             [bk.b], [stg.b])
                    P.dma("pool", o_ssm[hf * 1024:(hf + 1) * 1024, :].rearrange("(t p) n -> p t n", p=128),
                          stg[:].rearrange("p (t n) -> p t n", t=8), db("o_ssm_%s" % grp, b), stg.b, "st")
                P.dma("pool", o_conv, halo[:], db("o_conv_%s" % grp, b), halo.b, "st")
            end_stage()

        def stage_ssd2(layer, j, src, dst):
            begin_stage()
            C = load_consts()
            Wz = sb([128, 8, DIN], BF16, "Wz")
            for kc in range(8):
                P.dma("pool", Wz[:, kc, :], ssd_in_w[j, kc * 128:(kc + 1) * 128, 0:DIN], Wz.b, cst, "ld", par=True)
            Wo = sb([128, 16, D], BF16, "Wo")
            load_w(Wo, ssd_out_w[j])
            gB = sb([128, D], F32, "gB")
            load_bcast(gB, norm_mix[layer])
            nwB = sb([128, DIN], F32, "nwB")
            load_bcast(nwB, ssd_norm_w[j])
            xinr = ring(4, [128, D], F32, "xin")
            ssr = ring(6, [128, 4], F32, "ss")
            xnr = ring(4, [128, D], BF16, "xn")
            hnTs = ring(3, [128, 8, 128], BF16, "hnT")
            yinr = ring(3, [128, DIN], F32, "yin")
            szr = ring(2, [128, DIN], F32, "sz")
            gsr = ring(3, [128, 4, NG], F32, "gs")
            ynbr = ring(2, [128, DIN], BF16, "ynb")
            ynTr = ring(2, [128, 16, 128], BF16, "ynT")
            ystr = ring(2, [128, D], F32, "yst")
            junk2 = sb([128, 256], F32, "junk2")
            TS = {}

            def stA(i):
                tl = tiles[i]
                nt, r0 = tl["nt"], tl["row0"]
                xin = xinr.next()
                yin = yinr.next()
                P.dma("sp", xin[:nt, :], src[r0:r0 + nt, :], xin.b, db(src.name, r0), "ld")
                P.dma("sp", yin[:nt, :], Ys[r0:r0 + nt, :], yin.b, db("Ys", r0), "ld")
                ss = ssr.next()
                junk = xnr.next()
                P.op("act", "activation", dict(out=junk[:nt, :], in_=xin[:nt, :], func=AF.Square, accum_out=ss[:nt, 0:1]),
                     [xin.b], [junk.b, ss.b])
                P.op("act", "activation", dict(out=ss[:nt, 2:3], in_=ss[:nt, 0:1], func=AF.Ln, scale=1.0 / D, bias=EPS), [ss.b], [ss.b])
                P.op("act", "activation", dict(out=ss[:nt, 3:4], in_=ss[:nt, 2:3], func=AF.Exp, scale=-0.5), [ss.b], [ss.b])
                xn = xnr.next()
                P.op("dve", "scalar_tensor_tensor", dict(out=xn[:nt, :], in0=xin[:nt, :], scalar=ss[:nt, 3:4], in1=gB[:nt, :],
                                                         op0=ALU.mult, op1=ALU.mult), [xin.b, ss.b, gB.b], [xn.b])
                TS[i] = dict(xin=xin, yin=yin, xn=xn)

            def stB(i):
                nt = tiles[i]["nt"]
                xn = TS[i]["xn"]
                hnT = hnTs.next()
                bk = pb()
                for kc in range(8):
                    P.op("pe", "transpose", dict(out=bk.bf[:, kc * 128:kc * 128 + nt], in_=xn[:nt, kc * 128:(kc + 1) * 128],
                                                 identity=C["identb"][:nt, :nt]), [xn.b, C["identb"].b], [bk.b], inc=(kc == 7))
                P.op("act", "activation", dict(out=hnT[:, :, :nt], in_=bk.bf.rearrange("p (k t) -> p k t", k=8)[:, :, :nt], func=AF.Copy),
                     [bk.b], [hnT.b])
                TS[i]["hnT"] = hnT

            def stC(i, cs):
                nt = tiles[i]["nt"]
                d = TS[i]
                hnT, yin = d["hnT"], d["yin"]
                if "sz" not in d:
                    d["sz"] = szr.next()
                sz = d["sz"]
                for c in cs:
                    bk = pb()
                    for kc in range(8):
                        P.op("pe", "matmul", dict(out=bk.f32[:nt, :], lhsT=hnT[:, kc, :nt], rhs=Wz[:, kc, c * 512:(c + 1) * 512],
                                                  start=(kc == 0), stop=(kc == 7)), [hnT.b, Wz.b], [bk.b], inc=(kc == 7))
                    sl = sz[:nt, c * 512:(c + 1) * 512]
                    P.op("act", "activation", dict(out=sl, in_=bk.f32[:nt, :], func=AF.Exp, scale=-1.0), [bk.b], [sz.b])
                    P.op("act", "activation", dict(out=sl, in_=sl, func=AF.Ln, bias=1.0), [sz.b], [sz.b])
                    P.op("act", "activation", dict(out=sl, in_=sl, func=AF.Exp, scale=-1.0), [sz.b], [sz.b])
                    P.op("dve", "tensor_tensor", dict(out=sl, in0=sl, in1=yin[:nt, c * 512:(c + 1) * 512], op=ALU.mult),
                         [sz.b, yin.b], [sz.b])
                    P.op("dve", "tensor_tensor", dict(out=sl, in0=bk.f32[:nt, :], in1=sl, op=ALU.mult), [bk.b, sz.b], [sz.b])

            def stD(i):
                nt = tiles[i]["nt"]
                d = TS[i]
                sz = d["sz"]
                gs = gsr.next()
                for g in range(NG):
                    P.op("act", "activation", dict(out=junk2[:nt, :], in_=sz[:nt, g * 256:(g + 1) * 256], func=AF.Square,
                                                   accum_out=gs[:nt, 0, g:g + 1]), [sz.b], [junk2.b, gs.b])
                P.op("act", "activation", dict(out=gs[:nt, 2, :], in_=gs[:nt, 0, :], func=AF.Ln, scale=1.0 / 256, bias=EPS),
                     [gs.b], [gs.b])
                P.op("act", "activation", dict(out=gs[:nt, 3, :], in_=gs[:nt, 2, :], func=AF.Exp, scale=-0.5), [gs.b], [gs.b])
                P.op("dve", "tensor_tensor", dict(out=sz[:nt, :].rearrange("p (g d) -> p g d", g=NG),
                                                  in0=sz[:nt, :].rearrange("p (g d) -> p g d", g=NG),
                                                  in1=gs[:nt, 3, :].unsqueeze(2).to_broadcast([nt, NG, 256]), op=ALU.mult),
                     [sz.b, gs.b], [sz.b])
                ynb = ynbr.next()
                P.op("dve", "tensor_tensor", dict(out=ynb[:nt, :], in0=sz[:nt, :], in1=nwB[:nt, :], op=ALU.mult),
                     [sz.b, nwB.b], [ynb.b])
                d["ynb"] = ynb

            def stE(i):
                nt = tiles[i]["nt"]
                d = TS[i]
                ynb = d["ynb"]
                ynT = ynTr.next()
                for half in range(2):
                    bk = pb()
                    for k8 in range(8):
                        ct = half * 8 + k8
                        P.op("pe", "transpose", dict(out=bk.bf[:, k8 * 128:k8 * 128 + nt], in_=ynb[:nt, ct * 128:(ct + 1) * 128],
                                                     identity=C["identb"][:nt, :nt]), [ynb.b, C["identb"].b], [bk.b], inc=(k8 == 7))
                    P.op("act", "activation", dict(out=ynT[:, half * 8:(half + 1) * 8, :nt],
                                                   in_=bk.bf.rearrange("p (k t) -> p k t", k=8)[:, :, :nt], func=AF.Copy),
                         [bk.b], [ynT.b])
                d["ynT"] = ynT

            def stF(i):
                tl = tiles[i]
                nt, r0 = tl["nt"], tl["row0"]
                d = TS.pop(i)
                ynT, xin = d["ynT"], d["xin"]
                y = ystr.next()
                for c in range(2):
                    bk = pb()
                    for ct in range(16):
                        P.op("pe", "matmul", dict(out=bk.f32[:nt, :], lhsT=ynT[:, ct, :nt], rhs=Wo[:, ct, c * 512:(c + 1) * 512],
                                                  start=(ct == 0), stop=(ct == 15)), [ynT.b, Wo.b], [bk.b], inc=(ct == 15))
                    P.op("dve", "tensor_tensor", dict(out=y[:nt, c * 512:(c + 1) * 512], in0=bk.f32[:nt, :],
                                                      in1=xin[:nt, c * 512:(c + 1) * 512], op=ALU.add), [bk.b, xin.b], [y.b])
                P.dma("pool", dst[r0:r0 + nt, :], y[:nt, :], db(dst.name, r0), y.b, "st")

            NTL = len(tiles)
            stA(0)
            if NTL > 1:
                stA(1)
            stB(0)
            for i in range(NTL):
                if i + 2 < NTL:
                    stA(i + 2)
                stC(i, (0, 1))
                if i + 1 < NTL:
                    stB(i + 1)
                if i >= 1:
                    stE(i - 1)
                stC(i, (2, 3))
                if i >= 1:
                    stF(i - 1)
                stD(i)
            stE(NTL - 1)
            stF(NTL - 1)
            end_stage()

        def stage_sb1(layer, j, src):
            begin_stage()
            C = load_consts()
            Wqkv = sb([128, 8, 3 * D], BF16, "Wqkv")
            load_w(Wqkv, sb_qkv_w[j])
            gB = sb([128, D], F32, "gB")
            load_bcast(gB, norm_mix[layer])
            qg = sb([128, AD], F32, "qg")
            kg = sb([128, AD], F32, "kg")
            load_bcast(qg, sb_q_gain[j])
            load_bcast(kg, sb_k_gain[j])
            P.op("dve", "tensor_scalar", dict(out=qg[:], in0=qg[:], scalar1=0.125, scalar2=None, op0=ALU.mult), [qg.b], [qg.b])
            xinr = ring(2, [128, D], F32, "xin")
            ssr = ring(6, [128, 4], F32, "ss")
            xnr = ring(4, [128, D], BF16, "xn")
            xnTs = ring(3, [128, 8, 128], BF16, "xnT")
            sqr = ring(2, [128, 512], F32, "sq")
            t1r = ring(2, [128, 512], F32, "t1")
            ssqr = ring(4, [128, 4, 8], F32, "ssq")
            koutr = ring(2, [128, D], F32, "kout")
            voutr = ring(2, [128, D], F32, "vout")
            knbr = ring(3, [128, D], BF16, "knb")
            qnbr = ring(3, [128, D], BF16, "qnb")
            vbr = ring(2, [128, D], BF16, "vb")
            qTr = ring(2, [128, 8, 128], BF16, "qT")
            kTr = ring(2, [128, 8, 128], BF16, "kT")
            ckr = ring(2, [128, D], BF16, "ck")

            def transpose_store(srcT, nt, stg, dram_ap, dbuf):
                bk = pb()
                for hp in range(8):
                    P.op("pe", "transpose", dict(out=bk.bf[:, hp * 128:hp * 128 + nt], in_=srcT[:nt, hp * 128:(hp + 1) * 128],
                                                 identity=C["identb"][:nt, :nt]), [srcT.b, C["identb"].b], [bk.b], inc=(hp == 7))
                P.op("act", "activation", dict(out=stg[:, :, :nt], in_=bk.bf.rearrange("p (k t) -> p k t", k=8)[:, :, :nt],
                                               func=AF.Copy), [bk.b], [stg.b])
                P.dma("pool", dram_ap, stg[:, :, :nt], dbuf, stg.b, "st")

            TS = {}

            def stA(i):
                tl = tiles[i]
                nt, r0 = tl["nt"], tl["row0"]
                xin = xinr.next()
                P.dma("sp", xin[:nt, :], src[r0:r0 + nt, :], xin.b, db(src.name, r0), "ld")
                ss = ssr.next()
                junk = xnr.next()
                P.op("act", "activation", dict(out=junk[:nt, :], in_=xin[:nt, :], func=AF.Square, accum_out=ss[:nt, 0:1]),
                     [xin.b], [junk.b, ss.b])
                P.op("act", "activation", dict(out=ss[:nt, 2:3], in_=ss[:nt, 0:1], func=AF.Ln, scale=1.0 / D, bias=EPS), [ss.b], [ss.b])
                P.op("act", "activation", dict(out=ss[:nt, 3:4], in_=ss[:nt, 2:3], func=AF.Exp, scale=-0.5), [ss.b], [ss.b])
                xn = xnr.next()
                P.op("dve", "scalar_tensor_tensor", dict(out=xn[:nt, :], in0=xin[:nt, :], scalar=ss[:nt, 3:4], in1=gB[:nt, :],
                                                         op0=ALU.mult, op1=ALU.mult), [xin.b, ss.b, gB.b], [xn.b])
                TS[i] = dict(xn=xn)

            def stB(i):
                tl = tiles[i]
                nt = tl["nt"]
                xn = TS[i]["xn"]
                xnT = xnTs.next()
                bk = pb()
                for kc in range(8):
                    P.op("pe", "transpose", dict(out=bk.bf[:, kc * 128:kc * 128 + nt], in_=xn[:nt, kc * 128:(kc + 1) * 128],
                                                 identity=C["identb"][:nt, :nt]), [xn.b, C["identb"].b], [bk.b], inc=(kc == 7))
                P.op("act", "activation", dict(out=xnT[:, :, :nt], in_=bk.bf.rearrange("p (k t) -> p k t", k=8)[:, :, :nt], func=AF.Copy),
                     [bk.b], [xnT.b])
                TS[i]["xnT"] = xnT

            def stC(i, cs):
                tl = tiles[i]
                nt, r0, grp, b, pos = tl["nt"], tl["row0"], tl["grp"], tl["seq"], tl["pos"]
                d = TS[i]
                xnT = d["xnT"]
                if "kout" not in d:
                    d.update(kout=koutr.next(), vout=voutr.next(), knb=knbr.next(), qnb=qnbr.next(), vb=vbr.next())
                kout, vout, knb, qnb, vb = d["kout"], d["vout"], d["knb"], d["qnb"], d["vb"]
                for c in cs:
                    bk = pb()
                    for kc in range(8):
                        P.op("pe", "matmul", dict(out=bk.f32[:nt, :], lhsT=xnT[:, kc, :nt], rhs=Wqkv[:, kc, c * 512:(c + 1) * 512],
                                                  start=(kc == 0), stop=(kc == 7)), [xnT.b, Wqkv.b], [bk.b], inc=(kc == 7))
                    if c < 4:
                        sq = sqr.next()
                        ssq = ssqr.next()
                        t1 = t1r.next()
                        P.op("act", "activation", dict(out=sq[:nt, :], in_=bk.f32[:nt, :], func=AF.Square), [bk.b], [sq.b])
                        P.op("dve", "tensor_reduce", dict(out=ssq[:nt, 0, :], in_=sq[:nt, :].rearrange("p (h d) -> p h d", h=8),
                                                          axis=AX.X, op=ALU.add), [sq.b], [ssq.b])
                        P.op("act", "activation", dict(out=ssq[:nt, 2, :], in_=ssq[:nt, 0, :], func=AF.Ln, scale=1.0 / AD, bias=EPS),
                             [ssq.b], [ssq.b])
                        P.op("act", "activation", dict(out=ssq[:nt, 3, :], in_=ssq[:nt, 2, :], func=AF.Exp, scale=-0.5),
                             [ssq.b], [ssq.b])
                        P.op("dve", "tensor_tensor", dict(out=t1[:nt, :].rearrange("p (h d) -> p h d", h=8),
                                                          in0=bk.f32[:nt, :].rearrange("p (h d) -> p h d", h=8),
                                                          in1=ssq[:nt, 3, :].unsqueeze(2).to_broadcast([nt, 8, AD]), op=ALU.mult),
                             [bk.b, ssq.b], [t1.b])
                        if c < 2:
                            P.op("dve", "tensor_tensor", dict(out=qnb[:nt, c * 512:(c + 1) * 512].rearrange("p (h d) -> p h d", h=8),
                                                              in0=t1[:nt, :].rearrange("p (h d) -> p h d", h=8),
                                                              in1=qg[:nt, :].unsqueeze(1).to_broadcast([nt, 8, AD]), op=ALU.mult),
                                 [t1.b, qg.b], [qnb.b])
                        else:
                            cc = c - 2
                            P.op("dve", "tensor_tensor", dict(out=kout[:nt, cc * 512:(cc + 1) * 512].rearrange("p (h d) -> p h d", h=8),
                                                              in0=t1[:nt, :].rearrange("p (h d) -> p h d", h=8),
                                                              in1=kg[:nt, :].unsqueeze(1).to_broadcast([nt, 8, AD]), op=ALU.mult),
                                 [t1.b, kg.b], [kout.b])
                            P.op("act", "activation", dict(out=knb[:nt, cc * 512:(cc + 1) * 512], in_=kout[:nt, cc * 512:(cc + 1) * 512],
                                                           func=AF.Copy), [kout.b], [knb.b])
                    else:
                        cc = c - 4
                        P.op("act", "activation", dict(out=vout[:nt, cc * 512:(cc + 1) * 512], in_=bk.f32[:nt, :], func=AF.Copy),
                             [bk.b], [vout.b])
                        P.op("dve", "tensor_copy", dict(out=vb[:nt, cc * 512:(cc + 1) * 512], in_=bk.f32[:nt, :]), [bk.b], [vb.b])
                if 5 in cs:
                    if grp == "p":
                        P.dma("pool", o_pk[j, b, pos:pos + nt, :], kout[:nt, :], db("o_pk", r0), kout.b, "st")
                        P.dma("pool", o_pv[j, b, pos:pos + nt, :], vout[:nt, :], db("o_pv", r0), vout.b, "st")
                        P.dma("pool", Vp[b, pos:pos + nt, :], vb[:nt, :], db("Vp", b), vb.b, "st")
                    else:
                        P.dma("pool", o_sk[j, b, 0:nt, :], kout[:nt, :], db("o_sk", r0), kout.b, "st")
                        P.dma("pool", o_sv[j, b, 0:nt, :], vout[:nt, :], db("o_sv", r0), vout.b, "st")
                        P.dma("pool", Vs[b, 0:nt, :], vb[:nt, :], db("Vs", b), vb.b, "st")

            def stD(i):
                tl = tiles[i]
                nt, grp, b, pos = tl["nt"], tl["grp"], tl["seq"], tl["pos"]
                d = TS.pop(i)
                if grp == "p":
                    transpose_store(d["qnb"], nt, qTr.next(), QTp[b, :, :, pos:pos + nt], db("QTp", b))
                    transpose_store(d["knb"], nt, kTr.next(), KTp[b, :, :, pos:pos + nt], db("KTp", b))
                else:
                    transpose_store(d["qnb"], nt, qTr.next(), QTs[b, :, :, 0:nt], db("QTs", b))
                    transpose_store(d["knb"], nt, kTr.next(), KTs[b, :, :, PAST:PAST + nt], db("KTs", b))

            NTL = len(tiles)
            stA(0)
            if NTL > 1:
                stA(1)
            stB(0)
            for i in range(NTL):
                if i + 2 < NTL:
                    stA(i + 2)
                stC(i, (0, 1, 2))
                if i + 1 < NTL:
                    stB(i + 1)
                stC(i, (3, 4, 5))
                if i >= 1:
                    stD(i - 1)
            stD(NTL - 1)
            for b in range(NS):
                for kb in range(PAST // 128):
                    ck = ckr.next()
                    for hf in range(2):
                        P.dma("pool", ck[:, hf * 512:(hf + 1) * 512], cache_k[j, b, kb * 128:(kb + 1) * 128, hf * 512:(hf + 1) * 512],
                              ck.b, cst, "ld")
                    transpose_store(ck, 128, kTr.next(), KTs[b, :, :, kb * 128:(kb + 1) * 128], db("KTs", b))
            end_stage()

        def stage_sb2(layer, j, src, dst):
            begin_stage()
            Wo = sb([128, 8, D], BF16, "Wo")
            load_w(Wo, sb_out_w[j])
            suin = sb([128, 128], BF16, "suin")
            sutmp = sb([128, 128], BF16, "sutmp")
            onesb = sb([128, 128], BF16, "onesb")
            nonesb = sb([128, 128], BF16, "nonesb")
            maskb = sb([128, 4, 512], BF16, "maskb")
            masksb = sb([128, 32], BF16, "masksb")
            P.dma("pool", sutmp[:], c_tri[:, :], sutmp.b, cst, "ld")
            P.dma("pool", suin[:], c_su[:, :], suin.b, cst, "ld")
            P.dma("pool", sutmp[:], c_ident[:, :], sutmp.b, cst, "ld")
            P.op("dve", "tensor_tensor", dict(out=suin[:], in0=suin[:], in1=sutmp[:], op=ALU.add), [suin.b, sutmp.b], [suin.b])
            P.op("dve", "tensor_scalar", dict(out=suin[:], in0=suin[:], scalar1=-1.0, scalar2=None, op0=ALU.mult), [suin.b], [suin.b])
            P.op("dve", "memset", dict(ap=onesb[:], constant=1.0), [], [onesb.b])
            P.op("dve", "tensor_scalar", dict(out=nonesb[:], in0=sutmp[:], scalar1=-1.0, scalar2=None, op0=ALU.mult),
                 [sutmp.b], [nonesb.b])
            for i in range(4):
                P.dma("pool", maskb[:, i, :], c_mask[:, i, :], maskb.b, cst, "ld")
            P.dma("pool", masksb[:], c_masks[:, :], masksb.b, cst, "ld")
            KLEN = max(SEQ, KPAD)
            KT = sb([128, 8, KLEN], BF16, "KT")
            Vb = sb([128, KLEN // 128, D], BF16, "Vb")
            QWM = 512
            QTr = ring(2, [128, 8, QWM], BF16, "QT")
            OT = sb([128, 8, QWM], BF16, "OT")
            e1r = ring(4, [128, QWM], F32, "e1")
            lgr = ring(6, [128, QWM], BF16, "lg")
            wbr = ring(6, [128, QWM], BF16, "wb")
            Sbfr = [ring(2, [128, QWM], BF16, "Sbf0"), ring(2, [128, QWM], BF16, "Sbf1")]
            xinr = ring(2, [128, D], F32, "xin")
            ystr = ring(2, [128, D], F32, "yst")
            bkR = [hold(), hold()]
            bkOs = [hold(), hold()]

            seqs = [("p", b) for b in range(NB)] + [("s", b) for b in range(NS)]
            for (grp, b) in seqs:
                if grp == "p":
                    base, L, QW = b * SEQ, SEQ, 512
                    for hp in range(8):
                        P.dma("sp", KT[:, hp, :SEQ], KTp[b, :, hp, :], KT.b, db("KTp", b), "ld", par=True)
                    for t4 in range(0, SEQ // 128, 4):
                        P.dma("sp", Vb[:, t4:t4 + 4, :], Vp[b, t4 * 128:(t4 + 4) * 128, :].rearrange("(t p) c -> p t c", p=128),
                              Vb.b, db("Vp", b), "ld", par=True)
                else:
                    base, L, QW = NTP + b * DSEQ, DSEQ, DSEQ
                    nkc = PAST // 128
                    if KPAD > KTOT:
                        P.op("dve", "memset", dict(ap=KT[:, :, KTOT:KPAD], constant=0.0), [], [KT.b])
                    P.op("dve", "memset", dict(ap=Vb[:, nkc, :], constant=0.0), [], [Vb.b])
                    for hp in range(8):
                        P.dma("sp", KT[:, hp, :KTOT], KTs[b, :, hp, :KTOT], KT.b, db("KTs", b), "ld")
                    for t in range(nkc):
                        for hf in range(2):
                            P.dma("pool", Vb[:, t, hf * 512:(hf + 1) * 512],
                                  cache_v[j, b, t * 128:(t + 1) * 128, hf * 512:(hf + 1) * 512], Vb.b, cst, "ld")
                    P.dma("sp", Vb[:DSEQ, nkc, :], Vs[b, :, :], Vb.b, db("Vs", b), "ld")
                for q0 in range(0, L, QW):
                    QT = QTr.next()
                    if grp == "p":
                        P.dma("sp", QT[:, :, :QW], QTp[b, :, :, q0:q0 + QW], QT.b, db("QTp", b), "ld")
                        nkb = (q0 + QW) // 128
                        def mask_of(kb, q0=q0):
                            i = kb - q0 // 128
                            return maskb[:, i, :] if i >= 0 else None
                    else:
                        P.dma("sp", QT[:, :, :QW], QTs[b, :, :, :], QT.b, db("QTs", b), "ld")
                        nkb = KPAD // 128
                        def mask_of(kb, nkb=nkb):
                            return masksb[:, :] if kb == nkb - 1 else None
                    its = []
                    for hp in range(8):
                        for kb in range(nkb - 1, -1, -1):
                            for hh in range(2):
                                its.append((hp, kb, hh))
                    st = {}

                    def col0_of(kb, q0=q0, grp=grp):
                        if grp != "p" or not TRIM:
                            return 0
                        return max(0, (kb - q0 // 128) * 128)

                    def stageA(it):
                        hp, kb, hh = it
                        po = hh * 64
                        a = col0_of(kb)
                        bkT = pb()
                        P.op("pe", "matmul", dict(out=bkT.f32[:, a:QW], lhsT=KT[po:po + 64, hp, kb * 128:(kb + 1) * 128],
                                                  rhs=QT[po:po + 64, hp, a:QW], start=True, stop=False), [KT.b, QT.b], [bkT.b])
                        e1 = e1r.next()
                        P.op("act", "activation", dict(out=e1[:, a:QW], in_=bkT.f32[:, a:QW], func=AF.Exp), [bkT.b], [e1.b])
                        lg = lgr.next()
                        P.op("act", "activation", dict(out=lg[:, a:QW], in_=e1[:, a:QW], func=AF.Ln, bias=1.0), [e1.b], [lg.b])
                        m = mask_of(kb)
                        if m is not None:
                            a2 = min(QW, a + 128)
                            P.op("dve", "tensor_tensor", dict(out=lg[:, a:a2], in0=lg[:, a:a2], in1=m[:, a:a2], op=ALU.mult),
                                 [lg.b, maskb.b, masksb.b], [lg.b])
                        st[it] = dict(bkT=bkT, lg=lg)

                    def stageA2(it0, it1):
                        pre = []
                        for it in (it0, it1):
                            hp, kb, hh = it
                            po = hh * 64
                            a = col0_of(kb)
                            bkT = pb()
                            P.op("pe", "matmul", dict(out=bkT.f32[:, a:QW], lhsT=KT[po:po + 64, hp, kb * 128:(kb + 1) * 128],
                                                      rhs=QT[po:po + 64, hp, a:QW], start=True, stop=False), [KT.b, QT.b], [bkT.b])
                            pre.append((it, bkT, a))
                        e1s = []
                        for (it, bkT, a) in pre:
                            e1 = e1r.next()
                            P.op("act", "activation", dict(out=e1[:, a:QW], in_=bkT.f32[:, a:QW], func=AF.Exp), [bkT.b], [e1.b])
                            e1s.append(e1)
                        for (it, bkT, a), e1 in zip(pre, e1s):
                            hp, kb, hh = it
                            lg = lgr.next()
                            P.op("act", "activation", dict(out=lg[:, a:QW], in_=e1[:, a:QW], func=AF.Ln, bias=1.0), [e1.b], [lg.b])
                            m = mask_of(kb)
                            if m is not None:
                                a2 = min(QW, a + 128)
                                P.op("dve", "tensor_tensor", dict(out=lg[:, a:a2], in0=lg[:, a:a2], in1=m[:, a:a2], op=ALU.mult),
                                     [lg.b, maskb.b, masksb.b], [lg.b])
                            st[it] = dict(bkT=bkT, lg=lg)

                    def stageB(it):
                        hp, kb, hh = it
                        d = st[it]
                        bkT, lg = d["bkT"], d["lg"]
                        first = (kb == nkb - 1)
                        lastb = (kb == 0)
                        a = col0_of(kb)
                        P.op("pe", "matmul", dict(out=bkT.f32[:, a:QW], lhsT=suin[:], rhs=lg[:, a:QW], start=False, stop=first),
                             [suin.b, lg.b], [bkT.b], inc=first)
                        if not first:
                            Sbf = d["Sbf"] = st[(hp, kb + 1, hh)]["Snext"]
                            P.op("pe", "matmul", dict(out=bkT.f32[:, a:QW], lhsT=nonesb[:], rhs=Sbf[:, a:QW], start=False, stop=True),
                                 [nonesb.b, Sbf.b], [bkT.b])
                        if not lastb:
                            P.op("pe", "matmul", dict(out=bkR[hh].f32[:, a:QW], lhsT=onesb[:], rhs=lg[:, a:QW], start=first,
                                                      stop=(kb == 1)), [onesb.b, lg.b], [bkR[hh].b])
                            Sn = Sbfr[hh].next()
                            P.op("dve", "tensor_copy", dict(out=Sn[:, a:QW], in_=bkR[hh].f32[:, a:QW]), [bkR[hh].b], [Sn.b])
                            an = col0_of(kb - 1)
                            if an < a:
                                P.op("dve", "memset", dict(ap=Sn[:, an:a], constant=0.0), [], [Sn.b])
                            d["Snext"] = Sn
                        wb = wbr.next()
                        P.op("act", "activation", dict(out=wb[:, a:QW], in_=bkT.f32[:, a:QW], func=AF.Exp), [bkT.b], [wb.b])
                        m = mask_of(kb)
                        if m is not None:
                            a2 = min(QW, a + 128)
                            P.op("dve", "tensor_tensor", dict(out=wb[:, a:a2], in0=wb[:, a:a2], in1=m[:, a:a2], op=ALU.mult),
                                 [wb.b, maskb.b, masksb.b], [wb.b])
                        d["wb"] = wb

                    def stageC(it):
                        hp, kb, hh = it
                        d = st[it]
                        po = hh * 64
                        h = hp * 2 + hh
                        a = col0_of(kb)
                        bkO = bkOs[hp % 2]
                        P.op("pe", "matmul", dict(out=bkO.f32[po:po + 64, a:QW], lhsT=Vb[:, kb, h * 64:(h + 1) * 64],
                                                  rhs=d["wb"][:, a:QW], start=(kb == nkb - 1), stop=(kb == 0)),
                             [Vb.b, d["wb"].b], [bkO.b])
                        if kb == 0 and hh == 1:
                            P.op("act", "activation", dict(out=OT[:, hp, :QW], in_=bkO.f32[:, :QW], func=AF.Copy), [bkO.b], [OT.b])

                    n = len(its) // 2
                    for step in range(n + 2):
                        if step < n:
                            stageA2(its[2 * step], its[2 * step + 1])
                        if 0 <= step - 1 < n:
                            stageB(its[2 * (step - 1)])
                            stageB(its[2 * (step - 1) + 1])
                        if 0 <= step - 2 < n:
                            stageC(its[2 * (step - 2)])
                            stageC(its[2 * (step - 2) + 1])
                    for i0 in range(0, QW, 128):
                        nt = min(128, QW - i0)
                        r0 = base + q0 + i0
                        xin = xinr.next()
                        P.dma("sp", xin[:nt, :], src[r0:r0 + nt, :], xin.b, db(src.name, r0), "ld")
                        y = ystr.next()
                        for c in range(2):
                            bk = pb()
                            for hp in range(8):
                                P.op("pe", "matmul", dict(out=bk.f32[:nt, :], lhsT=OT[:, hp, i0:i0 + nt], rhs=Wo[:, hp, c * 512:(c + 1) * 512],
                                                          start=(hp == 0), stop=(hp == 7)), [OT.b, Wo.b], [bk.b], inc=(hp == 7))
                            P.op("dve", "tensor_tensor", dict(out=y[:nt, c * 512:(c + 1) * 512], in0=bk.f32[:nt, :],
                                                              in1=xin[:nt, c * 512:(c + 1) * 512], op=ALU.add), [bk.b, xin.b], [y.b])
                        P.dma("pool", dst[r0:r0 + nt, :], y[:nt, :], db(dst.name, r0), y.b, "st")
            for bk in bkR + bkOs:
                release(bk)
            end_stage()

        src = x_all
        for layer in range(DEPTH):
            j = layer // 2
            last = (layer == DEPTH - 1)
            if only is not None and "mix" not in only:
                stage_copy(src, xs)
            elif layer % 2 == 0:
                stage_ssd1(layer, j, src)
                stage_ssd2(layer, j, src, xs)
            else:
                stage_sb1(layer, j, src)
                stage_sb2(layer, j, src, xs)
            src = xs
            stage_mlp(layer, xs, y_all if last else xs)
        P.barrier()
        P.emit()
    return nc


def _consts():
    i = np.arange(128)
    ident = np.eye(128, dtype=np.float32)
    tri = (i[:, None] <= i[None, :]).astype(np.float32)
    su = (i[:, None] > i[None, :]).astype(np.float32)
    q = np.arange(512)
    mask = np.stack([(i[:, None] + 128 * k < q[None, :]) for k in range(4)], axis=1).astype(np.float32)
    qs = np.arange(32)
    masks = ((i[:, None] < qs[None, :]) & (i[:, None] < 32)).astype(np.float32)
    return dict(c_ident=ident, c_tri=tri, c_su=su, c_mask=np.ascontiguousarray(mask), c_masks=masks)


def make_in_maps(inp, n_cores, NB, SEQ, NS, DSEQ, PAST, DEPTH):
    N_SSD = (DEPTH + 1) // 2
    N_SB = DEPTH // 2
    f = lambda a: np.ascontiguousarray(np.asarray(a, dtype=np.float32))
    shared = dict(
        norm_mix=f(inp["norm_mix"]), norm_mlp=f(inp["norm_mlp"]), ssd_in_w=f(inp["ssd_in_w"]),
        ssd_conv_w=f(np.asarray(inp["ssd_conv_w"]).reshape(N_SSD, 4, 32, 128).transpose(0, 3, 2, 1)),
        ssd_conv_b=f(np.asarray(inp["ssd_conv_b"]).reshape(N_SSD, 32, 128).transpose(0, 2, 1)),
        ssd_dt_bias=f(inp["ssd_dt_bias"]), ssd_a_log=f(inp["ssd_a_log"]), ssd_d=f(inp["ssd_d"]),
        ssd_norm_w=f(inp["ssd_norm_w"]), ssd_out_w=f(inp["ssd_out_w"]),
        mlp_up=f(inp["mlp_up"]), mlp_down=f(inp["mlp_down"]))
    if N_SB > 0:
        shared.update(sb_qkv_w=f(inp["sb_qkv_w"]), sb_q_gain=f(inp["sb_q_gain"]), sb_k_gain=f(inp["sb_k_gain"]),
                      sb_out_w=f(inp["sb_out_w"]))
    else:
        shared.update(sb_qkv_w=np.zeros((1, D, 3 * D), np.float32), sb_q_gain=np.zeros((1, AD), np.float32),
                      sb_k_gain=np.zeros((1, AD), np.float32), sb_out_w=np.zeros((1, D, D), np.float32))
    shared.update(_consts())
    xp = np.asarray(inp["x_prompt"], dtype=np.float32)
    xsm = np.asarray(inp["x_sample"], dtype=np.float32)
    sssm = np.asarray(inp["state_ssm"], dtype=np.float32)
    sconv = np.asarray(inp["state_conv"], dtype=np.float32)
    ck = np.asarray(inp["cache_k"], dtype=np.float32)
    cv = np.asarray(inp["cache_v"], dtype=np.float32)
    maps = []
    for c in range(n_cores):
        m = dict(shared)
        m["x_all"] = f(np.concatenate([xp[c * NB:(c + 1) * NB].reshape(NB * SEQ, D),
                                       xsm[c * NS:(c + 1) * NS].reshape(NS * DSEQ, D)], axis=0))
        m["state_ssm"] = f(sssm[:, c * NS:(c + 1) * NS].reshape(N_SSD, NS, NH * HP, NST))
        m["state_conv"] = f(sconv[:, c * NS:(c + 1) * NS].reshape(N_SSD, NS, 3, 32, 128).transpose(0, 1, 4, 3, 2))
        if N_SB > 0:
            m["cache_k"] = f(ck[:, c * NS:(c + 1) * NS].reshape(N_SB, NS, PAST, D))
            m["cache_v"] = f(cv[:, c * NS:(c + 1) * NS].reshape(N_SB, NS, PAST, D))
        else:
            m["cache_k"] = np.zeros((1, NS, PAST, D), np.float32)
            m["cache_v"] = np.zeros((1, NS, PAST, D), np.float32)
        maps.append(m)
    return maps


def assemble(results, n_cores, NB, SEQ, NS, DSEQ, PAST, DEPTH):
    N_SSD = (DEPTH + 1) // 2
    N_SB = DEPTH // 2
    cat = lambda k, ax: np.concatenate([np.asarray(r[k]) for r in results], axis=ax)
    y_all = [np.asarray(r["y_all"]) for r in results]
    y_p = np.concatenate([y[:NB * SEQ].reshape(NB, SEQ, D) for y in y_all], axis=0)
    y_s = np.concatenate([y[NB * SEQ:].reshape(NS, DSEQ, D) for y in y_all], axis=0)
    pssm = cat("o_pssm", 1).reshape(N_SSD, n_cores * NB, NH, HP, NST)
    sssm = cat("o_sssm", 1).reshape(N_SSD, n_cores * NS, NH, HP, NST)
    pconv = np.ascontiguousarray(cat("o_pconv", 1).transpose(0, 1, 4, 3, 2)).reshape(N_SSD, n_cores * NB, 3, CONVD)
    sconv = np.ascontiguousarray(cat("o_sconv", 1).transpose(0, 1, 4, 3, 2)).reshape(N_SSD, n_cores * NS, 3, CONVD)
    pk = cat("o_pk", 1)[:N_SB].reshape(N_SB, n_cores * NB, SEQ, AH, AD)
    pv = cat("o_pv", 1)[:N_SB].reshape(N_SB, n_cores * NB, SEQ, AH, AD)
    sk = cat("o_sk", 1)[:N_SB].reshape(N_SB, n_cores * NS, DSEQ, AH, AD)
    sv = cat("o_sv", 1)[:N_SB].reshape(N_SB, n_cores * NS, DSEQ, AH, AD)
    outs = (y_p, y_s, pssm, pconv, pk, pv, sssm, sconv, sk, sv)
    return tuple(np.ascontiguousarray(o, dtype=np.float32) for o in outs)


def kernel(**inputs):
    n = 8
    NB, SEQ, NS, DSEQ, PAST, DEPTH = 4, 2048, 2, 32, 1024, 4
    nc = build(NB, SEQ, NS, DSEQ, PAST, DEPTH)
    maps = make_in_maps(inputs, n, NB, SEQ, NS, DSEQ, PAST, DEPTH)
    res = run_bass_kernel_spmd(nc, maps, core_ids=list(range(n)))
    return assemble(res.results, n, NB, SEQ, NS, DSEQ, PAST, DEPTH)
```

```python
import numpy as np
from contextlib import ExitStack
import concourse.bass as bass
import concourse.mybir as mybir
from concourse.bass_utils import run_bass_kernel_spmd

F32 = mybir.dt.float32
BF16 = mybir.dt.bfloat16
AF = mybir.ActivationFunctionType
ALU = mybir.AluOpType
AX = mybir.AxisListType

D = 1024
DFF = 4096
DIN = 2048
NH = 32
HP = 64
NG = 8
NST = 128
CONVD = 4096
DPROJ = 6176
AH = 16
AD = 64
EPS = 1e-6
import os as _os
TRIM = _os.environ.get('K_TRIM', '1') == '1'


class Buf:
    __slots__ = ("name", "lw", "rd")

    def __init__(s, name):
        s.name = name
        s.lw = None
        s.rd = {}


class Prog:
    ENG = ("pe", "act", "dve", "pool", "sp")

    def __init__(s, nc, es):
        s.nc = nc
        s.es = es
        s.sems = {}
        s.cnt = {}
        s.ops = {e: [] for e in s.ENG}
        s.waited = {e: {} for e in s.ENG}
        s.nbuf = 0
        s.nops = 0
        s.free = []
        s.stage_keys = []

    def buf(s, name=None):
        s.nbuf += 1
        return Buf((name or "b") + "_%d" % s.nbuf)

    def sem(s, key):
        if key not in s.sems:
            if key in s.ENG or not s.free:
                s.sems[key] = s.es.enter_context(s.nc.semaphore("s%d" % len(s.sems)))
                s.cnt[key] = 0
            else:
                h, c0 = s.free.pop()
                s.sems[key] = h
                s.cnt[key] = c0
            if key not in s.ENG:
                s.stage_keys.append(key)
        return s.sems[key]

    def retire(s):
        for k in s.stage_keys:
            s.free.append((s.sems[k], s.cnt[k]))
        s.stage_keys = []

    def _waits(s, eng, deps):
        w = s.waited[eng]
        best = {}
        for d in deps:
            if d is None:
                continue
            k, v = d
            if w.get(k, 0) >= v:
                continue
            if best.get(k, 0) < v:
                best[k] = v
        out = []
        for k, v in best.items():
            w[k] = v
            out.append((k, v))
        return out

    def op(s, eng, name, kw, reads=(), writes=(), inc=True):
        deps = []
        for b in reads:
            deps.append(b.lw)
        for b in writes:
            deps.append(b.lw)
            for k, v in b.rd.items():
                if k != eng:
                    deps.append((k, v))
        if eng == "pe":
            deps = [d for d in deps if d is not None and d[0] != "pe"]
        waits = s._waits(eng, deps)
        s.sem(eng)
        val = s.cnt[eng] + 1
        if inc:
            s.cnt[eng] = val
        for b in reads:
            if b.rd.get(eng, 0) < val:
                b.rd[eng] = val
        for b in writes:
            b.lw = (eng, val)
            b.rd = {}
        s.ops[eng].append((waits, name, kw, (eng, 1) if inc else None))
        s.nops += 1

    def dma(s, q, out_ap, in_ap, dst, src, kind, par=False, **kw):
        key = ("ld:" + dst.name) if kind == "ld" else ("st:" + src.name)
        s.sem(key)
        deps = [src.lw] + list(dst.rd.items())
        if par and kind == "ld" and dst.lw is not None and dst.lw[0] == key:
            pass
        else:
            deps.append(dst.lw)
            if s.cnt[key] > 0:
                deps.append((key, s.cnt[key]))
        waits = s._waits(q, deps)
        val = s.cnt[key] + 16
        s.cnt[key] = val
        if src.rd.get(key, 0) < val:
            src.rd[key] = val
        dst.lw = (key, val)
        dst.rd = {}
        k2 = dict(out=out_ap, in_=in_ap)
        k2.update(kw)
        s.ops[q].append((waits, "dma_start", k2, (key, 16)))
        s.nops += 1

    def barrier(s):
        for e in s.ENG:
            deps = [(k, v) for k, v in s.cnt.items() if v > 0 and k != e]
            waits = s._waits(e, deps)
            if waits:
                s.ops[e].append((waits, None, None, None))

    def emit(s):
        nc = s.nc
        with nc.Block() as block:
            def mk(eng):
                def f(e):
                    for waits, name, kw, inc in s.ops[eng]:
                        for k, v in waits:
                            e.wait_ge(s.sems[k], v)
                        if name is not None:
                            ins = getattr(e, name)(**kw)
                            if inc:
                                ins.then_inc(s.sems[inc[0]], inc[1])
                return f
            block.tensor(mk("pe"))
            block.scalar(mk("act"))
            block.vector(mk("dve"))
            block.gpsimd(mk("pool"))
            block.sync(mk("sp"))


class T:
    __slots__ = ("t", "b")

    def __init__(s, t, b):
        s.t = t
        s.b = b

    def __getitem__(s, k):
        return s.t[k]


class Ring:
    def __init__(s, items):
        s.items = items
        s.i = 0

    def next(s):
        it = s.items[s.i % len(s.items)]
        s.i += 1
        return it


class Bank:
    __slots__ = ("f32", "bf", "b")

    def __init__(s, t, b):
        s.f32 = t
        s.bf = t[:].bitcast(BF16)
        s.b = b


def build(NB, SEQ, NS, DSEQ, PAST, DEPTH, only=None):
    nc = bass.Bass("TRN2", target_bir_lowering=False)
    NTP = NB * SEQ
    NTS = NS * DSEQ
    NTOK = NTP + NTS
    N_SSD = (DEPTH + 1) // 2
    N_SB = DEPTH // 2
    NSB1 = max(N_SB, 1)
    KTOT = PAST + DSEQ
    KPAD = ((KTOT + 127) // 128) * 128

    def di(n, sh, dt=F32):
        return nc.dram_tensor(n, list(sh), dt, kind="ExternalInput").ap()

    def do(n, sh):
        return nc.dram_tensor(n, list(sh), F32, kind="ExternalOutput").ap()

    def dx(n, sh, dt=F32):
        return nc.dram_tensor(n, list(sh), dt).ap()

    x_all = di("x_all", [NTOK, D])
    state_ssm = di("state_ssm", [N_SSD, NS, NH * HP, NST])
    state_conv = di("state_conv", [N_SSD, NS, 128, 32, 3])
    cache_k = di("cache_k", [NSB1, NS, PAST, D])
    cache_v = di("cache_v", [NSB1, NS, PAST, D])
    norm_mix = di("norm_mix", [DEPTH, D])
    norm_mlp = di("norm_mlp", [DEPTH, D])
    ssd_in_w = di("ssd_in_w", [N_SSD, D, DPROJ])
    ssd_conv_w = di("ssd_conv_w", [N_SSD, 128, 32, 4])
    ssd_conv_b = di("ssd_conv_b", [N_SSD, 128, 32])
    ssd_dt_bias = di("ssd_dt_bias", [N_SSD, NH])
    ssd_a_log = di("ssd_a_log", [N_SSD, NH])
    ssd_d = di("ssd_d", [N_SSD, NH])
    ssd_norm_w = di("ssd_norm_w", [N_SSD, DIN])
    ssd_out_w = di("ssd_out_w", [N_SSD, DIN, D])
    sb_qkv_w = di("sb_qkv_w", [NSB1, D, 3 * D])
    sb_q_gain = di("sb_q_gain", [NSB1, AD])
    sb_k_gain = di("sb_k_gain", [NSB1, AD])
    sb_out_w = di("sb_out_w", [NSB1, D, D])
    mlp_up = di("mlp_up", [DEPTH, D, DFF])
    mlp_down = di("mlp_down", [DEPTH, DFF, D])
    c_ident = di("c_ident", [128, 128])
    c_tri = di("c_tri", [128, 128])
    c_su = di("c_su", [128, 128])
    c_mask = di("c_mask", [128, 4, 512])
    c_masks = di("c_masks", [128, 32])

    y_all = do("y_all", [NTOK, D])
    o_pssm = do("o_pssm", [N_SSD, NB, NH * HP, NST])
    o_pconv = do("o_pconv", [N_SSD, NB, 128, 32, 3])
    o_pk = do("o_pk", [NSB1, NB, SEQ, D])
    o_pv = do("o_pv", [NSB1, NB, SEQ, D])
    o_sssm = do("o_sssm", [N_SSD, NS, NH * HP, NST])
    o_sconv = do("o_sconv", [N_SSD, NS, 128, 32, 3])
    o_sk = do("o_sk", [NSB1, NS, DSEQ, D])
    o_sv = do("o_sv", [NSB1, NS, DSEQ, D])

    xs = dx("xs", [NTOK, D])
    Ys = dx("Ys", [NTOK, DIN])
    QTp = dx("QTp", [NB, 128, 8, SEQ], BF16)
    KTp = dx("KTp", [NB, 128, 8, SEQ], BF16)
    Vp = dx("Vp", [NB, SEQ, D], BF16)
    QTs = dx("QTs", [NS, 128, 8, DSEQ], BF16)
    KTs = dx("KTs", [NS, 128, 8, KPAD], BF16)
    Vs = dx("Vs", [NS, DSEQ, D], BF16)

    top = ExitStack()
    with top:
        P = Prog(nc, top)
        dbufs = {}

        def db(name, idx=0):
            k = (name, idx)
            if k not in dbufs:
                dbufs[k] = P.buf("d_" + name)
            return dbufs[k]

        cst = db("const")

        banks = [Bank(top.enter_context(nc.psum_tensor("ps%d" % i, [128, 512], F32)), P.buf("ps%d" % i))
                 for i in range(8)]
        bstate = {"i": 0, "held": set()}

        def pb():
            while True:
                i = bstate["i"] % 8
                bstate["i"] += 1
                if i not in bstate["held"]:
                    return banks[i]

        def hold():
            b = pb()
            bstate["held"].add(banks.index(b))
            return b

        def release(b):
            bstate["held"].discard(banks.index(b))

        cur = {"es": None, "n": 0}

        def sb(shape, dt, name=None):
            cur["n"] += 1
            nm = (name or "t") + "_%d" % cur["n"]
            t = cur["es"].enter_context(nc.sbuf_tensor(nm, list(shape), dt))
            return T(t, P.buf(nm))

        def ring(n, shape, dt, name=None):
            return Ring([sb(shape, dt, name) for _ in range(n)])

        tiles = []
        for b in range(NB):
            for i in range(SEQ // 128):
                tiles.append(dict(row0=b * SEQ + i * 128, nt=128, grp="p", seq=b, pos=i * 128))
        for b in range(NS):
            tiles.append(dict(row0=NTP + b * DSEQ, nt=DSEQ, grp="s", seq=b, pos=0))

        def load_consts():
            c = {}
            c["identb"] = sb([128, 128], BF16, "identb")
            P.dma("pool", c["identb"][:], c_ident[:, :], c["identb"].b, cst, "ld")
            return c

        def front(C, src, tl, gB, dstT, col0, rings):
            nt = tl["nt"]
            r0 = tl["row0"]
            xin = rings["xin"].next()
            P.dma("sp", xin[:nt, :], src[r0:r0 + nt, :], xin.b, db(src.name, r0), "ld")
            ss = rings["ss"].next()
            junk = rings["junk"].next()
            P.op("act", "activation", dict(out=junk[:nt, :], in_=xin[:nt, :], func=AF.Square, accum_out=ss[:nt, 0:1]),
                 [xin.b], [junk.b, ss.b])
            P.op("act", "activation", dict(out=ss[:nt, 2:3], in_=ss[:nt, 0:1], func=AF.Ln, scale=1.0 / D, bias=EPS),
                 [ss.b], [ss.b])
            P.op("act", "activation", dict(out=ss[:nt, 3:4], in_=ss[:nt, 2:3], func=AF.Exp, scale=-0.5), [ss.b], [ss.b])
            xn = rings["xn"].next()
            P.op("dve", "scalar_tensor_tensor", dict(out=xn[:nt, :], in0=xin[:nt, :], scalar=ss[:nt, 3:4], in1=gB[:nt, :],
                                                     op0=ALU.mult, op1=ALU.mult), [xin.b, ss.b, gB.b], [xn.b])
            bk = pb()
            for kc in range(8):
                P.op("pe", "transpose", dict(out=bk.bf[:, kc * 128:kc * 128 + nt], in_=xn[:nt, kc * 128:(kc + 1) * 128],
                                             identity=C["identb"][:nt, :nt]), [xn.b, C["identb"].b], [bk.b], inc=(kc == 7))
            P.op("act", "activation", dict(out=dstT[:, :, col0:col0 + nt],
                                           in_=bk.bf.rearrange("p (k t) -> p k t", k=8)[:, :, :nt], func=AF.Copy),
                 [bk.b], [dstT.b])
            return xin

        def load_w(dst, dram_ap, rows_per_dma=128):
            K, N = dram_ap.shape
            for kc in range(K // 128):
                for n0 in range(0, N, 2048):
                    n1 = min(N, n0 + 2048)
                    P.dma("pool", dst[:, kc, n0:n1], dram_ap[kc * 128:(kc + 1) * 128, n0:n1], dst.b, cst, "ld", par=True)

        def load_bcast(dst, dram_1d):
            P.dma("sp", dst[:], dram_1d.partition_broadcast(128), dst.b, cst, "ld")

        def begin_stage():
            P.barrier()
            cur["es"] = ExitStack()
            cur["es"].__enter__()

        def end_stage():
            P.barrier()
            P.retire()
            cur["es"].__exit__(None, None, None)
            cur["es"] = None

        def run_streams(items, make_gen, width=2, stagger=0):
            active = []
            free_lanes = list(range(width))
            it = iter(items)
            pending = [True]
            first = [True]

            def start_more():
                while free_lanes and pending[0]:
                    try:
                        x = next(it)
                    except StopIteration:
                        pending[0] = False
                        break
                    lane = free_lanes.pop(0)
                    g = make_gen(x, lane)
                    active.append((lane, g))
                    if first[0]:
                        first[0] = False
                        for _ in range(stagger):
                            try:
                                next(g)
                            except StopIteration:
                                active.remove((lane, g))
                                free_lanes.append(lane)
                                break
            while True:
                start_more()
                if not active:
                    break
                for (lane, g) in list(active):
                    try:
                        next(g)
                    except StopIteration:
                        active.remove((lane, g))
                        free_lanes.append(lane)

        def stage_copy(src, dst):
            begin_stage()
            rg = ring(2, [128, D], F32, "cp")
            for tl in tiles:
                t = rg.next()
                nt, r0 = tl["nt"], tl["row0"]
                P.dma("sp", t[:nt, :], src[r0:r0 + nt, :], t.b, db(src.name, r0), "ld")
                P.dma("pool", dst[r0:r0 + nt, :], t[:nt, :], db(dst.name, r0), t.b, "st")
            end_stage()

        def stage_mlp(layer, src, dst):
            begin_stage()
            C = load_consts()
            Wup = sb([128, 8, DFF], BF16, "Wup")
            Wdn = sb([128, 32, D], BF16, "Wdn")
            load_w(Wup, mlp_up[layer])
            load_w(Wdn, mlp_down[layer])
            gB = sb([128, D], F32, "gB")
            load_bcast(gB, norm_mlp[layer])
            rings = dict(xin=ring(4, [128, D], F32, "xin"), ss=ring(4, [128, 4], F32, "ss"),
                         junk=ring(1, [128, D], BF16, "junk"), xn=ring(2, [128, D], BF16, "xn"))
            xnTs = ring(2, [128, 8, 256], BF16, "xnT")
            hT = sb([128, 32, 256], BF16, "hT")
            rr = ring(2, [128, 256], F32, "relu")
            yst = ring(2, [128, D], F32, "yst")
            macros = []
            curm = []
            tot = 0
            for tl in tiles:
                if tot + tl["nt"] > 256 or (curm and curm[-1]["grp"] != tl["grp"]):
                    macros.append(curm)
                    curm = []
                    tot = 0
                curm.append(tl)
                tot += tl["nt"]
            if curm:
                macros.append(curm)

            def do_front(m):
                xnT = xnTs.next()
                col = 0
                ent = []
                for tl in m:
                    xin = front(C, src, tl, gB, xnT, col, rings)
                    ent.append((xin, col, tl))
                    col += tl["nt"]
                return xnT, ent, col

            nxt = do_front(macros[0])
            for mi, m in enumerate(macros):
                xnT, ent, NT = nxt
                for f in range(32):
                    bk = pb()
                    for kc in range(8):
                        P.op("pe", "matmul", dict(out=bk.f32[:, :NT], lhsT=Wup[:, kc, f * 128:(f + 1) * 128],
                                                  rhs=xnT[:, kc, :NT], start=(kc == 0), stop=(kc == 7)),
                             [Wup.b, xnT.b], [bk.b], inc=(kc == 7))
                    r = rr.next()
                    P.op("act", "activation", dict(out=r[:, :NT], in_=bk.f32[:, :NT], func=AF.Relu), [bk.b], [r.b])
                    P.op("dve", "tensor_tensor", dict(out=hT[:, f, :NT], in0=r[:, :NT], in1=r[:, :NT], op=ALU.mult),
                         [r.b], [hT.b])
                if mi + 1 < len(macros):
                    nxt = do_front(macros[mi + 1])
                for (xin, col, tl) in ent:
                    nt, r0 = tl["nt"], tl["row0"]
                    y = yst.next()
                    for c in range(2):
                        bk = pb()
                        for f in range(32):
                            P.op("pe", "matmul", dict(out=bk.f32[:nt, :], lhsT=hT[:, f, col:col + nt],
                                                      rhs=Wdn[:, f, c * 512:(c + 1) * 512], start=(f == 0), stop=(f == 31)),
                                 [hT.b, Wdn.b], [bk.b], inc=(f == 31))
                        P.op("dve", "tensor_tensor", dict(out=y[:nt, c * 512:(c + 1) * 512], in0=bk.f32[:nt, :],
                                                          in1=xin[:nt, c * 512:(c + 1) * 512], op=ALU.add),
                             [bk.b, xin.b], [y.b])
                    P.dma("pool", dst[r0:r0 + nt, :], y[:nt, :], db(dst.name, r0), y.b, "st")
            end_stage()

        def run_sched(lane_queues):
            events = set()
            n = len(lane_queues)
            curg = [None] * n
            idx = [0] * n
            waiting = [None] * n
            while True:
                progressed = False
                alive = False
                for li, q in enumerate(lane_queues):
                    if curg[li] is None:
                        if idx[li] < len(q):
                            curg[li] = q[idx[li]]()
                            idx[li] += 1
                            waiting[li] = None
                        else:
                            continue
                    alive = True
                    if waiting[li] is not None:
                        if waiting[li] in events:
                            waiting[li] = None
                        else:
                            continue
                    try:
                        r = next(curg[li])
                        progressed = True
                        while isinstance(r, tuple) and r[0] == "set":
                            events.add(r[1])
                            r = next(curg[li])
                        if isinstance(r, tuple) and r[0] == "wait" and r[1] not in events:
                            waiting[li] = r[1]
                    except StopIteration:
                        curg[li] = None
                        progressed = True
                if not alive:
                    break
                assert progressed, "emission scheduler deadlock"

        def stage_ssd1(layer, j, src):
            begin_stage()
            C = load_consts()
            Wx = sb([128, 8, CONVD + NH], BF16, "Wx")
            for kc in range(8):
                for n0 in range(0, CONVD + NH, 2048):
                    n1 = min(CONVD + NH, n0 + 2048)
                    P.dma("pool", Wx[:, kc, n0:n1], ssd_in_w[j, kc * 128:(kc + 1) * 128, DIN + n0:DIN + n1], Wx.b, cst, "ld", par=True)
            gB = sb([128, D], F32, "gB")
            load_bcast(gB, norm_mix[layer])
            trib = sb([128, 128], BF16, "trib")
            sub = sb([128, 128], BF16, "sub")
            onesb = sb([128, 128], BF16, "onesb")
            identf = sb([128, 128], F32, "identf")
            P.dma("pool", trib[:], c_tri[:, :], trib.b, cst, "ld")
            P.dma("pool", sub[:], c_su[:, :], sub.b, cst, "ld")
            P.dma("sp", identf[:], c_ident[:, :], identf.b, cst, "ld")
            P.op("dve", "memset", dict(ap=onesb[:], constant=1.0), [], [onesb.b])
            cw = sb([128, 32, 4], F32, "cw")
            cbias = sb([128, 32], F32, "cbias")
            P.dma("sp", cw[:], ssd_conv_w[j], cw.b, cst, "ld")
            P.dma("sp", cbias[:], ssd_conv_b[j], cbias.b, cst, "ld")
            Abc = sb([128, NH], F32, "Abc")
            dtb = sb([128, NH], F32, "dtb")
            Dbc = sb([128, NH], F32, "Dbc")
            load_bcast(Abc, ssd_a_log[j])
            load_bcast(dtb, ssd_dt_bias[j])
            load_bcast(Dbc, ssd_d[j])
            P.op("act", "activation", dict(out=Abc[:], in_=Abc[:], func=AF.Exp), [Abc.b], [Abc.b])
            P.op("dve", "tensor_scalar", dict(out=Abc[:], in0=Abc[:], scalar1=-1.0, scalar2=None, op0=ALU.mult),
                 [Abc.b], [Abc.b])
            DI = sb([128, NH, 128], BF16, "DI")
            for h in range(NH):
                P.op("dve", "tensor_scalar", dict(out=DI[:, h, :], in0=C["identb"][:], scalar1=Dbc[:, h:h + 1], scalar2=None,
                                                  op0=ALU.mult), [C["identb"].b, Dbc.b], [DI.b])
            xnr = ring(2, [128, D], BF16, "xn")
            rings = dict(xin=ring(2, [128, D], F32, "xin"), ss=ring(4, [128, 4], F32, "ss"), junk=xnr, xn=xnr)
            MTM = 256
            hnT = sb([128, 8, MTM], BF16, "hnT")
            xTs = [sb([128, 16, MTM], BF16, "xT") for _ in range(2)]
            BTs = [sb([128, 8, MTM], BF16, "BT") for _ in range(2)]
            CTs = [sb([128, 8, MTM], BF16, "CT") for _ in range(2)]
            dtraws = [sb([128, 2, NH], F32, "dtraw") for _ in range(2)]
            Rr = ring(3, [128, MTM + 4], BF16, "R")
            dgr = ring(16, [128, 128], BF16, "dg")
            sgr = ring(3, [128, MTM], F32, "sg")
            halo16 = sb([128, 32, 3], BF16, "halo16")
            ncb = sb([128, 32], F32, "ncb")
            P.op("dve", "tensor_scalar", dict(out=ncb[:], in0=cbias[:], scalar1=-1.0, scalar2=None, op0=ALU.mult), [cbias.b], [ncb.b])
            hTq = [sb([128, 512], F32, "hT") for _ in range(4)]
            hTbq = [sb([128, 512], BF16, "hTb") for _ in range(4)]
            halo = sb([128, 32, 3], F32, "halo")
            tmpr = ring(2, [128, 512], F32, "ytmp")
            ystr = ring(2, [128, 1024], F32, "yst")
            lanes = []
            for ln in range(2):
                lanes.append(dict(sm=sb([128, 12, NH], F32, "sm"), adtb=sb([128, NH], BF16, "adtb"),
                                  xtok=sb([128, DIN], BF16, "xtok"), Btok=sb([128, D], BF16, "Btok"),
                                  Wbh=sb([128, 16 * 128], BF16, "Wbh"), Ebh=sb([128, 16 * 128], BF16, "Ebh"),
                                  cbm=sb([128, NG * 128], BF16, "cbm"), xddh=sb([128, 1024], BF16, "xddh")))

            seqs = [("p", b, SEQ) for b in range(NB)] + [("s", b, DSEQ) for b in range(NS)]
            for si, (grp, b, L) in enumerate(seqs):
                base = (b * SEQ) if grp == "p" else (NTP + b * DSEQ)
                CL = min(128, L)
                MT = min(MTM, L)
                NCH = MT // CL
                NM = L // MT
                if grp == "p":
                    for q in range(4):
                        P.op("dve", "memset", dict(ap=hTq[q][:], constant=0.0), [], [hTq[q].b])
                    P.op("dve", "memset", dict(ap=halo[:], constant=0.0), [], [halo.b])
                    P.op("dve", "memset", dict(ap=halo16[:], constant=0.0), [], [halo16.b])
                else:
                    P.dma("sp", halo[:], state_conv[j, b], halo.b, cst, "ld")
                    P.op("dve", "tensor_copy", dict(out=halo16[:], in_=halo[:]), [halo.b], [halo16.b])
                    for hf in range(2):
                        stg = ystr.next()
                        P.dma("sp", stg[:].rearrange("p (t n) -> p t n", t=8),
                              state_ssm[j, b, hf * 1024:(hf + 1) * 1024, :].rearrange("(t p) n -> p t n", p=128), stg.b, cst, "ld")
                        for qq in range(2):
                            q = hf * 2 + qq
                            bk = pb()
                            for i in range(4):
                                t = qq * 4 + i
                                P.op("pe", "transpose", dict(out=bk.f32[:, i * 128:(i + 1) * 128], in_=stg[:, t * 128:(t + 1) * 128],
                                                             identity=identf[:]), [stg.b, identf.b], [bk.b], inc=(i == 3))
                            P.op("act", "activation", dict(out=hTq[q][:], in_=bk.f32[:, :], func=AF.Copy), [bk.b], [hTq[q].b])
                for q in range(4):
                    P.op("act", "activation", dict(out=hTbq[q][:], in_=hTq[q][:], func=AF.Copy), [hTq[q].b], [hTbq[q].b])

                def conv_stream(m, si=si, base=base, CL=CL, MT=MT, NCH=NCH, NM=NM):
                    st = m % 2
                    xT, BT, CT, dtraw = xTs[st], BTs[st], CTs[st], dtraws[st]
                    m0 = m * MT
                    if m >= 2:
                        for ci in range(NCH):
                            yield ("wait", ("cdone", si, (m - 2) * NCH + ci))
                    if m >= 1:
                        yield ("wait", ("convdone", si, m - 1))
                    for i in range(NCH):
                        tl = dict(row0=base + m0 + i * CL, nt=CL)
                        front(C, src, tl, gB, hnT, i * CL, rings)
                        yield
                    for ci in range(NCH):
                        bk = pb()
                        for kc in range(8):
                            P.op("pe", "matmul", dict(out=bk.f32[:CL, :NH], lhsT=hnT[:, kc, ci * CL:(ci + 1) * CL],
                                                      rhs=Wx[:, kc, CONVD:CONVD + NH], start=(kc == 0), stop=(kc == 7)),
                                 [hnT.b, Wx.b], [bk.b], inc=(kc == 7))
                        P.op("act", "activation", dict(out=dtraw[:CL, ci, :], in_=bk.f32[:CL, :NH], func=AF.Copy), [bk.b], [dtraw.b])
                    yield
                    pipe = {}

                    def s0(ct):
                        dgs = []
                        for tap in range(4):
                            dg = dgr.next()
                            P.op("dve", "tensor_scalar", dict(out=dg[:], in0=C["identb"][:], scalar1=cw[:, ct, tap:tap + 1], scalar2=None,
                                                              op0=ALU.mult), [C["identb"].b, cw.b], [dg.b])
                            dgs.append(dg)
                        pipe[ct] = dict(dgs=dgs)

                    def s1(ct):
                        bk = pb()
                        for kc in range(8):
                            P.op("pe", "matmul", dict(out=bk.f32[:, :MT], lhsT=Wx[:, kc, ct * 128:(ct + 1) * 128],
                                                      rhs=hnT[:, kc, :MT], start=(kc == 0), stop=(kc == 7)),
                                 [Wx.b, hnT.b], [bk.b], inc=(kc == 7))
                        R = Rr.next()
                        P.op("dve", "tensor_copy", dict(out=R[:, 0:3], in_=halo16[:, ct, :]), [halo16.b], [R.b])
                        P.op("act", "activation", dict(out=R[:, 3:3 + MT], in_=bk.f32[:, :MT], func=AF.Copy), [bk.b], [R.b])
                        P.op("dve", "tensor_copy", dict(out=halo16[:, ct, :], in_=R[:, MT:MT + 3]), [R.b], [halo16.b])
                        if m == NM - 1:
                            P.op("dve", "tensor_copy", dict(out=halo[:, ct, :], in_=bk.f32[:, MT - 3:MT]), [bk.b], [halo.b])
                        pipe[ct]["R"] = R

                    def s2(ct):
                        R, dgs = pipe[ct]["R"], pipe[ct]["dgs"]
                        bk2 = hold()
                        for tap in range(4):
                            P.op("pe", "matmul", dict(out=bk2.f32[:, :MT], lhsT=dgs[tap][:], rhs=R[:, tap:tap + MT],
                                                      start=(tap == 0), stop=(tap == 3)), [dgs[tap].b, R.b], [bk2.b], inc=(tap == 3))
                        sg = sgr.next()
                        P.op("act", "activation", dict(out=sg[:, :MT], in_=bk2.f32[:, :MT], func=AF.Exp, scale=-1.0,
                                                       bias=ncb[:, ct:ct + 1]), [bk2.b, ncb.b], [sg.b])
                        P.op("act", "activation", dict(out=sg[:, :MT], in_=sg[:, :MT], func=AF.Ln, bias=1.0), [sg.b], [sg.b])
                        P.op("act", "activation", dict(out=sg[:, :MT], in_=sg[:, :MT], func=AF.Exp, scale=-1.0), [sg.b], [sg.b])
                        pipe[ct]["bk2"] = bk2
                        pipe[ct]["sg"] = sg

                    def s3(ct):
                        bk2, sg = pipe[ct]["bk2"], pipe[ct]["sg"]
                        if ct < 16:
                            dT, dap = xT, xT[:, ct, :MT]
                        elif ct < 24:
                            dT, dap = BT, BT[:, ct - 16, :MT]
                        else:
                            dT, dap = CT, CT[:, ct - 24, :MT]
                        P.op("dve", "scalar_tensor_tensor", dict(out=dap, in0=bk2.f32[:, :MT], scalar=cbias[:, ct:ct + 1], in1=sg[:, :MT],
                                                                 op0=ALU.add, op1=ALU.mult), [bk2.b, cbias.b, sg.b], [dT.b])
                        release(bk2)
                        del pipe[ct]

                    for step in range(32 + 3):
                        if step < 32:
                            s0(step)
                        if 0 <= step - 1 < 32:
                            s1(step - 1)
                        if 0 <= step - 2 < 32:
                            s2(step - 2)
                        if 0 <= step - 3 < 32:
                            s3(step - 3)
                        yield
                    yield ("set", ("convdone", si, m))

                def chunk_stream(c, lane, si=si, base=base, CL=CL, MT=MT, NCH=NCH):
                    Ln = lanes[lane]
                    m, ci = divmod(c, NCH)
                    st = m % 2
                    xT, BT, CT, dtraw = xTs[st], BTs[st], CTs[st], dtraws[st]
                    c0 = ci * CL
                    row0 = base + m * MT + c0
                    sm, adtb, xtok, Btok, Wbh, Ebh, cbm, xddh = (Ln["sm"], Ln["adtb"], Ln["xtok"], Ln["Btok"], Ln["Wbh"],
                                                                 Ln["Ebh"], Ln["cbm"], Ln["xddh"])
                    yield ("wait", ("convdone", si, m))
                    P.op("dve", "tensor_tensor", dict(out=sm[:CL, 0, :], in0=dtraw[:CL, ci, :], in1=dtb[:CL, :], op=ALU.add),
                         [dtraw.b, dtb.b], [sm.b])
                    P.op("dve", "scalar_tensor_tensor", dict(out=sm[:CL, 1, :], in0=sm[:CL, 0, :], scalar=-1.0,
                                                             in1=sm[:CL, 0, :], op0=ALU.mult, op1=ALU.max), [sm.b], [sm.b])
                    P.op("act", "activation", dict(out=sm[:CL, 2, :], in_=sm[:CL, 1, :], func=AF.Exp, scale=-1.0), [sm.b], [sm.b])
                    P.op("act", "activation", dict(out=sm[:CL, 3, :], in_=sm[:CL, 2, :], func=AF.Ln, bias=1.0), [sm.b], [sm.b])
                    P.op("dve", "scalar_tensor_tensor", dict(out=sm[:CL, 4, :], in0=sm[:CL, 0, :], scalar=0.0,
                                                             in1=sm[:CL, 3, :], op0=ALU.max, op1=ALU.add), [sm.b], [sm.b])
                    dt = sm[:CL, 4, :]
                    P.op("dve", "tensor_tensor", dict(out=adtb[:CL, :], in0=dt, in1=Abc[:CL, :], op=ALU.mult),
                         [sm.b, Abc.b], [adtb.b])
                    P.op("act", "activation", dict(out=sm[:CL, 5, :], in_=dt, func=AF.Ln), [sm.b], [sm.b])
                    yield
                    bk2 = pb()
                    P.op("pe", "matmul", dict(out=bk2.f32[:CL, 0:NH], lhsT=trib[:CL, :CL], rhs=adtb[:CL, :],
                                              start=True, stop=True), [trib.b, adtb.b], [bk2.b], inc=False)
                    P.op("pe", "matmul", dict(out=bk2.f32[:, NH:2 * NH], lhsT=onesb[:CL, :], rhs=adtb[:CL, :],
                                              start=True, stop=True), [onesb.b, adtb.b], [bk2.b])
                    P.op("act", "activation", dict(out=sm[:CL, 6, :], in_=bk2.f32[:CL, 0:NH], func=AF.Exp), [bk2.b], [sm.b])
                    P.op("act", "activation", dict(out=sm[:, 7, :], in_=bk2.f32[:, NH:2 * NH], func=AF.Exp), [bk2.b], [sm.b])
                    P.op("act", "activation", dict(out=sm[:CL, 8, :], in_=bk2.f32[:CL, 0:NH], func=AF.Copy), [bk2.b], [sm.b])
                    P.op("dve", "tensor_tensor", dict(out=sm[:CL, 9, :], in0=bk2.f32[:CL, NH:2 * NH], in1=sm[:CL, 8, :],
                                                      op=ALU.subtract), [bk2.b, sm.b], [sm.b])
                    P.op("act", "activation", dict(out=sm[:CL, 10, :], in_=sm[:CL, 9, :], func=AF.Exp), [sm.b], [sm.b])
                    P.op("dve", "tensor_tensor", dict(out=sm[:CL, 11, :], in0=sm[:CL, 10, :], in1=dt, op=ALU.mult), [sm.b], [sm.b])
                    yield
                    for half in range(2):
                        bk = pb()
                        for i in range(8):
                            P.op("pe", "transpose", dict(out=bk.bf[:CL, i * 128:(i + 1) * 128],
                                                         in_=xT[:, half * 8 + i, c0:c0 + CL], identity=C["identb"][:]),
                                 [xT.b, C["identb"].b], [bk.b], inc=(i == 7))
                        P.op("act", "activation", dict(out=xtok[:CL, half * 1024:(half + 1) * 1024], in_=bk.bf[:CL, :],
                                                       func=AF.Copy), [bk.b], [xtok.b])
                        yield
                    bk = pb()
                    for g in range(8):
                        P.op("pe", "transpose", dict(out=bk.bf[:CL, g * 128:(g + 1) * 128], in_=BT[:, g, c0:c0 + CL],
                                                     identity=C["identb"][:]), [BT.b, C["identb"].b], [bk.b], inc=(g == 7))
                    P.op("act", "activation", dict(out=Btok[:CL, :], in_=bk.bf[:CL, :], func=AF.Copy), [bk.b], [Btok.b])
                    yield
                    cv = cbm[:CL, 0:NG * CL].rearrange("p (g t) -> p g t", g=NG)
                    for half in range(2):
                        bk = pb()
                        for gi in range(4):
                            g = half * 4 + gi
                            P.op("pe", "matmul", dict(out=bk.f32[:CL, gi * CL:(gi + 1) * CL], lhsT=BT[:, g, c0:c0 + CL],
                                                      rhs=CT[:, g, c0:c0 + CL], start=True, stop=True),
                                 [BT.b, CT.b], [bk.b], inc=(gi == 3))
                        P.op("dve", "tensor_tensor", dict(out=cv[:, half * 4:(half + 1) * 4, :],
                                                          in0=bk.f32[:CL, 0:4 * CL].rearrange("p (g t) -> p g t", g=4),
                                                          in1=trib[:CL, :CL].unsqueeze(1).to_broadcast([CL, 4, CL]),
                                                          op=ALU.mult), [bk.b, trib.b], [cbm.b])
                        yield
                    Wv = Wbh[:CL, 0:16 * CL].rearrange("p (h t) -> p h t", h=16)
                    Ev = Ebh[:CL, 0:16 * CL].rearrange("p (h t) -> p h t", h=16)
                    for hf in range(2):
                        P.op("dve", "tensor_tensor", dict(out=Wv, in0=trib[:CL, :CL].unsqueeze(1).to_broadcast([CL, 16, CL]),
                                                           in1=adtb[:CL, hf * 16:(hf + 1) * 16].unsqueeze(2).to_broadcast([CL, 16, CL]),
                                                           op=ALU.mult), [trib.b, adtb.b], [Wbh.b])
                        for gi in range(4):
                            bk = pb()
                            P.op("pe", "matmul", dict(out=bk.f32[:CL, 0:4 * CL], lhsT=sub[:CL, :CL],
                                                      rhs=Wbh[:CL, gi * 4 * CL:(gi + 1) * 4 * CL], start=True, stop=True),
                                 [sub.b, Wbh.b], [bk.b])
                            for hi in range(4):
                                hl = gi * 4 + hi
                                h = hf * 16 + hl
                                P.op("act", "activation", dict(out=Ev[:, hl, :], in_=bk.f32[:CL, hi * CL:(hi + 1) * CL],
                                                               func=AF.Exp, bias=sm[:CL, 5, h:h + 1]), [bk.b, sm.b], [Ebh.b])
                            yield
                        E4 = Ebh[:CL, 0:16 * CL].rearrange("p (g r t) -> p g r t", g=4, r=4)
                        P.op("dve", "tensor_tensor", dict(out=E4, in0=E4,
                                                          in1=cv[:, hf * 4:(hf + 1) * 4, :].unsqueeze(2).to_broadcast([CL, 4, 4, CL]),
                                                          op=ALU.mult), [Ebh.b, cbm.b], [Ebh.b])
                        yield
                        if c > 0:
                            yield ("wait", ("state", si, c - 1, hf))
                        yst = ystr.next()
                        for qq in range(2):
                            q = hf * 2 + qq
                            bkY = pb()
                            for hh in range(8):
                                hl = qq * 8 + hh
                                h = hf * 16 + hl
                                P.op("pe", "matmul", dict(out=bkY.f32[:CL, hh * 64:(hh + 1) * 64], lhsT=Ev[:, hl, :],
                                                          rhs=xtok[:CL, h * 64:(h + 1) * 64], start=True, stop=False),
                                     [Ebh.b, xtok.b], [bkY.b], inc=False)
                                P.op("pe", "matmul", dict(out=bkY.f32[:CL, hh * 64:(hh + 1) * 64], lhsT=DI[:CL, h, :CL],
                                                          rhs=xtok[:CL, h * 64:(h + 1) * 64], start=False, stop=True),
                                     [DI.b, xtok.b], [bkY.b], inc=(hh == 7))
                            bkO = pb()
                            for gg in range(2):
                                g = q * 2 + gg
                                P.op("pe", "matmul", dict(out=bkO.f32[:CL, gg * 256:(gg + 1) * 256], lhsT=CT[:, g, c0:c0 + CL],
                                                          rhs=hTbq[q][:, gg * 256:(gg + 1) * 256], start=True, stop=True),
                                     [CT.b, hTbq[q].b], [bkO.b], inc=(gg == 1))
                            tmp = tmpr.next()
                            P.op("dve", "tensor_tensor", dict(out=tmp[:CL, :].rearrange("p (h d) -> p h d", h=8),
                                                              in0=bkO.f32[:CL, :].rearrange("p (h d) -> p h d", h=8),
                                                              in1=sm[:CL, 6, q * 8:(q + 1) * 8].unsqueeze(2).to_broadcast([CL, 8, 64]),
                                                              op=ALU.mult), [bkO.b, sm.b], [tmp.b])
                            P.op("dve", "tensor_tensor", dict(out=yst[:CL, qq * 512:(qq + 1) * 512], in0=bkY.f32[:CL, :],
                                                              in1=tmp[:CL, :], op=ALU.add), [bkY.b, tmp.b], [yst.b])
                            yield
                        P.dma("pool", Ys[row0:row0 + CL, hf * 1024:(hf + 1) * 1024], yst[:CL, :], db("Ys", row0), yst.b, "st")
                        P.op("dve", "tensor_tensor", dict(out=xddh[:CL, :].rearrange("p (h d) -> p h d", h=16),
                                                          in0=xtok[:CL, hf * 1024:(hf + 1) * 1024].rearrange("p (h d) -> p h d", h=16),
                                                          in1=sm[:CL, 11, hf * 16:(hf + 1) * 16].unsqueeze(2).to_broadcast([CL, 16, 64]),
                                                          op=ALU.mult), [xtok.b, sm.b], [xddh.b])
                        for qq in range(2):
                            q = hf * 2 + qq
                            bkH = pb()
                            for gg in range(2):
                                g = q * 2 + gg
                                P.op("pe", "matmul", dict(out=bkH.f32[:, gg * 256:(gg + 1) * 256], lhsT=Btok[:CL, g * 128:(g + 1) * 128],
                                                          rhs=xddh[:CL, (qq * 2 + gg) * 256:(qq * 2 + gg + 1) * 256], start=True, stop=True),
                                     [Btok.b, xddh.b], [bkH.b], inc=(gg == 1))
                            hv = hTq[q]
                            P.op("dve", "tensor_tensor", dict(out=hv[:].rearrange("p (h d) -> p h d", h=8),
                                                              in0=hv[:].rearrange("p (h d) -> p h d", h=8),
                                                              in1=sm[:, 7, q * 8:(q + 1) * 8].unsqueeze(2).to_broadcast([128, 8, 64]),
                                                              op=ALU.mult), [hv.b, sm.b], [hv.b])
                            P.op("dve", "tensor_tensor", dict(out=hv[:], in0=bkH.f32[:, :], in1=hv[:], op=ALU.add), [bkH.b, hv.b], [hv.b])
                            P.op("act", "activation", dict(out=hTbq[q][:], in_=hv[:], func=AF.Copy), [hv.b], [hTbq[q].b])
                            yield
                        yield ("set", ("state", si, c, hf))
                    yield ("set", ("cdone", si, c))

                nchunks = NM * NCH
                convq = [(lambda m=m: conv_stream(m)) for m in range(NM)]
                laneA = [(lambda c=c: chunk_stream(c, 0)) for c in range(0, nchunks, 2)]
                laneB = [(lambda c=c: chunk_stream(c, 1)) for c in range(1, nchunks, 2)]
                run_sched([convq, laneA, laneB])

                o_ssm = (o_pssm if grp == "p" else o_sssm)[j, b]
                o_conv = (o_pconv if grp == "p" else o_sconv)[j, b]
                for hf in range(2):
                    stg = ystr.next()
                    for qq in range(2):
                        q = hf * 2 + qq
                        bk = pb()
                        for i in range(4):
                            P.op("pe", "transpose", dict(out=bk.f32[:, i * 128:(i + 1) * 128], in_=hTq[q][:, i * 128:(i + 1) * 128],
                                                         identity=identf[:]), [hTq[q].b, identf.b], [bk.b], inc=(i == 3))
                        P.op("act", "activation", dict(out=stg[:, qq * 512:(qq + 1) * 512], in_=bk.f32[:, :], func=AF.Copy),
                             [bk.b], [stg.b])
                    P.dma("pool", o_ssm[hf * 1024:(hf + 1) * 1024, :].rearrange("(t p) n -> p t n", p=128),
                          stg[:].rearrange("p (t n) -> p t n", t=8), db("o_ssm_%s" % grp, b), stg.b, "st")
                P.dma("pool", o_conv, halo[:], db("o_conv_%s" % grp, b), halo.b, "st")
            end_stage()

        def stage_ssd2(layer, j, src, dst):
            begin_stage()
            C = load_consts()
            Wz = sb([128, 8, DIN], BF16, "Wz")
            for kc in range(8):
                P.dma("pool", Wz[:, kc, :], ssd_in_w[j, kc * 128:(kc + 1) * 128, 0:DIN], Wz.b, cst, "ld", par=True)
            Wo = sb([128, 16, D], BF16, "Wo")
            load_w(Wo, ssd_out_w[j])
            gB = sb([128, D], F32, "gB")
            load_bcast(gB, norm_mix[layer])
            nwB = sb([128, DIN], F32, "nwB")
            load_bcast(nwB, ssd_norm_w[j])
            xinr = ring(4, [128, D], F32, "xin")
            ssr = ring(6, [128, 4], F32, "ss")
            xnr = ring(4, [128, D], BF16, "xn")
            hnTs = ring(3, [128, 8, 128], BF16, "hnT")
            yinr = ring(3, [128, DIN], F32, "yin")
            szr = ring(2, [128, DIN], F32, "sz")
            gsr = ring(3, [128, 4, NG], F32, "gs")
            ynbr = ring(2, [128, DIN], BF16, "ynb")
            ynTr = ring(2, [128, 16, 128], BF16, "ynT")
            ystr = ring(2, [128, D], F32, "yst")
            junk2 = sb([128, 256], F32, "junk2")
            TS = {}

            def stA(i):
                tl = tiles[i]
                nt, r0 = tl["nt"], tl["row0"]
                xin = xinr.next()
                yin = yinr.next()
                P.dma("sp", xin[:nt, :], src[r0:r0 + nt, :], xin.b, db(src.name, r0), "ld")
                P.dma("sp", yin[:nt, :], Ys[r0:r0 + nt, :], yin.b, db("Ys", r0), "ld")
                ss = ssr.next()
                junk = xnr.next()
                P.op("act", "activation", dict(out=junk[:nt, :], in_=xin[:nt, :], func=AF.Square, accum_out=ss[:nt, 0:1]),
                     [xin.b], [junk.b, ss.b])
                P.op("act", "activation", dict(out=ss[:nt, 2:3], in_=ss[:nt, 0:1], func=AF.Ln, scale=1.0 / D, bias=EPS), [ss.b], [ss.b])
                P.op("act", "activation", dict(out=ss[:nt, 3:4], in_=ss[:nt, 2:3], func=AF.Exp, scale=-0.5), [ss.b], [ss.b])
                xn = xnr.next()
                P.op("dve", "scalar_tensor_tensor", dict(out=xn[:nt, :], in0=xin[:nt, :], scalar=ss[:nt, 3:4], in1=gB[:nt, :],
                                                         op0=ALU.mult, op1=ALU.mult), [xin.b, ss.b, gB.b], [xn.b])
                TS[i] = dict(xin=xin, yin=yin, xn=xn)

            def stB(i):
                nt = tiles[i]["nt"]
                xn = TS[i]["xn"]
                hnT = hnTs.next()
                bk = pb()
                for kc in range(8):
                    P.op("pe", "transpose", dict(out=bk.bf[:, kc * 128:kc * 128 + nt], in_=xn[:nt, kc * 128:(kc + 1) * 128],
                                                 identity=C["identb"][:nt, :nt]), [xn.b, C["identb"].b], [bk.b], inc=(kc == 7))
                P.op("act", "activation", dict(out=hnT[:, :, :nt], in_=bk.bf.rearrange("p (k t) -> p k t", k=8)[:, :, :nt], func=AF.Copy),
                     [bk.b], [hnT.b])
                TS[i]["hnT"] = hnT

            def stC(i, cs):
                nt = tiles[i]["nt"]
                d = TS[i]
                hnT, yin = d["hnT"], d["yin"]
                if "sz" not in d:
                    d["sz"] = szr.next()
                sz = d["sz"]
                for c in cs:
                    bk = pb()
                    for kc in range(8):
                        P.op("pe", "matmul", dict(out=bk.f32[:nt, :], lhsT=hnT[:, kc, :nt], rhs=Wz[:, kc, c * 512:(c + 1) * 512],
                                                  start=(kc == 0), stop=(kc == 7)), [hnT.b, Wz.b], [bk.b], inc=(kc == 7))
                    sl = sz[:nt, c * 512:(c + 1) * 512]
                    P.op("act", "activation", dict(out=sl, in_=bk.f32[:nt, :], func=AF.Exp, scale=-1.0), [bk.b], [sz.b])
                    P.op("act", "activation", dict(out=sl, in_=sl, func=AF.Ln, bias=1.0), [sz.b], [sz.b])
                    P.op("act", "activation", dict(out=sl, in_=sl, func=AF.Exp, scale=-1.0), [sz.b], [sz.b])
                    P.op("dve", "tensor_tensor", dict(out=sl, in0=sl, in1=yin[:nt, c * 512:(c + 1) * 512], op=ALU.mult),
                         [sz.b, yin.b], [sz.b])
                    P.op("dve", "tensor_tensor", dict(out=sl, in0=bk.f32[:nt, :], in1=sl, op=ALU.mult), [bk.b, sz.b], [sz.b])

            def stD(i):
                nt = tiles[i]["nt"]
                d = TS[i]
                sz = d["sz"]
                gs = gsr.next()
                for g in range(NG):
                    P.op("act", "activation", dict(out=junk2[:nt, :], in_=sz[:nt, g * 256:(g + 1) * 256], func=AF.Square,
                                                   accum_out=gs[:nt, 0, g:g + 1]), [sz.b], [junk2.b, gs.b])
                P.op("act", "activation", dict(out=gs[:nt, 2, :], in_=gs[:nt, 0, :], func=AF.Ln, scale=1.0 / 256, bias=EPS),
                     [gs.b], [gs.b])
                P.op("act", "activation", dict(out=gs[:nt, 3, :], in_=gs[:nt, 2, :], func=AF.Exp, scale=-0.5), [gs.b], [gs.b])
                P.op("dve", "tensor_tensor", dict(out=sz[:nt, :].rearrange("p (g d) -> p g d", g=NG),
                                                  in0=sz[:nt, :].rearrange("p (g d) -> p g d", g=NG),
                                                  in1=gs[:nt, 3, :].unsqueeze(2).to_broadcast([nt, NG, 256]), op=ALU.mult),
                     [sz.b, gs.b], [sz.b])
                ynb = ynbr.next()
                P.op("dve", "tensor_tensor", dict(out=ynb[:nt, :], in0=sz[:nt, :], in1=nwB[:nt, :], op=ALU.mult),
                     [sz.b, nwB.b], [ynb.b])
                d["ynb"] = ynb

            def stE(i):
                nt = tiles[i]["nt"]
                d = TS[i]
                ynb = d["ynb"]
                ynT = ynTr.next()
                for half in range(2):
                    bk = pb()
                    for k8 in range(8):
                        ct = half * 8 + k8
                        P.op("pe", "transpose", dict(out=bk.bf[:, k8 * 128:k8 * 128 + nt], in_=ynb[:nt, ct * 128:(ct + 1) * 128],
                                                     identity=C["identb"][:nt, :nt]), [ynb.b, C["identb"].b], [bk.b], inc=(k8 == 7))
                    P.op("act", "activation", dict(out=ynT[:, half * 8:(half + 1) * 8, :nt],
                                                   in_=bk.bf.rearrange("p (k t) -> p k t", k=8)[:, :, :nt], func=AF.Copy),
                         [bk.b], [ynT.b])
                d["ynT"] = ynT

            def stF(i):
                tl = tiles[i]
                nt, r0 = tl["nt"], tl["row0"]
                d = TS.pop(i)
                ynT, xin = d["ynT"], d["xin"]
                y = ystr.next()
                for c in range(2):
                    bk = pb()
                    for ct in range(16):
                        P.op("pe", "matmul", dict(out=bk.f32[:nt, :], lhsT=ynT[:, ct, :nt], rhs=Wo[:, ct, c * 512:(c + 1) * 512],
                                                  start=(ct == 0), stop=(ct == 15)), [ynT.b, Wo.b], [bk.b], inc=(ct == 15))
                    P.op("dve", "tensor_tensor", dict(out=y[:nt, c * 512:(c + 1) * 512], in0=bk.f32[:nt, :],
                                                      in1=xin[:nt, c * 512:(c + 1) * 512], op=ALU.add), [bk.b, xin.b], [y.b])
                P.dma("pool", dst[r0:r0 + nt, :], y[:nt, :], db(dst.name, r0), y.b, "st")

            NTL = len(tiles)
            stA(0)
            if NTL > 1:
                stA(1)
            stB(0)
            for i in range(NTL):
                if i + 2 < NTL:
                    stA(i + 2)
                stC(i, (0, 1))
                if i + 1 < NTL:
                    stB(i + 1)
                if i >= 1:
                    stE(i - 1)
                stC(i, (2, 3))
                if i >= 1:
                    stF(i - 1)
                stD(i)
            stE(NTL - 1)
            stF(NTL - 1)
            end_stage()

        def stage_sb1(layer, j, src):
            begin_stage()
            C = load_consts()
            Wqkv = sb([128, 8, 3 * D], BF16, "Wqkv")
            load_w(Wqkv, sb_qkv_w[j])
            gB = sb([128, D], F32, "gB")
            load_bcast(gB, norm_mix[layer])
            qg = sb([128, AD], F32, "qg")
            kg = sb([128, AD], F32, "kg")
            load_bcast(qg, sb_q_gain[j])
            load_bcast(kg, sb_k_gain[j])
            P.op("dve", "tensor_scalar", dict(out=qg[:], in0=qg[:], scalar1=0.125, scalar2=None, op0=ALU.mult), [qg.b], [qg.b])
            xinr = ring(2, [128, D], F32, "xin")
            ssr = ring(6, [128, 4], F32, "ss")
            xnr = ring(4, [128, D], BF16, "xn")
            xnTs = ring(3, [128, 8, 128], BF16, "xnT")
            sqr = ring(2, [128, 512], F32, "sq")
            t1r = ring(2, [128, 512], F32, "t1")
            ssqr = ring(4, [128, 4, 8], F32, "ssq")
            koutr = ring(2, [128, D], F32, "kout")
            voutr = ring(2, [128, D], F32, "vout")
            knbr = ring(3, [128, D], BF16, "knb")
            qnbr = ring(3, [128, D], BF16, "qnb")
            vbr = ring(2, [128, D], BF16, "vb")
            qTr = ring(2, [128, 8, 128], BF16, "qT")
            kTr = ring(2, [128, 8, 128], BF16, "kT")
            ckr = ring(2, [128, D], BF16, "ck")

            def transpose_store(srcT, nt, stg, dram_ap, dbuf):
                bk = pb()
                for hp in range(8):
                    P.op("pe", "transpose", dict(out=bk.bf[:, hp * 128:hp * 128 + nt], in_=srcT[:nt, hp * 128:(hp + 1) * 128],
                                                 identity=C["identb"][:nt, :nt]), [srcT.b, C["identb"].b], [bk.b], inc=(hp == 7))
                P.op("act", "activation", dict(out=stg[:, :, :nt], in_=bk.bf.rearrange("p (k t) -> p k t", k=8)[:, :, :nt],
                                               func=AF.Copy), [bk.b], [stg.b])
                P.dma("pool", dram_ap, stg[:, :, :nt], dbuf, stg.b, "st")

            TS = {}

            def stA(i):
                tl = tiles[i]
                nt, r0 = tl["nt"], tl["row0"]
                xin = xinr.next()
                P.dma("sp", xin[:nt, :], src[r0:r0 + nt, :], xin.b, db(src.name, r0), "ld")
                ss = ssr.next()
                junk = xnr.next()
                P.op("act", "activation", dict(out=junk[:nt, :], in_=xin[:nt, :], func=AF.Square, accum_out=ss[:nt, 0:1]),
                     [xin.b], [junk.b, ss.b])
                P.op("act", "activation", dict(out=ss[:nt, 2:3], in_=ss[:nt, 0:1], func=AF.Ln, scale=1.0 / D, bias=EPS), [ss.b], [ss.b])
                P.op("act", "activation", dict(out=ss[:nt, 3:4], in_=ss[:nt, 2:3], func=AF.Exp, scale=-0.5), [ss.b], [ss.b])
                xn = xnr.next()
                P.op("dve", "scalar_tensor_tensor", dict(out=xn[:nt, :], in0=xin[:nt, :], scalar=ss[:nt, 3:4], in1=gB[:nt, :],
                                                         op0=ALU.mult, op1=ALU.mult), [xin.b, ss.b, gB.b], [xn.b])
                TS[i] = dict(xn=xn)

            def stB(i):
                tl = tiles[i]
                nt = tl["nt"]
                xn = TS[i]["xn"]
                xnT = xnTs.next()
                bk = pb()
                for kc in range(8):
                    P.op("pe", "transpose", dict(out=bk.bf[:, kc * 128:kc * 128 + nt], in_=xn[:nt, kc * 128:(kc + 1) * 128],
                                                 identity=C["identb"][:nt, :nt]), [xn.b, C["identb"].b], [bk.b], inc=(kc == 7))
                P.op("act", "activation", dict(out=xnT[:, :, :nt], in_=bk.bf.rearrange("p (k t) -> p k t", k=8)[:, :, :nt], func=AF.Copy),
                     [bk.b], [xnT.b])
                TS[i]["xnT"] = xnT

            def stC(i, cs):
                tl = tiles[i]
                nt, r0, grp, b, pos = tl["nt"], tl["row0"], tl["grp"], tl["seq"], tl["pos"]
                d = TS[i]
                xnT = d["xnT"]
                if "kout" not in d:
                    d.update(kout=koutr.next(), vout=voutr.next(), knb=knbr.next(), qnb=qnbr.next(), vb=vbr.next())
                kout, vout, knb, qnb, vb = d["kout"], d["vout"], d["knb"], d["qnb"], d["vb"]
                for c in cs:
                    bk = pb()
                    for kc in range(8):
                        P.op("pe", "matmul", dict(out=bk.f32[:nt, :], lhsT=xnT[:, kc, :nt], rhs=Wqkv[:, kc, c * 512:(c + 1) * 512],
                                                  start=(kc == 0), stop=(kc == 7)), [xnT.b, Wqkv.b], [bk.b], inc=(kc == 7))
                    if c < 4:
                        sq = sqr.next()
                        ssq = ssqr.next()
                        t1 = t1r.next()
                        P.op("act", "activation", dict(out=sq[:nt, :], in_=bk.f32[:nt, :], func=AF.Square), [bk.b], [sq.b])
                        P.op("dve", "tensor_reduce", dict(out=ssq[:nt, 0, :], in_=sq[:nt, :].rearrange("p (h d) -> p h d", h=8),
                                                          axis=AX.X, op=ALU.add), [sq.b], [ssq.b])
                        P.op("act", "activation", dict(out=ssq[:nt, 2, :], in_=ssq[:nt, 0, :], func=AF.Ln, scale=1.0 / AD, bias=EPS),
                             [ssq.b], [ssq.b])
                        P.op("act", "activation", dict(out=ssq[:nt, 3, :], in_=ssq[:nt, 2, :], func=AF.Exp, scale=-0.5),
                             [ssq.b], [ssq.b])
                        P.op("dve", "tensor_tensor", dict(out=t1[:nt, :].rearrange("p (h d) -> p h d", h=8),
                                                          in0=bk.f32[:nt, :].rearrange("p (h d) -> p h d", h=8),
                                                          in1=ssq[:nt, 3, :].unsqueeze(2).to_broadcast([nt, 8, AD]), op=ALU.mult),
                             [bk.b, ssq.b], [t1.b])
                        if c < 2:
                            P.op("dve", "tensor_tensor", dict(out=qnb[:nt, c * 512:(c + 1) * 512].rearrange("p (h d) -> p h d", h=8),
                                                              in0=t1[:nt, :].rearrange("p (h d) -> p h d", h=8),
                                                              in1=qg[:nt, :].unsqueeze(1).to_broadcast([nt, 8, AD]), op=ALU.mult),
                                 [t1.b, qg.b], [qnb.b])
                        else:
                            cc = c - 2
                            P.op("dve", "tensor_tensor", dict(out=kout[:nt, cc * 512:(cc + 1) * 512].rearrange("p (h d) -> p h d", h=8),
                                                              in0=t1[:nt, :].rearrange("p (h d) -> p h d", h=8),
                                                              in1=kg[:nt, :].unsqueeze(1).to_broadcast([nt, 8, AD]), op=ALU.mult),
                                 [t1.b, kg.b], [kout.b])
                            P.op("act", "activation", dict(out=knb[:nt, cc * 512:(cc + 1) * 512], in_=kout[:nt, cc * 512:(cc + 1) * 512],
                                                           func=AF.Copy), [kout.b], [knb.b])
                    else:
                        cc = c - 4
                        P.op("act", "activation", dict(out=vout[:nt, cc * 512:(cc + 1) * 512], in_=bk.f32[:nt, :], func=AF.Copy),
                             [bk.b], [vout.b])
                        P.op("dve", "tensor_copy", dict(out=vb[:nt, cc * 512:(cc + 1) * 512], in_=bk.f32[:nt, :]), [bk.b], [vb.b])
                if 5 in cs:
                    if grp == "p":
                        P.dma("pool", o_pk[j, b, pos:pos + nt, :], kout[:nt, :], db("o_pk", r0), kout.b, "st")
                        P.dma("pool", o_pv[j, b, pos:pos + nt, :], vout[:nt, :], db("o_pv", r0), vout.b, "st")
                        P.dma("pool", Vp[b, pos:pos + nt, :], vb[:nt, :], db("Vp", b), vb.b, "st")
                    else:
                        P.dma("pool", o_sk[j, b, 0:nt, :], kout[:nt, :], db("o_sk", r0), kout.b, "st")
                        P.dma("pool", o_sv[j, b, 0:nt, :], vout[:nt, :], db("o_sv", r0), vout.b, "st")
                        P.dma("pool", Vs[b, 0:nt, :], vb[:nt, :], db("Vs", b), vb.b, "st")

            def stD(i):
                tl = tiles[i]
                nt, grp, b, pos = tl["nt"], tl["grp"], tl["seq"], tl["pos"]
                d = TS.pop(i)
                if grp == "p":
                    transpose_store(d["qnb"], nt, qTr.next(), QTp[b, :, :, pos:pos + nt], db("QTp", b))
                    transpose_store(d["knb"], nt, kTr.next(), KTp[b, :, :, pos:pos + nt], db("KTp", b))
                else:
                    transpose_store(d["qnb"], nt, qTr.next(), QTs[b, :, :, 0:nt], db("QTs", b))
                    transpose_store(d["knb"], nt, kTr.next(), KTs[b, :, :, PAST:PAST + nt], db("KTs", b))

            NTL = len(tiles)
            stA(0)
            if NTL > 1:
                stA(1)
            stB(0)
            for i in range(NTL):
                if i + 2 < NTL:
                    stA(i + 2)
                stC(i, (0, 1, 2))
                if i + 1 < NTL:
                    stB(i + 1)
                stC(i, (3, 4, 5))
                if i >= 1:
                    stD(i - 1)
            stD(NTL - 1)
            for b in range(NS):
                for kb in range(PAST // 128):
                    ck = ckr.next()
                    for hf in range(2):
                        P.dma("pool", ck[:, hf * 512:(hf + 1) * 512], cache_k[j, b, kb * 128:(kb + 1) * 128, hf * 512:(hf + 1) * 512],
                              ck.b, cst, "ld")
                    transpose_store(ck, 128, kTr.next(), KTs[b, :, :, kb * 128:(kb + 1) * 128], db("KTs", b))
            end_stage()

        def stage_sb2(layer, j, src, dst):
            begin_stage()
            Wo = sb([128, 8, D], BF16, "Wo")
            load_w(Wo, sb_out_w[j])
            suin = sb([128, 128], BF16, "suin")
            sutmp = sb([128, 128], BF16, "sutmp")
            onesb = sb([128, 128], BF16, "onesb")
            nonesb = sb([128, 128], BF16, "nonesb")
            maskb = sb([128, 4, 512], BF16, "maskb")
            masksb = sb([128, 32], BF16, "masksb")
            P.dma("pool", sutmp[:], c_tri[:, :], sutmp.b, cst, "ld")
            P.dma("pool", suin[:], c_su[:, :], suin.b, cst, "ld")
            P.dma("pool", sutmp[:], c_ident[:, :], sutmp.b, cst, "ld")
            P.op("dve", "tensor_tensor", dict(out=suin[:], in0=suin[:], in1=sutmp[:], op=ALU.add), [suin.b, sutmp.b], [suin.b])
            P.op("dve", "tensor_scalar", dict(out=suin[:], in0=suin[:], scalar1=-1.0, scalar2=None, op0=ALU.mult), [suin.b], [suin.b])
            P.op("dve", "memset", dict(ap=onesb[:], constant=1.0), [], [onesb.b])
            QTpad = sb([128, 16, 32], BF16, "QTpad")
            zerob = sb([128, 128], BF16, "zerob")
            Sz = sb([128, 512], BF16, "Sz")
            P.op("dve", "memset", dict(ap=zerob[:], constant=0.0), [], [zerob.b])
            P.op("dve", "memset", dict(ap=Sz[:], constant=0.0), [], [Sz.b])
            P.op("dve", "tensor_scalar", dict(out=nonesb[:], in0=sutmp[:], scalar1=-1.0, scalar2=None, op0=ALU.mult),
                 [sutmp.b], [nonesb.b])
            for i in range(4):
                P.dma("pool", maskb[:, i, :], c_mask[:, i, :], maskb.b, cst, "ld")
            P.dma("pool", masksb[:], c_masks[:, :], masksb.b, cst, "ld")
            KLEN = max(SEQ, KPAD)
            KT = sb([128, 8, KLEN], BF16, "KT")
            Vb = sb([128, KLEN // 128, D], BF16, "Vb")
            QWM = 512
            QTr = ring(2, [128, 8, QWM], BF16, "QT")
            OT = sb([128, 8, QWM], BF16, "OT")
            e1r = ring(4, [128, QWM], F32, "e1")
            lgr = ring(6, [128, QWM], BF16, "lg")
            wbr = ring(6, [128, QWM], BF16, "wb")
            Sbfr = [ring(2, [128, QWM], BF16, "Sbf0"), ring(2, [128, QWM], BF16, "Sbf1")]
            xinr = ring(2, [128, D], F32, "xin")
            ystr = ring(2, [128, D], F32, "yst")
            bkR = [hold(), hold()]
            bkOs = [hold(), hold()]

            seqs = [("p", b) for b in range(NB)] + [("s", b) for b in range(NS)]
            for (grp, b) in seqs:
                if grp == "p":
                    base, L, QW = b * SEQ, SEQ, 512
                    for hp in range(8):
                        P.dma("sp", KT[:, hp, :SEQ], KTp[b, :, hp, :], KT.b, db("KTp", b), "ld", par=True)
                    for t4 in range(0, SEQ // 128, 4):
                        P.dma("sp", Vb[:, t4:t4 + 4, :], Vp[b, t4 * 128:(t4 + 4) * 128, :].rearrange("(t p) c -> p t c", p=128),
                              Vb.b, db("Vp", b), "ld", par=True)
                else:
                    base, L, QW = NTP + b * DSEQ, DSEQ, DSEQ
                    nkc = PAST // 128
                    if KPAD > KTOT:
                        P.op("dve", "memset", dict(ap=KT[:, :, KTOT:KPAD], constant=0.0), [], [KT.b])
                    P.op("dve", "memset", dict(ap=Vb[:, nkc, :], constant=0.0), [], [Vb.b])
                    for hp in range(8):
                        P.dma("sp", KT[:, hp, :KTOT], KTs[b, :, hp, :KTOT], KT.b, db("KTs", b), "ld")
                    for t in range(nkc):
                        for hf in range(2):
                            P.dma("pool", Vb[:, t, hf * 512:(hf + 1) * 512],
                                  cache_v[j, b, t * 128:(t + 1) * 128, hf * 512:(hf + 1) * 512], Vb.b, cst, "ld")
                    P.dma("sp", Vb[:DSEQ, nkc, :], Vs[b, :, :], Vb.b, db("Vs", b), "ld")
                if grp == "s":
                    QT = QTr.next()
                    P.dma("sp", QT[:, :, :QW], QTs[b, :, :, :], QT.b, db("QTs", b), "ld")
                    nkb = KPAD // 128
                    WQ = 16 * QW
                    bkO = bkOs[0]
                    P.op("dve", "memset", dict(ap=QTpad[:], constant=0.0), [], [QTpad.b])
                    for hh in range(2):
                        P.op("dve", "tensor_copy", dict(out=QTpad[hh * 64:(hh + 1) * 64, :, :QW].rearrange("p (k two) q -> p k two q", two=2)[:, :, hh, :],
                                                        in_=QT[hh * 64:(hh + 1) * 64, :, :QW]), [QT.b], [QTpad.b])
                    P.op("pe", "matmul", dict(out=bkO.f32[:, :8 * QW], lhsT=zerob[:], rhs=Sz[:, :8 * QW], start=True, stop=False),
                         [zerob.b, Sz.b], [bkO.b])
                    Sprev = None
                    for kb in range(nkb - 1, -1, -1):
                        first = (kb == nkb - 1)
                        lastb = (kb == 0)
                        bkT = pb()
                        P.op("pe", "matmul", dict(out=bkT.f32[:, :WQ], lhsT=zerob[:], rhs=Sz[:, :WQ],
                                                  start=True, stop=False), [zerob.b, Sz.b], [bkT.b], inc=False)
                        for h in range(16):
                            hp = h // 2
                            P.op("pe", "matmul", dict(out=bkT.f32[:, h * QW:(h + 1) * QW], lhsT=KT[:, hp, kb * 128:(kb + 1) * 128],
                                                      rhs=QTpad[:, h, :QW], start=False, stop=False), [KT.b, QTpad.b], [bkT.b], inc=(h == 15))
                        e1 = e1r.next()
                        lg = lgr.next()
                        P.op("act", "activation", dict(out=e1[:, :WQ], in_=bkT.f32[:, :WQ], func=AF.Exp), [bkT.b], [e1.b])
                        P.op("act", "activation", dict(out=lg[:, :WQ], in_=e1[:, :WQ], func=AF.Ln, bias=1.0), [e1.b], [lg.b])
                        if first:
                            P.op("dve", "tensor_tensor", dict(out=lg[:, :WQ].rearrange("p (h q) -> p h q", h=16),
                                                              in0=lg[:, :WQ].rearrange("p (h q) -> p h q", h=16),
                                                              in1=masksb[:, :QW].unsqueeze(1).to_broadcast([128, 16, QW]), op=ALU.mult),
                                 [lg.b, masksb.b], [lg.b])
                        P.op("pe", "matmul", dict(out=bkT.f32[:, :WQ], lhsT=suin[:], rhs=lg[:, :WQ], start=False, stop=first),
                             [suin.b, lg.b], [bkT.b], inc=first)
                        if not first:
                            P.op("pe", "matmul", dict(out=bkT.f32[:, :WQ], lhsT=nonesb[:], rhs=Sprev[:, :WQ], start=False, stop=True),
                                 [nonesb.b, Sprev.b], [bkT.b])
                        if not lastb:
                            P.op("pe", "matmul", dict(out=bkR[0].f32[:, :WQ], lhsT=onesb[:], rhs=lg[:, :WQ], start=first, stop=(kb == 1)),
                                 [onesb.b, lg.b], [bkR[0].b])
                            Sn = Sbfr[0].next()
                            P.op("dve", "tensor_copy", dict(out=Sn[:, :WQ], in_=bkR[0].f32[:, :WQ]), [bkR[0].b], [Sn.b])
                            Sprev = Sn
                        wb = wbr.next()
                        P.op("act", "activation", dict(out=wb[:, :WQ], in_=bkT.f32[:, :WQ], func=AF.Exp), [bkT.b], [wb.b])
                        if first:
                            P.op("dve", "tensor_tensor", dict(out=wb[:, :WQ].rearrange("p (h q) -> p h q", h=16),
                                                              in0=wb[:, :WQ].rearrange("p (h q) -> p h q", h=16),
                                                              in1=masksb[:, :QW].unsqueeze(1).to_broadcast([128, 16, QW]), op=ALU.mult),
                                 [wb.b, masksb.b], [wb.b])
                        for h in range(16):
                            hp, po = h // 2, (h % 2) * 64
                            P.op("pe", "matmul", dict(out=bkO.f32[po:po + 64, hp * QW:(hp + 1) * QW], lhsT=Vb[:, kb, h * 64:(h + 1) * 64],
                                                      rhs=wb[:, h * QW:(h + 1) * QW], start=False, stop=(kb == 0)),
                                 [Vb.b, wb.b], [bkO.b], inc=(h == 15))
                    P.op("act", "activation", dict(out=OT[:, :, :QW], in_=bkO.f32[:, :8 * QW].rearrange("p (k q) -> p k q", k=8),
                                                   func=AF.Copy), [bkO.b], [OT.b])
                    r0 = base
                    nt = QW
                    xin = xinr.next()
                    P.dma("sp", xin[:nt, :], src[r0:r0 + nt, :], xin.b, db(src.name, r0), "ld")
                    y = ystr.next()
                    for c in range(2):
                        bk = pb()
                        for hp in range(8):
                            P.op("pe", "matmul", dict(out=bk.f32[:nt, :], lhsT=OT[:, hp, 0:nt], rhs=Wo[:, hp, c * 512:(c + 1) * 512],
                                                      start=(hp == 0), stop=(hp == 7)), [OT.b, Wo.b], [bk.b], inc=(hp == 7))
                        P.op("dve", "tensor_tensor", dict(out=y[:nt, c * 512:(c + 1) * 512], in0=bk.f32[:nt, :],
                                                          in1=xin[:nt, c * 512:(c + 1) * 512], op=ALU.add), [bk.b, xin.b], [y.b])
                    P.dma("pool", dst[r0:r0 + nt, :], y[:nt, :], db(dst.name, r0), y.b, "st")
                    continue
                for q0 in range(0, L, QW):
                    QT = QTr.next()
                    if grp == "p":
                        P.dma("sp", QT[:, :, :QW], QTp[b, :, :, q0:q0 + QW], QT.b, db("QTp", b), "ld")
                        nkb = (q0 + QW) // 128
                        def mask_of(kb, q0=q0):
                            i = kb - q0 // 128
                            return maskb[:, i, :] if i >= 0 else None
                    else:
                        P.dma("sp", QT[:, :, :QW], QTs[b, :, :, :], QT.b, db("QTs", b), "ld")
                        nkb = KPAD // 128
                        def mask_of(kb, nkb=nkb):
                            return masksb[:, :] if kb == nkb - 1 else None
                    its = []
                    for hp in range(8):
                        for kb in range(nkb - 1, -1, -1):
                            for hh in range(2):
                                its.append((hp, kb, hh))
                    st = {}

                    def col0_of(kb, q0=q0, grp=grp):
                        if grp != "p" or not TRIM:
                            return 0
                        return max(0, (kb - q0 // 128) * 128)

                    def stageA(it):
                        hp, kb, hh = it
                        po = hh * 64
                        a = col0_of(kb)
                        bkT = pb()
                        P.op("pe", "matmul", dict(out=bkT.f32[:, a:QW], lhsT=KT[po:po + 64, hp, kb * 128:(kb + 1) * 128],
                                                  rhs=QT[po:po + 64, hp, a:QW], start=True, stop=False), [KT.b, QT.b], [bkT.b])
                        e1 = e1r.next()
                        P.op("act", "activation", dict(out=e1[:, a:QW], in_=bkT.f32[:, a:QW], func=AF.Exp), [bkT.b], [e1.b])
                        lg = lgr.next()
                        P.op("act", "activation", dict(out=lg[:, a:QW], in_=e1[:, a:QW], func=AF.Ln, bias=1.0), [e1.b], [lg.b])
                        m = mask_of(kb)
                        if m is not None:
                            a2 = min(QW, a + 128)
                            P.op("dve", "tensor_tensor", dict(out=lg[:, a:a2], in0=lg[:, a:a2], in1=m[:, a:a2], op=ALU.mult),
                                 [lg.b, maskb.b, masksb.b], [lg.b])
                        st[it] = dict(bkT=bkT, lg=lg)

                    def stageA2(it0, it1):
                        pre = []
                        for it in (it0, it1):
                            hp, kb, hh = it
                            po = hh * 64
                            a = col0_of(kb)
                            bkT = pb()
                            P.op("pe", "matmul", dict(out=bkT.f32[:, a:QW], lhsT=KT[po:po + 64, hp, kb * 128:(kb + 1) * 128],
                                                      rhs=QT[po:po + 64, hp, a:QW], start=True, stop=False), [KT.b, QT.b], [bkT.b])
                            pre.append((it, bkT, a))
                        e1s = []
                        for (it, bkT, a) in pre:
                            e1 = e1r.next()
                            P.op("act", "activation", dict(out=e1[:, a:QW], in_=bkT.f32[:, a:QW], func=AF.Exp), [bkT.b], [e1.b])
                            e1s.append(e1)
                        for (it, bkT, a), e1 in zip(pre, e1s):
                            hp, kb, hh = it
                            lg = lgr.next()
                            P.op("act", "activation", dict(out=lg[:, a:QW], in_=e1[:, a:QW], func=AF.Ln, bias=1.0), [e1.b], [lg.b])
                            m = mask_of(kb)
                            if m is not None:
                                a2 = min(QW, a + 128)
                                P.op("dve", "tensor_tensor", dict(out=lg[:, a:a2], in0=lg[:, a:a2], in1=m[:, a:a2], op=ALU.mult),
                                     [lg.b, maskb.b, masksb.b], [lg.b])
                            st[it] = dict(bkT=bkT, lg=lg)

                    def stageB(it):
                        hp, kb, hh = it
                        d = st[it]
                        bkT, lg = d["bkT"], d["lg"]
                        first = (kb == nkb - 1)
                        lastb = (kb == 0)
                        a = col0_of(kb)
                        P.op("pe", "matmul", dict(out=bkT.f32[:, a:QW], lhsT=suin[:], rhs=lg[:, a:QW], start=False, stop=first),
                             [suin.b, lg.b], [bkT.b], inc=first)
                        if not first:
                            Sbf = d["Sbf"] = st[(hp, kb + 1, hh)]["Snext"]
                            P.op("pe", "matmul", dict(out=bkT.f32[:, a:QW], lhsT=nonesb[:], rhs=Sbf[:, a:QW], start=False, stop=True),
                                 [nonesb.b, Sbf.b], [bkT.b])
                        if not lastb:
                            P.op("pe", "matmul", dict(out=bkR[hh].f32[:, a:QW], lhsT=onesb[:], rhs=lg[:, a:QW], start=first,
                                                      stop=(kb == 1)), [onesb.b, lg.b], [bkR[hh].b])
                            Sn = Sbfr[hh].next()
                            P.op("dve", "tensor_copy", dict(out=Sn[:, a:QW], in_=bkR[hh].f32[:, a:QW]), [bkR[hh].b], [Sn.b])
                            an = col0_of(kb - 1)
                            if an < a:
                                P.op("dve", "memset", dict(ap=Sn[:, an:a], constant=0.0), [], [Sn.b])
                            d["Snext"] = Sn
                        wb = wbr.next()
                        P.op("act", "activation", dict(out=wb[:, a:QW], in_=bkT.f32[:, a:QW], func=AF.Exp), [bkT.b], [wb.b])
                        m = mask_of(kb)
                        if m is not None:
                            a2 = min(QW, a + 128)
                            P.op("dve", "tensor_tensor", dict(out=wb[:, a:a2], in0=wb[:, a:a2], in1=m[:, a:a2], op=ALU.mult),
                                 [wb.b, maskb.b, masksb.b], [wb.b])
                        d["wb"] = wb

                    def stageC(it):
                        hp, kb, hh = it
                        d = st[it]
                        po = hh * 64
                        h = hp * 2 + hh
                        a = col0_of(kb)
                        bkO = bkOs[hp % 2]
                        P.op("pe", "matmul", dict(out=bkO.f32[po:po + 64, a:QW], lhsT=Vb[:, kb, h * 64:(h + 1) * 64],
                                                  rhs=d["wb"][:, a:QW], start=(kb == nkb - 1), stop=(kb == 0)),
                             [Vb.b, d["wb"].b], [bkO.b])
                        if kb == 0 and hh == 1:
                            P.op("act", "activation", dict(out=OT[:, hp, :QW], in_=bkO.f32[:, :QW], func=AF.Copy), [bkO.b], [OT.b])

                    n = len(its) // 2
                    for step in range(n + 2):
                        if step < n:
                            stageA2(its[2 * step], its[2 * step + 1])
                        if 0 <= step - 1 < n:
                            stageB(its[2 * (step - 1)])
                            stageB(its[2 * (step - 1) + 1])
                        if 0 <= step - 2 < n:
                            stageC(its[2 * (step - 2)])
                            stageC(its[2 * (step - 2) + 1])
                    for i0 in range(0, QW, 128):
                        nt = min(128, QW - i0)
                        r0 = base + q0 + i0
                        xin = xinr.next()
                        P.dma("sp", xin[:nt, :], src[r0:r0 + nt, :], xin.b, db(src.name, r0), "ld")
                        y = ystr.next()
                        for c in range(2):
                            bk = pb()
                            for hp in range(8):
                                P.op("pe", "matmul", dict(out=bk.f32[:nt, :], lhsT=OT[:, hp, i0:i0 + nt], rhs=Wo[:, hp, c * 512:(c + 1) * 512],
                                                          start=(hp == 0), stop=(hp == 7)), [OT.b, Wo.b], [bk.b], inc=(hp == 7))
                            P.op("dve", "tensor_tensor", dict(out=y[:nt, c * 512:(c + 1) * 512], in0=bk.f32[:nt, :],
                                                              in1=xin[:nt, c * 512:(c + 1) * 512], op=ALU.add), [bk.b, xin.b], [y.b])
                        P.dma("pool", dst[r0:r0 + nt, :], y[:nt, :], db(dst.name, r0), y.b, "st")
            for bk in bkR + bkOs:
                release(bk)
            end_stage()

        src = x_all
        for layer in range(DEPTH):
            j = layer // 2
            last = (layer == DEPTH - 1)
            if only is not None and "mix" not in only:
                stage_copy(src, xs)
            elif layer % 2 == 0:
                stage_ssd1(layer, j, src)
                stage_ssd2(layer, j, src, xs)
            else:
                stage_sb1(layer, j, src)
                stage_sb2(layer, j, src, xs)
            src = xs
            stage_mlp(layer, xs, y_all if last else xs)
        P.barrier()
        P.emit()
    return nc


def _consts():
    i = np.arange(128)
    ident = np.eye(128, dtype=np.float32)
    tri = (i[:, None] <= i[None, :]).astype(np.float32)
    su = (i[:, None] > i[None, :]).astype(np.float32)
    q = np.arange(512)
    mask = np.stack([(i[:, None] + 128 * k < q[None, :]) for k in range(4)], axis=1).astype(np.float32)
    qs = np.arange(32)
    masks = ((i[:, None] < qs[None, :]) & (i[:, None] < 32)).astype(np.float32)
    return dict(c_ident=ident, c_tri=tri, c_su=su, c_mask=np.ascontiguousarray(mask), c_masks=masks)


def make_in_maps(inp, n_cores, NB, SEQ, NS, DSEQ, PAST, DEPTH):
    N_SSD = (DEPTH + 1) // 2
    N_SB = DEPTH // 2
    f = lambda a: np.ascontiguousarray(np.asarray(a, dtype=np.float32))
    shared = dict(
        norm_mix=f(inp["norm_mix"]), norm_mlp=f(inp["norm_mlp"]), ssd_in_w=f(inp["ssd_in_w"]),
        ssd_conv_w=f(np.asarray(inp["ssd_conv_w"]).reshape(N_SSD, 4, 32, 128).transpose(0, 3, 2, 1)),
        ssd_conv_b=f(np.asarray(inp["ssd_conv_b"]).reshape(N_SSD, 32, 128).transpose(0, 2, 1)),
        ssd_dt_bias=f(inp["ssd_dt_bias"]), ssd_a_log=f(inp["ssd_a_log"]), ssd_d=f(inp["ssd_d"]),
        ssd_norm_w=f(inp["ssd_norm_w"]), ssd_out_w=f(inp["ssd_out_w"]),
        mlp_up=f(inp["mlp_up"]), mlp_down=f(inp["mlp_down"]))
    if N_SB > 0:
        shared.update(sb_qkv_w=f(inp["sb_qkv_w"]), sb_q_gain=f(inp["sb_q_gain"]), sb_k_gain=f(inp["sb_k_gain"]),
                      sb_out_w=f(inp["sb_out_w"]))
    else:
        shared.update(sb_qkv_w=np.zeros((1, D, 3 * D), np.float32), sb_q_gain=np.zeros((1, AD), np.float32),
                      sb_k_gain=np.zeros((1, AD), np.float32), sb_out_w=np.zeros((1, D, D), np.float32))
    shared.update(_consts())
    xp = np.asarray(inp["x_prompt"], dtype=np.float32)
    xsm = np.asarray(inp["x_sample"], dtype=np.float32)
    sssm = np.asarray(inp["state_ssm"], dtype=np.float32)
    sconv = np.asarray(inp["state_conv"], dtype=np.float32)
    ck = np.asarray(inp["cache_k"], dtype=np.float32)
    cv = np.asarray(inp["cache_v"], dtype=np.float32)
    maps = []
    for c in range(n_cores):
        m = dict(shared)
        m["x_all"] = f(np.concatenate([xp[c * NB:(c + 1) * NB].reshape(NB * SEQ, D),
                                       xsm[c * NS:(c + 1) * NS].reshape(NS * DSEQ, D)], axis=0))
        m["state_ssm"] = f(sssm[:, c * NS:(c + 1) * NS].reshape(N_SSD, NS, NH * HP, NST))
        m["state_conv"] = f(sconv[:, c * NS:(c + 1) * NS].reshape(N_SSD, NS, 3, 32, 128).transpose(0, 1, 4, 3, 2))
        if N_SB > 0:
            m["cache_k"] = f(ck[:, c * NS:(c + 1) * NS].reshape(N_SB, NS, PAST, D))
            m["cache_v"] = f(cv[:, c * NS:(c + 1) * NS].reshape(N_SB, NS, PAST, D))
        else:
            m["cache_k"] = np.zeros((1, NS, PAST, D), np.float32)
            m["cache_v"] = np.zeros((1, NS, PAST, D), np.float32)
        maps.append(m)
    return maps


def assemble(results, n_cores, NB, SEQ, NS, DSEQ, PAST, DEPTH):
    N_SSD = (DEPTH + 1) // 2
    N_SB = DEPTH // 2
    cat = lambda k, ax: np.concatenate([np.asarray(r[k]) for r in results], axis=ax)
    y_all = [np.asarray(r["y_all"]) for r in results]
    y_p = np.concatenate([y[:NB * SEQ].reshape(NB, SEQ, D) for y in y_all], axis=0)
    y_s = np.concatenate([y[NB * SEQ:].reshape(NS, DSEQ, D) for y in y_all], axis=0)
    pssm = cat("o_pssm", 1).reshape(N_SSD, n_cores * NB, NH, HP, NST)
    sssm = cat("o_sssm", 1).reshape(N_SSD, n_cores * NS, NH, HP, NST)
    pconv = np.ascontiguousarray(cat("o_pconv", 1).transpose(0, 1, 4, 3, 2)).reshape(N_SSD, n_cores * NB, 3, CONVD)
    sconv = np.ascontiguousarray(cat("o_sconv", 1).transpose(0, 1, 4, 3, 2)).reshape(N_SSD, n_cores * NS, 3, CONVD)
    pk = cat("o_pk", 1)[:N_SB].reshape(N_SB, n_cores * NB, SEQ, AH, AD)
    pv = cat("o_pv", 1)[:N_SB].reshape(N_SB, n_cores * NB, SEQ, AH, AD)
    sk = cat("o_sk", 1)[:N_SB].reshape(N_SB, n_cores * NS, DSEQ, AH, AD)
    sv = cat("o_sv", 1)[:N_SB].reshape(N_SB, n_cores * NS, DSEQ, AH, AD)
    outs = (y_p, y_s, pssm, pconv, pk, pv, sssm, sconv, sk, sv)
    return tuple(np.ascontiguousarray(o, dtype=np.float32) for o in outs)


def kernel(**inputs):
    n = 8
    NB, SEQ, NS, DSEQ, PAST, DEPTH = 4, 2048, 2, 32, 1024, 4
    nc = build(NB, SEQ, NS, DSEQ, PAST, DEPTH)
    maps = make_in_maps(inputs, n, NB, SEQ, NS, DSEQ, PAST, DEPTH)
    res = run_bass_kernel_spmd(nc, maps, core_ids=list(range(n)))
    return assemble(res.results, n, NB, SEQ, NS, DSEQ, PAST, DEPTH)
```

```python
import numpy as np
from contextlib import ExitStack
import concourse.bass as bass
import concourse.mybir as mybir
from concourse.bass_utils import run_bass_kernel_spmd

F32 = mybir.dt.float32
BF16 = mybir.dt.bfloat16
AF = mybir.ActivationFunctionType
ALU = mybir.AluOpType
AX = mybir.AxisListType

D = 1024
DFF = 4096
DIN = 2048
NH = 32
HP = 64
NG = 8
NST = 128
CONVD = 4096
DPROJ = 6176
AH = 16
AD = 64
EPS = 1e-6
import os as _os
TRIM = _os.environ.get('K_TRIM', '1') == '1'


class Buf:
    __slots__ = ("name", "lw", "rd")

    def __init__(s, name):
        s.name = name
        s.lw = None
        s.rd = {}


class Prog:
    ENG = ("pe", "act", "dve", "pool", "sp")

    def __init__(s, nc, es):
        s.nc = nc
        s.es = es
        s.sems = {}
        s.cnt = {}
        s.ops = {e: [] for e in s.ENG}
        s.waited = {e: {} for e in s.ENG}
        s.nbuf = 0
        s.nops = 0
        s.free = []
        s.stage_keys = []

    def buf(s, name=None):
        s.nbuf += 1
        return Buf((name or "b") + "_%d" % s.nbuf)

    def sem(s, key):
        if key not in s.sems:
            if key in s.ENG or not s.free:
                s.sems[key] = s.es.enter_context(s.nc.semaphore("s%d" % len(s.sems)))
                s.cnt[key] = 0
            else:
                h, c0 = s.free.pop()
                s.sems[key] = h
                s.cnt[key] = c0
            if key not in s.ENG:
                s.stage_keys.append(key)
        return s.sems[key]

    def retire(s):
        for k in s.stage_keys:
            s.free.append((s.sems[k], s.cnt[k]))
        s.stage_keys = []

    def _waits(s, eng, deps):
        w = s.waited[eng]
        best = {}
        for d in deps:
            if d is None:
                continue
            k, v = d
            if w.get(k, 0) >= v:
                continue
            if best.get(k, 0) < v:
                best[k] = v
        out = []
        for k, v in best.items():
            w[k] = v
            out.append((k, v))
        return out

    def op(s, eng, name, kw, reads=(), writes=(), inc=True):
        deps = []
        for b in reads:
            deps.append(b.lw)
        for b in writes:
            deps.append(b.lw)
            for k, v in b.rd.items():
                if k != eng:
                    deps.append((k, v))
        if eng == "pe":
            deps = [d for d in deps if d is not None and d[0] != "pe"]
        waits = s._waits(eng, deps)
        s.sem(eng)
        val = s.cnt[eng] + 1
        if inc:
            s.cnt[eng] = val
        for b in reads:
            if b.rd.get(eng, 0) < val:
                b.rd[eng] = val
        for b in writes:
            b.lw = (eng, val)
            b.rd = {}
        s.ops[eng].append((waits, name, kw, (eng, 1) if inc else None))
        s.nops += 1

    def dma(s, q, out_ap, in_ap, dst, src, kind, par=False, **kw):
        key = ("ld:" + dst.name) if kind == "ld" else ("st:" + src.name)
        s.sem(key)
        deps = [src.lw] + list(dst.rd.items())
        if par and kind == "ld" and dst.lw is not None and dst.lw[0] == key:
            pass
        else:
            deps.append(dst.lw)
            if s.cnt[key] > 0:
                deps.append((key, s.cnt[key]))
        waits = s._waits(q, deps)
        val = s.cnt[key] + 16
        s.cnt[key] = val
        if src.rd.get(key, 0) < val:
            src.rd[key] = val
        dst.lw = (key, val)
        dst.rd = {}
        k2 = dict(out=out_ap, in_=in_ap)
        k2.update(kw)
        s.ops[q].append((waits, "dma_start", k2, (key, 16)))
        s.nops += 1

    def barrier(s):
        for e in s.ENG:
            deps = [(k, v) for k, v in s.cnt.items() if v > 0 and k != e]
            waits = s._waits(e, deps)
            if waits:
                s.ops[e].append((waits, None, None, None))

    def emit(s):
        nc = s.nc
        with nc.Block() as block:
            def mk(eng):
                def f(e):
                    for waits, name, kw, inc in s.ops[eng]:
                        for k, v in waits:
                            e.wait_ge(s.sems[k], v)
                        if name is not None:
                            ins = getattr(e, name)(**kw)
                            if inc:
                                ins.then_inc(s.sems[inc[0]], inc[1])
                return f
            block.tensor(mk("pe"))
            block.scalar(mk("act"))
            block.vector(mk("dve"))
            block.gpsimd(mk("pool"))
            block.sync(mk("sp"))


class T:
    __slots__ = ("t", "b")

    def __init__(s, t, b):
        s.t = t
        s.b = b

    def __getitem__(s, k):
        return s.t[k]


class Ring:
    def __init__(s, items):
        s.items = items
        s.i = 0

    def next(s):
        it = s.items[s.i % len(s.items)]
        s.i += 1
        return it


class Bank:
    __slots__ = ("f32", "bf", "b")

    def __init__(s, t, b):
        s.f32 = t
        s.bf = t[:].bitcast(BF16)
        s.b = b


def build(NB, SEQ, NS, DSEQ, PAST, DEPTH, only=None):
    nc = bass.Bass("TRN2", target_bir_lowering=False)
    NTP = NB * SEQ
    NTS = NS * DSEQ
    NTOK = NTP + NTS
    N_SSD = (DEPTH + 1) // 2
    N_SB = DEPTH // 2
    NSB1 = max(N_SB, 1)
    KTOT = PAST + DSEQ
    KPAD = ((KTOT + 127) // 128) * 128

    def di(n, sh, dt=F32):
        return nc.dram_tensor(n, list(sh), dt, kind="ExternalInput").ap()

    def do(n, sh):
        return nc.dram_tensor(n, list(sh), F32, kind="ExternalOutput").ap()

    def dx(n, sh, dt=F32):
        return nc.dram_tensor(n, list(sh), dt).ap()

    x_all = di("x_all", [NTOK, D])
    state_ssm = di("state_ssm", [N_SSD, NS, NH * HP, NST])
    state_conv = di("state_conv", [N_SSD, NS, 128, 32, 3])
    cache_k = di("cache_k", [NSB1, NS, PAST, D])
    cache_v = di("cache_v", [NSB1, NS, PAST, D])
    norm_mix = di("norm_mix", [DEPTH, D])
    norm_mlp = di("norm_mlp", [DEPTH, D])
    ssd_in_w = di("ssd_in_w", [N_SSD, D, DPROJ])
    ssd_conv_w = di("ssd_conv_w", [N_SSD, 128, 32, 4])
    ssd_conv_b = di("ssd_conv_b", [N_SSD, 128, 32])
    ssd_dt_bias = di("ssd_dt_bias", [N_SSD, NH])
    ssd_a_log = di("ssd_a_log", [N_SSD, NH])
    ssd_d = di("ssd_d", [N_SSD, NH])
    ssd_norm_w = di("ssd_norm_w", [N_SSD, DIN])
    ssd_out_w = di("ssd_out_w", [N_SSD, DIN, D])
    sb_qkv_w = di("sb_qkv_w", [NSB1, D, 3 * D])
    sb_q_gain = di("sb_q_gain", [NSB1, AD])
    sb_k_gain = di("sb_k_gain", [NSB1, AD])
    sb_out_w = di("sb_out_w", [NSB1, D, D])
    mlp_up = di("mlp_up", [DEPTH, D, DFF])
    mlp_down = di("mlp_down", [DEPTH, DFF, D])
    c_ident = di("c_ident", [128, 128])
    c_tri = di("c_tri", [128, 128])
    c_su = di("c_su", [128, 128])
    c_mask = di("c_mask", [128, 4, 512])
    c_masks = di("c_masks", [128, 32])

    y_all = do("y_all", [NTOK, D])
    o_pssm = do("o_pssm", [N_SSD, NB, NH * HP, NST])
    o_pconv = do("o_pconv", [N_SSD, NB, 128, 32, 3])
    o_pk = do("o_pk", [NSB1, NB, SEQ, D])
    o_pv = do("o_pv", [NSB1, NB, SEQ, D])
    o_sssm = do("o_sssm", [N_SSD, NS, NH * HP, NST])
    o_sconv = do("o_sconv", [N_SSD, NS, 128, 32, 3])
    o_sk = do("o_sk", [NSB1, NS, DSEQ, D])
    o_sv = do("o_sv", [NSB1, NS, DSEQ, D])

    xs = dx("xs", [NTOK, D])
    Ys = dx("Ys", [NTOK, DIN])
    QTp = dx("QTp", [NB, 128, 8, SEQ], BF16)
    KTp = dx("KTp", [NB, 128, 8, SEQ], BF16)
    Vp = dx("Vp", [NB, SEQ, D], BF16)
    QTs = dx("QTs", [NS, 128, 8, DSEQ], BF16)
    KTs = dx("KTs", [NS, 128, 8, KPAD], BF16)
    Vs = dx("Vs", [NS, DSEQ, D], BF16)

    top = ExitStack()
    with top:
        P = Prog(nc, top)
        dbufs = {}

        def db(name, idx=0):
            k = (name, idx)
            if k not in dbufs:
                dbufs[k] = P.buf("d_" + name)
            return dbufs[k]

        cst = db("const")

        banks = [Bank(top.enter_context(nc.psum_tensor("ps%d" % i, [128, 512], F32)), P.buf("ps%d" % i))
                 for i in range(8)]
        bstate = {"i": 0, "held": set()}

        def pb():
            while True:
                i = bstate["i"] % 8
                bstate["i"] += 1
                if i not in bstate["held"]:
                    return banks[i]

        def hold():
            b = pb()
            bstate["held"].add(banks.index(b))
            return b

        def release(b):
            bstate["held"].discard(banks.index(b))

        cur = {"es": None, "n": 0}

        def sb(shape, dt, name=None):
            cur["n"] += 1
            nm = (name or "t") + "_%d" % cur["n"]
            t = cur["es"].enter_context(nc.sbuf_tensor(nm, list(shape), dt))
            return T(t, P.buf(nm))

        def ring(n, shape, dt, name=None):
            return Ring([sb(shape, dt, name) for _ in range(n)])

        tiles = []
        for b in range(NB):
            for i in range(SEQ // 128):
                tiles.append(dict(row0=b * SEQ + i * 128, nt=128, grp="p", seq=b, pos=i * 128))
        for b in range(NS):
            tiles.append(dict(row0=NTP + b * DSEQ, nt=DSEQ, grp="s", seq=b, pos=0))

        def load_consts():
            c = {}
            c["identb"] = sb([128, 128], BF16, "identb")
            P.dma("pool", c["identb"][:], c_ident[:, :], c["identb"].b, cst, "ld")
            return c

        def front(C, src, tl, gB, dstT, col0, rings):
            nt = tl["nt"]
            r0 = tl["row0"]
            xin = rings["xin"].next()
            P.dma("sp", xin[:nt, :], src[r0:r0 + nt, :], xin.b, db(src.name, r0), "ld")
            ss = rings["ss"].next()
            junk = rings["junk"].next()
            P.op("act", "activation", dict(out=junk[:nt, :], in_=xin[:nt, :], func=AF.Square, accum_out=ss[:nt, 0:1]),
                 [xin.b], [junk.b, ss.b])
            P.op("act", "activation", dict(out=ss[:nt, 2:3], in_=ss[:nt, 0:1], func=AF.Ln, scale=1.0 / D, bias=EPS),
                 [ss.b], [ss.b])
            P.op("act", "activation", dict(out=ss[:nt, 3:4], in_=ss[:nt, 2:3], func=AF.Exp, scale=-0.5), [ss.b], [ss.b])
            xn = rings["xn"].next()
            P.op("dve", "scalar_tensor_tensor", dict(out=xn[:nt, :], in0=xin[:nt, :], scalar=ss[:nt, 3:4], in1=gB[:nt, :],
                                                     op0=ALU.mult, op1=ALU.mult), [xin.b, ss.b, gB.b], [xn.b])
            bk = pb()
            for kc in range(8):
                P.op("pe", "transpose", dict(out=bk.bf[:, kc * 128:kc * 128 + nt], in_=xn[:nt, kc * 128:(kc + 1) * 128],
                                             identity=C["identb"][:nt, :nt]), [xn.b, C["identb"].b], [bk.b], inc=(kc == 7))
            P.op("act", "activation", dict(out=dstT[:, :, col0:col0 + nt],
                                           in_=bk.bf.rearrange("p (k t) -> p k t", k=8)[:, :, :nt], func=AF.Copy),
                 [bk.b], [dstT.b])
            return xin

        def load_w(dst, dram_ap, rows_per_dma=128):
            K, N = dram_ap.shape
            for kc in range(K // 128):
                for n0 in range(0, N, 2048):
                    n1 = min(N, n0 + 2048)
                    P.dma("pool", dst[:, kc, n0:n1], dram_ap[kc * 128:(kc + 1) * 128, n0:n1], dst.b, cst, "ld", par=True)

        def load_bcast(dst, dram_1d):
            P.dma("sp", dst[:], dram_1d.partition_broadcast(128), dst.b, cst, "ld")

        def begin_stage():
            P.barrier()
            cur["es"] = ExitStack()
            cur["es"].__enter__()

        def end_stage():
            P.barrier()
            P.retire()
            cur["es"].__exit__(None, None, None)
            cur["es"] = None

        def run_streams(items, make_gen, width=2, stagger=0):
            active = []
            free_lanes = list(range(width))
            it = iter(items)
            pending = [True]
            first = [True]

            def start_more():
                while free_lanes and pending[0]:
                    try:
                        x = next(it)
                    except StopIteration:
                        pending[0] = False
                        break
                    lane = free_lanes.pop(0)
                    g = make_gen(x, lane)
                    active.append((lane, g))
                    if first[0]:
                        first[0] = False
                        for _ in range(stagger):
                            try:
                                next(g)
                            except StopIteration:
                                active.remove((lane, g))
                                free_lanes.append(lane)
                                break
            while True:
                start_more()
                if not active:
                    break
                for (lane, g) in list(active):
                    try:
                        next(g)
                    except StopIteration:
                        active.remove((lane, g))
                        free_lanes.append(lane)

        def stage_copy(src, dst):
            begin_stage()
            rg = ring(2, [128, D], F32, "cp")
            for tl in tiles:
                t = rg.next()
                nt, r0 = tl["nt"], tl["row0"]
                P.dma("sp", t[:nt, :], src[r0:r0 + nt, :], t.b, db(src.name, r0), "ld")
                P.dma("pool", dst[r0:r0 + nt, :], t[:nt, :], db(dst.name, r0), t.b, "st")
            end_stage()

        def stage_mlp(layer, src, dst):
            begin_stage()
            C = load_consts()
            Wup = sb([128, 8, DFF], BF16, "Wup")
            Wdn = sb([128, 32, D], BF16, "Wdn")
            load_w(Wup, mlp_up[layer])
            load_w(Wdn, mlp_down[layer])
            gB = sb([128, D], F32, "gB")
            load_bcast(gB, norm_mlp[layer])
            rings = dict(xin=ring(4, [128, D], F32, "xin"), ss=ring(6, [128, 4], F32, "ss"),
                         junk=ring(1, [128, D], BF16, "junk"), xn=ring(4, [128, D], BF16, "xn"))
            xnTs = ring(2, [128, 8, 256], BF16, "xnT")
            hT = sb([128, 32, 256], BF16, "hT")
            rr = ring(2, [128, 256], F32, "relu")
            yst = ring(2, [128, D], F32, "yst")
            macros = []
            curm = []
            tot = 0
            for tl in tiles:
                if tot + tl["nt"] > 256 or (curm and curm[-1]["grp"] != tl["grp"]):
                    macros.append(curm)
                    curm = []
                    tot = 0
                curm.append(tl)
                tot += tl["nt"]
            if curm:
                macros.append(curm)

            def do_front_a(m):
                xnT = xnTs.next()
                col = 0
                ent = []
                for tl in m:
                    nt, r0 = tl["nt"], tl["row0"]
                    xin = rings["xin"].next()
                    P.dma("sp", xin[:nt, :], src[r0:r0 + nt, :], xin.b, db(src.name, r0), "ld")
                    ss = rings["ss"].next()
                    junk = rings["junk"].next()
                    P.op("act", "activation", dict(out=junk[:nt, :], in_=xin[:nt, :], func=AF.Square, accum_out=ss[:nt, 0:1]),
                         [xin.b], [junk.b, ss.b])
                    P.op("act", "activation", dict(out=ss[:nt, 2:3], in_=ss[:nt, 0:1], func=AF.Ln, scale=1.0 / D, bias=EPS), [ss.b], [ss.b])
                    P.op("act", "activation", dict(out=ss[:nt, 3:4], in_=ss[:nt, 2:3], func=AF.Exp, scale=-0.5), [ss.b], [ss.b])
                    xn = rings["xn"].next()
                    P.op("dve", "scalar_tensor_tensor", dict(out=xn[:nt, :], in0=xin[:nt, :], scalar=ss[:nt, 3:4], in1=gB[:nt, :],
                                                             op0=ALU.mult, op1=ALU.mult), [xin.b, ss.b, gB.b], [xn.b])
                    ent.append((xin, col, tl, xn))
                    col += nt
                return xnT, ent, col

            def do_front_b(fr):
                xnT, ent, NT = fr
                for (xin, col, tl, xn) in ent:
                    nt = tl["nt"]
                    bk = pb()
                    for kc in range(8):
                        P.op("pe", "transpose", dict(out=bk.bf[:, kc * 128:kc * 128 + nt], in_=xn[:nt, kc * 128:(kc + 1) * 128],
                                                     identity=C["identb"][:nt, :nt]), [xn.b, C["identb"].b], [bk.b], inc=(kc == 7))
                    P.op("act", "activation", dict(out=xnT[:, :, col:col + nt], in_=bk.bf.rearrange("p (k t) -> p k t", k=8)[:, :, :nt],
                                                   func=AF.Copy), [bk.b], [xnT.b])
                return xnT, [(xin, col, tl) for (xin, col, tl, xn) in ent], NT

            nxt = do_front_b(do_front_a(macros[0]))
            for mi, m in enumerate(macros):
                xnT, ent, NT = nxt
                fa = None
                for f in range(32):
                    bk = pb()
                    for kc in range(8):
                        P.op("pe", "matmul", dict(out=bk.f32[:, :NT], lhsT=Wup[:, kc, f * 128:(f + 1) * 128],
                                                  rhs=xnT[:, kc, :NT], start=(kc == 0), stop=(kc == 7)),
                             [Wup.b, xnT.b], [bk.b], inc=(kc == 7))
                    r = rr.next()
                    P.op("act", "activation", dict(out=r[:, :NT], in_=bk.f32[:, :NT], func=AF.Relu), [bk.b], [r.b])
                    P.op("dve", "tensor_tensor", dict(out=hT[:, f, :NT], in0=r[:, :NT], in1=r[:, :NT], op=ALU.mult),
                         [r.b], [hT.b])
                    if f == 11 and mi + 1 < len(macros):
                        fa = do_front_a(macros[mi + 1])
                if fa is not None:
                    nxt = do_front_b(fa)
                for (xin, col, tl) in ent:
                    nt, r0 = tl["nt"], tl["row0"]
                    y = yst.next()
                    for c in range(2):
                        bk = pb()
                        for f in range(32):
                            P.op("pe", "matmul", dict(out=bk.f32[:nt, :], lhsT=hT[:, f, col:col + nt],
                                                      rhs=Wdn[:, f, c * 512:(c + 1) * 512], start=(f == 0), stop=(f == 31)),
                                 [hT.b, Wdn.b], [bk.b], inc=(f == 31))
                        P.op("dve", "tensor_tensor", dict(out=y[:nt, c * 512:(c + 1) * 512], in0=bk.f32[:nt, :],
                                                          in1=xin[:nt, c * 512:(c + 1) * 512], op=ALU.add),
                             [bk.b, xin.b], [y.b])
                    P.dma("pool", dst[r0:r0 + nt, :], y[:nt, :], db(dst.name, r0), y.b, "st")
            end_stage()

        def run_sched(lane_queues):
            events = set()
            n = len(lane_queues)
            curg = [None] * n
            idx = [0] * n
            waiting = [None] * n
            while True:
                progressed = False
                alive = False
                for li, q in enumerate(lane_queues):
                    if curg[li] is None:
                        if idx[li] < len(q):
                            curg[li] = q[idx[li]]()
                            idx[li] += 1
                            waiting[li] = None
                        else:
                            continue
                    alive = True
                    if waiting[li] is not None:
                        if waiting[li] in events:
                            waiting[li] = None
                        else:
                            continue
                    try:
                        r = next(curg[li])
                        progressed = True
                        while isinstance(r, tuple) and r[0] == "set":
                            events.add(r[1])
                            r = next(curg[li])
                        if isinstance(r, tuple) and r[0] == "wait" and r[1] not in events:
                            waiting[li] = r[1]
                    except StopIteration:
                        curg[li] = None
                        progressed = True
                if not alive:
                    break
                assert progressed, "emission scheduler deadlock"

        def stage_ssd1(layer, j, src):
            begin_stage()
            C = load_consts()
            Wx = sb([128, 8, CONVD + NH], BF16, "Wx")
            for kc in range(8):
                for n0 in range(0, CONVD + NH, 2048):
                    n1 = min(CONVD + NH, n0 + 2048)
                    P.dma("pool", Wx[:, kc, n0:n1], ssd_in_w[j, kc * 128:(kc + 1) * 128, DIN + n0:DIN + n1], Wx.b, cst, "ld", par=True)
            gB = sb([128, D], F32, "gB")
            load_bcast(gB, norm_mix[layer])
            trib = sb([128, 128], BF16, "trib")
            sub = sb([128, 128], BF16, "sub")
            onesb = sb([128, 128], BF16, "onesb")
            identf = sb([128, 128], F32, "identf")
            P.dma("pool", trib[:], c_tri[:, :], trib.b, cst, "ld")
            P.dma("pool", sub[:], c_su[:, :], sub.b, cst, "ld")
            P.dma("sp", identf[:], c_ident[:, :], identf.b, cst, "ld")
            P.op("dve", "memset", dict(ap=onesb[:], constant=1.0), [], [onesb.b])
            cw = sb([128, 32, 4], F32, "cw")
            cbias = sb([128, 32], F32, "cbias")
            P.dma("sp", cw[:], ssd_conv_w[j], cw.b, cst, "ld")
            P.dma("sp", cbias[:], ssd_conv_b[j], cbias.b, cst, "ld")
            Abc = sb([128, NH], F32, "Abc")
            dtb = sb([128, NH], F32, "dtb")
            Dbc = sb([128, NH], F32, "Dbc")
            load_bcast(Abc, ssd_a_log[j])
            load_bcast(dtb, ssd_dt_bias[j])
            load_bcast(Dbc, ssd_d[j])
            P.op("act", "activation", dict(out=Abc[:], in_=Abc[:], func=AF.Exp), [Abc.b], [Abc.b])
            P.op("dve", "tensor_scalar", dict(out=Abc[:], in0=Abc[:], scalar1=-1.0, scalar2=None, op0=ALU.mult),
                 [Abc.b], [Abc.b])
            DI = sb([128, NH, 128], BF16, "DI")
            for h in range(NH):
                P.op("dve", "tensor_scalar", dict(out=DI[:, h, :], in0=C["identb"][:], scalar1=Dbc[:, h:h + 1], scalar2=None,
                                                  op0=ALU.mult), [C["identb"].b, Dbc.b], [DI.b])
            xnr = ring(2, [128, D], BF16, "xn")
            rings = dict(xin=ring(2, [128, D], F32, "xin"), ss=ring(4, [128, 4], F32, "ss"), junk=xnr, xn=xnr)
            MTM = 256
            hnT = sb([128, 8, MTM], BF16, "hnT")
            xTs = [sb([128, 16, MTM], BF16, "xT") for _ in range(2)]
            BTs = [sb([128, 8, MTM], BF16, "BT") for _ in range(2)]
            CTs = [sb([128, 8, MTM], BF16, "CT") for _ in range(2)]
            dtraws = [sb([128, 2, NH], F32, "dtraw") for _ in range(2)]
            Rr = ring(3, [128, MTM + 4], BF16, "R")
            dgr = ring(16, [128, 128], BF16, "dg")
            sgr = ring(3, [128, MTM], F32, "sg")
            halo16 = sb([128, 32, 3], BF16, "halo16")
            ncb = sb([128, 32], F32, "ncb")
            P.op("dve", "tensor_scalar", dict(out=ncb[:], in0=cbias[:], scalar1=-1.0, scalar2=None, op0=ALU.mult), [cbias.b], [ncb.b])
            hTq = [sb([128, 512], F32, "hT") for _ in range(4)]
            hTbq = [sb([128, 512], BF16, "hTb") for _ in range(4)]
            halo = sb([128, 32, 3], F32, "halo")
            tmpr = ring(2, [128, 512], F32, "ytmp")
            ystr = ring(2, [128, 1024], F32, "yst")
            lanes = []
            for ln in range(2):
                lanes.append(dict(sm=sb([128, 12, NH], F32, "sm"), adtb=sb([128, NH], BF16, "adtb"),
                                  xtok=sb([128, DIN], BF16, "xtok"), Btok=sb([128, D], BF16, "Btok"),
                                  Wbh=sb([128, 16 * 128], BF16, "Wbh"), Ebh=sb([128, 16 * 128], BF16, "Ebh"),
                                  cbm=sb([128, NG * 128], BF16, "cbm"), xddh=sb([128, 1024], BF16, "xddh")))

            seqs = [("p", b, SEQ) for b in range(NB)] + [("s", b, DSEQ) for b in range(NS)]
            for si, (grp, b, L) in enumerate(seqs):
                base = (b * SEQ) if grp == "p" else (NTP + b * DSEQ)
                CL = min(128, L)
                MT = min(MTM, L)
                NCH = MT // CL
                NM = L // MT
                if grp == "p":
                    for q in range(4):
                        P.op("dve", "memset", dict(ap=hTq[q][:], constant=0.0), [], [hTq[q].b])
                    P.op("dve", "memset", dict(ap=halo[:], constant=0.0), [], [halo.b])
                    P.op("dve", "memset", dict(ap=halo16[:], constant=0.0), [], [halo16.b])
                else:
                    P.dma("sp", halo[:], state_conv[j, b], halo.b, cst, "ld")
                    P.op("dve", "tensor_copy", dict(out=halo16[:], in_=halo[:]), [halo.b], [halo16.b])
                    for hf in range(2):
                        stg = ystr.next()
                        P.dma("sp", stg[:].rearrange("p (t n) -> p t n", t=8),
                              state_ssm[j, b, hf * 1024:(hf + 1) * 1024, :].rearrange("(t p) n -> p t n", p=128), stg.b, cst, "ld")
                        for qq in range(2):
                            q = hf * 2 + qq
                            bk = pb()
                            for i in range(4):
                                t = qq * 4 + i
                                P.op("pe", "transpose", dict(out=bk.f32[:, i * 128:(i + 1) * 128], in_=stg[:, t * 128:(t + 1) * 128],
                                                             identity=identf[:]), [stg.b, identf.b], [bk.b], inc=(i == 3))
                            P.op("act", "activation", dict(out=hTq[q][:], in_=bk.f32[:, :], func=AF.Copy), [bk.b], [hTq[q].b])
                for q in range(4):
                    P.op("act", "activation", dict(out=hTbq[q][:], in_=hTq[q][:], func=AF.Copy), [hTq[q].b], [hTbq[q].b])

                def conv_stream(m, si=si, base=base, CL=CL, MT=MT, NCH=NCH, NM=NM):
                    st = m % 2
                    xT, BT, CT, dtraw = xTs[st], BTs[st], CTs[st], dtraws[st]
                    m0 = m * MT
                    if m >= 2:
                        for ci in range(NCH):
                            yield ("wait", ("cdone", si, (m - 2) * NCH + ci))
                    if m >= 1:
                        yield ("wait", ("convdone", si, m - 1))
                    for i in range(NCH):
                        tl = dict(row0=base + m0 + i * CL, nt=CL)
                        front(C, src, tl, gB, hnT, i * CL, rings)
                        yield
                    for ci in range(NCH):
                        bk = pb()
                        for kc in range(8):
                            P.op("pe", "matmul", dict(out=bk.f32[:CL, :NH], lhsT=hnT[:, kc, ci * CL:(ci + 1) * CL],
                                                      rhs=Wx[:, kc, CONVD:CONVD + NH], start=(kc == 0), stop=(kc == 7)),
                                 [hnT.b, Wx.b], [bk.b], inc=(kc == 7))
                        P.op("act", "activation", dict(out=dtraw[:CL, ci, :], in_=bk.f32[:CL, :NH], func=AF.Copy), [bk.b], [dtraw.b])
                    yield
                    pipe = {}

                    def s0(ct):
                        dgs = []
                        for tap in range(4):
                            dg = dgr.next()
                            P.op("dve", "tensor_scalar", dict(out=dg[:], in0=C["identb"][:], scalar1=cw[:, ct, tap:tap + 1], scalar2=None,
                                                              op0=ALU.mult), [C["identb"].b, cw.b], [dg.b])
                            dgs.append(dg)
                        pipe[ct] = dict(dgs=dgs)

                    def s1(ct):
                        bk = pb()
                        for kc in range(8):
                            P.op("pe", "matmul", dict(out=bk.f32[:, :MT], lhsT=Wx[:, kc, ct * 128:(ct + 1) * 128],
                                                      rhs=hnT[:, kc, :MT], start=(kc == 0), stop=(kc == 7)),
                                 [Wx.b, hnT.b], [bk.b], inc=(kc == 7))
                        R = Rr.next()
                        P.op("dve", "tensor_copy", dict(out=R[:, 0:3], in_=halo16[:, ct, :]), [halo16.b], [R.b])
                        P.op("act", "activation", dict(out=R[:, 3:3 + MT], in_=bk.f32[:, :MT], func=AF.Copy), [bk.b], [R.b])
                        P.op("dve", "tensor_copy", dict(out=halo16[:, ct, :], in_=R[:, MT:MT + 3]), [R.b], [halo16.b])
                        if m == NM - 1:
                            P.op("dve", "tensor_copy", dict(out=halo[:, ct, :], in_=bk.f32[:, MT - 3:MT]), [bk.b], [halo.b])
                        pipe[ct]["R"] = R

                    def s2(ct):
                        R, dgs = pipe[ct]["R"], pipe[ct]["dgs"]
                        bk2 = hold()
                        for tap in range(4):
                            P.op("pe", "matmul", dict(out=bk2.f32[:, :MT], lhsT=dgs[tap][:], rhs=R[:, tap:tap + MT],
                                                      start=(tap == 0), stop=(tap == 3)), [dgs[tap].b, R.b], [bk2.b], inc=(tap == 3))
                        sg = sgr.next()
                        P.op("act", "activation", dict(out=sg[:, :MT], in_=bk2.f32[:, :MT], func=AF.Exp, scale=-1.0,
                                                       bias=ncb[:, ct:ct + 1]), [bk2.b, ncb.b], [sg.b])
                        P.op("act", "activation", dict(out=sg[:, :MT], in_=sg[:, :MT], func=AF.Ln, bias=1.0), [sg.b], [sg.b])
                        P.op("act", "activation", dict(out=sg[:, :MT], in_=sg[:, :MT], func=AF.Exp, scale=-1.0), [sg.b], [sg.b])
                        pipe[ct]["bk2"] = bk2
                        pipe[ct]["sg"] = sg

                    def s3(ct):
                        bk2, sg = pipe[ct]["bk2"], pipe[ct]["sg"]
                        if ct < 16:
                            dT, dap = xT, xT[:, ct, :MT]
                        elif ct < 24:
                            dT, dap = BT, BT[:, ct - 16, :MT]
                        else:
                            dT, dap = CT, CT[:, ct - 24, :MT]
                        P.op("dve", "scalar_tensor_tensor", dict(out=dap, in0=bk2.f32[:, :MT], scalar=cbias[:, ct:ct + 1], in1=sg[:, :MT],
                                                                 op0=ALU.add, op1=ALU.mult), [bk2.b, cbias.b, sg.b], [dT.b])
                        release(bk2)
                        del pipe[ct]

                    for step in range(32 + 3):
                        if step < 32:
                            s0(step)
                        if 0 <= step - 1 < 32:
                            s1(step - 1)
                        if 0 <= step - 2 < 32:
                            s2(step - 2)
                        if 0 <= step - 3 < 32:
                            s3(step - 3)
                        yield
                    yield ("set", ("convdone", si, m))

                def chunk_stream(c, lane, si=si, base=base, CL=CL, MT=MT, NCH=NCH):
                    Ln = lanes[lane]
                    m, ci = divmod(c, NCH)
                    st = m % 2
                    xT, BT, CT, dtraw = xTs[st], BTs[st], CTs[st], dtraws[st]
                    c0 = ci * CL
                    row0 = base + m * MT + c0
                    sm, adtb, xtok, Btok, Wbh, Ebh, cbm, xddh = (Ln["sm"], Ln["adtb"], Ln["xtok"], Ln["Btok"], Ln["Wbh"],
                                                                 Ln["Ebh"], Ln["cbm"], Ln["xddh"])
                    yield ("wait", ("convdone", si, m))
                    P.op("dve", "tensor_tensor", dict(out=sm[:CL, 0, :], in0=dtraw[:CL, ci, :], in1=dtb[:CL, :], op=ALU.add),
                         [dtraw.b, dtb.b], [sm.b])
                    P.op("dve", "scalar_tensor_tensor", dict(out=sm[:CL, 1, :], in0=sm[:CL, 0, :], scalar=-1.0,
                                                             in1=sm[:CL, 0, :], op0=ALU.mult, op1=ALU.max), [sm.b], [sm.b])
                    P.op("act", "activation", dict(out=sm[:CL, 2, :], in_=sm[:CL, 1, :], func=AF.Exp, scale=-1.0), [sm.b], [sm.b])
                    P.op("act", "activation", dict(out=sm[:CL, 3, :], in_=sm[:CL, 2, :], func=AF.Ln, bias=1.0), [sm.b], [sm.b])
                    P.op("dve", "scalar_tensor_tensor", dict(out=sm[:CL, 4, :], in0=sm[:CL, 0, :], scalar=0.0,
                                                             in1=sm[:CL, 3, :], op0=ALU.max, op1=ALU.add), [sm.b], [sm.b])
                    dt = sm[:CL, 4, :]
                    P.op("dve", "tensor_tensor", dict(out=adtb[:CL, :], in0=dt, in1=Abc[:CL, :], op=ALU.mult),
                         [sm.b, Abc.b], [adtb.b])
                    P.op("act", "activation", dict(out=sm[:CL, 5, :], in_=dt, func=AF.Ln), [sm.b], [sm.b])
                    yield
                    bk2 = pb()
                    P.op("pe", "matmul", dict(out=bk2.f32[:CL, 0:NH], lhsT=trib[:CL, :CL], rhs=adtb[:CL, :],
                                              start=True, stop=True), [trib.b, adtb.b], [bk2.b], inc=False)
                    P.op("pe", "matmul", dict(out=bk2.f32[:, NH:2 * NH], lhsT=onesb[:CL, :], rhs=adtb[:CL, :],
                                              start=True, stop=True), [onesb.b, adtb.b], [bk2.b])
                    P.op("act", "activation", dict(out=sm[:CL, 6, :], in_=bk2.f32[:CL, 0:NH], func=AF.Exp), [bk2.b], [sm.b])
                    P.op("act", "activation", dict(out=sm[:, 7, :], in_=bk2.f32[:, NH:2 * NH], func=AF.Exp), [bk2.b], [sm.b])
                    P.op("act", "activation", dict(out=sm[:CL, 8, :], in_=bk2.f32[:CL, 0:NH], func=AF.Copy), [bk2.b], [sm.b])
                    P.op("dve", "tensor_tensor", dict(out=sm[:CL, 9, :], in0=bk2.f32[:CL, NH:2 * NH], in1=sm[:CL, 8, :],
                                                      op=ALU.subtract), [bk2.b, sm.b], [sm.b])
                    P.op("act", "activation", dict(out=sm[:CL, 10, :], in_=sm[:CL, 9, :], func=AF.Exp), [sm.b], [sm.b])
                    P.op("dve", "tensor_tensor", dict(out=sm[:CL, 11, :], in0=sm[:CL, 10, :], in1=dt, op=ALU.mult), [sm.b], [sm.b])
                    yield
                    for half in range(2):
                        bk = pb()
                        for i in range(8):
                            P.op("pe", "transpose", dict(out=bk.bf[:CL, i * 128:(i + 1) * 128],
                                                         in_=xT[:, half * 8 + i, c0:c0 + CL], identity=C["identb"][:]),
                                 [xT.b, C["identb"].b], [bk.b], inc=(i == 7))
                        P.op("act", "activation", dict(out=xtok[:CL, half * 1024:(half + 1) * 1024], in_=bk.bf[:CL, :],
                                                       func=AF.Copy), [bk.b], [xtok.b])
                        yield
                    bk = pb()
                    for g in range(8):
                        P.op("pe", "transpose", dict(out=bk.bf[:CL, g * 128:(g + 1) * 128], in_=BT[:, g, c0:c0 + CL],
                                                     identity=C["identb"][:]), [BT.b, C["identb"].b], [bk.b], inc=(g == 7))
                    P.op("act", "activation", dict(out=Btok[:CL, :], in_=bk.bf[:CL, :], func=AF.Copy), [bk.b], [Btok.b])
                    yield
                    cv = cbm[:CL, 0:NG * CL].rearrange("p (g t) -> p g t", g=NG)
                    for half in range(2):
                        bk = pb()
                        for gi in range(4):
                            g = half * 4 + gi
                            P.op("pe", "matmul", dict(out=bk.f32[:CL, gi * CL:(gi + 1) * CL], lhsT=BT[:, g, c0:c0 + CL],
                                                      rhs=CT[:, g, c0:c0 + CL], start=True, stop=True),
                                 [BT.b, CT.b], [bk.b], inc=(gi == 3))
                        P.op("dve", "tensor_tensor", dict(out=cv[:, half * 4:(half + 1) * 4, :],
                                                          in0=bk.f32[:CL, 0:4 * CL].rearrange("p (g t) -> p g t", g=4),
                                                          in1=trib[:CL, :CL].unsqueeze(1).to_broadcast([CL, 4, CL]),
                                                          op=ALU.mult), [bk.b, trib.b], [cbm.b])
                        yield
                    Wv = Wbh[:CL, 0:16 * CL].rearrange("p (h t) -> p h t", h=16)
                    Ev = Ebh[:CL, 0:16 * CL].rearrange("p (h t) -> p h t", h=16)
                    for hf in range(2):
                        P.op("dve", "tensor_tensor", dict(out=Wv, in0=trib[:CL, :CL].unsqueeze(1).to_broadcast([CL, 16, CL]),
                                                           in1=adtb[:CL, hf * 16:(hf + 1) * 16].unsqueeze(2).to_broadcast([CL, 16, CL]),
                                                           op=ALU.mult), [trib.b, adtb.b], [Wbh.b])
                        for gi in range(4):
                            bk = pb()
                            P.op("pe", "matmul", dict(out=bk.f32[:CL, 0:4 * CL], lhsT=sub[:CL, :CL],
                                                      rhs=Wbh[:CL, gi * 4 * CL:(gi + 1) * 4 * CL], start=True, stop=True),
                                 [sub.b, Wbh.b], [bk.b])
                            for hi in range(4):
                                hl = gi * 4 + hi
                                h = hf * 16 + hl
                                P.op("act", "activation", dict(out=Ev[:, hl, :], in_=bk.f32[:CL, hi * CL:(hi + 1) * CL],
                                                               func=AF.Exp, bias=sm[:CL, 5, h:h + 1]), [bk.b, sm.b], [Ebh.b])
                            yield
                        E4 = Ebh[:CL, 0:16 * CL].rearrange("p (g r t) -> p g r t", g=4, r=4)
                        P.op("dve", "tensor_tensor", dict(out=E4, in0=E4,
                                                          in1=cv[:, hf * 4:(hf + 1) * 4, :].unsqueeze(2).to_broadcast([CL, 4, 4, CL]),
                                                          op=ALU.mult), [Ebh.b, cbm.b], [Ebh.b])
                        yield
                        if c > 0:
                            yield ("wait", ("state", si, c - 1, hf))
                        yst = ystr.next()
                        for qq in range(2):
                            q = hf * 2 + qq
                            bkY = pb()
                            for hh in range(8):
                                hl = qq * 8 + hh
                                h = hf * 16 + hl
                                P.op("pe", "matmul", dict(out=bkY.f32[:CL, hh * 64:(hh + 1) * 64], lhsT=Ev[:, hl, :],
                                                          rhs=xtok[:CL, h * 64:(h + 1) * 64], start=True, stop=False),
                                     [Ebh.b, xtok.b], [bkY.b], inc=False)
                                P.op("pe", "matmul", dict(out=bkY.f32[:CL, hh * 64:(hh + 1) * 64], lhsT=DI[:CL, h, :CL],
                                                          rhs=xtok[:CL, h * 64:(h + 1) * 64], start=False, stop=True),
                                     [DI.b, xtok.b], [bkY.b], inc=(hh == 7))
                            bkO = pb()
                            for gg in range(2):
                                g = q * 2 + gg
                                P.op("pe", "matmul", dict(out=bkO.f32[:CL, gg * 256:(gg + 1) * 256], lhsT=CT[:, g, c0:c0 + CL],
                                                          rhs=hTbq[q][:, gg * 256:(gg + 1) * 256], start=True, stop=True),
                                     [CT.b, hTbq[q].b], [bkO.b], inc=(gg == 1))
                            tmp = tmpr.next()
                            P.op("dve", "tensor_tensor", dict(out=tmp[:CL, :].rearrange("p (h d) -> p h d", h=8),
                                                              in0=bkO.f32[:CL, :].rearrange("p (h d) -> p h d", h=8),
                                                              in1=sm[:CL, 6, q * 8:(q + 1) * 8].unsqueeze(2).to_broadcast([CL, 8, 64]),
                                                              op=ALU.mult), [bkO.b, sm.b], [tmp.b])
                            P.op("dve", "tensor_tensor", dict(out=yst[:CL, qq * 512:(qq + 1) * 512], in0=bkY.f32[:CL, :],
                                                              in1=tmp[:CL, :], op=ALU.add), [bkY.b, tmp.b], [yst.b])
                            yield
                        P.dma("pool", Ys[row0:row0 + CL, hf * 1024:(hf + 1) * 1024], yst[:CL, :], db("Ys", row0), yst.b, "st")
                        P.op("dve", "tensor_tensor", dict(out=xddh[:CL, :].rearrange("p (h d) -> p h d", h=16),
                                                          in0=xtok[:CL, hf * 1024:(hf + 1) * 1024].rearrange("p (h d) -> p h d", h=16),
                                                          in1=sm[:CL, 11, hf * 16:(hf + 1) * 16].unsqueeze(2).to_broadcast([CL, 16, 64]),
                                                          op=ALU.mult), [xtok.b, sm.b], [xddh.b])
                        for qq in range(2):
                            q = hf * 2 + qq
                            bkH = pb()
                            for gg in range(2):
                                g = q * 2 + gg
                                P.op("pe", "matmul", dict(out=bkH.f32[:, gg * 256:(gg + 1) * 256], lhsT=Btok[:CL, g * 128:(g + 1) * 128],
                                                          rhs=xddh[:CL, (qq * 2 + gg) * 256:(qq * 2 + gg + 1) * 256], start=True, stop=True),
                                     [Btok.b, xddh.b], [bkH.b], inc=(gg == 1))
                            hv = hTq[q]
                            P.op("dve", "tensor_tensor", dict(out=hv[:].rearrange("p (h d) -> p h d", h=8),
                                                              in0=hv[:].rearrange("p (h d) -> p h d", h=8),
                                                              in1=sm[:, 7, q * 8:(q + 1) * 8].unsqueeze(2).to_broadcast([128, 8, 64]),
                                                              op=ALU.mult), [hv.b, sm.b], [hv.b])
                            P.op("dve", "tensor_tensor", dict(out=hv[:], in0=bkH.f32[:, :], in1=hv[:], op=ALU.add), [bkH.b, hv.b], [hv.b])
                            P.op("act", "activation", dict(out=hTbq[q][:], in_=hv[:], func=AF.Copy), [hv.b], [hTbq[q].b])
                            yield
                        yield ("set", ("state", si, c, hf))
                    yield ("set", ("cdone", si, c))

                nchunks = NM * NCH
                convq = [(lambda m=m: conv_stream(m)) for m in range(NM)]
                laneA = [(lambda c=c: chunk_stream(c, 0)) for c in range(0, nchunks, 2)]
                laneB = [(lambda c=c: chunk_stream(c, 1)) for c in range(1, nchunks, 2)]
                run_sched([convq, laneA, laneB])

                o_ssm = (o_pssm if grp == "p" else o_sssm)[j, b]
                o_conv = (o_pconv if grp == "p" else o_sconv)[j, b]
                for hf in range(2):
                    stg = ystr.next()
                    for qq in range(2):
                        q = hf * 2 + qq
                        bk = pb()
                        for i in range(4):
                            P.op("pe", "transpose", dict(out=bk.f32[:, i * 128:(i + 1) * 128], in_=hTq[q][:, i * 128:(i + 1) * 128],
                                                         identity=identf[:]), [hTq[q].b, identf.b], [bk.b], inc=(i == 3))
                        P.op("act", "activation", dict(out=stg[:, qq * 512:(qq + 1) * 512], in_=bk.f32[:, :], func=AF.Copy),
                             [bk.b], [stg.b])
                    P.dma("pool", o_ssm[hf * 1024:(hf + 1) * 1024, :].rearrange("(t p) n -> p t n", p=128),
                          stg[:].rearrange("p (t n) -> p t n", t=8), db("o_ssm_%s" % grp, b), stg.b, "st")
                P.dma("pool", o_conv, halo[:], db("o_conv_%s" % grp, b), halo.b, "st")
            end_stage()

        def stage_ssd2(layer, j, src, dst):
            begin_stage()
            C = load_consts()
            Wz = sb([128, 8, DIN], BF16, "Wz")
            for kc in range(8):
                P.dma("pool", Wz[:, kc, :], ssd_in_w[j, kc * 128:(kc + 1) * 128, 0:DIN], Wz.b, cst, "ld", par=True)
            Wo = sb([128, 16, D], BF16, "Wo")
            load_w(Wo, ssd_out_w[j])
            gB = sb([128, D], F32, "gB")
            load_bcast(gB, norm_mix[layer])
            nwB = sb([128, DIN], F32, "nwB")
            load_bcast(nwB, ssd_norm_w[j])
            xinr = ring(4, [128, D], F32, "xin")
            ssr = ring(6, [128, 4], F32, "ss")
            xnr = ring(4, [128, D], BF16, "xn")
            hnTs = ring(3, [128, 8, 128], BF16, "hnT")
            yinr = ring(3, [128, DIN], F32, "yin")
            szr = ring(2, [128, DIN], F32, "sz")
            gsr = ring(3, [128, 4, NG], F32, "gs")
            ynbr = ring(2, [128, DIN], BF16, "ynb")
            ynTr = ring(2, [128, 16, 128], BF16, "ynT")
            ystr = ring(2, [128, D], F32, "yst")
            junk2 = sb([128, 256], F32, "junk2")
            TS = {}

            def stA(i):
                tl = tiles[i]
                nt, r0 = tl["nt"], tl["row0"]
                xin = xinr.next()
                yin = yinr.next()
                P.dma("sp", xin[:nt, :], src[r0:r0 + nt, :], xin.b, db(src.name, r0), "ld")
                P.dma("sp", yin[:nt, :], Ys[r0:r0 + nt, :], yin.b, db("Ys", r0), "ld")
                ss = ssr.next()
                junk = xnr.next()
                P.op("act", "activation", dict(out=junk[:nt, :], in_=xin[:nt, :], func=AF.Square, accum_out=ss[:nt, 0:1]),
                     [xin.b], [junk.b, ss.b])
                P.op("act", "activation", dict(out=ss[:nt, 2:3], in_=ss[:nt, 0:1], func=AF.Ln, scale=1.0 / D, bias=EPS), [ss.b], [ss.b])
                P.op("act", "activation", dict(out=ss[:nt, 3:4], in_=ss[:nt, 2:3], func=AF.Exp, scale=-0.5), [ss.b], [ss.b])
                xn = xnr.next()
                P.op("dve", "scalar_tensor_tensor", dict(out=xn[:nt, :], in0=xin[:nt, :], scalar=ss[:nt, 3:4], in1=gB[:nt, :],
                                                         op0=ALU.mult, op1=ALU.mult), [xin.b, ss.b, gB.b], [xn.b])
                TS[i] = dict(xin=xin, yin=yin, xn=xn)

            def stB(i):
                nt = tiles[i]["nt"]
                xn = TS[i]["xn"]
                hnT = hnTs.next()
                bk = pb()
                for kc in range(8):
                    P.op("pe", "transpose", dict(out=bk.bf[:, kc * 128:kc * 128 + nt], in_=xn[:nt, kc * 128:(kc + 1) * 128],
                                                 identity=C["identb"][:nt, :nt]), [xn.b, C["identb"].b], [bk.b], inc=(kc == 7))
                P.op("act", "activation", dict(out=hnT[:, :, :nt], in_=bk.bf.rearrange("p (k t) -> p k t", k=8)[:, :, :nt], func=AF.Copy),
                     [bk.b], [hnT.b])
                TS[i]["hnT"] = hnT

            def stC(i, cs):
                nt = tiles[i]["nt"]
                d = TS[i]
                hnT, yin = d["hnT"], d["yin"]
                if "sz" not in d:
                    d["sz"] = szr.next()
                sz = d["sz"]
                for c in cs:
                    bk = pb()
                    for kc in range(8):
                        P.op("pe", "matmul", dict(out=bk.f32[:nt, :], lhsT=hnT[:, kc, :nt], rhs=Wz[:, kc, c * 512:(c + 1) * 512],
                                                  start=(kc == 0), stop=(kc == 7)), [hnT.b, Wz.b], [bk.b], inc=(kc == 7))
                    sl = sz[:nt, c * 512:(c + 1) * 512]
                    P.op("act", "activation", dict(out=sl, in_=bk.f32[:nt, :], func=AF.Exp, scale=-1.0), [bk.b], [sz.b])
                    P.op("act", "activation", dict(out=sl, in_=sl, func=AF.Ln, bias=1.0), [sz.b], [sz.b])
                    P.op("act", "activation", dict(out=sl, in_=sl, func=AF.Exp, scale=-1.0), [sz.b], [sz.b])
                    P.op("dve", "tensor_tensor", dict(out=sl, in0=sl, in1=yin[:nt, c * 512:(c + 1) * 512], op=ALU.mult),
                         [sz.b, yin.b], [sz.b])
                    P.op("dve", "tensor_tensor", dict(out=sl, in0=bk.f32[:nt, :], in1=sl, op=ALU.mult), [bk.b, sz.b], [sz.b])

            def stD(i):
                nt = tiles[i]["nt"]
                d = TS[i]
                sz = d["sz"]
                gs = gsr.next()
                for g in range(NG):
                    P.op("act", "activation", dict(out=junk2[:nt, :], in_=sz[:nt, g * 256:(g + 1) * 256], func=AF.Square,
                                                   accum_out=gs[:nt, 0, g:g + 1]), [sz.b], [junk2.b, gs.b])
                P.op("act", "activation", dict(out=gs[:nt, 2, :], in_=gs[:nt, 0, :], func=AF.Ln, scale=1.0 / 256, bias=EPS),
                     [gs.b], [gs.b])
                P.op("act", "activation", dict(out=gs[:nt, 3, :], in_=gs[:nt, 2, :], func=AF.Exp, scale=-0.5), [gs.b], [gs.b])
                P.op("dve", "tensor_tensor", dict(out=sz[:nt, :].rearrange("p (g d) -> p g d", g=NG),
                                                  in0=sz[:nt, :].rearrange("p (g d) -> p g d", g=NG),
                                                  in1=gs[:nt, 3, :].unsqueeze(2).to_broadcast([nt, NG, 256]), op=ALU.mult),
                     [sz.b, gs.b], [sz.b])
                ynb = ynbr.next()
                P.op("dve", "tensor_tensor", dict(out=ynb[:nt, :], in0=sz[:nt, :], in1=nwB[:nt, :], op=ALU.mult),
                     [sz.b, nwB.b], [ynb.b])
                d["ynb"] = ynb

            def stE(i):
                nt = tiles[i]["nt"]
                d = TS[i]
                ynb = d["ynb"]
                ynT = ynTr.next()
                for half in range(2):
                    bk = pb()
                    for k8 in range(8):
                        ct = half * 8 + k8
                        P.op("pe", "transpose", dict(out=bk.bf[:, k8 * 128:k8 * 128 + nt], in_=ynb[:nt, ct * 128:(ct + 1) * 128],
                                                     identity=C["identb"][:nt, :nt]), [ynb.b, C["identb"].b], [bk.b], inc=(k8 == 7))
                    P.op("act", "activation", dict(out=ynT[:, half * 8:(half + 1) * 8, :nt],
                                                   in_=bk.bf.rearrange("p (k t) -> p k t", k=8)[:, :, :nt], func=AF.Copy),
                         [bk.b], [ynT.b])
                d["ynT"] = ynT

            def stF(i):
                tl = tiles[i]
                nt, r0 = tl["nt"], tl["row0"]
                d = TS.pop(i)
                ynT, xin = d["ynT"], d["xin"]
                y = ystr.next()
                for c in range(2):
                    bk = pb()
                    for ct in range(16):
                        P.op("pe", "matmul", dict(out=bk.f32[:nt, :], lhsT=ynT[:, ct, :nt], rhs=Wo[:, ct, c * 512:(c + 1) * 512],
                                                  start=(ct == 0), stop=(ct == 15)), [ynT.b, Wo.b], [bk.b], inc=(ct == 15))
                    P.op("dve", "tensor_tensor", dict(out=y[:nt, c * 512:(c + 1) * 512], in0=bk.f32[:nt, :],
                                                      in1=xin[:nt, c * 512:(c + 1) * 512], op=ALU.add), [bk.b, xin.b], [y.b])
                P.dma("pool", dst[r0:r0 + nt, :], y[:nt, :], db(dst.name, r0), y.b, "st")

            NTL = len(tiles)
            stA(0)
            if NTL > 1:
                stA(1)
            stB(0)
            for i in range(NTL):
                if i + 2 < NTL:
                    stA(i + 2)
                stC(i, (0, 1))
                if i + 1 < NTL:
                    stB(i + 1)
                if i >= 1:
                    stE(i - 1)
                stC(i, (2, 3))
                if i >= 1:
                    stF(i - 1)
                stD(i)
            stE(NTL - 1)
            stF(NTL - 1)
            end_stage()

        def stage_sb1(layer, j, src):
            begin_stage()
            C = load_consts()
            Wqkv = sb([128, 8, 3 * D], BF16, "Wqkv")
            load_w(Wqkv, sb_qkv_w[j])
            gB = sb([128, D], F32, "gB")
            load_bcast(gB, norm_mix[layer])
            qg = sb([128, AD], F32, "qg")
            kg = sb([128, AD], F32, "kg")
            load_bcast(qg, sb_q_gain[j])
            load_bcast(kg, sb_k_gain[j])
            P.op("dve", "tensor_scalar", dict(out=qg[:], in0=qg[:], scalar1=0.125, scalar2=None, op0=ALU.mult), [qg.b], [qg.b])
            xinr = ring(2, [128, D], F32, "xin")
            ssr = ring(6, [128, 4], F32, "ss")
            xnr = ring(4, [128, D], BF16, "xn")
            xnTs = ring(3, [128, 8, 128], BF16, "xnT")
            sqr = ring(2, [128, 512], F32, "sq")
            t1r = ring(2, [128, 512], F32, "t1")
            ssqr = ring(4, [128, 4, 8], F32, "ssq")
            koutr = ring(2, [128, D], F32, "kout")
            voutr = ring(2, [128, D], F32, "vout")
            knbr = ring(3, [128, D], BF16, "knb")
            qnbr = ring(3, [128, D], BF16, "qnb")
            vbr = ring(2, [128, D], BF16, "vb")
            qTr = ring(2, [128, 8, 128], BF16, "qT")
            kTr = ring(2, [128, 8, 128], BF16, "kT")
            ckr = ring(2, [128, D], BF16, "ck")

            def transpose_store(srcT, nt, stg, dram_ap, dbuf):
                bk = pb()
                for hp in range(8):
                    P.op("pe", "transpose", dict(out=bk.bf[:, hp * 128:hp * 128 + nt], in_=srcT[:nt, hp * 128:(hp + 1) * 128],
                                                 identity=C["identb"][:nt, :nt]), [srcT.b, C["identb"].b], [bk.b], inc=(hp == 7))
                P.op("act", "activation", dict(out=stg[:, :, :nt], in_=bk.bf.rearrange("p (k t) -> p k t", k=8)[:, :, :nt],
                                               func=AF.Copy), [bk.b], [stg.b])
                P.dma("pool", dram_ap, stg[:, :, :nt], dbuf, stg.b, "st")

            TS = {}

            def stA(i):
                tl = tiles[i]
                nt, r0 = tl["nt"], tl["row0"]
                xin = xinr.next()
                P.dma("sp", xin[:nt, :], src[r0:r0 + nt, :], xin.b, db(src.name, r0), "ld")
                ss = ssr.next()
                junk = xnr.next()
                P.op("act", "activation", dict(out=junk[:nt, :], in_=xin[:nt, :], func=AF.Square, accum_out=ss[:nt, 0:1]),
                     [xin.b], [junk.b, ss.b])
                P.op("act", "activation", dict(out=ss[:nt, 2:3], in_=ss[:nt, 0:1], func=AF.Ln, scale=1.0 / D, bias=EPS), [ss.b], [ss.b])
                P.op("act", "activation", dict(out=ss[:nt, 3:4], in_=ss[:nt, 2:3], func=AF.Exp, scale=-0.5), [ss.b], [ss.b])
                xn = xnr.next()
                P.op("dve", "scalar_tensor_tensor", dict(out=xn[:nt, :], in0=xin[:nt, :], scalar=ss[:nt, 3:4], in1=gB[:nt, :],
                                                         op0=ALU.mult, op1=ALU.mult), [xin.b, ss.b, gB.b], [xn.b])
                TS[i] = dict(xn=xn)

            def stB(i):
                tl = tiles[i]
                nt = tl["nt"]
                xn = TS[i]["xn"]
                xnT = xnTs.next()
                bk = pb()
                for kc in range(8):
                    P.op("pe", "transpose", dict(out=bk.bf[:, kc * 128:kc * 128 + nt], in_=xn[:nt, kc * 128:(kc + 1) * 128],
                                                 identity=C["identb"][:nt, :nt]), [xn.b, C["identb"].b], [bk.b], inc=(kc == 7))
                P.op("act", "activation", dict(out=xnT[:, :, :nt], in_=bk.bf.rearrange("p (k t) -> p k t", k=8)[:, :, :nt], func=AF.Copy),
                     [bk.b], [xnT.b])
                TS[i]["xnT"] = xnT

            def stC(i, cs):
                tl = tiles[i]
                nt, r0, grp, b, pos = tl["nt"], tl["row0"], tl["grp"], tl["seq"], tl["pos"]
                d = TS[i]
                xnT = d["xnT"]
                if "kout" not in d:
                    d.update(kout=koutr.next(), vout=voutr.next(), knb=knbr.next(), qnb=qnbr.next(), vb=vbr.next())
                kout, vout, knb, qnb, vb = d["kout"], d["vout"], d["knb"], d["qnb"], d["vb"]
                for c in cs:
                    bk = pb()
                    for kc in range(8):
                        P.op("pe", "matmul", dict(out=bk.f32[:nt, :], lhsT=xnT[:, kc, :nt], rhs=Wqkv[:, kc, c * 512:(c + 1) * 512],
                                                  start=(kc == 0), stop=(kc == 7)), [xnT.b, Wqkv.b], [bk.b], inc=(kc == 7))
                    if c < 4:
                        sq = sqr.next()
                        ssq = ssqr.next()
                        t1 = t1r.next()
                        P.op("act", "activation", dict(out=sq[:nt, :], in_=bk.f32[:nt, :], func=AF.Square), [bk.b], [sq.b])
                        P.op("dve", "tensor_reduce", dict(out=ssq[:nt, 0, :], in_=sq[:nt, :].rearrange("p (h d) -> p h d", h=8),
                                                          axis=AX.X, op=ALU.add), [sq.b], [ssq.b])
                        P.op("act", "activation", dict(out=ssq[:nt, 2, :], in_=ssq[:nt, 0, :], func=AF.Ln, scale=1.0 / AD, bias=EPS),
                             [ssq.b], [ssq.b])
                        P.op("act", "activation", dict(out=ssq[:nt, 3, :], in_=ssq[:nt, 2, :], func=AF.Exp, scale=-0.5),
                             [ssq.b], [ssq.b])
                        P.op("dve", "tensor_tensor", dict(out=t1[:nt, :].rearrange("p (h d) -> p h d", h=8),
                                                          in0=bk.f32[:nt, :].rearrange("p (h d) -> p h d", h=8),
                                                          in1=ssq[:nt, 3, :].unsqueeze(2).to_broadcast([nt, 8, AD]), op=ALU.mult),
                             [bk.b, ssq.b], [t1.b])
                        if c < 2:
                            P.op("dve", "tensor_tensor", dict(out=qnb[:nt, c * 512:(c + 1) * 512].rearrange("p (h d) -> p h d", h=8),
                                                              in0=t1[:nt, :].rearrange("p (h d) -> p h d", h=8),
                                                              in1=qg[:nt, :].unsqueeze(1).to_broadcast([nt, 8, AD]), op=ALU.mult),
                                 [t1.b, qg.b], [qnb.b])
                        else:
                            cc = c - 2
                            P.op("dve", "tensor_tensor", dict(out=kout[:nt, cc * 512:(cc + 1) * 512].rearrange("p (h d) -> p h d", h=8),
                                                              in0=t1[:nt, :].rearrange("p (h d) -> p h d", h=8),
                                                              in1=kg[:nt, :].unsqueeze(1).to_broadcast([nt, 8, AD]), op=ALU.mult),
                                 [t1.b, kg.b], [kout.b])
                            P.op("act", "activation", dict(out=knb[:nt, cc * 512:(cc + 1) * 512], in_=kout[:nt, cc * 512:(cc + 1) * 512],
                                                           func=AF.Copy), [kout.b], [knb.b])
                    else:
                        cc = c - 4
                        P.op("act", "activation", dict(out=vout[:nt, cc * 512:(cc + 1) * 512], in_=bk.f32[:nt, :], func=AF.Copy),
                             [bk.b], [vout.b])
                        P.op("dve", "tensor_copy", dict(out=vb[:nt, cc * 512:(cc + 1) * 512], in_=vout[:nt, cc * 512:(cc + 1) * 512]),
                             [vout.b], [vb.b])
                if 5 in cs:
                    if grp == "p":
                        P.dma("pool", o_pk[j, b, pos:pos + nt, :], kout[:nt, :], db("o_pk", r0), kout.b, "st")
                        P.dma("pool", o_pv[j, b, pos:pos + nt, :], vout[:nt, :], db("o_pv", r0), vout.b, "st")
                        P.dma("pool", Vp[b, pos:pos + nt, :], vb[:nt, :], db("Vp", b), vb.b, "st")
                    else:
                        P.dma("pool", o_sk[j, b, 0:nt, :], kout[:nt, :], db("o_sk", r0), kout.b, "st")
                        P.dma("pool", o_sv[j, b, 0:nt, :], vout[:nt, :], db("o_sv", r0), vout.b, "st")
                        P.dma("pool", Vs[b, 0:nt, :], vb[:nt, :], db("Vs", b), vb.b, "st")

            def stD(i):
                tl = tiles[i]
                nt, grp, b, pos = tl["nt"], tl["grp"], tl["seq"], tl["pos"]
                d = TS.pop(i)
                if grp == "p":
                    transpose_store(d["qnb"], nt, qTr.next(), QTp[b, :, :, pos:pos + nt], db("QTp", b))
                    transpose_store(d["knb"], nt, kTr.next(), KTp[b, :, :, pos:pos + nt], db("KTp", b))
                else:
                    transpose_store(d["qnb"], nt, qTr.next(), QTs[b, :, :, 0:nt], db("QTs", b))
                    transpose_store(d["knb"], nt, kTr.next(), KTs[b, :, :, PAST:PAST + nt], db("KTs", b))

            NTL = len(tiles)
            stA(0)
            if NTL > 1:
                stA(1)
            stB(0)
            for i in range(NTL):
                if i + 2 < NTL:
                    stA(i + 2)
                stC(i, (0, 1, 2))
                if i + 1 < NTL:
                    stB(i + 1)
                stC(i, (3, 4, 5))
                if i >= 1:
                    stD(i - 1)
            stD(NTL - 1)
            for b in range(NS):
                for kb in range(PAST // 128):
                    ck = ckr.next()
                    for hf in range(2):
                        P.dma("pool", ck[:, hf * 512:(hf + 1) * 512], cache_k[j, b, kb * 128:(kb + 1) * 128, hf * 512:(hf + 1) * 512],
                              ck.b, cst, "ld")
                    transpose_store(ck, 128, kTr.next(), KTs[b, :, :, kb * 128:(kb + 1) * 128], db("KTs", b))
            end_stage()

        def stage_sb2(layer, j, src, dst):
            begin_stage()
            Wo = sb([128, 8, D], BF16, "Wo")
            load_w(Wo, sb_out_w[j])
            suin = sb([128, 128], BF16, "suin")
            sutmp = sb([128, 128], BF16, "sutmp")
            onesb = sb([128, 128], BF16, "onesb")
            nonesb = sb([128, 128], BF16, "nonesb")
            maskb = sb([128, 4, 512], BF16, "maskb")
            masksb = sb([128, 32], BF16, "masksb")
            P.dma("pool", sutmp[:], c_tri[:, :], sutmp.b, cst, "ld")
            P.dma("pool", suin[:], c_su[:, :], suin.b, cst, "ld")
            P.dma("pool", sutmp[:], c_ident[:, :], sutmp.b, cst, "ld")
            P.op("dve", "tensor_tensor", dict(out=suin[:], in0=suin[:], in1=sutmp[:], op=ALU.add), [suin.b, sutmp.b], [suin.b])
            P.op("dve", "tensor_scalar", dict(out=suin[:], in0=suin[:], scalar1=-1.0, scalar2=None, op0=ALU.mult), [suin.b], [suin.b])
            P.op("dve", "memset", dict(ap=onesb[:], constant=1.0), [], [onesb.b])
            P.op("dve", "tensor_scalar", dict(out=nonesb[:], in0=sutmp[:], scalar1=-1.0, scalar2=None, op0=ALU.mult),
                 [sutmp.b], [nonesb.b])
            for i in range(4):
                P.dma("pool", maskb[:, i, :], c_mask[:, i, :], maskb.b, cst, "ld")
            P.dma("pool", masksb[:], c_masks[:, :], masksb.b, cst, "ld")
            KLEN = max(SEQ, KPAD)
            KT = sb([128, 8, KLEN], BF16, "KT")
            Vb = sb([128, KLEN // 128, D], BF16, "Vb")
            QWM = 512
            QTr = ring(2, [128, 8, QWM], BF16, "QT")
            OT = sb([128, 8, QWM], BF16, "OT")
            e1r = ring(4, [128, QWM], F32, "e1")
            lgr = ring(6, [128, QWM], BF16, "lg")
            wbr = ring(6, [128, QWM], BF16, "wb")
            Sbfr = [ring(2, [128, QWM], BF16, "Sbf0"), ring(2, [128, QWM], BF16, "Sbf1")]
            xinr = ring(2, [128, D], F32, "xin")
            ystr = ring(2, [128, D], F32, "yst")
            bkR = [hold(), hold()]
            bkOs = [hold(), hold()]

            seqs = [("p", b) for b in range(NB)] + [("s", b) for b in range(NS)]
            for (grp, b) in seqs:
                if grp == "p":
                    base, L, QW = b * SEQ, SEQ, 512
                    for hp in range(8):
                        P.dma("sp", KT[:, hp, :SEQ], KTp[b, :, hp, :], KT.b, db("KTp", b), "ld", par=True)
                    for t4 in range(0, SEQ // 128, 4):
                        P.dma("sp", Vb[:, t4:t4 + 4, :], Vp[b, t4 * 128:(t4 + 4) * 128, :].rearrange("(t p) c -> p t c", p=128),
                              Vb.b, db("Vp", b), "ld", par=True)
                else:
                    base, L, QW = NTP + b * DSEQ, DSEQ, DSEQ
                    nkc = PAST // 128
                    if KPAD > KTOT:
                        P.op("dve", "memset", dict(ap=KT[:, :, KTOT:KPAD], constant=0.0), [], [KT.b])
                    P.op("dve", "memset", dict(ap=Vb[:, nkc, :], constant=0.0), [], [Vb.b])
                    for hp in range(8):
                        P.dma("sp", KT[:, hp, :KTOT], KTs[b, :, hp, :KTOT], KT.b, db("KTs", b), "ld")
                    for t in range(nkc):
                        for hf in range(2):
                            P.dma("pool", Vb[:, t, hf * 512:(hf + 1) * 512],
                                  cache_v[j, b, t * 128:(t + 1) * 128, hf * 512:(hf + 1) * 512], Vb.b, cst, "ld")
                    P.dma("sp", Vb[:DSEQ, nkc, :], Vs[b, :, :], Vb.b, db("Vs", b), "ld")
                for q0 in range(0, L, QW):
                    QT = QTr.next()
                    if grp == "p":
                        P.dma("sp", QT[:, :, :QW], QTp[b, :, :, q0:q0 + QW], QT.b, db("QTp", b), "ld")
                        nkb = (q0 + QW) // 128
                        def mask_of(kb, q0=q0):
                            i = kb - q0 // 128
                            return maskb[:, i, :] if i >= 0 else None
                    else:
                        P.dma("sp", QT[:, :, :QW], QTs[b, :, :, :], QT.b, db("QTs", b), "ld")
                        nkb = KPAD // 128
                        def mask_of(kb, nkb=nkb):
                            return masksb[:, :] if kb == nkb - 1 else None
                    its = []
                    for hp in range(8):
                        for kb in range(nkb - 1, -1, -1):
                            for hh in range(2):
                                its.append((hp, kb, hh))
                    st = {}

                    def col0_of(kb, q0=q0, grp=grp):
                        if grp != "p" or not TRIM:
                            return 0
                        return max(0, (kb - q0 // 128) * 128)

                    def stageA(it):
                        hp, kb, hh = it
                        po = hh * 64
                        a = col0_of(kb)
                        bkT = pb()
                        P.op("pe", "matmul", dict(out=bkT.f32[:, a:QW], lhsT=KT[po:po + 64, hp, kb * 128:(kb + 1) * 128],
                                                  rhs=QT[po:po + 64, hp, a:QW], start=True, stop=False), [KT.b, QT.b], [bkT.b])
                        e1 = e1r.next()
                        P.op("act", "activation", dict(out=e1[:, a:QW], in_=bkT.f32[:, a:QW], func=AF.Exp), [bkT.b], [e1.b])
                        lg = lgr.next()
                        P.op("act", "activation", dict(out=lg[:, a:QW], in_=e1[:, a:QW], func=AF.Ln, bias=1.0), [e1.b], [lg.b])
                        m = mask_of(kb)
                        if m is not None:
                            a2 = min(QW, a + 128)
                            P.op("dve", "tensor_tensor", dict(out=lg[:, a:a2], in0=lg[:, a:a2], in1=m[:, a:a2], op=ALU.mult),
                                 [lg.b, maskb.b, masksb.b], [lg.b])
                        st[it] = dict(bkT=bkT, lg=lg)

                    def stageA2(it0, it1):
                        pre = []
                        for it in (it0, it1):
                            hp, kb, hh = it
                            po = hh * 64
                            a = col0_of(kb)
                            bkT = pb()
                            P.op("pe", "matmul", dict(out=bkT.f32[:, a:QW], lhsT=KT[po:po + 64, hp, kb * 128:(kb + 1) * 128],
                                                      rhs=QT[po:po + 64, hp, a:QW], start=True, stop=False), [KT.b, QT.b], [bkT.b])
                            pre.append((it, bkT, a))
                        e1s = []
                        for (it, bkT, a) in pre:
                            e1 = e1r.next()
                            P.op("act", "activation", dict(out=e1[:, a:QW], in_=bkT.f32[:, a:QW], func=AF.Exp), [bkT.b], [e1.b])
                            e1s.append(e1)
                        for (it, bkT, a), e1 in zip(pre, e1s):
                            hp, kb, hh = it
                            lg = lgr.next()
                            P.op("act", "activation", dict(out=lg[:, a:QW], in_=e1[:, a:QW], func=AF.Ln, bias=1.0), [e1.b], [lg.b])
                            m = mask_of(kb)
                            if m is not None:
                                a2 = min(QW, a + 128)
                                P.op("dve", "tensor_tensor", dict(out=lg[:, a:a2], in0=lg[:, a:a2], in1=m[:, a:a2], op=ALU.mult),
                                     [lg.b, maskb.b, masksb.b], [lg.b])
                            st[it] = dict(bkT=bkT, lg=lg)

                    def stageB(it):
                        hp, kb, hh = it
                        d = st[it]
                        bkT, lg = d["bkT"], d["lg"]
                        first = (kb == nkb - 1)
                        lastb = (kb == 0)
                        a = col0_of(kb)
                        P.op("pe", "matmul", dict(out=bkT.f32[:, a:QW], lhsT=suin[:], rhs=lg[:, a:QW], start=False, stop=first),
                             [suin.b, lg.b], [bkT.b], inc=first)
                        if not first:
                            Sbf = d["Sbf"] = st[(hp, kb + 1, hh)]["Snext"]
                            P.op("pe", "matmul", dict(out=bkT.f32[:, a:QW], lhsT=nonesb[:], rhs=Sbf[:, a:QW], start=False, stop=True),
                                 [nonesb.b, Sbf.b], [bkT.b])
                        if not lastb:
                            P.op("pe", "matmul", dict(out=bkR[hh].f32[:, a:QW], lhsT=onesb[:], rhs=lg[:, a:QW], start=first,
                                                      stop=(kb == 1)), [onesb.b, lg.b], [bkR[hh].b])
                            Sn = Sbfr[hh].next()
                            P.op("dve", "tensor_copy", dict(out=Sn[:, a:QW], in_=bkR[hh].f32[:, a:QW]), [bkR[hh].b], [Sn.b])
                            an = col0_of(kb - 1)
                            if an < a:
                                P.op("dve", "memset", dict(ap=Sn[:, an:a], constant=0.0), [], [Sn.b])
                            d["Snext"] = Sn
                        wb = wbr.next()
                        P.op("act", "activation", dict(out=wb[:, a:QW], in_=bkT.f32[:, a:QW], func=AF.Exp), [bkT.b], [wb.b])
                        m = mask_of(kb)
                        if m is not None:
                            a2 = min(QW, a + 128)
                            P.op("dve", "tensor_tensor", dict(out=wb[:, a:a2], in0=wb[:, a:a2], in1=m[:, a:a2], op=ALU.mult),
                                 [wb.b, maskb.b, masksb.b], [wb.b])
                        d["wb"] = wb

                    def stageC(it):
                        hp, kb, hh = it
                        d = st[it]
                        po = hh * 64
                        h = hp * 2 + hh
                        a = col0_of(kb)
                        bkO = bkOs[hp % 2]
                        P.op("pe", "matmul", dict(out=bkO.f32[po:po + 64, a:QW], lhsT=Vb[:, kb, h * 64:(h + 1) * 64],
                                                  rhs=d["wb"][:, a:QW], start=(kb == nkb - 1), stop=(kb == 0)),
                             [Vb.b, d["wb"].b], [bkO.b])
                        if kb == 0 and hh == 1:
                            P.op("act", "activation", dict(out=OT[:, hp, :QW], in_=bkO.f32[:, :QW], func=AF.Copy), [bkO.b], [OT.b])

                    n = len(its) // 2
                    for step in range(n + 2):
                        if step < n:
                            stageA2(its[2 * step], its[2 * step + 1])
                        if 0 <= step - 1 < n:
                            stageB(its[2 * (step - 1)])
                            stageB(its[2 * (step - 1) + 1])
                        if 0 <= step - 2 < n:
                            stageC(its[2 * (step - 2)])
                            stageC(its[2 * (step - 2) + 1])
                    for i0 in range(0, QW, 128):
                        nt = min(128, QW - i0)
                        r0 = base + q0 + i0
                        xin = xinr.next()
                        P.dma("sp", xin[:nt, :], src[r0:r0 + nt, :], xin.b, db(src.name, r0), "ld")
                        y = ystr.next()
                        for c in range(2):
                            bk = pb()
                            for hp in range(8):
                                P.op("pe", "matmul", dict(out=bk.f32[:nt, :], lhsT=OT[:, hp, i0:i0 + nt], rhs=Wo[:, hp, c * 512:(c + 1) * 512],
                                                          start=(hp == 0), stop=(hp == 7)), [OT.b, Wo.b], [bk.b], inc=(hp == 7))
                            P.op("dve", "tensor_tensor", dict(out=y[:nt, c * 512:(c + 1) * 512], in0=bk.f32[:nt, :],
                                                              in1=xin[:nt, c * 512:(c + 1) * 512], op=ALU.add), [bk.b, xin.b], [y.b])
                        P.dma("pool", dst[r0:r0 + nt, :], y[:nt, :], db(dst.name, r0), y.b, "st")
            for bk in bkR + bkOs:
                release(bk)
            end_stage()

        src = x_all
        for layer in range(DEPTH):
            j = layer // 2
            last = (layer == DEPTH - 1)
            if only is not None and "mix" not in only:
                stage_copy(src, xs)
            elif layer % 2 == 0:
                stage_ssd1(layer, j, src)
                stage_ssd2(layer, j, src, xs)
            else:
                stage_sb1(layer, j, src)
                stage_sb2(layer, j, src, xs)
            src = xs
            stage_mlp(layer, xs, y_all if last else xs)
        P.barrier()
        P.emit()
    return nc


def _consts():
    i = np.arange(128)
    ident = np.eye(128, dtype=np.float32)
    tri = (i[:, None] <= i[None, :]).astype(np.float32)
    su = (i[:, None] > i[None, :]).astype(np.float32)
    q = np.arange(512)
    mask = np.stack([(i[:, None] + 128 * k < q[None, :]) for k in range(4)], axis=1).astype(np.float32)
    qs = np.arange(32)
    masks = ((i[:, None] < qs[None, :]) & (i[:, None] < 32)).astype(np.float32)
    return dict(c_ident=ident, c_tri=tri, c_su=su, c_mask=np.ascontiguousarray(mask), c_masks=masks)


def make_in_maps(inp, n_cores, NB, SEQ, NS, DSEQ, PAST, DEPTH):
    N_SSD = (DEPTH + 1) // 2
    N_SB = DEPTH // 2
    f = lambda a: np.ascontiguousarray(np.asarray(a, dtype=np.float32))
    shared = dict(
        norm_mix=f(inp["norm_mix"]), norm_mlp=f(inp["norm_mlp"]), ssd_in_w=f(inp["ssd_in_w"]),
        ssd_conv_w=f(np.asarray(inp["ssd_conv_w"]).reshape(N_SSD, 4, 32, 128).transpose(0, 3, 2, 1)),
        ssd_conv_b=f(np.asarray(inp["ssd_conv_b"]).reshape(N_SSD, 32, 128).transpose(0, 2, 1)),
        ssd_dt_bias=f(inp["ssd_dt_bias"]), ssd_a_log=f(inp["ssd_a_log"]), ssd_d=f(inp["ssd_d"]),
        ssd_norm_w=f(inp["ssd_norm_w"]), ssd_out_w=f(inp["ssd_out_w"]),
        mlp_up=f(inp["mlp_up"]), mlp_down=f(inp["mlp_down"]))
    if N_SB > 0:
        shared.update(sb_qkv_w=f(inp["sb_qkv_w"]), sb_q_gain=f(inp["sb_q_gain"]), sb_k_gain=f(inp["sb_k_gain"]),
                      sb_out_w=f(inp["sb_out_w"]))
    else:
        shared.update(sb_qkv_w=np.zeros((1, D, 3 * D), np.float32), sb_q_gain=np.zeros((1, AD), np.float32),
                      sb_k_gain=np.zeros((1, AD), np.float32), sb_out_w=np.zeros((1, D, D), np.float32))
    shared.update(_consts())
    xp = np.asarray(inp["x_prompt"], dtype=np.float32)
    xsm = np.asarray(inp["x_sample"], dtype=np.float32)
    sssm = np.asarray(inp["state_ssm"], dtype=np.float32)
    sconv = np.asarray(inp["state_conv"], dtype=np.float32)
    ck = np.asarray(inp["cache_k"], dtype=np.float32)
    cv = np.asarray(inp["cache_v"], dtype=np.float32)
    maps = []
    for c in range(n_cores):
        m = dict(shared)
        m["x_all"] = f(np.concatenate([xp[c * NB:(c + 1) * NB].reshape(NB * SEQ, D),
                                       xsm[c * NS:(c + 1) * NS].reshape(NS * DSEQ, D)], axis=0))
        m["state_ssm"] = f(sssm[:, c * NS:(c + 1) * NS].reshape(N_SSD, NS, NH * HP, NST))
        m["state_conv"] = f(sconv[:, c * NS:(c + 1) * NS].reshape(N_SSD, NS, 3, 32, 128).transpose(0, 1, 4, 3, 2))
        if N_SB > 0:
            m["cache_k"] = f(ck[:, c * NS:(c + 1) * NS].reshape(N_SB, NS, PAST, D))
            m["cache_v"] = f(cv[:, c * NS:(c + 1) * NS].reshape(N_SB, NS, PAST, D))
        else:
            m["cache_k"] = np.zeros((1, NS, PAST, D), np.float32)
            m["cache_v"] = np.zeros((1, NS, PAST, D), np.float32)
        maps.append(m)
    return maps


def assemble(results, n_cores, NB, SEQ, NS, DSEQ, PAST, DEPTH):
    N_SSD = (DEPTH + 1) // 2
    N_SB = DEPTH // 2
    cat = lambda k, ax: np.concatenate([np.asarray(r[k]) for r in results], axis=ax)
    y_all = [np.asarray(r["y_all"]) for r in results]
    y_p = np.concatenate([y[:NB * SEQ].reshape(NB, SEQ, D) for y in y_all], axis=0)
    y_s = np.concatenate([y[NB * SEQ:].reshape(NS, DSEQ, D) for y in y_all], axis=0)
    pssm = cat("o_pssm", 1).reshape(N_SSD, n_cores * NB, NH, HP, NST)
    sssm = cat("o_sssm", 1).reshape(N_SSD, n_cores * NS, NH, HP, NST)
    pconv = np.ascontiguousarray(cat("o_pconv", 1).transpose(0, 1, 4, 3, 2)).reshape(N_SSD, n_cores * NB, 3, CONVD)
    sconv = np.ascontiguousarray(cat("o_sconv", 1).transpose(0, 1, 4, 3, 2)).reshape(N_SSD, n_cores * NS, 3, CONVD)
    pk = cat("o_pk", 1)[:N_SB].reshape(N_SB, n_cores * NB, SEQ, AH, AD)
    pv = cat("o_pv", 1)[:N_SB].reshape(N_SB, n_cores * NB, SEQ, AH, AD)
    sk = cat("o_sk", 1)[:N_SB].reshape(N_SB, n_cores * NS, DSEQ, AH, AD)
    sv = cat("o_sv", 1)[:N_SB].reshape(N_SB, n_cores * NS, DSEQ, AH, AD)
    outs = (y_p, y_s, pssm, pconv, pk, pv, sssm, sconv, sk, sv)
    return tuple(np.ascontiguousarray(o, dtype=np.float32) for o in outs)


def kernel(**inputs):
    n = 8
    NB, SEQ, NS, DSEQ, PAST, DEPTH = 4, 2048, 2, 32, 1024, 4
    nc = build(NB, SEQ, NS, DSEQ, PAST, DEPTH)
    maps = make_in_maps(inputs, n, NB, SEQ, NS, DSEQ, PAST, DEPTH)
    res = run_bass_kernel_spmd(nc, maps, core_ids=list(range(n)))
    return assemble(res.results, n, NB, SEQ, NS, DSEQ, PAST, DEPTH)
```
